# Optimizing a Trainium2 kernel written in Bass

```python
import math
import jax, jax.numpy as jnp
from jax import lax
import numpy as np

D_MODEL = 2048
BATCH = 4
SEQ = 4096
DEPTH = 2

N_EVEN = (DEPTH + 1) // 2
N_ODD = DEPTH // 2

CONV_CH = 1024
CONV_W = 31
DA_HEADS = 8
DA_HEAD_DIM = 64
DA_V_DIM = 2 * DA_HEAD_DIM
DA_QK = DA_HEADS * 2 * DA_HEAD_DIM
DA_WIDTH = DA_HEADS * DA_V_DIM
ROPE_THETA = 10000.0
Q_BLOCK = 128
SPLIT0 = [CONV_CH, CONV_CH, CONV_CH, DA_QK, DA_QK, DA_WIDTH, DA_WIDTH]
IN0 = sum(SPLIT0)
MIX0 = CONV_CH + DA_WIDTH
S5_WIDTH = D_MODEL
S5_GROUP = 16
S5_GROUPS = S5_WIDTH // S5_GROUP
S5_STATE = 64
S5_CHUNK = 128
EPS = 1e-6

kernel_name = "hybrid_conv_diffattn_s5_gated"


def rms_norm(x, g):
    xf = x.astype(jnp.float32)
    y = xf * lax.rsqrt(jnp.mean(xf * xf, axis=-1, keepdims=True) + EPS)
    return (y * g.astype(jnp.float32)).astype(x.dtype)


def layer_norm(x, g, b):
    xf = x.astype(jnp.float32)
    xc = xf - jnp.mean(xf, axis=-1, keepdims=True)
    var = jnp.mean(xc * xc, axis=-1, keepdims=True)
    return (xc * lax.rsqrt(var + EPS) * g.astype(jnp.float32) + b.astype(jnp.float32)).astype(x.dtype)


def rotary(x, pos):
    half = x.shape[-1] // 2
    freqs = ROPE_THETA ** (-jnp.arange(half, dtype=jnp.float32) / half)
    ang = pos.astype(jnp.float32)[:, None] * freqs[None, :]
    cos = jnp.cos(ang)[None, :, None, None, :]
    sin = jnp.sin(ang)[None, :, None, None, :]
    xf = x.astype(jnp.float32)
    x1, x2 = xf[..., :half], xf[..., half:]
    return jnp.concatenate([x1 * cos - x2 * sin, x2 * cos + x1 * sin], axis=-1)


def diff_attention(q, k, v, lam):
    b_, h_, _, s_, dh = q.shape
    nb = s_ // Q_BLOCK
    scale = dh ** -0.5
    qb = q.reshape(b_, h_, 2, nb, Q_BLOCK, dh).transpose(3, 0, 1, 2, 4, 5)
    k_pos = jnp.arange(s_)

    def block(args):
        qblk, i = args
        s = jnp.einsum('bhcqd,bhckd->bhcqk', qblk, k) * scale
        q_pos = i * Q_BLOCK + jnp.arange(Q_BLOCK)
        mask = k_pos[None, :] <= q_pos[:, None]
        s = jnp.where(mask, s, -jnp.inf)
        p = jax.nn.softmax(s, axis=-1)
        w = p[:, :, 0] - lam * p[:, :, 1]
        return jnp.einsum('bhqk,bhkd->bhqd', w, v)

    o = lax.map(block, (qb, jnp.arange(nb)))
    return o.transpose(1, 0, 3, 2, 4).reshape(b_, s_, h_, 2 * dh)


def even_layer(x, layer_idx, norm_g, w_in, conv_w, conv_b, cln_g, cln_b, qn_g, kn_g,
               lam_q1, lam_k1, lam_q2, lam_k2, subln_g, w_out):
    b_, s_, _ = x.shape
    pos = jnp.arange(s_)
    h = rms_norm(x, norm_g)
    proj = h @ w_in
    idx = [int(v) for v in np.cumsum(SPLIT0)[:-1]]
    a_val, a_glu, a_gate, q, k, v, b_gate = jnp.split(proj, idx, axis=-1)

    u = a_val * jax.nn.sigmoid(a_glu)
    kern = conv_w[:, None, :].astype(u.dtype)
    c = lax.conv_general_dilated(u, kern, window_strides=(1,), padding=[(CONV_W - 1, 0)],
                                 dimension_numbers=('NWC', 'WIO', 'NWC'),
                                 feature_group_count=CONV_CH) + conv_b
    c = jax.nn.silu(layer_norm(c, cln_g, cln_b))
    out_a = c * jax.nn.silu(a_gate)

    q = rotary(rms_norm(q.reshape(b_, s_, DA_HEADS, 2, DA_HEAD_DIM), qn_g), pos)
    k = rotary(rms_norm(k.reshape(b_, s_, DA_HEADS, 2, DA_HEAD_DIM), kn_g), pos)
    q = q.transpose(0, 2, 3, 1, 4)
    k = k.transpose(0, 2, 3, 1, 4)
    v = v.reshape(b_, s_, DA_HEADS, DA_V_DIM).astype(jnp.float32).transpose(0, 2, 1, 3)
    lam_init = 0.8 - 0.6 * math.exp(-0.3 * layer_idx)
    f32 = jnp.float32
    lam = (jnp.exp(jnp.sum(lam_q1.astype(f32) * lam_k1.astype(f32)))
           - jnp.exp(jnp.sum(lam_q2.astype(f32) * lam_k2.astype(f32))) + lam_init)
    o = diff_attention(q, k, v, lam)
    o = rms_norm(o, subln_g) * (1.0 - lam_init)
    out_b = o.reshape(b_, s_, DA_WIDTH).astype(x.dtype) * jax.nn.silu(b_gate)

    y = jnp.concatenate([out_a, out_b], axis=-1) @ w_out
    return x + y.astype(x.dtype)


def s5_scan(u, a_re, a_im, log_dt, b_re, b_im, c_re, c_im):
    f32 = jnp.float32
    a_re, a_im = a_re.astype(f32), a_im.astype(f32)
    b_re, b_im = b_re.astype(f32), b_im.astype(f32)
    c_re, c_im = c_re.astype(f32), c_im.astype(f32)
    b_, s_, g_, m_ = u.shape
    p_ = a_re.shape[-1]
    dt = jnp.exp(log_dt.astype(f32))[:, None]
    mag = jnp.exp(a_re * dt)
    lb_re, lb_im = mag * jnp.cos(a_im * dt), mag * jnp.sin(a_im * dt)
    den = a_re * a_re + a_im * a_im
    nr, ni = lb_re - 1.0, lb_im
    fr = (nr * a_re + ni * a_im) / den
    fi = (ni * a_re - nr * a_im) / den
    bb_re = fr[..., None] * b_re - fi[..., None] * b_im
    bb_im = fr[..., None] * b_im + fi[..., None] * b_re

    nc = s_ // S5_CHUNK
    uc = u.reshape(b_, nc, S5_CHUNK, g_, m_).transpose(1, 0, 2, 3, 4)
    ar = jnp.broadcast_to(lb_re[None, None], (1, S5_CHUNK, g_, p_))
    ai = jnp.broadcast_to(lb_im[None, None], (1, S5_CHUNK, g_, p_))

    def combine(e1, e2):
        ar1, ai1, br1, bi1 = e1
        ar2, ai2, br2, bi2 = e2
        return (ar2 * ar1 - ai2 * ai1, ar2 * ai1 + ai2 * ar1,
                ar2 * br1 - ai2 * bi1 + br2, ar2 * bi1 + ai2 * br1 + bi2)

    def step(carry, u_blk):
        h_re0, h_im0 = carry
        bu_re = jnp.einsum('blgm,gpm->blgp', u_blk, bb_re)
        bu_im = jnp.einsum('blgm,gpm->blgp', u_blk, bb_im)
        pr, pi, sr, si = lax.associative_scan(combine, (ar, ai, bu_re, bu_im), axis=1)
        h_re = sr + pr * h_re0[:, None] - pi * h_im0[:, None]
        h_im = si + pr * h_im0[:, None] + pi * h_re0[:, None]
        y = (jnp.einsum('blgp,gmp->blgm', h_re, c_re)
             - jnp.einsum('blgp,gmp->blgm', h_im, c_im))
        return (h_re[:, -1], h_im[:, -1]), y

    init = (jnp.zeros((b_, g_, p_), f32), jnp.zeros((b_, g_, p_), f32))
    _, ys = lax.scan(step, init, uc)
    return ys.transpose(1, 0, 2, 3, 4).reshape(b_, s_, g_, m_)


def odd_layer(x, norm_g, w_in, a_re, a_im, log_dt, b_re, b_im, c_re, c_im, d_skip,
              w_glu, b_glu, w_out):
    b_, s_, _ = x.shape
    h = rms_norm(x, norm_g)
    u, gate = jnp.split(h @ w_in, [S5_WIDTH], axis=-1)
    uf = u.astype(jnp.float32)
    y = s5_scan(uf.reshape(b_, s_, S5_GROUPS, S5_GROUP), a_re, a_im, log_dt,
                b_re, b_im, c_re, c_im).reshape(b_, s_, S5_WIDTH)
    y = y + d_skip.astype(jnp.float32) * uf
    z = jax.nn.gelu(y)
    z = z * jax.nn.sigmoid(z @ w_glu.astype(jnp.float32) + b_glu.astype(jnp.float32))
    out = z.astype(x.dtype) * jax.nn.silu(gate)
    return x + (out @ w_out).astype(x.dtype)


def setup_inputs(seed: int = 0) -> dict:
    key = jax.random.key(seed)
    ks = jax.random.split(key, 32)
    f32 = jnp.float32

    def nrm(k, shape, scale):
        return jax.random.normal(k, shape, f32) * scale

    ne, no = N_EVEN, N_ODD
    G, P, M, E = S5_GROUPS, S5_STATE, S5_GROUP, S5_WIDTH
    a_im_base = jnp.broadcast_to(math.pi * jnp.arange(P, dtype=f32), (no, G, P))
    return {
        "x": nrm(ks[0], (BATCH, SEQ, D_MODEL), 1.0),
        "e_norm_g": 1.0 + nrm(ks[1], (ne, D_MODEL), 0.02),
        "e_w_in": nrm(ks[2], (ne, D_MODEL, IN0), D_MODEL ** -0.5),
        "e_conv_w": nrm(ks[3], (ne, CONV_W, CONV_CH), CONV_W ** -0.5),
        "e_conv_b": nrm(ks[4], (ne, CONV_CH), 0.02),
        "e_cln_g": 1.0 + nrm(ks[5], (ne, CONV_CH), 0.02),
        "e_cln_b": nrm(ks[6], (ne, CONV_CH), 0.02),
        "e_qn_g": 1.0 + nrm(ks[7], (ne, DA_HEAD_DIM), 0.02),
        "e_kn_g": 1.0 + nrm(ks[8], (ne, DA_HEAD_DIM), 0.02),
        "e_lam_q1": nrm(ks[9], (ne, DA_HEAD_DIM), 0.1),
        "e_lam_k1": nrm(ks[10], (ne, DA_HEAD_DIM), 0.1),
        "e_lam_q2": nrm(ks[11], (ne, DA_HEAD_DIM), 0.1),
        "e_lam_k2": nrm(ks[12], (ne, DA_HEAD_DIM), 0.1),
        "e_subln_g": 1.0 + nrm(ks[13], (ne, DA_V_DIM), 0.02),
        "e_w_out": nrm(ks[14], (ne, MIX0, D_MODEL), MIX0 ** -0.5),
        "o_norm_g": 1.0 + nrm(ks[15], (no, D_MODEL), 0.02),
        "o_w_in": nrm(ks[16], (no, D_MODEL, 2 * E), D_MODEL ** -0.5),
        "o_A_re": -0.5 + nrm(ks[17], (no, G, P), 0.01),
        "o_A_im": a_im_base + nrm(ks[18], (no, G, P), 0.01),
        "o_log_dt": jax.random.uniform(ks[19], (no, G), f32, math.log(1e-3), math.log(1e-1)),
        "o_B_re": nrm(ks[20], (no, G, P, M), (2 * M) ** -0.5),
        "o_B_im": nrm(ks[21], (no, G, P, M), (2 * M) ** -0.5),
        "o_C_re": nrm(ks[22], (no, G, M, P), P ** -0.5),
        "o_C_im": nrm(ks[23], (no, G, M, P), P ** -0.5),
        "o_D": nrm(ks[24], (no, E), 1.0),
        "o_w_glu": nrm(ks[25], (no, E, E), E ** -0.5),
        "o_b_glu": nrm(ks[26], (no, E), 0.02),
        "o_w_out": nrm(ks[27], (no, E, D_MODEL), E ** -0.5),
    }


def reference(x, e_norm_g, e_w_in, e_conv_w, e_conv_b, e_cln_g, e_cln_b, e_qn_g, e_kn_g,
              e_lam_q1, e_lam_k1, e_lam_q2, e_lam_k2, e_subln_g, e_w_out,
              o_norm_g, o_w_in, o_A_re, o_A_im, o_log_dt, o_B_re, o_B_im, o_C_re, o_C_im,
              o_D, o_w_glu, o_b_glu, o_w_out):
    for layer in range(DEPTH):
        j = layer // 2
        if layer % 2 == 0:
            x = even_layer(x, layer, e_norm_g[j], e_w_in[j], e_conv_w[j], e_conv_b[j],
                           e_cln_g[j], e_cln_b[j], e_qn_g[j], e_kn_g[j],
                           e_lam_q1[j], e_lam_k1[j], e_lam_q2[j], e_lam_k2[j],
                           e_subln_g[j], e_w_out[j])
        else:
            x = odd_layer(x, o_norm_g[j], o_w_in[j], o_A_re[j], o_A_im[j], o_log_dt[j],
                          o_B_re[j], o_B_im[j], o_C_re[j], o_C_im[j], o_D[j],
                          o_w_glu[j], o_b_glu[j], o_w_out[j])
    return x
```

```python
import math
from contextlib import ExitStack

import numpy as np
import concourse.bass as bass
import concourse.mybir as mybir
from concourse.bass_utils import run_bass_kernel_spmd

F32 = mybir.dt.float32
BF16 = mybir.dt.bfloat16
I32 = mybir.dt.int32
AF = mybir.ActivationFunctionType
ALU = mybir.AluOpType

S = 4096
D = 2048
NB = 8
EPS = 1e-6
NCORES = 4


class Buf:
    def __init__(self, name, t, persist=False):
        self.name = name
        self.t = t
        self.w = {}
        self.r = {}
        self.ext = []
        self.persist = persist

    def __getitem__(self, idx):
        return self.t[idx]


class Op:
    __slots__ = ("eng", "fn", "rd", "wr", "is_dma", "key", "deps", "ext", "signal", "sem", "val")

    def __init__(self, eng, fn, rd, wr, is_dma=False, key=None):
        self.eng = eng
        self.fn = fn
        self.rd = rd
        self.wr = wr
        self.is_dma = is_dma
        self.key = key
        self.deps = ()
        self.ext = []
        self.signal = False
        self.sem = None
        self.val = 0


class Rot:
    def __init__(self, bufs):
        self.bufs = bufs
        self.i = 0

    def next(self):
        b = self.bufs[self.i % len(self.bufs)]
        self.i += 1
        return b


class KB:
    def __init__(self, nc):
        self.nc = nc
        self.es = ExitStack()
        self.eng = {"pe": nc.tensor, "act": nc.scalar, "dve": nc.vector, "pool": nc.gpsimd, "sp": nc.sync}
        self.esem = {k: self.es.enter_context(nc.semaphore("s_" + k)) for k in self.eng}
        self.ecnt = {k: 0 for k in self.eng}
        self.seen = {k: {} for k in self.eng}
        self.dsem = {}
        self.deferred = set()
        self.ops = []
        self.bufs = []
        self.nins = {k: 0 for k in self.eng}

    def sb(self, stack, name, shape, dtype, persist=False):
        self.uid = getattr(self, "uid", 0) + 1
        name = f"{name}_{self.uid}"
        t = stack.enter_context(self.nc.sbuf_tensor(name, list(shape), dtype))
        b = Buf(name, t, persist)
        self.bufs.append(b)
        return b

    def ps(self, stack, name, shape, dtype):
        self.uid = getattr(self, "uid", 0) + 1
        name = f"{name}_{self.uid}"
        t = stack.enter_context(self.nc.psum_tensor(name, list(shape), dtype))
        b = Buf(name, t)
        self.bufs.append(b)
        return b

    def dram(self, name, shape, dtype, kind="Internal", persist=False):
        t = self.nc.dram_tensor(name, list(shape), dtype, kind=kind).ap()
        b = Buf(name, t, persist)
        self.bufs.append(b)
        return b

    def op(self, eng, fn, rd=(), wr=()):
        self.ops.append(Op(eng, fn, list(rd), list(wr)))

    def dma(self, q, out, in_, rd=(), wr=(), key=None, defer=False, **kw):
        h = self.eng[q]
        if key is None:
            key = (wr[0].name if wr else rd[0].name + "_st")
        if key not in self.dsem:
            self.dsem[key] = [self.es.enter_context(self.nc.semaphore("d_" + key)), 0]
        if defer:
            self.deferred.add(key)
        self.ops.append(Op(q, lambda: h.dma_start(out=out, in_=in_, **kw), list(rd), list(wr), True, key))

    def flush(self, barrier=True):
        ops = self.ops
        for i, op in enumerate(ops):
            deps = set()
            ext = []
            for b in op.rd:
                deps.update(b.w.values())
                ext.extend(b.ext)
            for b in op.wr:
                deps.update(b.w.values())
                deps.update(b.r.values())
                ext.extend(b.ext)
            if op.eng == "pe" and not op.is_dma:
                deps = {d for d in deps if ops[d].is_dma or ops[d].eng != "pe"}
            deps.discard(i)
            op.deps = sorted(deps)
            op.ext = ext
            k = ("dma", op.key) if op.is_dma else op.eng
            for b in op.wr:
                b.w[k] = i
            for b in op.rd:
                b.r[k] = i
            for d in deps:
                ops[d].signal = True
        last = {}
        for i, op in enumerate(ops):
            if not op.is_dma:
                last[op.eng] = i
        for i in last.values():
            ops[i].signal = True
        for op in ops:
            e = op.eng
            h = self.eng[e]
            need = [(ops[d].sem, ops[d].val) for d in op.deps] + list(op.ext)
            for sem, val in need:
                sk = id(sem)
                if self.seen[e].get(sk, 0) < val:
                    h.wait_ge(sem, val)
                    self.seen[e][sk] = val
            ins = op.fn()
            self.nins[e] += 1
            if op.is_dma:
                ent = self.dsem[op.key]
                ent[1] += 16
                ins.then_inc(ent[0], 16)
                op.sem, op.val = ent[0], ent[1]
            elif op.signal:
                self.ecnt[e] += 1
                ins.then_inc(self.esem[e], 1)
                op.sem, op.val = self.esem[e], self.ecnt[e]
        for b in self.bufs:
            if b.persist:
                for d in list(b.w.values()):
                    b.ext.append((ops[d].sem, ops[d].val))
            b.w = {}
            b.r = {}
        self.ops = []
        if barrier:
            for e, h in self.eng.items():
                for e2 in self.eng:
                    if e2 != e and self.ecnt[e2] > self.seen[e].get(id(self.esem[e2]), 0):
                        h.wait_ge(self.esem[e2], self.ecnt[e2])
                        self.seen[e][id(self.esem[e2])] = self.ecnt[e2]
                for key, (sem, cnt) in self.dsem.items():
                    if key in self.deferred:
                        continue
                    if cnt > self.seen[e].get(id(sem), 0):
                        h.wait_ge(sem, cnt)
                        self.seen[e][id(sem)] = cnt

    def final_wait(self):
        for e, h in self.eng.items():
            for key, (sem, cnt) in self.dsem.items():
                if cnt > self.seen[e].get(id(sem), 0):
                    h.wait_ge(sem, cnt)
                    self.seen[e][id(sem)] = cnt


def bc(ap, shape):
    return ap.to_broadcast(list(shape))


PARAMS = [
    ("e_norm_g", [D]), ("e_w_in", [D, 7168]), ("e_conv_w", [31, 1024]), ("e_conv_b", [1024]),
    ("e_cln_g", [1024]), ("e_cln_b", [1024]), ("e_qn_g", [64]), ("e_kn_g", [64]),
    ("e_lam_q1", [64]), ("e_lam_k1", [64]), ("e_lam_q2", [64]), ("e_lam_k2", [64]),
    ("e_subln_g", [128]), ("e_w_out", [D, D]),
    ("o_norm_g", [D]), ("o_w_in", [D, 2 * D]), ("o_A_re", [128, 64]), ("o_A_im", [128, 64]),
    ("o_log_dt", [128]), ("o_B_re", [128, 64, 16]), ("o_B_im", [128, 64, 16]),
    ("o_C_re", [128, 16, 64]), ("o_C_im", [128, 16, 64]), ("o_D", [D]),
    ("o_w_glu", [D, D]), ("o_b_glu", [D]), ("o_w_out", [D, D]),
]


def build(dbg=(), stop_after=None):
    nc = bass.Bass("TRN2", target_bir_lowering=False)
    kb = KB(nc)
    eng = kb.eng
    pe, act_, dve, pool = eng["pe"], eng["act"], eng["dve"], eng["pool"]

    def kind(name):
        return "ExternalOutput" if name in dbg else "Internal"

    din = {"x": nc.dram_tensor("x", [S, D], F32, kind="ExternalInput").ap()}
    for n, shp in PARAMS:
        din[n] = nc.dram_tensor(n, shp, F32, kind="ExternalInput").ap()
    out_d = nc.dram_tensor("out", [S, D], F32, kind="ExternalOutput").ap()

    Wb_in0 = kb.dram("Wb_in0", [14, 128, 16, 512], BF16, persist=True)
    Wb_out0 = kb.dram("Wb_out0", [4, 128, 16, 512], BF16, persist=True)
    Wb_in1 = kb.dram("Wb_in1", [8, 128, 16, 512], BF16, persist=True)
    Wb_glu = kb.dram("Wb_glu", [4, 128, 16, 512], BF16, persist=True)
    Wb_out1 = kb.dram("Wb_out1", [4, 128, 16, 512], BF16, persist=True)
    ROT = kb.dram("ROT", [2, 128, S], F32, kind=kind("ROT"))
    QT = kb.dram("QT", [8, 128, S], BF16, kind=kind("QT"))
    KT = kb.dram("KT", [8, 128, S], BF16, kind=kind("KT"))
    VV = kb.dram("VV", [8, 128, 32, 128], BF16, kind=kind("VV"))
    GB = kb.dram("GB", [8, 128, S], BF16, kind=kind("GB"))
    MIXT = kb.dram("MIXT", [16, 128, S], BF16, kind=kind("MIXT"))
    X1 = kb.dram("X1", [S, D], F32, kind=kind("X1"))

    def act(out, in_, func, rd, wr, bias=None, scale=None, accum=None):
        kw = {}
        if bias is not None:
            kw["bias"] = bias
        if scale is not None:
            kw["scale"] = scale
        if accum is not None:
            kw["accum_out"] = accum
        kb.op("act", lambda: act_.activation(out=out, in_=in_, func=func, **kw), rd, wr)

    def tt(e, out, in0, in1, op, rd, wr):
        h = eng[e]
        kb.op(e, lambda: h.tensor_tensor(out=out, in0=in0, in1=in1, op=op), rd, wr)

    def ts(e, out, in0, s1, s2, op0, op1, rd, wr):
        h = eng[e]
        if op1 is None:
            kb.op(e, lambda: h.tensor_scalar(out=out, in0=in0, scalar1=s1, scalar2=None, op0=op0), rd, wr)
        else:
            kb.op(e, lambda: h.tensor_scalar(out=out, in0=in0, scalar1=s1, scalar2=s2, op0=op0, op1=op1), rd, wr)

    def stt(out, in0, scalar, in1, op0, op1, rd, wr):
        kb.op("dve", lambda: dve.scalar_tensor_tensor(out=out, in0=in0, scalar=scalar, in1=in1, op0=op0, op1=op1), rd, wr)

    def mm(out, lhsT, rhs, start, stop, rd, wr):
        kb.op("pe", lambda: pe.matmul(out, lhsT=lhsT, rhs=rhs, start=start, stop=stop), rd, wr)

    def recip(out, in_, rd, wr):
        kb.op("dve", lambda: dve.reciprocal(out=out, in_=in_), rd, wr)

    def copy(e, out, in_, rd, wr):
        h = eng[e]
        kb.op(e, lambda: h.tensor_copy(out=out, in_=in_), rd, wr)

    def memset(e, ap, val, wr):
        h = eng[e]
        kb.op(e, lambda: h.memset(ap, val), (), wr)

    cs = kb.es

    def conv_w(dst, src, ng, key):
        v = src.rearrange("(c p) (g n) -> g p c n", p=128, n=512)
        for g in range(ng):
            kb.dma("pool", dst.t[g], v[g], wr=[dst], key=key, defer=True)

    conv_w(Wb_in0, din["e_w_in"], 14, "wc0")

    ident_f = kb.sb(cs, "ident_f", [128, 128], F32)
    ident_b = kb.sb(cs, "ident_b", [128, 128], BF16)
    ones_b = kb.sb(cs, "ones_b", [128, 128], BF16)
    ones_f = kb.sb(cs, "ones_f", [128, 128], F32)
    avg1024 = kb.sb(cs, "avg1024", [128, 128], F32)
    avg128 = kb.sb(cs, "avg128", [128, 128], F32)
    bd64 = kb.sb(cs, "bd64", [128, 128], BF16)
    Pm = kb.sb(cs, "Pm", [128, 128], BF16)
    M4 = kb.sb(cs, "M4", [128, 4, 512], BF16)
    eps_t = kb.sb(cs, "eps_t", [128, 1], F32)
    gn0 = kb.sb(cs, "gn0", [128, 16], F32)
    gn1 = kb.sb(cs, "gn1", [128, 16], F32)
    kw_t = kb.sb(cs, "kw_t", [128, 8, 31], F32)
    kw_b = kb.sb(cs, "kw_b", [128, 8, 31], BF16)
    convb = kb.sb(cs, "convb", [128, 8], F32)
    clng = kb.sb(cs, "clng", [128, 8], F32)
    clnb = kb.sb(cs, "clnb", [128, 8], F32)
    qng = kb.sb(cs, "qng", [128, 1], F32)
    kng = kb.sb(cs, "kng", [128, 1], F32)
    sgs = kb.sb(cs, "sgs", [128, 1], F32)
    neglam = kb.sb(cs, "neglam", [128, 1], F32)
    bglu = kb.sb(cs, "bglu", [128, 16], F32)
    sgn = kb.sb(cs, "sgn", [128, 1], F32)

    with ExitStack() as st:
        iota_i = kb.sb(st, "iota_i", [128, 128], I32)
        iota_f = kb.sb(st, "iota_f", [128, 128], F32)
        pidx_i = kb.sb(st, "pidx_i", [128, 1], I32)
        ptmp_i = kb.sb(st, "ptmp_i", [128, 1], I32)
        m_hi = kb.sb(st, "m_hi", [128, 1], F32)
        m_lo = kb.sb(st, "m_lo", [128, 1], F32)
        Am = kb.sb(st, "Am", [128, 128], F32)
        Bm = kb.sb(st, "Bm", [128, 128], F32)
        Pf = kb.sb(st, "Pf", [128, 128], F32)
        ones512 = kb.sb(st, "ones512", [128, 512], BF16)
        freq = kb.sb(st, "freq", [128, 1], F32)
        halfpi = kb.sb(st, "halfpi", [128, 1], F32)
        cn = kb.sb(st, "cn", [128, 1], F32)
        sn = kb.sb(st, "sn", [128, 1], F32)
        nsn = kb.sb(st, "nsn", [128, 1], F32)
        tq = kb.sb(st, "tq", [128, 1], F32)
        COS = kb.sb(st, "COS", [128, S], F32)
        SIN = kb.sb(st, "SIN", [128, S], F32)
        T1 = kb.sb(st, "T1", [128, S // 2], F32)
        lamv = kb.sb(st, "lamv", [64, 4], F32)
        prod = kb.sb(st, "prod", [64, 2], F32)
        lps = kb.ps(st, "lps", [128, 512], F32)
        e2 = kb.sb(st, "e2", [128, 2], F32)
        sgl_ = kb.sb(st, "sgl_", [128, 1], F32)

        kb.op("pool", lambda: pool.iota(iota_i[:], pattern=[[1, 128]], base=0, channel_multiplier=-1), (), [iota_i])
        kb.op("pool", lambda: pool.iota(pidx_i[:], pattern=[[0, 1]], base=0, channel_multiplier=1), (), [pidx_i])
        copy("dve", iota_f[:], iota_i[:], [iota_i], [iota_f])
        kb.op("dve", lambda: dve.tensor_single_scalar(out=ident_f[:], in_=iota_f[:], scalar=0.0, op=ALU.is_equal), [iota_f], [ident_f])
        copy("dve", ident_b[:], ident_f[:], [ident_f], [ident_b])
        kb.op("dve", lambda: dve.tensor_single_scalar(out=Am[:], in_=iota_f[:], scalar=32.0, op=ALU.is_equal), [iota_f], [Am])
        kb.op("dve", lambda: dve.tensor_single_scalar(out=Bm[:], in_=iota_f[:], scalar=-32.0, op=ALU.is_equal), [iota_f], [Bm])
        kb.op("dve", lambda: dve.tensor_single_scalar(out=ptmp_i[:], in_=pidx_i[:], scalar=32, op=ALU.bitwise_and), [pidx_i], [ptmp_i])
        copy("dve", m_hi[:], ptmp_i[:], [ptmp_i], [m_hi])
        ts("dve", m_hi[:], m_hi[:], 1.0 / 32.0, None, ALU.mult, None, [m_hi], [m_hi])
        ts("dve", m_lo[:], m_hi[:], -1.0, 1.0, ALU.mult, ALU.add, [m_hi], [m_lo])
        ts("dve", sgn[:], m_hi[:], 2.0, -1.0, ALU.mult, ALU.add, [m_hi], [sgn])
        ts("dve", Pf[:], Am[:], m_lo[:, 0:1], None, ALU.mult, None, [Am, m_lo], [Pf])
        stt(Pm[:], Bm[:], m_hi[:, 0:1], Pf[:], ALU.mult, ALU.add, [Bm, m_hi, Pf], [Pm])
        memset("pool", ones_b[:], 1.0, [ones_b])
        memset("pool", ones_f[:], 1.0, [ones_f])
        memset("pool", avg1024[:], 1.0 / 1024.0, [avg1024])
        memset("pool", avg128[:], 1.0 / 128.0, [avg128])
        memset("pool", bd64[:], 0.0, [bd64])
        memset("pool", bd64[0:64, 0:64], 1.0 / 64.0, [bd64])
        memset("pool", bd64[64:128, 64:128], 1.0 / 64.0, [bd64])
        memset("pool", eps_t[:], EPS, [eps_t])
        memset("pool", halfpi[:], math.pi / 2, [halfpi])
        memset("pool", ones512[:], 1.0, [ones512])
        for o in range(4):
            kb.op("pool", lambda o=o: pool.affine_select(out=M4[:, o, :], in_=ones512[:], pattern=[[1, 512]],
                                                          compare_op=ALU.is_ge, fill=0.0, base=-128 * o,
                                                          channel_multiplier=-1), [ones512], [M4])
        def ld(b, dst, src):
            kb.dma("sp", dst, src, wr=[b], key="small", allow_slow_non_contiguous=True)

        ld(gn0, gn0[:], din["e_norm_g"].rearrange("(c p) -> p c", p=128))
        ld(gn1, gn1[:], din["o_norm_g"].rearrange("(c p) -> p c", p=128))
        for t_ in range(8):
            ld(kw_t, kw_t[:, t_, :], din["e_conv_w"][:, t_ * 128:(t_ + 1) * 128].rearrange("w p -> p w"))
        ld(convb, convb[:], din["e_conv_b"].rearrange("(t p) -> p t", p=128))
        ld(clng, clng[:], din["e_cln_g"].rearrange("(t p) -> p t", p=128))
        ld(clnb, clnb[:], din["e_cln_b"].rearrange("(t p) -> p t", p=128))
        ld(bglu, bglu[:], din["o_b_glu"].rearrange("(t p) -> p t", p=128))
        for hh in range(2):
            ld(qng, qng[hh * 64:(hh + 1) * 64, :], din["e_qn_g"].rearrange("(p o) -> p o", o=1))
            ld(kng, kng[hh * 64:(hh + 1) * 64, :], din["e_kn_g"].rearrange("(p o) -> p o", o=1))
        ld(sgl_, sgl_[:], din["e_subln_g"].rearrange("(p o) -> p o", o=1))
        for i, nme in enumerate(["e_lam_q1", "e_lam_k1", "e_lam_q2", "e_lam_k2"]):
            ld(lamv, lamv[:, i:i + 1], din[nme].rearrange("(p o) -> p o", o=1))
        copy("dve", kw_b[:], kw_t[:], [kw_t], [kw_b])
        lam_init = 0.8 - 0.6 * math.exp(-0.3 * 0)
        ts("dve", sgs[:], sgl_[:], 1.0 - lam_init, None, ALU.mult, None, [sgl_], [sgs])
        tt("dve", prod[:, 0:1], lamv[:, 0:1], lamv[:, 1:2], ALU.mult, [lamv], [prod])
        tt("dve", prod[:, 1:2], lamv[:, 2:3], lamv[:, 3:4], ALU.mult, [lamv], [prod])
        mm(lps[:, 0:2], ones_f[0:64, :], prod[:, :], True, True, [ones_f, prod], [lps])
        act(e2[:], lps[:, 0:2], AF.Exp, [lps], [e2])
        tt("dve", neglam[:], e2[:, 1:2], e2[:, 0:1], ALU.subtract, [e2], [neglam])
        ts("dve", neglam[:], neglam[:], -lam_init, None, ALU.add, None, [neglam], [neglam])

        kb.op("dve", lambda: dve.tensor_single_scalar(out=ptmp_i[:], in_=pidx_i[:], scalar=31, op=ALU.bitwise_and), [pidx_i, m_hi], [ptmp_i])
        copy("dve", freq[:], ptmp_i[:], [ptmp_i], [freq])
        act(freq[:], freq[:], AF.Exp, [freq], [freq], scale=-math.log(10000.0) / 32.0)
        act(cn[:], freq[:], AF.Sin, [freq, halfpi], [cn], bias=halfpi[:, 0:1])
        act(sn[:], freq[:], AF.Sin, [freq], [sn])
        ts("dve", nsn[:], sn[:], -1.0, None, ALU.mult, None, [sn], [nsn])
        memset("dve", COS[:, 0:1], 1.0, [COS])
        memset("dve", SIN[:, 0:1], 0.0, [SIN])
        n = 1
        while n < S:
            ts("dve", T1[:, 0:n], COS[:, 0:n], cn[:, 0:1], None, ALU.mult, None, [COS, cn], [T1])
            stt(COS[:, n:2 * n], SIN[:, 0:n], nsn[:, 0:1], T1[:, 0:n], ALU.mult, ALU.add, [SIN, nsn, T1, COS], [COS])
            ts("dve", T1[:, 0:n], SIN[:, 0:n], cn[:, 0:1], None, ALU.mult, None, [SIN, cn, COS], [T1])
            stt(SIN[:, n:2 * n], COS[:, 0:n], sn[:, 0:1], T1[:, 0:n], ALU.mult, ALU.add, [COS, sn, T1, SIN], [SIN])
            if 2 * n < S:
                ts("dve", tq[:], cn[:], cn[:, 0:1], None, ALU.mult, None, [cn, SIN], [tq])
                stt(tq[:], sn[:], nsn[:, 0:1], tq[:], ALU.mult, ALU.add, [sn, nsn, tq], [tq])
                ts("dve", sn[:], cn[:], sn[:, 0:1], 2.0, ALU.mult, ALU.mult, [cn, sn], [sn])
                copy("dve", cn[:], tq[:], [tq, sn], [cn])
                ts("dve", nsn[:], sn[:], -1.0, None, ALU.mult, None, [sn], [nsn])
            n *= 2
        ts("dve", SIN[:], SIN[:], sgn[:, 0:1], None, ALU.mult, None, [SIN, sgn], [SIN])
        kb.dma("sp", ROT.t[0], COS[:], rd=[COS], wr=[ROT], key="rot_st")
        kb.dma("sp", ROT.t[1], SIN[:], rd=[SIN], wr=[ROT], key="rot_st")
        conv_w(Wb_out0, din["e_w_out"], 4, "wc1")
        conv_w(Wb_in1, din["o_w_in"], 8, "wc2")
        conv_w(Wb_glu, din["o_w_glu"], 4, "wc3")
        conv_w(Wb_out1, din["o_w_out"], 4, "wc4")
        kb.flush()

    if stop_after == "setup":
        kb.final_wait()
        return nc

    x = din["x"]
    with ExitStack() as st:
        hT = kb.sb(st, "hT", [128, 16, 512], BF16)
        wgs = Rot([kb.sb(st, f"wg{i}", [128, 16, 512], BF16) for i in range(2)])
        rcs = Rot([kb.sb(st, f"rc{i}", [128, 2, 512], F32) for i in range(2)])
        xts = Rot([kb.sb(st, f"xt{i}", [128, D], F32) for i in range(2)])
        hn = kb.sb(st, "hn", [128, D], BF16)
        junk = kb.sb(st, "junk", [128, D], BF16)
        ss = kb.sb(st, "ss", [128, 1], F32)
        sd1 = kb.sb(st, "sd1", [128, 1], F32)
        rs1 = kb.sb(st, "rs1", [128, 1], F32)
        u = kb.sb(st, "u", [128, 8, 542], BF16)
        sgl = kb.sb(st, "sgl", [128, 4, 512], BF16)
        sga = kb.sb(st, "sga", [128, 8, 512], BF16)
        cc = kb.sb(st, "cc", [128, 8, 512], F32)
        csqs = Rot([kb.sb(st, f"csq{i}", [128, 512], F32) for i in range(2)])
        DG = kb.sb(st, "DG", [128, 31, 128], BF16)
        oa = kb.sb(st, "oa", [128, 8, 512], BF16)
        mean_sb = kb.sb(st, "mean_sb", [128, 512], F32)
        var = kb.sb(st, "var", [128, 512], F32)
        rsl = kb.sb(st, "rsl", [128, 512], F32)
        tln = Rot([kb.sb(st, f"tln{i}", [128, 512], F32) for i in range(2)])
        aln = Rot([kb.sb(st, f"aln{i}", [128, 512], BF16) for i in range(2)])
        qgs = Rot([kb.sb(st, f"qg{i}", [128, 512], BF16) for i in range(2)])
        sqs = Rot([kb.sb(st, f"sq{i}", [128, 512], BF16) for i in range(2)])
        rsq = kb.sb(st, "rsq", [128, 512], F32)
        t1 = kb.sb(st, "t1", [128, 512], F32)
        t2 = kb.sb(st, "t2", [128, 512], F32)
        qos = Rot([kb.sb(st, f"qo{i}", [128, 512], BF16) for i in range(2)])
        vts = Rot([kb.sb(st, f"vt{i}", [128, 512], BF16) for i in range(2)])
        gbts = Rot([kb.sb(st, f"gbt{i}", [128, 512], BF16) for i in range(2)])
        pss = Rot([kb.ps(st, f"ps{i}", [128, 512], F32) for i in range(4)])
        ptrs = Rot([kb.ps(st, f"ptr{i}", [128, 4, 128], BF16) for i in range(2)])
        mps = kb.ps(st, "mps", [128, 512], F32)
        qps = kb.ps(st, "qps", [128, 512], F32)

        memset("pool", u[:, :, 0:30], 0.0, [u])

        def load_wg(src, gi):
            wg = wgs.next()
            kb.dma("sp", wg[:], src.t[gi], rd=[src], wr=[wg])
            return wg

        def fm_tile(wg, j, hTb):
            ps = pss.next()
            for c in range(16):
                mm(ps[:], wg[:, c, j * 128:(j + 1) * 128], hTb[:, c, :], c == 0, c == 15, [wg, hTb], [ps])
            return ps

        def norm_transpose(xt, gn, hTb, col_ap_fn):
            act(junk[:], xt[:], AF.Square, [xt], [junk, ss], accum=ss[:, 0:1])
            act(sd1[:], ss[:], AF.Sqrt, [ss, eps_t], [sd1], bias=eps_t[:, 0:1], scale=1.0 / D)
            recip(rs1[:], sd1[:], [sd1], [rs1])
            act(hn[:], xt[:], AF.Copy, [xt, rs1], [hn], scale=rs1[:, 0:1])
            for c4 in range(4):
                ptr = ptrs.next()
                for q_ in range(4):
                    c = c4 * 4 + q_
                    kb.op("pe", lambda c=c, q_=q_, ptr=ptr: pe.transpose(out=ptr[:, q_, :], in_=hn[:, c * 128:(c + 1) * 128],
                                                                           identity=ident_b[:]), [hn, ident_b], [ptr])
                tt("dve", col_ap_fn(hTb, c4), ptr[:], bc(gn[:, c4 * 4:c4 * 4 + 4].unsqueeze(2), [128, 4, 128]), ALU.mult,
                   [ptr, gn], [hTb])

        for bi in range(NB):
            t0 = bi * 512
            rc = rcs.next()
            kb.dma("sp", rc[:], ROT.t[:, :, t0:t0 + 512].rearrange("a p t -> p a t"), rd=[ROT], wr=[rc])
            for tti in range(4):
                xt = xts.next()
                kb.dma("sp", xt[:], x[t0 + tti * 128:t0 + (tti + 1) * 128, :], wr=[xt])
                norm_transpose(xt, gn0, hT, lambda hTb, c4, tti=tti: hTb[:, c4 * 4:c4 * 4 + 4, tti * 128:(tti + 1) * 128])

            for half in range(2):
                wg = load_wg(Wb_in0, 2 + half)
                for j in range(4):
                    ps = fm_tile(wg, j, hT)
                    act(sgl[:, j, :], ps[:], AF.Sigmoid, [ps], [sgl])
                wg = load_wg(Wb_in0, 0 + half)
                for j in range(4):
                    jj = half * 4 + j
                    ps = fm_tile(wg, j, hT)
                    tt("dve", u[:, jj, 30:542], ps[:], sgl[:, j, :], ALU.mult, [ps, sgl], [u])
                    tt("pool", DG[:], bc(ident_b[:].unsqueeze(1), [128, 31, 128]),
                       bc(kw_b[:, jj, :].unsqueeze(2), [128, 31, 128]), ALU.mult, [ident_b, kw_b], [DG])
                    cps = pss.next()
                    for tap in range(31):
                        mm(cps[:], DG[:, tap, :], u[:, jj, tap:tap + 512], tap == 0, tap == 30, [DG, u], [cps])
                    act(cc[:, jj, :], cps[:], AF.Identity, [cps, convb], [cc], bias=convb[:, jj:jj + 1])
                    csq = csqs.next()
                    act(csq[:], cps[:], AF.Square, [cps, convb], [csq], bias=convb[:, jj:jj + 1])
                    mm(mps[:], avg1024[:], cc[:, jj, :], jj == 0, jj == 7, [avg1024, cc], [mps])
                    mm(qps[:], avg1024[:], csq[:], jj == 0, jj == 7, [avg1024, csq], [qps])
            copy("pool", u[:, :, 0:30], u[:, :, 512:542], [u], [u])
            for half in range(2):
                wg = load_wg(Wb_in0, 4 + half)
                for j in range(4):
                    ps = fm_tile(wg, j, hT)
                    act(sga[:, half * 4 + j, :], ps[:], AF.Silu, [ps], [sga])
            act(mean_sb[:], mps[:], AF.Copy, [mps], [mean_sb])
            tt("dve", var[:], mean_sb[:], mean_sb[:], ALU.mult, [mean_sb], [var])
            tt("dve", var[:], qps[:], var[:], ALU.subtract, [qps, var], [var])
            act(var[:], var[:], AF.Sqrt, [var, eps_t], [var], bias=eps_t[:, 0:1])
            recip(rsl[:], var[:], [var], [rsl])
            for jj in range(8):
                tl = tln.next()
                al = aln.next()
                tt("dve", tl[:], cc[:, jj, :], mean_sb[:], ALU.subtract, [cc, mean_sb], [tl])
                tt("dve", tl[:], tl[:], rsl[:], ALU.mult, [tl, rsl], [tl])
                act(al[:], tl[:], AF.Silu, [tl, clng, clnb], [al], scale=clng[:, jj:jj + 1], bias=clnb[:, jj:jj + 1])
                tt("pool", oa[:, jj, :], al[:], sga[:, jj, :], ALU.mult, [al, sga], [oa])
            kb.dma("pool", MIXT.t[0:8, :, t0:t0 + 512].rearrange("j p t -> p j t"), oa[:], rd=[oa], wr=[MIXT])

            for gi in (6, 7, 8, 9):
                wg = load_wg(Wb_in0, gi)
                isq = gi < 8
                gvec = qng if isq else kng
                dst = QT if isq else KT
                for j in range(4):
                    hh = (gi % 2) * 4 + j
                    ps = fm_tile(wg, j, hT)
                    qg = qgs.next()
                    sq = sqs.next()
                    qo = qos.next()
                    act(qg[:], ps[:], AF.Copy, [ps, gvec], [qg], scale=gvec[:, 0:1])
                    act(sq[:], ps[:], AF.Square, [ps], [sq])
                    stp = pss.next()
                    mm(stp[:], bd64[:], sq[:], True, True, [bd64, sq], [stp])
                    rtp = pss.next()
                    mm(rtp[:], Pm[:], qg[:], True, True, [Pm, qg], [rtp])
                    act(rsq[:], stp[:], AF.Sqrt, [stp, eps_t], [rsq], bias=eps_t[:, 0:1])
                    recip(rsq[:], rsq[:], [rsq], [rsq])
                    tt("pool", t1[:], qg[:], rc[:, 0, :], ALU.mult, [qg, rc], [t1])
                    tt("dve", t2[:], rtp[:], rc[:, 1, :], ALU.mult, [rtp, rc], [t2])
                    tt("pool", t1[:], t1[:], t2[:], ALU.add, [t1, t2], [t1])
                    stt(qo[:], t1[:], 0.125 if isq else 1.0, rsq[:], ALU.mult, ALU.mult, [t1, rsq], [qo])
                    kb.dma("pool", dst.t[hh, :, t0:t0 + 512], qo[:], rd=[qo], wr=[dst])
            for gi in (10, 11):
                wg = load_wg(Wb_in0, gi)
                for tti in range(4):
                    ps = pss.next()
                    for c in range(16):
                        mm(ps[:], hT[:, c, tti * 128:(tti + 1) * 128], wg[:, c, :], c == 0, c == 15, [wg, hT], [ps])
                    vt = vts.next()
                    act(vt[:], ps[:], AF.Copy, [ps], [vt])
                    h0 = (gi - 10) * 4
                    kb.dma("pool", VV.t[h0:h0 + 4, :, bi * 4 + tti, :].rearrange("h p d -> p h d"),
                           vt[:].rearrange("p (h d) -> p h d", h=4), rd=[vt], wr=[VV])
            for gi in (12, 13):
                wg = load_wg(Wb_in0, gi)
                for j in range(4):
                    hh = (gi - 12) * 4 + j
                    ps = fm_tile(wg, j, hT)
                    gbt = gbts.next()
                    act(gbt[:], ps[:], AF.Silu, [ps], [gbt])
                    kb.dma("pool", GB.t[hh, :, t0:t0 + 512], gbt[:], rd=[gbt], wr=[GB])
        kb.flush()

    if stop_after == "l0p":
        kb.final_wait()
        return nc

    with ExitStack() as st:
        kTs = Rot([kb.sb(st, f"kT{i}", [128, S], BF16) for i in range(2)])
        qTs = Rot([kb.sb(st, f"qT{i}", [128, S], BF16) for i in range(2)])
        vhs = Rot([kb.sb(st, f"vh{i}", [128, 32, 128], BF16) for i in range(2)])
        gbs = Rot([kb.sb(st, f"gbh{i}", [128, S], BF16) for i in range(2)])
        e1s = Rot([kb.sb(st, f"e1_{i}", [128, 512], BF16) for i in range(3)])
        e2s = Rot([kb.sb(st, f"e2_{i}", [128, 512], BF16) for i in range(3)])
        pscore = Rot([kb.ps(st, f"psc{i}", [128, 512], F32) for i in range(4)])
        o1 = kb.ps(st, "o1", [128, 512], F32)
        d1 = kb.ps(st, "d1", [128, 512], F32)
        o2 = kb.ps(st, "o2", [128, 512], F32)
        d2 = kb.ps(st, "d2", [128, 512], F32)
        r1 = kb.sb(st, "r1", [128, 512], F32)
        ta = kb.sb(st, "ta", [128, 512], F32)
        tb = kb.sb(st, "tb", [128, 512], F32)
        od = kb.sb(st, "od", [128, 512], F32)
        osq = kb.sb(st, "osq", [128, 512], F32)
        rsf = kb.sb(st, "rsf", [128, 512], F32)
        obs = Rot([kb.sb(st, f"ob{i}", [128, 512], BF16) for i in range(2)])
        for h in range(8):
            kT = kTs.next(); qT = qTs.next(); vh = vhs.next(); gb = gbs.next()
            kb.dma("sp", kT[:], KT.t[h], rd=[KT], wr=[kT])
            kb.dma("sp", qT[:], QT.t[h], rd=[QT], wr=[qT])
            kb.dma("sp", vh[:], VV.t[h], rd=[VV], wr=[vh])
            kb.dma("sp", gb[:], GB.t[h], rd=[GB], wr=[gb])
            for qb in range(8):
                qs = slice(qb * 512, (qb + 1) * 512)
                nkt = 4 * (qb + 1)
                for kt in range(nkt):
                    ks = slice(kt * 128, (kt + 1) * 128)
                    s1 = pscore.next(); s2 = pscore.next()
                    mm(s1[:], kT[0:64, ks], qT[0:64, qs], True, True, [kT, qT], [s1])
                    mm(s2[:], kT[64:128, ks], qT[64:128, qs], True, True, [kT, qT], [s2])
                    e1 = e1s.next(); e2 = e2s.next()
                    act(e1[:], s1[:], AF.Exp, [s1], [e1])
                    act(e2[:], s2[:], AF.Exp, [s2], [e2])
                    if kt >= 4 * qb:
                        o = kt - 4 * qb
                        tt("pool", e1[:], e1[:], M4[:, o, :], ALU.mult, [e1, M4], [e1])
                        tt("pool", e2[:], e2[:], M4[:, o, :], ALU.mult, [e2, M4], [e2])
                    first, lastk = kt == 0, kt == nkt - 1
                    mm(o1[:], vh[:, kt, :], e1[:], first, lastk, [vh, e1], [o1])
                    mm(d1[:], ones_b[:], e1[:], first, lastk, [ones_b, e1], [d1])
                    mm(o2[:], vh[:, kt, :], e2[:], first, lastk, [vh, e2], [o2])
                    mm(d2[:], ones_b[:], e2[:], first, lastk, [ones_b, e2], [d2])
                recip(r1[:], d1[:], [d1], [r1])
                tt("dve", ta[:], o1[:], r1[:], ALU.mult, [o1, r1], [ta])
                recip(r1[:], d2[:], [d2, ta], [r1])
                tt("dve", tb[:], o2[:], r1[:], ALU.mult, [o2, r1], [tb])
                stt(od[:], tb[:], neglam[:, 0:1], ta[:], ALU.mult, ALU.add, [tb, neglam, ta], [od])
                act(osq[:], od[:], AF.Square, [od], [osq])
                stp = pscore.next()
                mm(stp[:], avg128[:], osq[:], True, True, [avg128, osq], [stp])
                act(rsf[:], stp[:], AF.Sqrt, [stp, eps_t], [rsf], bias=eps_t[:, 0:1])
                recip(rsf[:], rsf[:], [rsf], [rsf])
                tt("dve", od[:], od[:], rsf[:], ALU.mult, [od, rsf], [od])
                ob = obs.next()
                stt(ob[:], od[:], sgs[:, 0:1], gb[:, qs], ALU.mult, ALU.mult, [od, sgs, gb], [ob])
                kb.dma("pool", MIXT.t[8 + h, :, qs], ob[:], rd=[ob], wr=[MIXT])
        kb.flush()

    if stop_after == "l0a":
        kb.final_wait()
        return nc

    UF = kb.dram("UF", [16, 128, S], BF16, kind=kind("UF"))
    GF = kb.dram("GF", [16, 128, S], BF16, kind=kind("GF"))
    with ExitStack() as st:
        mixTs = Rot([kb.sb(st, f"mixT{i}", [128, 16, 512], BF16) for i in range(2)])
        wgs = Rot([kb.sb(st, f"wg{i}", [128, 16, 512], BF16) for i in range(2)])
        xts4 = [kb.sb(st, f"xq{i}", [128, D], F32) for i in range(4)]
        hn = kb.sb(st, "hn", [128, D], BF16)
        junk = kb.sb(st, "junk", [128, D], BF16)
        ss = kb.sb(st, "ss", [128, 1], F32)
        sd1 = kb.sb(st, "sd1", [128, 1], F32)
        rs1 = kb.sb(st, "rs1", [128, 1], F32)
        h1T = kb.sb(st, "h1T", [128, 16, 512], BF16)
        ubs = Rot([kb.sb(st, f"ub{i}", [128, 512], BF16) for i in range(3)])
        pss = Rot([kb.ps(st, f"ps{i}", [128, 512], F32) for i in range(4)])
        ptrs = Rot([kb.ps(st, f"ptr{i}", [128, 4, 128], BF16) for i in range(2)])

        def load_wg(src, gi):
            wg = wgs.next()
            kb.dma("sp", wg[:], src.t[gi], rd=[src], wr=[wg])
            return wg

        def norm_transpose_perm(xt, gn, hTb, tti):
            act(junk[:], xt[:], AF.Square, [xt], [junk, ss], accum=ss[:, 0:1])
            act(sd1[:], ss[:], AF.Sqrt, [ss, eps_t], [sd1], bias=eps_t[:, 0:1], scale=1.0 / D)
            recip(rs1[:], sd1[:], [sd1], [rs1])
            act(hn[:], xt[:], AF.Copy, [xt, rs1], [hn], scale=rs1[:, 0:1])
            for c4 in range(4):
                ptr = ptrs.next()
                for q_ in range(4):
                    c = c4 * 4 + q_
                    kb.op("pe", lambda c=c, q_=q_, ptr=ptr: pe.transpose(out=ptr[:, q_, :], in_=hn[:, c * 128:(c + 1) * 128],
                                                                           identity=ident_b[:]), [hn, ident_b], [ptr])
                oap = hTb[:, c4 * 4:c4 * 4 + 4, :].rearrange("p k (t c) -> p k t c", t=8)[:, :, :, 16 * tti:16 * tti + 16]
                iap = ptr[:].rearrange("p k (c t) -> p k t c", t=8)
                tt("dve", oap, iap, bc(gn[:, c4 * 4:c4 * 4 + 4].unsqueeze(2).unsqueeze(3), [128, 4, 8, 16]), ALU.mult,
                   [ptr, gn], [hTb])

        for bi in range(NB):
            t0 = bi * 512
            mixT = mixTs.next()
            kb.dma("sp", mixT[:], MIXT.t[:, :, t0:t0 + 512].rearrange("c p t -> p c t"), rd=[MIXT], wr=[mixT])
            for tti in range(4):
                kb.dma("sp", xts4[tti][:], x[t0 + tti * 128:t0 + (tti + 1) * 128, :], wr=[xts4[tti]])
            for og in range(4):
                wg = load_wg(Wb_out0, og)
                for tti in range(4):
                    ps = pss.next()
                    for c in range(16):
                        mm(ps[:], mixT[:, c, tti * 128:(tti + 1) * 128], wg[:, c, :], c == 0, c == 15, [mixT, wg], [ps])
                    xs = xts4[tti][:, og * 512:(og + 1) * 512]
                    tt("dve", xs, ps[:], xs, ALU.add, [ps, xts4[tti]], [xts4[tti]])
            for tti in range(4):
                kb.dma("pool", X1.t[t0 + tti * 128:t0 + (tti + 1) * 128, :], xts4[tti][:], rd=[xts4[tti]], wr=[X1])
                norm_transpose_perm(xts4[tti], gn1, h1T, tti)
            for gi in range(8):
                wg = load_wg(Wb_in1, gi)
                for j in range(4):
                    ps = pss.next()
                    for c in range(16):
                        mm(ps[:], wg[:, c, j * 128:(j + 1) * 128], h1T[:, c, :], c == 0, c == 15, [wg, h1T], [ps])
                    ub = ubs.next()
                    ft = (gi % 4) * 4 + j
                    dst = UF if gi < 4 else GF
                    act(ub[:], ps[:], AF.Copy if gi < 4 else AF.Silu, [ps], [ub])
                    kb.dma("pool", dst.t[ft].rearrange("p (t c) -> p t c", t=8)[:, :, bi * 64:(bi + 1) * 64],
                           ub[:].rearrange("p (t c) -> p t c", t=8), rd=[ub], wr=[dst])
        kb.flush()

    if stop_after == "l1p":
        kb.final_wait()
        return nc

    S5W = kb.dram("S5W", [128, 128, 4, 128], BF16, kind=kind("S5W"))
    CLv = kb.sb(cs, "CLv", [128, 9, 128], F32)
    SLv = kb.sb(cs, "SLv", [128, 9, 128], F32)
    RLt = kb.sb(cs, "RLt", [128, 128], F32)
    Dc = kb.sb(cs, "Dc", [128, 128], F32)
    with ExitStack() as st:
        def T(name):
            return kb.sb(st, name, [128, 128], F32)
        AN = T("AN"); are = T("are"); aim = T("aim"); ldt = T("ldt"); dtt = T("dtt")
        th = T("th"); xr = T("xr"); kk = T("kk"); rr = T("rr"); r2 = T("r2"); acc = T("acc")
        sinT = T("sinT"); cosT = T("cosT"); EE = T("EE"); lR = T("lR"); lI = T("lI")
        den = T("den"); nr = T("nr"); fR = T("fR"); fI = T("fI"); nuR = T("nuR"); nuI = T("nuI")
        muI = T("muI"); l2R = T("l2R"); l2I = T("l2I"); l4R = T("l4R"); l4I = T("l4I")
        l8R = T("l8R"); l8I = T("l8I"); l7R = T("l7R"); l7I = T("l7I"); x1_ = T("x1_"); x2_ = T("x2_")
        x3_ = T("x3_"); x4_ = T("x4_"); mask8 = T("mask8"); onesq = T("onesq")
        tps = Rot([kb.ps(st, f"tps{i}", [128, 128], F32) for i in range(4)])

        def D_(fn, rd, wr):
            kb.op("dve", fn, rd, wr)

        def mul(o, a, b):
            tt("dve", o[:], a[:], b[:], ALU.mult, [a, b], [o])

        def add(o, a, b):
            tt("dve", o[:], a[:], b[:], ALU.add, [a, b], [o])

        def sub(o, a, b):
            tt("dve", o[:], a[:], b[:], ALU.subtract, [a, b], [o])

        def cmul(oR, oI, aR, aI, bR, bI):
            mul(x1_, aR, bR); mul(x2_, aI, bI); mul(x3_, aR, bI); mul(x4_, aI, bR)
            sub(oR, x1_, x2_); add(oI, x3_, x4_)

        def horner(o, xx, coef):
            n_ = len(coef) - 1
            ts("dve", o[:], xx[:], float(coef[n_]), None, ALU.mult, None, [xx], [o])
            for k_ in range(n_ - 1, 0, -1):
                stt(o[:], o[:], float(coef[k_]), xx[:], ALU.add, ALU.mult, [o, xx], [o])
            ts("dve", o[:], o[:], float(coef[0]), None, ALU.add, None, [o], [o])

        for src, dstt in ((din["o_A_re"], are), (din["o_A_im"], aim)):
            kb.dma("sp", AN[:, 0:64], src, wr=[AN], key="small")
            kb.dma("sp", AN[:, 64:128], src, wr=[AN], key="small")
            tp = tps.next()
            kb.op("pe", lambda tp=tp: pe.transpose(out=tp[:], in_=AN[:], identity=ident_f[:]), [AN, ident_f], [tp])
            copy("dve", dstt[:], tp[:], [tp], [dstt])
        kb.dma("sp", ldt[:], din["o_log_dt"].partition_broadcast(128), wr=[ldt], key="small", allow_slow_non_contiguous=True)
        for tau in range(8):
            kb.dma("sp", Dc[tau * 16:(tau + 1) * 16, :], din["o_D"].rearrange("(g m) -> m g", m=16), wr=[Dc], key="small",
                   allow_slow_non_contiguous=True)
        act(dtt[:], ldt[:], AF.Exp, [ldt], [dtt])
        mul(th, aim, dtt)
        mul(xr, are, dtt)
        MAGIC = 12582912.0
        ts("dve", kk[:], th[:], 1.0 / (2 * math.pi), MAGIC, ALU.mult, ALU.add, [th], [kk])
        ts("dve", kk[:], kk[:], -MAGIC, None, ALU.add, None, [kk], [kk])
        c1 = 6.28125
        c2 = float(np.float32(np.float32(2 * math.pi - c1).view(np.uint32) & np.uint32(0xFFFFF000)).view(np.float32)) if False else 0.0019350051879882812
        c3 = 2 * math.pi - c1 - c2
        stt(rr[:], kk[:], -c1, th[:], ALU.mult, ALU.add, [kk, th], [rr])
        stt(rr[:], kk[:], -c2, rr[:], ALU.mult, ALU.add, [kk, rr], [rr])
        stt(rr[:], kk[:], -c3, rr[:], ALU.mult, ALU.add, [kk, rr], [rr])
        mul(r2, rr, rr)
        horner(acc, r2, [(-1.0) ** k_ / math.factorial(2 * k_ + 1) for k_ in range(11)])
        mul(sinT, acc, rr)
        horner(cosT, r2, [(-1.0) ** k_ / math.factorial(2 * k_) for k_ in range(12)])
        horner(EE, xr, [1.0 / math.factorial(k_) for k_ in range(8)])
        mul(lR, EE, cosT)
        mul(lI, EE, sinT)
        mul(den, are, are); mul(x1_, aim, aim); add(den, den, x1_)
        recip(den[:], den[:], [den], [den])
        ts("dve", nr[:], lR[:], -1.0, None, ALU.add, None, [lR], [nr])
        mul(x1_, nr, are); mul(x2_, lI, aim); add(x1_, x1_, x2_); mul(fR, x1_, den)
        mul(x1_, lI, are); mul(x2_, nr, aim); sub(x1_, x1_, x2_); mul(fI, x1_, den)
        mul(x1_, EE, EE)
        recip(x1_[:], x1_[:], [x1_], [x1_])
        mul(nuR, lR, x1_)
        mul(nuI, lI, x1_)
        ts("dve", nuI[:], nuI[:], -1.0, None, ALU.mult, None, [nuI], [nuI])
        ts("dve", muI[:], lI[:], -1.0, None, ALU.mult, None, [lI], [muI])
        cmul(l2R, l2I, lR, lI, lR, lI)
        cmul(l4R, l4I, l2R, l2I, l2R, l2I)
        cmul(l8R, l8I, l4R, l4I, l4R, l4I)
        cmul(l7R, l7I, l4R, l4I, l2R, l2I)
        cmul(l7R, l7I, l7R, l7I, lR, lI)
        mul(RLt, EE, EE); mul(RLt, RLt, RLt); mul(RLt, RLt, RLt)
        recip(x1_[:], RLt[:], [RLt], [x1_])
        tt("dve", CLv[:, 0, :], l8R[:], x1_[:], ALU.mult, [l8R, x1_], [CLv])
        tt("dve", SLv[:, 0, :], l8I[:], x1_[:], ALU.mult, [l8I, x1_], [SLv])
        for j in range(8):
            tt("dve", x2_[:], CLv[:, j, :], CLv[:, j, :], ALU.mult, [CLv], [x2_])
            tt("dve", x3_[:], SLv[:, j, :], SLv[:, j, :], ALU.mult, [SLv], [x3_])
            tt("dve", CLv[:, j + 1, :], x2_[:], x3_[:], ALU.subtract, [x2_, x3_], [CLv])
            tt("dve", x2_[:], CLv[:, j, :], SLv[:, j, :], ALU.mult, [CLv, SLv], [x2_])
            ts("dve", SLv[:, j + 1, :], x2_[:], 2.0, None, ALU.mult, None, [x2_], [SLv])
        memset("pool", onesq[:], 1.0, [onesq])
        kb.op("pool", lambda: pool.affine_select(out=mask8[:], in_=onesq[:], pattern=[[16, 8], [0, 16]], compare_op=ALU.is_ge,
                                                  fill=0.0, base=15, channel_multiplier=-1), [onesq], [mask8])

        GB_ = 16
        BA = kb.sb(st, "BA", [128, GB_, 16], F32)
        BAp = kb.sb(st, "BAp", [128, GB_, 16], F32)
        CN = kb.sb(st, "CN", [128, 128], F32)
        CNp = kb.sb(st, "CNp", [128, 128], F32)
        VA = kb.sb(st, "VA", [128, GB_, 8, 16], F32)
        VAp = kb.sb(st, "VAp", [128, GB_, 8, 16], F32)
        WA = kb.sb(st, "WA", [128, GB_, 9, 16], F32)
        WAp = kb.sb(st, "WAp", [128, GB_, 9, 16], F32)
        WBA = kb.sb(st, "WBA", [128, GB_, 8, 16], F32)
        WBAp = kb.sb(st, "WBAp", [128, GB_, 8, 16], F32)
        y1 = kb.sb(st, "y1", [128, GB_, 16], F32)
        y2 = kb.sb(st, "y2", [128, GB_, 16], F32)
        y3 = kb.sb(st, "y3", [128, GB_, 16], F32)
        y4 = kb.sb(st, "y4", [128, GB_, 16], F32)
        S5ts = Rot([kb.sb(st, f"S5t{i}", [128, GB_, 4, 128], BF16) for i in range(2)])

        def cmat(oA, oAp, iA, iAp, zR, zI, g0, rdo, wro):
            zr = bc(zR[:, g0:g0 + GB_].unsqueeze(2), [128, GB_, 16])
            zi = bc(zI[:, g0:g0 + GB_].unsqueeze(2), [128, GB_, 16])
            tt("dve", y1[:], iA, zr, ALU.mult, rdo + [zR], [y1])
            tt("dve", y2[:], iAp, zi, ALU.mult, rdo + [zI], [y2])
            tt("pool", y3[:], iAp, zr, ALU.mult, rdo + [zR], [y3])
            tt("pool", y4[:], iA, zi, ALU.mult, rdo + [zI], [y4])
            tt("dve", oA, y1[:], y2[:], ALU.add, [y1, y2], wro)
            tt("pool", oAp, y3[:], y4[:], ALU.subtract, [y3, y4], wro)

        for gb_ in range(128 // GB_):
            g0 = gb_ * GB_
            gsl = slice(g0, g0 + GB_)
            kb.dma("sp", BA[0:64], din["o_B_re"][gsl].rearrange("g p m -> p g m"), wr=[BA], key="s5ld")
            kb.dma("sp", BA[64:128], din["o_B_im"][gsl].rearrange("g p m -> p g m"), wr=[BA], key="s5ld")
            kb.dma("sp", BAp[0:64], din["o_B_im"][gsl].rearrange("g p m -> p g m"), wr=[BAp], key="s5ld")
            kb.dma("sp", BAp[64:128], din["o_B_re"][gsl].rearrange("g p m -> p g m"), wr=[BAp], key="s5ld")
            ts("dve", BAp[0:64], BAp[0:64], -1.0, None, ALU.mult, None, [BAp], [BAp])
            for hb in range(GB_ // 8):
                gs8 = slice(g0 + hb * 8, g0 + hb * 8 + 8)
                kb.dma("sp", CN[:, 0:64], din["o_C_re"][gs8].rearrange("g m p -> (g m) p"), wr=[CN], key="s5ld")
                kb.dma("sp", CN[:, 64:128], din["o_C_im"][gs8].rearrange("g m p -> (g m) p"), wr=[CN], key="s5ld")
                kb.dma("sp", CNp[:, 0:64], din["o_C_im"][gs8].rearrange("g m p -> (g m) p"), wr=[CNp], key="s5ld")
                kb.dma("sp", CNp[:, 64:128], din["o_C_re"][gs8].rearrange("g m p -> (g m) p"), wr=[CNp], key="s5ld")
                tp = tps.next()
                kb.op("pe", lambda tp=tp: pe.transpose(out=tp[:], in_=CN[:], identity=ident_f[:]), [CN, ident_f], [tp])
                copy("dve", WA[:, hb * 8:hb * 8 + 8, 0, :], tp[:].rearrange("p (g m) -> p g m", m=16), [tp], [WA])
                tp = tps.next()
                kb.op("pe", lambda tp=tp: pe.transpose(out=tp[:], in_=CNp[:], identity=ident_f[:]), [CNp, ident_f], [tp])
                copy("dve", WAp[:, hb * 8:hb * 8 + 8, 0, :], tp[:].rearrange("p (g m) -> p g m", m=16), [tp], [WAp])
            ts("dve", WA[64:128, :, 0, :], WA[64:128, :, 0, :], -1.0, None, ALU.mult, None, [WA], [WA])
            cmat(VA[:, :, 0, :], VAp[:, :, 0, :], BA[:], BAp[:], fR, fI, g0, [BA, BAp], [VA, VAp])
            for sg in range(7):
                cmat(VA[:, :, sg + 1, :], VAp[:, :, sg + 1, :], VA[:, :, sg, :], VAp[:, :, sg, :], nuR, nuI, g0, [VA, VAp], [VA, VAp])
            for k_ in range(8):
                cmat(WA[:, :, k_ + 1, :], WAp[:, :, k_ + 1, :], WA[:, :, k_, :], WAp[:, :, k_, :], lR, muI, g0, [WA, WAp], [WA, WAp])
            for sg in range(8):
                cmat(WBA[:, :, sg, :], WBAp[:, :, sg, :], VA[:, :, sg, :], VAp[:, :, sg, :], l7R, l7I, g0, [VA, VAp], [WBA, WBAp])
            S5t = S5ts.next()
            for g in range(GB_):
                tp = tps.next()
                kb.op("pe", lambda tp=tp, g=g: pe.matmul(tp[:], lhsT=VA[:, g, :, :].rearrange("p s m -> p (s m)"),
                                                          rhs=WA[:, g, 0:8, :].rearrange("p s m -> p (s m)"), start=True, stop=True),
                      [VA, WA], [tp])
                tt("dve", S5t[:, g, 2, :], tp[:], mask8[:], ALU.mult, [tp, mask8], [S5t])
                tp = tps.next()
                kb.op("pe", lambda tp=tp, g=g: pe.transpose(out=tp[:], in_=WBA[:, g, :, :].rearrange("p s m -> p (s m)"),
                                                             identity=ident_f[:]), [WBA, ident_f], [tp])
                act(S5t[:, g, 0, :], tp[:], AF.Copy, [tp], [S5t])
                tp = tps.next()
                kb.op("pe", lambda tp=tp, g=g: pe.transpose(out=tp[:], in_=WBAp[:, g, :, :].rearrange("p s m -> p (s m)"),
                                                             identity=ident_f[:]), [WBAp, ident_f], [tp])
                act(S5t[:, g, 1, :], tp[:], AF.Copy, [tp], [S5t], scale=-1.0)
            copy("pool", S5t[:, :, 3, :].rearrange("p g (s m) -> p g s m", m=16), WA[:, :, 1:9, :], [WA], [S5t])
            kb.dma("pool", S5W.t[gsl].rearrange("g p k c -> p g k c"), S5t[:], rd=[S5t], wr=[S5W])
        kb.flush()

    if stop_after == "s5setup":
        kb.final_wait()
        return nc

    ZF = kb.dram("ZF", [16, 128, S], BF16, kind=kind("ZF"))
    with ExitStack() as st:
        COSs = Rot([kb.sb(st, f"COSc{i}", [128, 8, 512], F32) for i in range(2)])
        SINs = Rot([kb.sb(st, f"SINc{i}", [128, 8, 512], F32) for i in range(2)])
        Y1 = kb.sb(st, "Y1", [128, 8, 256], F32)
        Y2 = kb.sb(st, "Y2", [128, 8, 256], F32)
        Ucs = Rot([kb.sb(st, f"Uc{i}", [128, 512], BF16) for i in range(3)])
        Wgs = Rot([kb.sb(st, f"Wg{i}", [128, 4, 128], BF16) for i in range(3)])
        pAs = Rot([kb.ps(st, f"pA{i}", [128, 512], F32) for i in range(2)])
        pBs = Rot([kb.ps(st, f"pB{i}", [128, 512], F32) for i in range(2)])
        pYs = Rot([kb.ps(st, f"pY{i}", [128, 512], F32) for i in range(2)])

        def F(name):
            return kb.sb(st, name, [128, 512], F32)
        t1 = F("st1"); t2 = F("st2"); t3 = F("st3"); t4 = F("st4"); cA = F("cA"); cB = F("cB")
        gA = F("gA"); gB = F("gB"); t5 = F("st5"); t6 = F("st6"); ysb = F("ysb"); y2 = F("y2"); sg = F("sgm")
        Hps = Rot([kb.sb(st, f"Hp{i}", [128, 512], BF16) for i in range(2)])
        zts = Rot([kb.sb(st, f"zt{i}", [128, 512], BF16) for i in range(2)])
        for hp in Hps.bufs:
            memset("pool", hp[:, 0:1], 0.0, [hp])
        for ft in range(16):
            COSc = COSs.next(); SINc = SINs.next()
            memset("pool", COSc[:, :, 0:1], 1.0, [COSc])
            memset("pool", SINc[:, :, 0:1], 0.0, [SINc])
            for j in range(9):
                n = 1 << j
                cn_ = bc(CLv[:, j, 8 * ft:8 * ft + 8].unsqueeze(2), [128, 8, n])
                sn_ = bc(SLv[:, j, 8 * ft:8 * ft + 8].unsqueeze(2), [128, 8, n])
                tt("pool", Y1[:, :, 0:n], COSc[:, :, 0:n], cn_, ALU.mult, [COSc, CLv], [Y1])
                tt("pool", Y2[:, :, 0:n], SINc[:, :, 0:n], sn_, ALU.mult, [SINc, SLv], [Y2])
                tt("pool", COSc[:, :, n:2 * n], Y1[:, :, 0:n], Y2[:, :, 0:n], ALU.subtract, [Y1, Y2], [COSc])
                tt("pool", Y1[:, :, 0:n], SINc[:, :, 0:n], cn_, ALU.mult, [SINc, CLv], [Y1])
                tt("pool", Y2[:, :, 0:n], COSc[:, :, 0:n], sn_, ALU.mult, [COSc, SLv], [Y2])
                tt("pool", SINc[:, :, n:2 * n], Y1[:, :, 0:n], Y2[:, :, 0:n], ALU.add, [Y1, Y2], [SINc])
            for j8 in range(8):
                g = 8 * ft + j8
                Uc = Ucs.next(); Wg = Wgs.next()
                for tau in range(8):
                    kb.dma("sp", Uc[16 * tau:16 * tau + 16, :], UF.t[ft, 16 * j8:16 * j8 + 16, tau * 512:(tau + 1) * 512], rd=[UF], wr=[Uc])
                kb.dma("sp", Wg[:], S5W.t[g], rd=[S5W], wr=[Wg])
                pA = pAs.next(); pB = pBs.next(); pY = pYs.next()
                mm(pA[:], Wg[:, 0, :], Uc[:], True, True, [Wg, Uc], [pA])
                mm(pB[:], Wg[:, 1, :], Uc[:], True, True, [Wg, Uc], [pB])
                co = COSc[:, j8, :]; si = SINc[:, j8, :]
                tt("dve", t1[:], pA[:], co, ALU.mult, [pA, COSc], [t1])
                tt("dve", t2[:], pB[:], si, ALU.mult, [pB, SINc], [t2])
                tt("pool", cA[:], t1[:], t2[:], ALU.add, [t1, t2], [cA])
                tt("dve", t3[:], pB[:], co, ALU.mult, [pB, COSc], [t3])
                tt("dve", t4[:], pA[:], si, ALU.mult, [pA, SINc], [t4])
                tt("pool", cB[:], t3[:], t4[:], ALU.subtract, [t3, t4], [cB])
                rl = bc(RLt[:, g:g + 1], [128, 512])
                kb.op("dve", lambda rl=rl: dve.tensor_tensor_scan(out=gA[:], data0=rl, data1=cA[:], initial=0.0, op0=ALU.mult, op1=ALU.add),
                      [RLt, cA], [gA])
                kb.op("dve", lambda rl=rl: dve.tensor_tensor_scan(out=gB[:], data0=rl, data1=cB[:], initial=0.0, op0=ALU.mult, op1=ALU.add),
                      [RLt, cB], [gB])
                tt("dve", t5[:], gA[:], co, ALU.mult, [gA, COSc], [t5])
                tt("pool", t6[:], gB[:], si, ALU.mult, [gB, SINc], [t6])
                Hp = Hps.next()
                tt("pool", Hp[:, 1:512], t5[:, 0:511], t6[:, 0:511], ALU.subtract, [t5, t6], [Hp])
                mm(pY[:], Wg[:, 2, :], Uc[:], True, False, [Wg, Uc], [pY])
                mm(pY[:], Wg[:, 3, :], Hp[:], False, True, [Wg, Hp], [pY])
                stt(ysb[:], Uc[:], Dc[:, g:g + 1], pY[:], ALU.mult, ALU.add, [Uc, Dc, pY], [ysb])
                act(y2[:], ysb[:], AF.Square, [ysb], [y2])
                ts("dve", y2[:], y2[:], 0.044715, 1.0, ALU.mult, ALU.add, [y2], [y2])
                tt("pool", y2[:], y2[:], ysb[:], ALU.mult, [y2, ysb], [y2])
                act(sg[:], y2[:], AF.Sigmoid, [y2], [sg], scale=2.0 * math.sqrt(2.0 / math.pi))
                zt = zts.next()
                tt("dve", zt[:], ysb[:], sg[:], ALU.mult, [ysb, sg], [zt])
                for tau in range(8):
                    kb.dma("pool", ZF.t[ft, 16 * j8:16 * j8 + 16, tau * 512:(tau + 1) * 512], zt[16 * tau:16 * tau + 16, :], rd=[zt], wr=[ZF])
        kb.flush()

    if stop_after == "l1s":
        kb.final_wait()
        return nc

    with ExitStack() as st:
        zTs = Rot([kb.sb(st, f"zT{i}", [128, 16, 512], BF16) for i in range(2)])
        gTs = Rot([kb.sb(st, f"gT{i}", [128, 16, 512], BF16) for i in range(2)])
        wgs = Rot([kb.sb(st, f"wg{i}", [128, 16, 512], BF16) for i in range(2)])
        oT = kb.sb(st, "oT", [128, 16, 512], BF16)
        xq = [kb.sb(st, f"xr{i}", [128, D], F32) for i in range(4)]
        sgs_ = Rot([kb.sb(st, f"sgg{i}", [128, 512], BF16) for i in range(2)])
        pss = Rot([kb.ps(st, f"ps{i}", [128, 512], F32) for i in range(6)])
        X1v = X1.t.rearrange("(c t) d -> t c d", t=8)
        OUTv = out_d.rearrange("(c t) d -> t c d", t=8)
        for bi in range(NB):
            zT = zTs.next(); gT = gTs.next()
            kb.dma("sp", zT[:], ZF.t[:, :, bi * 512:(bi + 1) * 512].rearrange("f p c -> p f c"), rd=[ZF], wr=[zT])
            kb.dma("sp", gT[:], GF.t[:, :, bi * 512:(bi + 1) * 512].rearrange("f p c -> p f c"), rd=[GF], wr=[gT])
            for tti in range(4):
                kb.dma("sp", xq[tti][:], X1v[bi, tti * 128:(tti + 1) * 128, :], rd=[X1], wr=[xq[tti]])
            for gg in range(4):
                wg = wgs.next()
                kb.dma("sp", wg[:], Wb_glu.t[gg], rd=[Wb_glu], wr=[wg])
                for j in range(4):
                    ft = gg * 4 + j
                    ps = pss.next()
                    for c in range(16):
                        mm(ps[:], wg[:, c, j * 128:(j + 1) * 128], zT[:, c, :], c == 0, c == 15, [wg, zT], [ps])
                    sgt = sgs_.next()
                    act(sgt[:], ps[:], AF.Sigmoid, [ps, bglu], [sgt], bias=bglu[:, ft:ft + 1])
                    tt("dve", sgt[:], sgt[:], zT[:, ft, :], ALU.mult, [sgt, zT], [sgt])
                    tt("pool", oT[:, ft, :], sgt[:], gT[:, ft, :], ALU.mult, [sgt, gT], [oT])
            for og in range(4):
                wg = wgs.next()
                kb.dma("sp", wg[:], Wb_out1.t[og], rd=[Wb_out1], wr=[wg])
                for tti in range(4):
                    ps = pss.next()
                    for c in range(16):
                        mm(ps[:], oT[:, c, tti * 128:(tti + 1) * 128], wg[:, c, :], c == 0, c == 15, [oT, wg], [ps])
                    xs = xq[tti][:, og * 512:(og + 1) * 512]
                    tt("dve", xs, ps[:], xs, ALU.add, [ps, xq[tti]], [xq[tti]])
            for tti in range(4):
                kb.dma("pool", OUTv[bi, tti * 128:(tti + 1) * 128, :], xq[tti][:], rd=[xq[tti]], key=f"outst{tti}")
        kb.flush()

    kb.final_wait()
    return nc


def make_in_maps(inputs):
    maps = []
    for c in range(NCORES):
        m = {"x": np.ascontiguousarray(inputs["x"][c % 4])}
        for n, shp in PARAMS:
            m[n] = np.ascontiguousarray(np.asarray(inputs[n]).reshape(shp))
        maps.append(m)
    return maps


def kernel(**inputs):
    nc = build()
    res = run_bass_kernel_spmd(nc, make_in_maps(inputs), core_ids=list(range(NCORES)))
    out = np.stack([res.results[c]["out"] for c in range(4)], axis=0)
    return out.astype(np.float32)
```

```python
import math
from contextlib import ExitStack

import numpy as np
import concourse.bass as bass
import concourse.mybir as mybir
from concourse.bass_utils import run_bass_kernel_spmd

F32 = mybir.dt.float32
BF16 = mybir.dt.bfloat16
I32 = mybir.dt.int32
AF = mybir.ActivationFunctionType
ALU = mybir.AluOpType

S = 4096
D = 2048
NB = 8
EPS = 1e-6
NCORES = 4


class Buf:
    def __init__(self, name, t, persist=False):
        self.name = name
        self.t = t
        self.w = {}
        self.r = {}
        self.ext = []
        self.persist = persist

    def __getitem__(self, idx):
        return self.t[idx]


class Op:
    __slots__ = ("eng", "fn", "rd", "wr", "is_dma", "key", "deps", "ext", "signal", "sem", "val")

    def __init__(self, eng, fn, rd, wr, is_dma=False, key=None):
        self.eng = eng
        self.fn = fn
        self.rd = rd
        self.wr = wr
        self.is_dma = is_dma
        self.key = key
        self.deps = ()
        self.ext = []
        self.signal = False
        self.sem = None
        self.val = 0


class Rot:
    def __init__(self, bufs):
        self.bufs = bufs
        self.i = 0

    def next(self):
        b = self.bufs[self.i % len(self.bufs)]
        self.i += 1
        return b


class KB:
    def __init__(self, nc):
        self.nc = nc
        self.es = ExitStack()
        self.eng = {"pe": nc.tensor, "act": nc.scalar, "dve": nc.vector, "pool": nc.gpsimd, "sp": nc.sync}
        self.esem = {k: self.es.enter_context(nc.semaphore("s_" + k)) for k in self.eng}
        self.ecnt = {k: 0 for k in self.eng}
        self.seen = {k: {} for k in self.eng}
        self.dsem = {}
        self.deferred = set()
        self.ops = []
        self.bufs = []
        self.nins = {k: 0 for k in self.eng}

    def sb(self, stack, name, shape, dtype, persist=False):
        self.uid = getattr(self, "uid", 0) + 1
        name = f"{name}_{self.uid}"
        t = stack.enter_context(self.nc.sbuf_tensor(name, list(shape), dtype))
        b = Buf(name, t, persist)
        self.bufs.append(b)
        return b

    def ps(self, stack, name, shape, dtype):
        self.uid = getattr(self, "uid", 0) + 1
        name = f"{name}_{self.uid}"
        t = stack.enter_context(self.nc.psum_tensor(name, list(shape), dtype))
        b = Buf(name, t)
        self.bufs.append(b)
        return b

    def dram(self, name, shape, dtype, kind="Internal", persist=False):
        t = self.nc.dram_tensor(name, list(shape), dtype, kind=kind).ap()
        b = Buf(name, t, persist)
        self.bufs.append(b)
        return b

    def op(self, eng, fn, rd=(), wr=()):
        self.ops.append(Op(eng, fn, list(rd), list(wr)))

    def dma(self, q, out, in_, rd=(), wr=(), key=None, defer=False, **kw):
        h = self.eng[q]
        if key is None:
            key = (wr[0].name if wr else rd[0].name + "_st")
        if key not in self.dsem:
            self.dsem[key] = [self.es.enter_context(self.nc.semaphore("d_" + key)), 0]
        if defer:
            self.deferred.add(key)
        self.ops.append(Op(q, lambda: h.dma_start(out=out, in_=in_, **kw), list(rd), list(wr), True, key))

    def flush(self, barrier=True):
        ops = self.ops
        for i, op in enumerate(ops):
            deps = set()
            ext = []
            for b in op.rd:
                deps.update(b.w.values())
                ext.extend(b.ext)
            for b in op.wr:
                deps.update(b.w.values())
                deps.update(b.r.values())
                ext.extend(b.ext)
            if op.eng == "pe" and not op.is_dma:
                deps = {d for d in deps if ops[d].is_dma or ops[d].eng != "pe"}
            deps.discard(i)
            op.deps = sorted(deps)
            op.ext = ext
            k = ("dma", op.key) if op.is_dma else op.eng
            for b in op.wr:
                b.w[k] = i
            for b in op.rd:
                b.r[k] = i
            for d in deps:
                ops[d].signal = True
        last = {}
        for i, op in enumerate(ops):
            if not op.is_dma:
                last[op.eng] = i
        for i in last.values():
            ops[i].signal = True
        for op in ops:
            e = op.eng
            h = self.eng[e]
            need = [(ops[d].sem, ops[d].val) for d in op.deps] + list(op.ext)
            for sem, val in need:
                sk = id(sem)
                if self.seen[e].get(sk, 0) < val:
                    h.wait_ge(sem, val)
                    self.seen[e][sk] = val
            ins = op.fn()
            self.nins[e] += 1
            if op.is_dma:
                ent = self.dsem[op.key]
                ent[1] += 16
                ins.then_inc(ent[0], 16)
                op.sem, op.val = ent[0], ent[1]
            elif op.signal:
                self.ecnt[e] += 1
                ins.then_inc(self.esem[e], 1)
                op.sem, op.val = self.esem[e], self.ecnt[e]
        for b in self.bufs:
            if b.persist:
                for d in list(b.w.values()):
                    b.ext.append((ops[d].sem, ops[d].val))
            b.w = {}
            b.r = {}
        self.ops = []
        if barrier:
            for e, h in self.eng.items():
                for e2 in self.eng:
                    if e2 != e and self.ecnt[e2] > self.seen[e].get(id(self.esem[e2]), 0):
                        h.wait_ge(self.esem[e2], self.ecnt[e2])
                        self.seen[e][id(self.esem[e2])] = self.ecnt[e2]
                for key, (sem, cnt) in self.dsem.items():
                    if key in self.deferred:
                        continue
                    if cnt > self.seen[e].get(id(sem), 0):
                        h.wait_ge(sem, cnt)
                        self.seen[e][id(sem)] = cnt

    def final_wait(self):
        for e, h in self.eng.items():
            for key, (sem, cnt) in self.dsem.items():
                if cnt > self.seen[e].get(id(sem), 0):
                    h.wait_ge(sem, cnt)
                    self.seen[e][id(sem)] = cnt


def bc(ap, shape):
    return ap.to_broadcast(list(shape))


PARAMS = [
    ("e_norm_g", [D]), ("e_w_in", [D, 7168]), ("e_conv_w", [31, 1024]), ("e_conv_b", [1024]),
    ("e_cln_g", [1024]), ("e_cln_b", [1024]), ("e_qn_g", [64]), ("e_kn_g", [64]),
    ("e_lam_q1", [64]), ("e_lam_k1", [64]), ("e_lam_q2", [64]), ("e_lam_k2", [64]),
    ("e_subln_g", [128]), ("e_w_out", [D, D]),
    ("o_norm_g", [D]), ("o_w_in", [D, 2 * D]), ("o_A_re", [128, 64]), ("o_A_im", [128, 64]),
    ("o_log_dt", [128]), ("o_B_re", [128, 64, 16]), ("o_B_im", [128, 64, 16]),
    ("o_C_re", [128, 16, 64]), ("o_C_im", [128, 16, 64]), ("o_D", [D]),
    ("o_w_glu", [D, D]), ("o_b_glu", [D]), ("o_w_out", [D, D]),
]


def build(dbg=(), stop_after=None):
    nc = bass.Bass("TRN2", target_bir_lowering=False)
    kb = KB(nc)
    eng = kb.eng
    pe, act_, dve, pool = eng["pe"], eng["act"], eng["dve"], eng["pool"]

    def kind(name):
        return "ExternalOutput" if name in dbg else "Internal"

    din = {"x": nc.dram_tensor("x", [S, D], F32, kind="ExternalInput").ap()}
    for n, shp in PARAMS:
        din[n] = nc.dram_tensor(n, shp, F32, kind="ExternalInput").ap()
    out_d = nc.dram_tensor("out", [S, D], F32, kind="ExternalOutput").ap()

    Wb_in0 = kb.dram("Wb_in0", [14, 128, 16, 512], BF16, persist=True)
    Wb_out0 = kb.dram("Wb_out0", [4, 128, 16, 512], BF16, persist=True)
    Wb_in1 = kb.dram("Wb_in1", [8, 128, 16, 512], BF16, persist=True)
    Wb_glu = kb.dram("Wb_glu", [4, 128, 16, 512], BF16, persist=True)
    Wb_out1 = kb.dram("Wb_out1", [4, 128, 16, 512], BF16, persist=True)
    ROT = kb.dram("ROT", [2, 128, S], F32, kind=kind("ROT"))
    QT = kb.dram("QT", [8, 128, S], BF16, kind=kind("QT"))
    KT = kb.dram("KT", [8, 128, S], BF16, kind=kind("KT"))
    VV = kb.dram("VV", [8, 128, 32, 128], BF16, kind=kind("VV"))
    GB = kb.dram("GB", [8, 128, S], BF16, kind=kind("GB"))
    MIXT = kb.dram("MIXT", [16, 128, S], BF16, kind=kind("MIXT"))
    X1 = kb.dram("X1", [S, D], F32, kind=kind("X1"))

    def act(out, in_, func, rd, wr, bias=None, scale=None, accum=None):
        kw = {}
        if bias is not None:
            kw["bias"] = bias
        if scale is not None:
            kw["scale"] = scale
        if accum is not None:
            kw["accum_out"] = accum
        kb.op("act", lambda: act_.activation(out=out, in_=in_, func=func, **kw), rd, wr)

    def tt(e, out, in0, in1, op, rd, wr):
        h = eng[e]
        kb.op(e, lambda: h.tensor_tensor(out=out, in0=in0, in1=in1, op=op), rd, wr)

    def ts(e, out, in0, s1, s2, op0, op1, rd, wr):
        h = eng[e]
        if op1 is None:
            kb.op(e, lambda: h.tensor_scalar(out=out, in0=in0, scalar1=s1, scalar2=None, op0=op0), rd, wr)
        else:
            kb.op(e, lambda: h.tensor_scalar(out=out, in0=in0, scalar1=s1, scalar2=s2, op0=op0, op1=op1), rd, wr)

    def stt(out, in0, scalar, in1, op0, op1, rd, wr):
        kb.op("dve", lambda: dve.scalar_tensor_tensor(out=out, in0=in0, scalar=scalar, in1=in1, op0=op0, op1=op1), rd, wr)

    def mm(out, lhsT, rhs, start, stop, rd, wr):
        kb.op("pe", lambda: pe.matmul(out, lhsT=lhsT, rhs=rhs, start=start, stop=stop), rd, wr)

    def recip(out, in_, rd, wr):
        kb.op("dve", lambda: dve.reciprocal(out=out, in_=in_), rd, wr)

    def copy(e, out, in_, rd, wr):
        h = eng[e]
        kb.op(e, lambda: h.tensor_copy(out=out, in_=in_), rd, wr)

    def memset(e, ap, val, wr):
        h = eng[e]
        kb.op(e, lambda: h.memset(ap, val), (), wr)

    cs = kb.es

    def conv_w(dst, src, ng, key):
        v = src.rearrange("(c p) (g n) -> g p c n", p=128, n=512)
        for g in range(ng):
            kb.dma("pool", dst.t[g], v[g], wr=[dst], key=key, defer=True)

    conv_w(Wb_in0, din["e_w_in"], 14, "wc0")

    ident_f = kb.sb(cs, "ident_f", [128, 128], F32)
    ident_b = kb.sb(cs, "ident_b", [128, 128], BF16)
    ones_b = kb.sb(cs, "ones_b", [128, 128], BF16)
    ones_f = kb.sb(cs, "ones_f", [128, 128], F32)
    avg1024 = kb.sb(cs, "avg1024", [128, 128], F32)
    avg128 = kb.sb(cs, "avg128", [128, 128], F32)
    bd64 = kb.sb(cs, "bd64", [128, 128], BF16)
    Pm = kb.sb(cs, "Pm", [128, 128], BF16)
    M4 = kb.sb(cs, "M4", [128, 4, 512], BF16)
    eps_t = kb.sb(cs, "eps_t", [128, 1], F32)
    gn0 = kb.sb(cs, "gn0", [128, 16], F32)
    gn1 = kb.sb(cs, "gn1", [128, 16], F32)
    kw_t = kb.sb(cs, "kw_t", [128, 8, 31], F32)
    kw_b = kb.sb(cs, "kw_b", [128, 8, 31], BF16)
    convb = kb.sb(cs, "convb", [128, 8], F32)
    clng = kb.sb(cs, "clng", [128, 8], F32)
    clnb = kb.sb(cs, "clnb", [128, 8], F32)
    qng = kb.sb(cs, "qng", [128, 1], F32)
    kng = kb.sb(cs, "kng", [128, 1], F32)
    sgs = kb.sb(cs, "sgs", [128, 1], F32)
    neglam = kb.sb(cs, "neglam", [128, 1], F32)
    bglu = kb.sb(cs, "bglu", [128, 16], F32)
    sgn = kb.sb(cs, "sgn", [128, 1], F32)
    PiT = kb.sb(cs, "PiT", [128, 128], F32)

    with ExitStack() as st:
        iota_i = kb.sb(st, "iota_i", [128, 128], I32)
        iota_f = kb.sb(st, "iota_f", [128, 128], F32)
        pidx_i = kb.sb(st, "pidx_i", [128, 1], I32)
        ptmp_i = kb.sb(st, "ptmp_i", [128, 1], I32)
        m_hi = kb.sb(st, "m_hi", [128, 1], F32)
        m_lo = kb.sb(st, "m_lo", [128, 1], F32)
        Am = kb.sb(st, "Am", [128, 128], F32)
        Bm = kb.sb(st, "Bm", [128, 128], F32)
        Pf = kb.sb(st, "Pf", [128, 128], F32)
        ones512 = kb.sb(st, "ones512", [128, 512], BF16)
        freq = kb.sb(st, "freq", [128, 1], F32)
        halfpi = kb.sb(st, "halfpi", [128, 1], F32)
        cn = kb.sb(st, "cn", [128, 1], F32)
        sn = kb.sb(st, "sn", [128, 1], F32)
        nsn = kb.sb(st, "nsn", [128, 1], F32)
        tq = kb.sb(st, "tq", [128, 1], F32)
        COS = kb.sb(st, "COS", [128, S], F32)
        SIN = kb.sb(st, "SIN", [128, S], F32)
        T1 = kb.sb(st, "T1", [128, S // 2], F32)
        lamv = kb.sb(st, "lamv", [64, 4], F32)
        prod = kb.sb(st, "prod", [64, 2], F32)
        lps = kb.ps(st, "lps", [128, 512], F32)
        e2 = kb.sb(st, "e2", [128, 2], F32)
        sgl_ = kb.sb(st, "sgl_", [128, 1], F32)

        kb.op("pool", lambda: pool.iota(iota_i[:], pattern=[[1, 128]], base=0, channel_multiplier=-1), (), [iota_i])
        kb.op("pool", lambda: pool.iota(pidx_i[:], pattern=[[0, 1]], base=0, channel_multiplier=1), (), [pidx_i])
        copy("dve", iota_f[:], iota_i[:], [iota_i], [iota_f])
        kb.op("dve", lambda: dve.tensor_single_scalar(out=ident_f[:], in_=iota_f[:], scalar=0.0, op=ALU.is_equal), [iota_f], [ident_f])
        copy("dve", ident_b[:], ident_f[:], [ident_f], [ident_b])
        kb.op("dve", lambda: dve.tensor_single_scalar(out=PiT[:], in_=iota_f[:], scalar=-64.0, op=ALU.is_equal), [iota_f], [PiT])
        kb.op("dve", lambda: dve.tensor_single_scalar(out=Am[:], in_=iota_f[:], scalar=64.0, op=ALU.is_equal), [iota_f], [Am])
        tt("dve", PiT[:], PiT[:], Am[:], ALU.subtract, [PiT, Am], [PiT])
        kb.op("dve", lambda: dve.tensor_single_scalar(out=Am[:], in_=iota_f[:], scalar=32.0, op=ALU.is_equal), [iota_f], [Am])
        kb.op("dve", lambda: dve.tensor_single_scalar(out=Bm[:], in_=iota_f[:], scalar=-32.0, op=ALU.is_equal), [iota_f], [Bm])
        kb.op("dve", lambda: dve.tensor_single_scalar(out=ptmp_i[:], in_=pidx_i[:], scalar=32, op=ALU.bitwise_and), [pidx_i], [ptmp_i])
        copy("dve", m_hi[:], ptmp_i[:], [ptmp_i], [m_hi])
        ts("dve", m_hi[:], m_hi[:], 1.0 / 32.0, None, ALU.mult, None, [m_hi], [m_hi])
        ts("dve", m_lo[:], m_hi[:], -1.0, 1.0, ALU.mult, ALU.add, [m_hi], [m_lo])
        ts("dve", sgn[:], m_hi[:], 2.0, -1.0, ALU.mult, ALU.add, [m_hi], [sgn])
        ts("dve", Pf[:], Am[:], m_lo[:, 0:1], None, ALU.mult, None, [Am, m_lo], [Pf])
        stt(Pm[:], Bm[:], m_hi[:, 0:1], Pf[:], ALU.mult, ALU.add, [Bm, m_hi, Pf], [Pm])
        memset("pool", ones_b[:], 1.0, [ones_b])
        memset("pool", ones_f[:], 1.0, [ones_f])
        memset("pool", avg1024[:], 1.0 / 1024.0, [avg1024])
        memset("pool", avg128[:], 1.0 / 128.0, [avg128])
        memset("pool", bd64[:], 0.0, [bd64])
        memset("pool", bd64[0:64, 0:64], 1.0 / 64.0, [bd64])
        memset("pool", bd64[64:128, 64:128], 1.0 / 64.0, [bd64])
        memset("pool", eps_t[:], EPS, [eps_t])
        memset("pool", halfpi[:], math.pi / 2, [halfpi])
        memset("pool", ones512[:], 1.0, [ones512])
        for o in range(4):
            kb.op("pool", lambda o=o: pool.affine_select(out=M4[:, o, :], in_=ones512[:], pattern=[[1, 512]],
                                                          compare_op=ALU.is_ge, fill=0.0, base=-128 * o,
                                                          channel_multiplier=-1), [ones512], [M4])
        def ld(b, dst, src):
            kb.dma("sp", dst, src, wr=[b], key="small", allow_slow_non_contiguous=True)

        ld(gn0, gn0[:], din["e_norm_g"].rearrange("(c p) -> p c", p=128))
        ld(gn1, gn1[:], din["o_norm_g"].rearrange("(c p) -> p c", p=128))
        for t_ in range(8):
            ld(kw_t, kw_t[:, t_, :], din["e_conv_w"][:, t_ * 128:(t_ + 1) * 128].rearrange("w p -> p w"))
        ld(convb, convb[:], din["e_conv_b"].rearrange("(t p) -> p t", p=128))
        ld(clng, clng[:], din["e_cln_g"].rearrange("(t p) -> p t", p=128))
        ld(clnb, clnb[:], din["e_cln_b"].rearrange("(t p) -> p t", p=128))
        ld(bglu, bglu[:], din["o_b_glu"].rearrange("(t p) -> p t", p=128))
        for hh in range(2):
            ld(qng, qng[hh * 64:(hh + 1) * 64, :], din["e_qn_g"].rearrange("(p o) -> p o", o=1))
            ld(kng, kng[hh * 64:(hh + 1) * 64, :], din["e_kn_g"].rearrange("(p o) -> p o", o=1))
        ld(sgl_, sgl_[:], din["e_subln_g"].rearrange("(p o) -> p o", o=1))
        for i, nme in enumerate(["e_lam_q1", "e_lam_k1", "e_lam_q2", "e_lam_k2"]):
            ld(lamv, lamv[:, i:i + 1], din[nme].rearrange("(p o) -> p o", o=1))
        copy("dve", kw_b[:], kw_t[:], [kw_t], [kw_b])
        lam_init = 0.8 - 0.6 * math.exp(-0.3 * 0)
        ts("dve", sgs[:], sgl_[:], 1.0 - lam_init, None, ALU.mult, None, [sgl_], [sgs])
        tt("dve", prod[:, 0:1], lamv[:, 0:1], lamv[:, 1:2], ALU.mult, [lamv], [prod])
        tt("dve", prod[:, 1:2], lamv[:, 2:3], lamv[:, 3:4], ALU.mult, [lamv], [prod])
        mm(lps[:, 0:2], ones_f[0:64, :], prod[:, :], True, True, [ones_f, prod], [lps])
        act(e2[:], lps[:, 0:2], AF.Exp, [lps], [e2])
        tt("dve", neglam[:], e2[:, 1:2], e2[:, 0:1], ALU.subtract, [e2], [neglam])
        ts("dve", neglam[:], neglam[:], -lam_init, None, ALU.add, None, [neglam], [neglam])

        kb.op("dve", lambda: dve.tensor_single_scalar(out=ptmp_i[:], in_=pidx_i[:], scalar=31, op=ALU.bitwise_and), [pidx_i, m_hi], [ptmp_i])
        copy("dve", freq[:], ptmp_i[:], [ptmp_i], [freq])
        act(freq[:], freq[:], AF.Exp, [freq], [freq], scale=-math.log(10000.0) / 32.0)
        act(cn[:], freq[:], AF.Sin, [freq, halfpi], [cn], bias=halfpi[:, 0:1])
        act(sn[:], freq[:], AF.Sin, [freq], [sn])
        ts("dve", nsn[:], sn[:], -1.0, None, ALU.mult, None, [sn], [nsn])
        memset("dve", COS[:, 0:1], 1.0, [COS])
        memset("dve", SIN[:, 0:1], 0.0, [SIN])
        n = 1
        while n < S:
            ts("dve", T1[:, 0:n], COS[:, 0:n], cn[:, 0:1], None, ALU.mult, None, [COS, cn], [T1])
            stt(COS[:, n:2 * n], SIN[:, 0:n], nsn[:, 0:1], T1[:, 0:n], ALU.mult, ALU.add, [SIN, nsn, T1, COS], [COS])
            ts("dve", T1[:, 0:n], SIN[:, 0:n], cn[:, 0:1], None, ALU.mult, None, [SIN, cn, COS], [T1])
            stt(SIN[:, n:2 * n], COS[:, 0:n], sn[:, 0:1], T1[:, 0:n], ALU.mult, ALU.add, [COS, sn, T1, SIN], [SIN])
            if 2 * n < S:
                ts("dve", tq[:], cn[:], cn[:, 0:1], None, ALU.mult, None, [cn, SIN], [tq])
                stt(tq[:], sn[:], nsn[:, 0:1], tq[:], ALU.mult, ALU.add, [sn, nsn, tq], [tq])
                ts("dve", sn[:], cn[:], sn[:, 0:1], 2.0, ALU.mult, ALU.mult, [cn, sn], [sn])
                copy("dve", cn[:], tq[:], [tq, sn], [cn])
                ts("dve", nsn[:], sn[:], -1.0, None, ALU.mult, None, [sn], [nsn])
            n *= 2
        ts("dve", SIN[:], SIN[:], sgn[:, 0:1], None, ALU.mult, None, [SIN, sgn], [SIN])
        kb.dma("sp", ROT.t[0], COS[:], rd=[COS], wr=[ROT], key="rot_st")
        kb.dma("sp", ROT.t[1], SIN[:], rd=[SIN], wr=[ROT], key="rot_st")
        conv_w(Wb_out0, din["e_w_out"], 4, "wc1")
        conv_w(Wb_in1, din["o_w_in"], 8, "wc2")
        conv_w(Wb_glu, din["o_w_glu"], 4, "wc3")
        conv_w(Wb_out1, din["o_w_out"], 4, "wc4")
        kb.flush()

    if stop_after == "setup":
        kb.final_wait()
        return nc

    x = din["x"]
    with ExitStack() as st:
        hT = kb.sb(st, "hT", [128, 16, 512], BF16)
        wgs = Rot([kb.sb(st, f"wg{i}", [128, 16, 512], BF16) for i in range(2)])
        rcs = Rot([kb.sb(st, f"rc{i}", [128, 2, 512], F32) for i in range(2)])
        xts = Rot([kb.sb(st, f"xt{i}", [128, D], F32) for i in range(2)])
        hn = kb.sb(st, "hn", [128, D], BF16)
        junk = kb.sb(st, "junk", [128, D], BF16)
        ss = kb.sb(st, "ss", [128, 1], F32)
        sd1 = kb.sb(st, "sd1", [128, 1], F32)
        rs1 = kb.sb(st, "rs1", [128, 1], F32)
        u = kb.sb(st, "u", [128, 8, 542], BF16)
        sgl = kb.sb(st, "sgl", [128, 4, 512], BF16)
        sga = kb.sb(st, "sga", [128, 8, 512], BF16)
        cc = kb.sb(st, "cc", [128, 8, 512], F32)
        csqs = Rot([kb.sb(st, f"csq{i}", [128, 512], F32) for i in range(2)])
        DG = kb.sb(st, "DG", [128, 31, 128], BF16)
        oa = kb.sb(st, "oa", [128, 8, 512], BF16)
        mean_sb = kb.sb(st, "mean_sb", [128, 512], F32)
        var = kb.sb(st, "var", [128, 512], F32)
        rsl = kb.sb(st, "rsl", [128, 512], F32)
        tln = Rot([kb.sb(st, f"tln{i}", [128, 512], F32) for i in range(2)])
        aln = Rot([kb.sb(st, f"aln{i}", [128, 512], BF16) for i in range(2)])
        qgs = Rot([kb.sb(st, f"qg{i}", [128, 512], BF16) for i in range(2)])
        sqs = Rot([kb.sb(st, f"sq{i}", [128, 512], BF16) for i in range(2)])
        rsq = kb.sb(st, "rsq", [128, 512], F32)
        t1 = kb.sb(st, "t1", [128, 512], F32)
        t2 = kb.sb(st, "t2", [128, 512], F32)
        qos = Rot([kb.sb(st, f"qo{i}", [128, 512], BF16) for i in range(2)])
        vts = Rot([kb.sb(st, f"vt{i}", [128, 512], BF16) for i in range(2)])
        gbts = Rot([kb.sb(st, f"gbt{i}", [128, 512], BF16) for i in range(2)])
        pss = Rot([kb.ps(st, f"ps{i}", [128, 512], F32) for i in range(4)])
        ptrs = Rot([kb.ps(st, f"ptr{i}", [128, 4, 128], BF16) for i in range(2)])
        mps = kb.ps(st, "mps", [128, 512], F32)
        qps = kb.ps(st, "qps", [128, 512], F32)

        memset("pool", u[:, :, 0:30], 0.0, [u])

        def load_wg(src, gi):
            wg = wgs.next()
            kb.dma("sp", wg[:], src.t[gi], rd=[src], wr=[wg])
            return wg

        def fm_tile(wg, j, hTb):
            ps = pss.next()
            for c in range(16):
                mm(ps[:], wg[:, c, j * 128:(j + 1) * 128], hTb[:, c, :], c == 0, c == 15, [wg, hTb], [ps])
            return ps

        def norm_transpose(xt, gn, hTb, col_ap_fn):
            act(junk[:], xt[:], AF.Square, [xt], [junk, ss], accum=ss[:, 0:1])
            act(sd1[:], ss[:], AF.Sqrt, [ss, eps_t], [sd1], bias=eps_t[:, 0:1], scale=1.0 / D)
            recip(rs1[:], sd1[:], [sd1], [rs1])
            act(hn[:], xt[:], AF.Copy, [xt, rs1], [hn], scale=rs1[:, 0:1])
            for c4 in range(4):
                ptr = ptrs.next()
                for q_ in range(4):
                    c = c4 * 4 + q_
                    kb.op("pe", lambda c=c, q_=q_, ptr=ptr: pe.transpose(out=ptr[:, q_, :], in_=hn[:, c * 128:(c + 1) * 128],
                                                                           identity=ident_b[:]), [hn, ident_b], [ptr])
                tt("dve", col_ap_fn(hTb, c4), ptr[:], bc(gn[:, c4 * 4:c4 * 4 + 4].unsqueeze(2), [128, 4, 128]), ALU.mult,
                   [ptr, gn], [hTb])

        for bi in range(NB):
            t0 = bi * 512
            rc = rcs.next()
            kb.dma("sp", rc[:], ROT.t[:, :, t0:t0 + 512].rearrange("a p t -> p a t"), rd=[ROT], wr=[rc])
            for tti in range(4):
                xt = xts.next()
                kb.dma("sp", xt[:], x[t0 + tti * 128:t0 + (tti + 1) * 128, :], wr=[xt])
                norm_transpose(xt, gn0, hT, lambda hTb, c4, tti=tti: hTb[:, c4 * 4:c4 * 4 + 4, tti * 128:(tti + 1) * 128])

            for half in range(2):
                wg = load_wg(Wb_in0, 2 + half)
                for j in range(4):
                    ps = fm_tile(wg, j, hT)
                    act(sgl[:, j, :], ps[:], AF.Sigmoid, [ps], [sgl])
                wg = load_wg(Wb_in0, 0 + half)
                for j in range(4):
                    jj = half * 4 + j
                    ps = fm_tile(wg, j, hT)
                    tt("dve", u[:, jj, 30:542], ps[:], sgl[:, j, :], ALU.mult, [ps, sgl], [u])
                    tt("pool", DG[:], bc(ident_b[:].unsqueeze(1), [128, 31, 128]),
                       bc(kw_b[:, jj, :].unsqueeze(2), [128, 31, 128]), ALU.mult, [ident_b, kw_b], [DG])
                    cps = pss.next()
                    for tap in range(31):
                        mm(cps[:], DG[:, tap, :], u[:, jj, tap:tap + 512], tap == 0, tap == 30, [DG, u], [cps])
                    act(cc[:, jj, :], cps[:], AF.Identity, [cps, convb], [cc], bias=convb[:, jj:jj + 1])
                    csq = csqs.next()
                    act(csq[:], cps[:], AF.Square, [cps, convb], [csq], bias=convb[:, jj:jj + 1])
                    mm(mps[:], avg1024[:], cc[:, jj, :], jj == 0, jj == 7, [avg1024, cc], [mps])
                    mm(qps[:], avg1024[:], csq[:], jj == 0, jj == 7, [avg1024, csq], [qps])
            copy("pool", u[:, :, 0:30], u[:, :, 512:542], [u], [u])
            for half in range(2):
                wg = load_wg(Wb_in0, 4 + half)
                for j in range(4):
                    ps = fm_tile(wg, j, hT)
                    act(sga[:, half * 4 + j, :], ps[:], AF.Silu, [ps], [sga])
            act(mean_sb[:], mps[:], AF.Copy, [mps], [mean_sb])
            tt("dve", var[:], mean_sb[:], mean_sb[:], ALU.mult, [mean_sb], [var])
            tt("dve", var[:], qps[:], var[:], ALU.subtract, [qps, var], [var])
            act(var[:], var[:], AF.Sqrt, [var, eps_t], [var], bias=eps_t[:, 0:1])
            recip(rsl[:], var[:], [var], [rsl])
            for jj in range(8):
                tl = tln.next()
                al = aln.next()
                tt("dve", tl[:], cc[:, jj, :], mean_sb[:], ALU.subtract, [cc, mean_sb], [tl])
                tt("dve", tl[:], tl[:], rsl[:], ALU.mult, [tl, rsl], [tl])
                act(al[:], tl[:], AF.Silu, [tl, clng, clnb], [al], scale=clng[:, jj:jj + 1], bias=clnb[:, jj:jj + 1])
                tt("pool", oa[:, jj, :], al[:], sga[:, jj, :], ALU.mult, [al, sga], [oa])
            kb.dma("pool", MIXT.t[0:8, :, t0:t0 + 512].rearrange("j p t -> p j t"), oa[:], rd=[oa], wr=[MIXT])

            for gi in (6, 7, 8, 9):
                wg = load_wg(Wb_in0, gi)
                isq = gi < 8
                gvec = qng if isq else kng
                dst = QT if isq else KT
                for j in range(4):
                    hh = (gi % 2) * 4 + j
                    ps = fm_tile(wg, j, hT)
                    qg = qgs.next()
                    sq = sqs.next()
                    qo = qos.next()
                    act(qg[:], ps[:], AF.Copy, [ps, gvec], [qg], scale=gvec[:, 0:1])
                    act(sq[:], ps[:], AF.Square, [ps], [sq])
                    stp = pss.next()
                    mm(stp[:], bd64[:], sq[:], True, True, [bd64, sq], [stp])
                    rtp = pss.next()
                    mm(rtp[:], Pm[:], qg[:], True, True, [Pm, qg], [rtp])
                    act(rsq[:], stp[:], AF.Sqrt, [stp, eps_t], [rsq], bias=eps_t[:, 0:1])
                    recip(rsq[:], rsq[:], [rsq], [rsq])
                    tt("pool", t1[:], qg[:], rc[:, 0, :], ALU.mult, [qg, rc], [t1])
                    tt("dve", t2[:], rtp[:], rc[:, 1, :], ALU.mult, [rtp, rc], [t2])
                    tt("pool", t1[:], t1[:], t2[:], ALU.add, [t1, t2], [t1])
                    stt(qo[:], t1[:], 0.125 if isq else 1.0, rsq[:], ALU.mult, ALU.mult, [t1, rsq], [qo])
                    kb.dma("pool", dst.t[hh, :, t0:t0 + 512], qo[:], rd=[qo], wr=[dst])
            for gi in (10, 11):
                wg = load_wg(Wb_in0, gi)
                for tti in range(4):
                    ps = pss.next()
                    for c in range(16):
                        mm(ps[:], hT[:, c, tti * 128:(tti + 1) * 128], wg[:, c, :], c == 0, c == 15, [wg, hT], [ps])
                    vt = vts.next()
                    act(vt[:], ps[:], AF.Copy, [ps], [vt])
                    h0 = (gi - 10) * 4
                    kb.dma("pool", VV.t[h0:h0 + 4, :, bi * 4 + tti, :].rearrange("h p d -> p h d"),
                           vt[:].rearrange("p (h d) -> p h d", h=4), rd=[vt], wr=[VV])
            for gi in (12, 13):
                wg = load_wg(Wb_in0, gi)
                for j in range(4):
                    hh = (gi - 12) * 4 + j
                    ps = fm_tile(wg, j, hT)
                    gbt = gbts.next()
                    act(gbt[:], ps[:], AF.Silu, [ps], [gbt])
                    kb.dma("pool", GB.t[hh, :, t0:t0 + 512], gbt[:], rd=[gbt], wr=[GB])
        kb.flush()

    if stop_after == "l0p":
        kb.final_wait()
        return nc

    with ExitStack() as st:
        kTs = Rot([kb.sb(st, f"kT{i}", [128, S], BF16) for i in range(2)])
        qTs = Rot([kb.sb(st, f"qT{i}", [128, S], BF16) for i in range(2)])
        vhs = Rot([kb.sb(st, f"vh{i}", [128, 32, 128], BF16) for i in range(2)])
        gbs = Rot([kb.sb(st, f"gbh{i}", [128, S], BF16) for i in range(2)])
        e1s = Rot([kb.sb(st, f"e1_{i}", [128, 512], BF16) for i in range(3)])
        e2s = Rot([kb.sb(st, f"e2_{i}", [128, 512], BF16) for i in range(3)])
        pscore = Rot([kb.ps(st, f"psc{i}", [128, 512], F32) for i in range(4)])
        o1 = kb.ps(st, "o1", [128, 512], F32)
        d1 = kb.ps(st, "d1", [128, 512], F32)
        o2 = kb.ps(st, "o2", [128, 512], F32)
        d2 = kb.ps(st, "d2", [128, 512], F32)
        r1 = kb.sb(st, "r1", [128, 512], F32)
        ta = kb.sb(st, "ta", [128, 512], F32)
        tb = kb.sb(st, "tb", [128, 512], F32)
        od = kb.sb(st, "od", [128, 512], F32)
        osq = kb.sb(st, "osq", [128, 512], F32)
        rsf = kb.sb(st, "rsf", [128, 512], F32)
        obs = Rot([kb.sb(st, f"ob{i}", [128, 512], BF16) for i in range(2)])
        for h in range(8):
            kT = kTs.next(); qT = qTs.next(); vh = vhs.next(); gb = gbs.next()
            kb.dma("sp", kT[:], KT.t[h], rd=[KT], wr=[kT])
            kb.dma("sp", qT[:], QT.t[h], rd=[QT], wr=[qT])
            kb.dma("sp", vh[:], VV.t[h], rd=[VV], wr=[vh])
            kb.dma("sp", gb[:], GB.t[h], rd=[GB], wr=[gb])
            for qb in range(8):
                qs = slice(qb * 512, (qb + 1) * 512)
                nkt = 4 * (qb + 1)

                def scores(kt, qb=qb, qs=qs):
                    ks = slice(kt * 128, (kt + 1) * 128)
                    s1 = pscore.next(); s2 = pscore.next()
                    mm(s1[:], kT[0:64, ks], qT[0:64, qs], True, True, [kT, qT], [s1])
                    mm(s2[:], kT[64:128, ks], qT[64:128, qs], True, True, [kT, qT], [s2])
                    e1 = e1s.next(); e2 = e2s.next()
                    act(e1[:], s1[:], AF.Exp, [s1], [e1])
                    act(e2[:], s2[:], AF.Exp, [s2], [e2])
                    if kt >= 4 * qb:
                        o = kt - 4 * qb
                        tt("pool", e1[:], e1[:], M4[:, o, :], ALU.mult, [e1, M4], [e1])
                        tt("pool", e2[:], e2[:], M4[:, o, :], ALU.mult, [e2, M4], [e2])
                    return e1, e2

                pend = scores(0)
                for kt in range(nkt):
                    nxt = scores(kt + 1) if kt + 1 < nkt else None
                    e1, e2 = pend
                    first, lastk = kt == 0, kt == nkt - 1
                    mm(o1[:], vh[:, kt, :], e1[:], first, lastk, [vh, e1], [o1])
                    mm(d1[:], ones_b[:], e1[:], first, lastk, [ones_b, e1], [d1])
                    mm(o2[:], vh[:, kt, :], e2[:], first, lastk, [vh, e2], [o2])
                    mm(d2[:], ones_b[:], e2[:], first, lastk, [ones_b, e2], [d2])
                    pend = nxt
                recip(r1[:], d1[:], [d1], [r1])
                tt("dve", ta[:], o1[:], r1[:], ALU.mult, [o1, r1], [ta])
                recip(r1[:], d2[:], [d2, ta], [r1])
                tt("dve", tb[:], o2[:], r1[:], ALU.mult, [o2, r1], [tb])
                stt(od[:], tb[:], neglam[:, 0:1], ta[:], ALU.mult, ALU.add, [tb, neglam, ta], [od])
                act(osq[:], od[:], AF.Square, [od], [osq])
                stp = pscore.next()
                mm(stp[:], avg128[:], osq[:], True, True, [avg128, osq], [stp])
                act(rsf[:], stp[:], AF.Sqrt, [stp, eps_t], [rsf], bias=eps_t[:, 0:1])
                recip(rsf[:], rsf[:], [rsf], [rsf])
                tt("dve", od[:], od[:], rsf[:], ALU.mult, [od, rsf], [od])
                ob = obs.next()
                stt(ob[:], od[:], sgs[:, 0:1], gb[:, qs], ALU.mult, ALU.mult, [od, sgs, gb], [ob])
                kb.dma("pool", MIXT.t[8 + h, :, qs], ob[:], rd=[ob], wr=[MIXT])
        kb.flush()

    if stop_after == "l0a":
        kb.final_wait()
        return nc

    UF = kb.dram("UF", [16, 128, S], BF16, kind=kind("UF"))
    GF = kb.dram("GF", [16, 128, S], BF16, kind=kind("GF"))
    with ExitStack() as st:
        mixTs = Rot([kb.sb(st, f"mixT{i}", [128, 16, 512], BF16) for i in range(2)])
        wgs = Rot([kb.sb(st, f"wg{i}", [128, 16, 512], BF16) for i in range(2)])
        xts4 = [kb.sb(st, f"xq{i}", [128, D], F32) for i in range(4)]
        hn = kb.sb(st, "hn", [128, D], BF16)
        junk = kb.sb(st, "junk", [128, D], BF16)
        ss = kb.sb(st, "ss", [128, 1], F32)
        sd1 = kb.sb(st, "sd1", [128, 1], F32)
        rs1 = kb.sb(st, "rs1", [128, 1], F32)
        h1T = kb.sb(st, "h1T", [128, 16, 512], BF16)
        ubs = Rot([kb.sb(st, f"ub{i}", [128, 512], BF16) for i in range(3)])
        pss = Rot([kb.ps(st, f"ps{i}", [128, 512], F32) for i in range(4)])
        ptrs = Rot([kb.ps(st, f"ptr{i}", [128, 4, 128], BF16) for i in range(2)])

        def load_wg(src, gi):
            wg = wgs.next()
            kb.dma("sp", wg[:], src.t[gi], rd=[src], wr=[wg])
            return wg

        def norm_transpose_perm(xt, gn, hTb, tti):
            act(junk[:], xt[:], AF.Square, [xt], [junk, ss], accum=ss[:, 0:1])
            act(sd1[:], ss[:], AF.Sqrt, [ss, eps_t], [sd1], bias=eps_t[:, 0:1], scale=1.0 / D)
            recip(rs1[:], sd1[:], [sd1], [rs1])
            act(hn[:], xt[:], AF.Copy, [xt, rs1], [hn], scale=rs1[:, 0:1])
            for c4 in range(4):
                ptr = ptrs.next()
                for q_ in range(4):
                    c = c4 * 4 + q_
                    kb.op("pe", lambda c=c, q_=q_, ptr=ptr: pe.transpose(out=ptr[:, q_, :], in_=hn[:, c * 128:(c + 1) * 128],
                                                                           identity=ident_b[:]), [hn, ident_b], [ptr])
                oap = hTb[:, c4 * 4:c4 * 4 + 4, :].rearrange("p k (t c) -> p k t c", t=8)[:, :, :, 16 * tti:16 * tti + 16]
                iap = ptr[:].rearrange("p k (c t) -> p k t c", t=8)
                tt("dve", oap, iap, bc(gn[:, c4 * 4:c4 * 4 + 4].unsqueeze(2).unsqueeze(3), [128, 4, 8, 16]), ALU.mult,
                   [ptr, gn], [hTb])

        for bi in range(NB):
            t0 = bi * 512
            mixT = mixTs.next()
            kb.dma("sp", mixT[:], MIXT.t[:, :, t0:t0 + 512].rearrange("c p t -> p c t"), rd=[MIXT], wr=[mixT])
            for tti in range(4):
                kb.dma("sp", xts4[tti][:], x[t0 + tti * 128:t0 + (tti + 1) * 128, :], wr=[xts4[tti]])
            for og in range(4):
                wg = load_wg(Wb_out0, og)
                for tti in range(4):
                    ps = pss.next()
                    for c in range(16):
                        mm(ps[:], mixT[:, c, tti * 128:(tti + 1) * 128], wg[:, c, :], c == 0, c == 15, [mixT, wg], [ps])
                    xs = xts4[tti][:, og * 512:(og + 1) * 512]
                    tt("dve", xs, ps[:], xs, ALU.add, [ps, xts4[tti]], [xts4[tti]])
            for tti in range(4):
                kb.dma("pool", X1.t[t0 + tti * 128:t0 + (tti + 1) * 128, :], xts4[tti][:], rd=[xts4[tti]], wr=[X1])
                norm_transpose_perm(xts4[tti], gn1, h1T, tti)
            for gi in range(8):
                wg = load_wg(Wb_in1, gi)
                for j in range(4):
                    ps = pss.next()
                    for c in range(16):
                        mm(ps[:], wg[:, c, j * 128:(j + 1) * 128], h1T[:, c, :], c == 0, c == 15, [wg, h1T], [ps])
                    ub = ubs.next()
                    ft = (gi % 4) * 4 + j
                    dst = UF if gi < 4 else GF
                    act(ub[:], ps[:], AF.Copy if gi < 4 else AF.Silu, [ps], [ub])
                    kb.dma("pool", dst.t[ft].rearrange("p (t c) -> p t c", t=8)[:, :, bi * 64:(bi + 1) * 64],
                           ub[:].rearrange("p (t c) -> p t c", t=8), rd=[ub], wr=[dst])
        kb.flush()

    if stop_after == "l1p":
        kb.final_wait()
        return nc

    S5W = kb.dram("S5W", [128, 128, 4, 128], BF16, kind=kind("S5W"))
    CLv = kb.sb(cs, "CLv", [128, 9, 128], F32)
    SLv = kb.sb(cs, "SLv", [128, 9, 128], F32)
    RLt = kb.sb(cs, "RLt", [128, 128], F32)
    Dc = kb.sb(cs, "Dc", [128, 128], F32)
    with ExitStack() as st:
        def T(name):
            return kb.sb(st, name, [128, 128], F32)
        AN = T("AN"); are = T("are"); aim = T("aim"); ldt = T("ldt"); dtt = T("dtt")
        th = T("th"); xr = T("xr"); kk = T("kk"); rr = T("rr"); r2 = T("r2"); acc = T("acc")
        sinT = T("sinT"); cosT = T("cosT"); EE = T("EE"); lR = T("lR"); lI = T("lI")
        den = T("den"); nr = T("nr"); fR = T("fR"); fI = T("fI"); nuR = T("nuR"); nuI = T("nuI")
        muI = T("muI"); l2R = T("l2R"); l2I = T("l2I"); l4R = T("l4R"); l4I = T("l4I")
        l8R = T("l8R"); l8I = T("l8I"); l7R = T("l7R"); l7I = T("l7I"); x1_ = T("x1_"); x2_ = T("x2_")
        x3_ = T("x3_"); x4_ = T("x4_"); mask8 = T("mask8"); onesq = T("onesq")
        tps = Rot([kb.ps(st, f"tps{i}", [128, 128], F32) for i in range(4)])

        def D_(fn, rd, wr):
            kb.op("dve", fn, rd, wr)

        def mul(o, a, b):
            tt("dve", o[:], a[:], b[:], ALU.mult, [a, b], [o])

        def add(o, a, b):
            tt("dve", o[:], a[:], b[:], ALU.add, [a, b], [o])

        def sub(o, a, b):
            tt("dve", o[:], a[:], b[:], ALU.subtract, [a, b], [o])

        def cmul(oR, oI, aR, aI, bR, bI):
            mul(x1_, aR, bR); mul(x2_, aI, bI); mul(x3_, aR, bI); mul(x4_, aI, bR)
            sub(oR, x1_, x2_); add(oI, x3_, x4_)

        def horner(o, xx, coef):
            n_ = len(coef) - 1
            ts("dve", o[:], xx[:], float(coef[n_]), None, ALU.mult, None, [xx], [o])
            for k_ in range(n_ - 1, 0, -1):
                stt(o[:], o[:], float(coef[k_]), xx[:], ALU.add, ALU.mult, [o, xx], [o])
            ts("dve", o[:], o[:], float(coef[0]), None, ALU.add, None, [o], [o])

        for src, dstt in ((din["o_A_re"], are), (din["o_A_im"], aim)):
            kb.dma("sp", AN[:, 0:64], src, wr=[AN], key="small")
            kb.dma("sp", AN[:, 64:128], src, wr=[AN], key="small")
            tp = tps.next()
            kb.op("pe", lambda tp=tp: pe.transpose(out=tp[:], in_=AN[:], identity=ident_f[:]), [AN, ident_f], [tp])
            copy("dve", dstt[:], tp[:], [tp], [dstt])
        kb.dma("sp", ldt[:], din["o_log_dt"].partition_broadcast(128), wr=[ldt], key="small", allow_slow_non_contiguous=True)
        for tau in range(8):
            kb.dma("sp", Dc[tau * 16:(tau + 1) * 16, :], din["o_D"].rearrange("(g m) -> m g", m=16), wr=[Dc], key="small",
                   allow_slow_non_contiguous=True)
        act(dtt[:], ldt[:], AF.Exp, [ldt], [dtt])
        mul(th, aim, dtt)
        mul(xr, are, dtt)
        MAGIC = 12582912.0
        ts("dve", kk[:], th[:], 1.0 / (2 * math.pi), MAGIC, ALU.mult, ALU.add, [th], [kk])
        ts("dve", kk[:], kk[:], -MAGIC, None, ALU.add, None, [kk], [kk])
        c1 = 6.28125
        c2 = float(np.float32(np.float32(2 * math.pi - c1).view(np.uint32) & np.uint32(0xFFFFF000)).view(np.float32)) if False else 0.0019350051879882812
        c3 = 2 * math.pi - c1 - c2
        stt(rr[:], kk[:], -c1, th[:], ALU.mult, ALU.add, [kk, th], [rr])
        stt(rr[:], kk[:], -c2, rr[:], ALU.mult, ALU.add, [kk, rr], [rr])
        stt(rr[:], kk[:], -c3, rr[:], ALU.mult, ALU.add, [kk, rr], [rr])
        mul(r2, rr, rr)
        horner(acc, r2, [(-1.0) ** k_ / math.factorial(2 * k_ + 1) for k_ in range(11)])
        mul(sinT, acc, rr)
        horner(cosT, r2, [(-1.0) ** k_ / math.factorial(2 * k_) for k_ in range(12)])
        horner(EE, xr, [1.0 / math.factorial(k_) for k_ in range(8)])
        mul(lR, EE, cosT)
        mul(lI, EE, sinT)
        mul(den, are, are); mul(x1_, aim, aim); add(den, den, x1_)
        recip(den[:], den[:], [den], [den])
        ts("dve", nr[:], lR[:], -1.0, None, ALU.add, None, [lR], [nr])
        mul(x1_, nr, are); mul(x2_, lI, aim); add(x1_, x1_, x2_); mul(fR, x1_, den)
        mul(x1_, lI, are); mul(x2_, nr, aim); sub(x1_, x1_, x2_); mul(fI, x1_, den)
        mul(x1_, EE, EE)
        recip(x1_[:], x1_[:], [x1_], [x1_])
        mul(nuR, lR, x1_)
        mul(nuI, lI, x1_)
        ts("dve", nuI[:], nuI[:], -1.0, None, ALU.mult, None, [nuI], [nuI])
        ts("dve", muI[:], lI[:], -1.0, None, ALU.mult, None, [lI], [muI])
        cmul(l2R, l2I, lR, lI, lR, lI)
        cmul(l4R, l4I, l2R, l2I, l2R, l2I)
        cmul(l8R, l8I, l4R, l4I, l4R, l4I)
        cmul(l7R, l7I, l4R, l4I, l2R, l2I)
        cmul(l7R, l7I, l7R, l7I, lR, lI)
        mul(RLt, EE, EE); mul(RLt, RLt, RLt); mul(RLt, RLt, RLt)
        recip(x1_[:], RLt[:], [RLt], [x1_])
        tt("dve", CLv[:, 0, :], l8R[:], x1_[:], ALU.mult, [l8R, x1_], [CLv])
        tt("dve", SLv[:, 0, :], l8I[:], x1_[:], ALU.mult, [l8I, x1_], [SLv])
        for j in range(8):
            tt("dve", x2_[:], CLv[:, j, :], CLv[:, j, :], ALU.mult, [CLv], [x2_])
            tt("dve", x3_[:], SLv[:, j, :], SLv[:, j, :], ALU.mult, [SLv], [x3_])
            tt("dve", CLv[:, j + 1, :], x2_[:], x3_[:], ALU.subtract, [x2_, x3_], [CLv])
            tt("dve", x2_[:], CLv[:, j, :], SLv[:, j, :], ALU.mult, [CLv, SLv], [x2_])
            ts("dve", SLv[:, j + 1, :], x2_[:], 2.0, None, ALU.mult, None, [x2_], [SLv])
        memset("pool", onesq[:], 1.0, [onesq])
        kb.op("pool", lambda: pool.affine_select(out=mask8[:], in_=onesq[:], pattern=[[16, 8], [0, 16]], compare_op=ALU.is_ge,
                                                  fill=0.0, base=15, channel_multiplier=-1), [onesq], [mask8])

        GB_ = 16
        BA = kb.sb(st, "BA", [128, GB_, 16], F32)
        BAp = kb.sb(st, "BAp", [128, GB_, 16], F32)
        CN = kb.sb(st, "CN", [128, 128], F32)
        CNp = kb.sb(st, "CNp", [128, 128], F32)
        VA = kb.sb(st, "VA", [128, GB_, 8, 16], F32)
        VAp = kb.sb(st, "VAp", [128, GB_, 8, 16], F32)
        WA = kb.sb(st, "WA", [128, GB_, 9, 16], F32)
        WAp = kb.sb(st, "WAp", [128, GB_, 9, 16], F32)
        WBA = kb.sb(st, "WBA", [128, GB_, 8, 16], F32)
        WBAp = kb.sb(st, "WBAp", [128, GB_, 8, 16], F32)
        y1 = kb.sb(st, "y1", [128, GB_, 16], F32)
        y2 = kb.sb(st, "y2", [128, GB_, 16], F32)
        y3 = kb.sb(st, "y3", [128, GB_, 16], F32)
        y4 = kb.sb(st, "y4", [128, GB_, 16], F32)
        S5ts = Rot([kb.sb(st, f"S5t{i}", [128, GB_, 4, 128], BF16) for i in range(2)])

        def cmat(oA, oAp, iA, iAp, zR, zI, g0, rdo, wro):
            zr = bc(zR[:, g0:g0 + GB_].unsqueeze(2), [128, GB_, 16])
            zi = bc(zI[:, g0:g0 + GB_].unsqueeze(2), [128, GB_, 16])
            tt("dve", y1[:], iA, zr, ALU.mult, rdo + [zR], [y1])
            tt("dve", y2[:], iAp, zi, ALU.mult, rdo + [zI], [y2])
            tt("pool", y3[:], iAp, zr, ALU.mult, rdo + [zR], [y3])
            tt("pool", y4[:], iA, zi, ALU.mult, rdo + [zI], [y4])
            tt("dve", oA, y1[:], y2[:], ALU.add, [y1, y2], wro)
            tt("pool", oAp, y3[:], y4[:], ALU.subtract, [y3, y4], wro)

        for gb_ in range(128 // GB_):
            g0 = gb_ * GB_
            gsl = slice(g0, g0 + GB_)
            kb.dma("sp", BA[0:64], din["o_B_re"][gsl].rearrange("g p m -> p g m"), wr=[BA], key="s5ld")
            kb.dma("sp", BA[64:128], din["o_B_im"][gsl].rearrange("g p m -> p g m"), wr=[BA], key="s5ld")
            kb.dma("sp", BAp[0:64], din["o_B_im"][gsl].rearrange("g p m -> p g m"), wr=[BAp], key="s5ld")
            kb.dma("sp", BAp[64:128], din["o_B_re"][gsl].rearrange("g p m -> p g m"), wr=[BAp], key="s5ld")
            ts("dve", BAp[0:64], BAp[0:64], -1.0, None, ALU.mult, None, [BAp], [BAp])
            for hb in range(GB_ // 8):
                gs8 = slice(g0 + hb * 8, g0 + hb * 8 + 8)
                kb.dma("sp", CN[:, 0:64], din["o_C_re"][gs8].rearrange("g m p -> (g m) p"), wr=[CN], key="s5ld")
                kb.dma("sp", CN[:, 64:128], din["o_C_im"][gs8].rearrange("g m p -> (g m) p"), wr=[CN], key="s5ld")
                kb.dma("sp", CNp[:, 0:64], din["o_C_im"][gs8].rearrange("g m p -> (g m) p"), wr=[CNp], key="s5ld")
                kb.dma("sp", CNp[:, 64:128], din["o_C_re"][gs8].rearrange("g m p -> (g m) p"), wr=[CNp], key="s5ld")
                tp = tps.next()
                kb.op("pe", lambda tp=tp: pe.transpose(out=tp[:], in_=CN[:], identity=ident_f[:]), [CN, ident_f], [tp])
                copy("dve", WA[:, hb * 8:hb * 8 + 8, 0, :], tp[:].rearrange("p (g m) -> p g m", m=16), [tp], [WA])
                tp = tps.next()
                kb.op("pe", lambda tp=tp: pe.transpose(out=tp[:], in_=CNp[:], identity=ident_f[:]), [CNp, ident_f], [tp])
                copy("dve", WAp[:, hb * 8:hb * 8 + 8, 0, :], tp[:].rearrange("p (g m) -> p g m", m=16), [tp], [WAp])
            ts("dve", WA[64:128, :, 0, :], WA[64:128, :, 0, :], -1.0, None, ALU.mult, None, [WA], [WA])
            cmat(VA[:, :, 0, :], VAp[:, :, 0, :], BA[:], BAp[:], fR, fI, g0, [BA, BAp], [VA, VAp])
            for sg in range(7):
                cmat(VA[:, :, sg + 1, :], VAp[:, :, sg + 1, :], VA[:, :, sg, :], VAp[:, :, sg, :], nuR, nuI, g0, [VA, VAp], [VA, VAp])
            for k_ in range(8):
                cmat(WA[:, :, k_ + 1, :], WAp[:, :, k_ + 1, :], WA[:, :, k_, :], WAp[:, :, k_, :], lR, muI, g0, [WA, WAp], [WA, WAp])
            for sg in range(8):
                cmat(WBA[:, :, sg, :], WBAp[:, :, sg, :], VA[:, :, sg, :], VAp[:, :, sg, :], l7R, l7I, g0, [VA, VAp], [WBA, WBAp])
            S5t = S5ts.next()
            for g in range(GB_):
                tp = tps.next()
                kb.op("pe", lambda tp=tp, g=g: pe.matmul(tp[:], lhsT=VA[:, g, :, :].rearrange("p s m -> p (s m)"),
                                                          rhs=WA[:, g, 0:8, :].rearrange("p s m -> p (s m)"), start=True, stop=True),
                      [VA, WA], [tp])
                tt("dve", S5t[:, g, 2, :], tp[:], mask8[:], ALU.mult, [tp, mask8], [S5t])
                tp = tps.next()
                kb.op("pe", lambda tp=tp, g=g: pe.transpose(out=tp[:], in_=WBA[:, g, :, :].rearrange("p s m -> p (s m)"),
                                                             identity=ident_f[:]), [WBA, ident_f], [tp])
                act(S5t[:, g, 0, :], tp[:], AF.Copy, [tp], [S5t])
                tp = tps.next()
                kb.op("pe", lambda tp=tp, g=g: pe.transpose(out=tp[:], in_=WBAp[:, g, :, :].rearrange("p s m -> p (s m)"),
                                                             identity=ident_f[:]), [WBAp, ident_f], [tp])
                act(S5t[:, g, 1, :], tp[:], AF.Copy, [tp], [S5t], scale=-1.0)
            copy("pool", S5t[:, :, 3, :].rearrange("p g (s m) -> p g s m", m=16), WA[:, :, 1:9, :], [WA], [S5t])
            kb.dma("pool", S5W.t[gsl].rearrange("g p k c -> p g k c"), S5t[:], rd=[S5t], wr=[S5W])
        kb.flush()

    if stop_after == "s5setup":
        kb.final_wait()
        return nc

    ZF = kb.dram("ZF", [16, 128, S], BF16, kind=kind("ZF"))
    with ExitStack() as st:
        TG = 4
        COSs = Rot([kb.sb(st, f"COSc{i}", [128, TG, 512], F32) for i in range(3)])
        SINs = Rot([kb.sb(st, f"SINc{i}", [128, TG, 512], F32) for i in range(3)])
        Y1 = kb.sb(st, "Y1", [128, TG, 256], F32)
        Y2 = kb.sb(st, "Y2", [128, TG, 256], F32)
        pAs = Rot([kb.ps(st, f"pA{i}", [128, 512], F32) for i in range(2)])
        pBs = Rot([kb.ps(st, f"pB{i}", [128, 512], F32) for i in range(2)])
        pYs = Rot([kb.ps(st, f"pY{i}", [128, 512], F32) for i in range(2)])
        pGs = Rot([kb.ps(st, f"pG{i}", [128, 512], F32) for i in range(2)])

        def FR(name, n, dt=F32):
            return Rot([kb.sb(st, f"{name}{i}", [128, 512], dt) for i in range(n)])
        t1s = FR("st1", 2); t2s = FR("st2", 2); cAs = FR("cA", 2); gAs = FR("gA", 4); t5s = FR("st5", 2); t6s = FR("st6", 2)
        ysbs = FR("ysb", 4); y2s = FR("y2", 2); sgms = FR("sgm", 3); Hps = FR("Hp", 3, BF16); zts = FR("zt", 3, BF16)
        Ucs = Rot([kb.sb(st, f"Ucm{i}", [128, 512], BF16) for i in range(8)])
        Wgs = Rot([kb.sb(st, f"Wgm{i}", [128, 4, 128], BF16) for i in range(8)])
        for hp in Hps.bufs:
            memset("pool", hp[:, 0:1], 0.0, [hp])
        tabs = {}

        def gen_tables(ts_):
            COSc = COSs.next(); SINc = SINs.next()
            memset("pool", COSc[:, :, 0:1], 1.0, [COSc])
            memset("pool", SINc[:, :, 0:1], 0.0, [SINc])
            gs_ = slice(TG * ts_, TG * ts_ + TG)
            for j in range(9):
                n = 1 << j
                cn_ = bc(CLv[:, j, gs_].unsqueeze(2), [128, TG, n])
                sn_ = bc(SLv[:, j, gs_].unsqueeze(2), [128, TG, n])
                tt("pool", Y1[:, :, 0:n], COSc[:, :, 0:n], cn_, ALU.mult, [COSc, CLv], [Y1])
                tt("pool", Y2[:, :, 0:n], SINc[:, :, 0:n], sn_, ALU.mult, [SINc, SLv], [Y2])
                tt("pool", COSc[:, :, n:2 * n], Y1[:, :, 0:n], Y2[:, :, 0:n], ALU.subtract, [Y1, Y2], [COSc])
                tt("pool", Y1[:, :, 0:n], SINc[:, :, 0:n], cn_, ALU.mult, [SINc, CLv], [Y1])
                tt("pool", Y2[:, :, 0:n], COSc[:, :, 0:n], sn_, ALU.mult, [COSc, SLv], [Y2])
                tt("pool", SINc[:, :, n:2 * n], Y1[:, :, 0:n], Y2[:, :, 0:n], ALU.add, [Y1, Y2], [SINc])
            tabs[ts_] = (COSc, SINc)

        ctx = {}

        def s0(g):
            ft, j8 = divmod(g, 8)
            c = ctx[g] = {}
            c["Uc"] = Uc = Ucs.next(); c["Wg"] = Wg = Wgs.next()
            kb.dma("sp", Uc[:], UF.t[ft, 16 * j8:16 * j8 + 16, :].rearrange("m (t c) -> t m c", t=8), rd=[UF], wr=[Uc])
            kb.dma("sp", Wg[:], S5W.t[g], rd=[S5W], wr=[Wg])
            c["pA"] = pA = pAs.next(); c["pB"] = pB = pBs.next()
            mm(pA[:], Wg[:, 0, :], Uc[:], True, True, [Wg, Uc], [pA])
            mm(pB[:], Wg[:, 1, :], Uc[:], True, True, [Wg, Uc], [pB])

        def s1(g):
            ft, j8 = divmod(g, 8)
            c = ctx[g]
            COSc, SINc = tabs[g // TG]
            co = COSc[:, g % TG, :]; si = SINc[:, g % TG, :]
            t1 = t1s.next(); t2 = t2s.next(); cA = cAs.next(); c["gA"] = gA = gAs.next()
            tt("dve", t1[:], c["pA"][:], co, ALU.mult, [c["pA"], COSc], [t1])
            tt("dve", t2[:], c["pB"][:], si, ALU.mult, [c["pB"], SINc], [t2])
            tt("dve", cA[:], t1[:], t2[:], ALU.add, [t1, t2], [cA])
            rl = bc(RLt[:, g:g + 1], [128, 512])
            kb.op("dve", lambda: dve.tensor_tensor_scan(out=gA[:], data0=rl, data1=cA[:], initial=0.0, op0=ALU.mult, op1=ALU.add),
                  [RLt, cA], [gA])

        def s2(g):
            c = ctx[g]
            c["pG"] = pG = pGs.next()
            mm(pG[:], PiT[:], c["gA"][:], True, True, [PiT, c["gA"]], [pG])

        def s3(g):
            ft, j8 = divmod(g, 8)
            c = ctx[g]
            COSc, SINc = tabs[g // TG]
            co = COSc[:, g % TG, :]; si = SINc[:, g % TG, :]
            t5 = t5s.next(); t6 = t6s.next(); c["Hp"] = Hp = Hps.next()
            tt("dve", t5[:], c["gA"][:], co, ALU.mult, [c["gA"], COSc], [t5])
            tt("dve", t6[:], c["pG"][:], si, ALU.mult, [c["pG"], SINc], [t6])
            tt("dve", Hp[:, 1:512], t5[:, 0:511], t6[:, 0:511], ALU.subtract, [t5, t6], [Hp])

        def s4(g):
            c = ctx[g]
            c["pY"] = pY = pYs.next()
            mm(pY[:], c["Wg"][:, 2, :], c["Uc"][:], True, False, [c["Wg"], c["Uc"]], [pY])
            mm(pY[:], c["Wg"][:, 3, :], c["Hp"][:], False, True, [c["Wg"], c["Hp"]], [pY])

        def s5(g):
            c = ctx[g]
            c["ysb"] = ysb = ysbs.next(); y2 = y2s.next(); c["y2"] = y2
            stt(ysb[:], c["Uc"][:], Dc[:, g:g + 1], c["pY"][:], ALU.mult, ALU.add, [c["Uc"], Dc, c["pY"]], [ysb])
            tt("dve", y2[:], ysb[:], ysb[:], ALU.mult, [ysb], [y2])
            ts("dve", y2[:], y2[:], 0.044715, 1.0, ALU.mult, ALU.add, [y2], [y2])
            tt("dve", y2[:], y2[:], ysb[:], ALU.mult, [y2, ysb], [y2])

        def s6(g):
            c = ctx[g]
            c["sg"] = sg = sgms.next()
            act(sg[:], c["y2"][:], AF.Sigmoid, [c["y2"]], [sg], scale=2.0 * math.sqrt(2.0 / math.pi))

        def s7(g):
            ft, j8 = divmod(g, 8)
            c = ctx.pop(g)
            zt = zts.next()
            tt("dve", zt[:], c["ysb"][:], c["sg"][:], ALU.mult, [c["ysb"], c["sg"]], [zt])
            kb.dma("sp", ZF.t[ft, 16 * j8:16 * j8 + 16, :].rearrange("m (t c) -> t m c", t=8), zt[:], rd=[zt], wr=[ZF])

        stages = [s0, s1, s2, s3, s4, s5, s6, s7]
        gen_tables(0)
        gen_tables(1)
        gen_tables(2)
        for it in range(128 + len(stages) - 1):
            if it >= TG + 3 and (it - (TG + 3)) % TG == 0:
                ts_ = (it - (TG + 3)) // TG + 3
                if ts_ < 128 // TG:
                    gen_tables(ts_)
            for si_ in range(len(stages) - 1, -1, -1):
                g = it - si_
                if 0 <= g < 128:
                    stages[si_](g)
        kb.flush()

    if stop_after == "l1s":
        kb.final_wait()
        return nc

    with ExitStack() as st:
        zTs = Rot([kb.sb(st, f"zT{i}", [128, 16, 512], BF16) for i in range(2)])
        gTs = Rot([kb.sb(st, f"gT{i}", [128, 16, 512], BF16) for i in range(2)])
        wgs = Rot([kb.sb(st, f"wg{i}", [128, 16, 512], BF16) for i in range(2)])
        oT = kb.sb(st, "oT", [128, 16, 512], BF16)
        xq = [kb.sb(st, f"xr{i}", [128, D], F32) for i in range(4)]
        sgs_ = Rot([kb.sb(st, f"sgg{i}", [128, 512], BF16) for i in range(2)])
        pss = Rot([kb.ps(st, f"ps{i}", [128, 512], F32) for i in range(6)])
        X1v = X1.t.rearrange("(c t) d -> t c d", t=8)
        OUTv = out_d.rearrange("(c t) d -> t c d", t=8)
        for bi in range(NB):
            zT = zTs.next(); gT = gTs.next()
            kb.dma("sp", zT[:], ZF.t[:, :, bi * 512:(bi + 1) * 512].rearrange("f p c -> p f c"), rd=[ZF], wr=[zT])
            kb.dma("sp", gT[:], GF.t[:, :, bi * 512:(bi + 1) * 512].rearrange("f p c -> p f c"), rd=[GF], wr=[gT])
            for tti in range(4):
                kb.dma("sp", xq[tti][:], X1v[bi, tti * 128:(tti + 1) * 128, :], rd=[X1], wr=[xq[tti]])
            for gg in range(4):
                wg = wgs.next()
                kb.dma("sp", wg[:], Wb_glu.t[gg], rd=[Wb_glu], wr=[wg])
                for j in range(4):
                    ft = gg * 4 + j
                    ps = pss.next()
                    for c in range(16):
                        mm(ps[:], wg[:, c, j * 128:(j + 1) * 128], zT[:, c, :], c == 0, c == 15, [wg, zT], [ps])
                    sgt = sgs_.next()
                    act(sgt[:], ps[:], AF.Sigmoid, [ps, bglu], [sgt], bias=bglu[:, ft:ft + 1])
                    tt("dve", sgt[:], sgt[:], zT[:, ft, :], ALU.mult, [sgt, zT], [sgt])
                    tt("pool", oT[:, ft, :], sgt[:], gT[:, ft, :], ALU.mult, [sgt, gT], [oT])
            for og in range(4):
                wg = wgs.next()
                kb.dma("sp", wg[:], Wb_out1.t[og], rd=[Wb_out1], wr=[wg])
                for tti in range(4):
                    ps = pss.next()
                    for c in range(16):
                        mm(ps[:], oT[:, c, tti * 128:(tti + 1) * 128], wg[:, c, :], c == 0, c == 15, [oT, wg], [ps])
                    xs = xq[tti][:, og * 512:(og + 1) * 512]
                    tt("dve", xs, ps[:], xs, ALU.add, [ps, xq[tti]], [xq[tti]])
            for tti in range(4):
                kb.dma("pool", OUTv[bi, tti * 128:(tti + 1) * 128, :], xq[tti][:], rd=[xq[tti]], key=f"outst{tti}")
        kb.flush()

    kb.final_wait()
    return nc


def make_in_maps(inputs):
    maps = []
    for c in range(NCORES):
        m = {"x": np.ascontiguousarray(inputs["x"][c % 4])}
        for n, shp in PARAMS:
            m[n] = np.ascontiguousarray(np.asarray(inputs[n]).reshape(shp))
        maps.append(m)
    return maps


def kernel(**inputs):
    nc = build()
    res = run_bass_kernel_spmd(nc, make_in_maps(inputs), core_ids=list(range(NCORES)))
    out = np.stack([res.results[c]["out"] for c in range(4)], axis=0)
    return out.astype(np.float32)
```

```python
import math
from contextlib import ExitStack

import numpy as np
import concourse.bass as bass
import concourse.mybir as mybir
from concourse.bass_utils import run_bass_kernel_spmd

F32 = mybir.dt.float32
BF16 = mybir.dt.bfloat16
I32 = mybir.dt.int32
AF = mybir.ActivationFunctionType
ALU = mybir.AluOpType

S = 4096
D = 2048
NB = 8
EPS = 1e-6
NCORES = 4


class Buf:
    def __init__(self, name, t, persist=False):
        self.name = name
        self.t = t
        self.w = {}
        self.r = {}
        self.ext = []
        self.persist = persist

    def __getitem__(self, idx):
        return self.t[idx]


class Op:
    __slots__ = ("eng", "fn", "rd", "wr", "is_dma", "key", "deps", "ext", "signal", "sem", "val")

    def __init__(self, eng, fn, rd, wr, is_dma=False, key=None):
        self.eng = eng
        self.fn = fn
        self.rd = rd
        self.wr = wr
        self.is_dma = is_dma
        self.key = key
        self.deps = ()
        self.ext = []
        self.signal = False
        self.sem = None
        self.val = 0


class Rot:
    def __init__(self, bufs):
        self.bufs = bufs
        self.i = 0

    def next(self):
        b = self.bufs[self.i % len(self.bufs)]
        self.i += 1
        return b


class KB:
    def __init__(self, nc):
        self.nc = nc
        self.es = ExitStack()
        self.eng = {"pe": nc.tensor, "act": nc.scalar, "dve": nc.vector, "pool": nc.gpsimd, "sp": nc.sync}
        self.esem = {k: self.es.enter_context(nc.semaphore("s_" + k)) for k in self.eng}
        self.ecnt = {k: 0 for k in self.eng}
        self.seen = {k: {} for k in self.eng}
        self.dsem = {}
        self.deferred = set()
        self.ops = []
        self.bufs = []
        self.nins = {k: 0 for k in self.eng}

    def sb(self, stack, name, shape, dtype, persist=False):
        self.uid = getattr(self, "uid", 0) + 1
        name = f"{name}_{self.uid}"
        t = stack.enter_context(self.nc.sbuf_tensor(name, list(shape), dtype))
        b = Buf(name, t, persist)
        self.bufs.append(b)
        return b

    def ps(self, stack, name, shape, dtype):
        self.uid = getattr(self, "uid", 0) + 1
        name = f"{name}_{self.uid}"
        t = stack.enter_context(self.nc.psum_tensor(name, list(shape), dtype))
        b = Buf(name, t)
        self.bufs.append(b)
        return b

    def dram(self, name, shape, dtype, kind="Internal", persist=False):
        t = self.nc.dram_tensor(name, list(shape), dtype, kind=kind).ap()
        b = Buf(name, t, persist)
        self.bufs.append(b)
        return b

    def op(self, eng, fn, rd=(), wr=()):
        self.ops.append(Op(eng, fn, list(rd), list(wr)))

    def dma(self, q, out, in_, rd=(), wr=(), key=None, defer=False, **kw):
        h = self.eng[q]
        if key is None:
            key = (wr[0].name if wr else rd[0].name + "_st")
        if key not in self.dsem:
            self.dsem[key] = [self.es.enter_context(self.nc.semaphore("d_" + key)), 0]
        if defer:
            self.deferred.add(key)
        self.ops.append(Op(q, lambda: h.dma_start(out=out, in_=in_, **kw), list(rd), list(wr), True, key))

    def flush(self, barrier=True):
        ops = self.ops
        for i, op in enumerate(ops):
            deps = set()
            ext = []
            for b in op.rd:
                deps.update(b.w.values())
                ext.extend(b.ext)
            for b in op.wr:
                deps.update(b.w.values())
                deps.update(b.r.values())
                ext.extend(b.ext)
            if op.eng == "pe" and not op.is_dma:
                deps = {d for d in deps if ops[d].is_dma or ops[d].eng != "pe"}
            deps.discard(i)
            op.deps = sorted(deps)
            op.ext = ext
            k = ("dma", op.key) if op.is_dma else op.eng
            for b in op.wr:
                b.w[k] = i
            for b in op.rd:
                b.r[k] = i
            for d in deps:
                ops[d].signal = True
        last = {}
        for i, op in enumerate(ops):
            if not op.is_dma:
                last[op.eng] = i
        for i in last.values():
            ops[i].signal = True
        for op in ops:
            e = op.eng
            h = self.eng[e]
            need = [(ops[d].sem, ops[d].val) for d in op.deps] + list(op.ext)
            for sem, val in need:
                sk = id(sem)
                if self.seen[e].get(sk, 0) < val:
                    h.wait_ge(sem, val)
                    self.seen[e][sk] = val
            ins = op.fn()
            self.nins[e] += 1
            if op.is_dma:
                ent = self.dsem[op.key]
                ent[1] += 16
                ins.then_inc(ent[0], 16)
                op.sem, op.val = ent[0], ent[1]
            elif op.signal:
                self.ecnt[e] += 1
                ins.then_inc(self.esem[e], 1)
                op.sem, op.val = self.esem[e], self.ecnt[e]
        for b in self.bufs:
            if b.persist:
                for d in list(b.w.values()):
                    b.ext.append((ops[d].sem, ops[d].val))
            b.w = {}
            b.r = {}
        self.ops = []
        if barrier:
            for e, h in self.eng.items():
                for e2 in self.eng:
                    if e2 != e and self.ecnt[e2] > self.seen[e].get(id(self.esem[e2]), 0):
                        h.wait_ge(self.esem[e2], self.ecnt[e2])
                        self.seen[e][id(self.esem[e2])] = self.ecnt[e2]
                for key, (sem, cnt) in self.dsem.items():
                    if key in self.deferred:
                        continue
                    if cnt > self.seen[e].get(id(sem), 0):
                        h.wait_ge(sem, cnt)
                        self.seen[e][id(sem)] = cnt

    def final_wait(self):
        for e, h in self.eng.items():
            for key, (sem, cnt) in self.dsem.items():
                if cnt > self.seen[e].get(id(sem), 0):
                    h.wait_ge(sem, cnt)
                    self.seen[e][id(sem)] = cnt


def bc(ap, shape):
    return ap.to_broadcast(list(shape))


PARAMS = [
    ("e_norm_g", [D]), ("e_w_in", [D, 7168]), ("e_conv_w", [31, 1024]), ("e_conv_b", [1024]),
    ("e_cln_g", [1024]), ("e_cln_b", [1024]), ("e_qn_g", [64]), ("e_kn_g", [64]),
    ("e_lam_q1", [64]), ("e_lam_k1", [64]), ("e_lam_q2", [64]), ("e_lam_k2", [64]),
    ("e_subln_g", [128]), ("e_w_out", [D, D]),
    ("o_norm_g", [D]), ("o_w_in", [D, 2 * D]), ("o_A_re", [128, 64]), ("o_A_im", [128, 64]),
    ("o_log_dt", [128]), ("o_B_re", [128, 64, 16]), ("o_B_im", [128, 64, 16]),
    ("o_C_re", [128, 16, 64]), ("o_C_im", [128, 16, 64]), ("o_D", [D]),
    ("o_w_glu", [D, D]), ("o_b_glu", [D]), ("o_w_out", [D, D]),
]


def build(dbg=(), stop_after=None):
    nc = bass.Bass("TRN2", target_bir_lowering=False)
    kb = KB(nc)
    eng = kb.eng
    pe, act_, dve, pool = eng["pe"], eng["act"], eng["dve"], eng["pool"]

    def kind(name):
        return "ExternalOutput" if name in dbg else "Internal"

    din = {"x": nc.dram_tensor("x", [S, D], F32, kind="ExternalInput").ap()}
    for n, shp in PARAMS:
        din[n] = nc.dram_tensor(n, shp, F32, kind="ExternalInput").ap()
    out_d = nc.dram_tensor("out", [S, D], F32, kind="ExternalOutput").ap()

    Wb_in0 = kb.dram("Wb_in0", [14, 128, 16, 512], BF16, persist=True)
    Wb_out0 = kb.dram("Wb_out0", [4, 128, 16, 512], BF16, persist=True)
    Wb_in1 = kb.dram("Wb_in1", [8, 128, 16, 512], BF16, persist=True)
    Wb_glu = kb.dram("Wb_glu", [4, 128, 16, 512], BF16, persist=True)
    Wb_out1 = kb.dram("Wb_out1", [4, 128, 16, 512], BF16, persist=True)
    ROT = kb.dram("ROT", [2, 128, S], F32, kind=kind("ROT"))
    QT = kb.dram("QT", [8, 128, S], BF16, kind=kind("QT"))
    KT = kb.dram("KT", [8, 128, S], BF16, kind=kind("KT"))
    VV = kb.dram("VV", [8, 128, 32, 128], BF16, kind=kind("VV"))
    GB = kb.dram("GB", [8, 128, S], BF16, kind=kind("GB"))
    MIXT = kb.dram("MIXT", [16, 128, S], BF16, kind=kind("MIXT"))
    X1 = kb.dram("X1", [S, D], F32, kind=kind("X1"))

    def act(out, in_, func, rd, wr, bias=None, scale=None, accum=None):
        kw = {}
        if bias is not None:
            kw["bias"] = bias
        if scale is not None:
            kw["scale"] = scale
        if accum is not None:
            kw["accum_out"] = accum
        kb.op("act", lambda: act_.activation(out=out, in_=in_, func=func, **kw), rd, wr)

    def tt(e, out, in0, in1, op, rd, wr):
        h = eng[e]
        kb.op(e, lambda: h.tensor_tensor(out=out, in0=in0, in1=in1, op=op), rd, wr)

    def ts(e, out, in0, s1, s2, op0, op1, rd, wr):
        h = eng[e]
        if op1 is None:
            kb.op(e, lambda: h.tensor_scalar(out=out, in0=in0, scalar1=s1, scalar2=None, op0=op0), rd, wr)
        else:
            kb.op(e, lambda: h.tensor_scalar(out=out, in0=in0, scalar1=s1, scalar2=s2, op0=op0, op1=op1), rd, wr)

    def stt(out, in0, scalar, in1, op0, op1, rd, wr):
        kb.op("dve", lambda: dve.scalar_tensor_tensor(out=out, in0=in0, scalar=scalar, in1=in1, op0=op0, op1=op1), rd, wr)

    def mm(out, lhsT, rhs, start, stop, rd, wr):
        kb.op("pe", lambda: pe.matmul(out, lhsT=lhsT, rhs=rhs, start=start, stop=stop), rd, wr)

    def recip(out, in_, rd, wr):
        kb.op("dve", lambda: dve.reciprocal(out=out, in_=in_), rd, wr)

    def copy(e, out, in_, rd, wr):
        h = eng[e]
        kb.op(e, lambda: h.tensor_copy(out=out, in_=in_), rd, wr)

    def memset(e, ap, val, wr):
        h = eng[e]
        kb.op(e, lambda: h.memset(ap, val), (), wr)

    cs = kb.es

    def conv_w(dst, src, ng, key):
        v = src.rearrange("(c p) (g n) -> g p c n", p=128, n=512)
        for g in range(ng):
            kb.dma("pool", dst.t[g], v[g], wr=[dst], key=key, defer=True)

    conv_w(Wb_in0, din["e_w_in"], 14, "wc0")

    ident_f = kb.sb(cs, "ident_f", [128, 128], F32)
    ident_b = kb.sb(cs, "ident_b", [128, 128], BF16)
    ones_b = kb.sb(cs, "ones_b", [128, 128], BF16)
    ones_f = kb.sb(cs, "ones_f", [128, 128], F32)
    avg1024 = kb.sb(cs, "avg1024", [128, 128], F32)
    avg128 = kb.sb(cs, "avg128", [128, 128], F32)
    bd64 = kb.sb(cs, "bd64", [128, 128], BF16)
    Pm = kb.sb(cs, "Pm", [128, 128], BF16)
    M4 = kb.sb(cs, "M4", [128, 4, 512], BF16)
    eps_t = kb.sb(cs, "eps_t", [128, 1], F32)
    gn0 = kb.sb(cs, "gn0", [128, 16], F32)
    gn1 = kb.sb(cs, "gn1", [128, 16], F32)
    kw_t = kb.sb(cs, "kw_t", [128, 8, 31], F32)
    kw_b = kb.sb(cs, "kw_b", [128, 8, 31], BF16)
    convb = kb.sb(cs, "convb", [128, 8], F32)
    clng = kb.sb(cs, "clng", [128, 8], F32)
    clnb = kb.sb(cs, "clnb", [128, 8], F32)
    qng = kb.sb(cs, "qng", [128, 1], F32)
    kng = kb.sb(cs, "kng", [128, 1], F32)
    sgs = kb.sb(cs, "sgs", [128, 1], F32)
    neglam = kb.sb(cs, "neglam", [128, 1], F32)
    bglu = kb.sb(cs, "bglu", [128, 16], F32)
    sgn = kb.sb(cs, "sgn", [128, 1], F32)
    PiT = kb.sb(cs, "PiT", [128, 128], F32)

    with ExitStack() as st:
        iota_i = kb.sb(st, "iota_i", [128, 128], I32)
        iota_f = kb.sb(st, "iota_f", [128, 128], F32)
        pidx_i = kb.sb(st, "pidx_i", [128, 1], I32)
        ptmp_i = kb.sb(st, "ptmp_i", [128, 1], I32)
        m_hi = kb.sb(st, "m_hi", [128, 1], F32)
        m_lo = kb.sb(st, "m_lo", [128, 1], F32)
        Am = kb.sb(st, "Am", [128, 128], F32)
        Bm = kb.sb(st, "Bm", [128, 128], F32)
        Pf = kb.sb(st, "Pf", [128, 128], F32)
        ones512 = kb.sb(st, "ones512", [128, 512], BF16)
        freq = kb.sb(st, "freq", [128, 1], F32)
        halfpi = kb.sb(st, "halfpi", [128, 1], F32)
        cn = kb.sb(st, "cn", [128, 1], F32)
        sn = kb.sb(st, "sn", [128, 1], F32)
        nsn = kb.sb(st, "nsn", [128, 1], F32)
        tq = kb.sb(st, "tq", [128, 1], F32)
        COS = kb.sb(st, "COS", [128, S], F32)
        SIN = kb.sb(st, "SIN", [128, S], F32)
        T1 = kb.sb(st, "T1", [128, S // 2], F32)
        lamv = kb.sb(st, "lamv", [64, 4], F32)
        prod = kb.sb(st, "prod", [64, 2], F32)
        lps = kb.ps(st, "lps", [128, 512], F32)
        e2 = kb.sb(st, "e2", [128, 2], F32)
        sgl_ = kb.sb(st, "sgl_", [128, 1], F32)

        kb.op("pool", lambda: pool.iota(iota_i[:], pattern=[[1, 128]], base=0, channel_multiplier=-1), (), [iota_i])
        kb.op("pool", lambda: pool.iota(pidx_i[:], pattern=[[0, 1]], base=0, channel_multiplier=1), (), [pidx_i])
        copy("dve", iota_f[:], iota_i[:], [iota_i], [iota_f])
        kb.op("dve", lambda: dve.tensor_single_scalar(out=ident_f[:], in_=iota_f[:], scalar=0.0, op=ALU.is_equal), [iota_f], [ident_f])
        copy("dve", ident_b[:], ident_f[:], [ident_f], [ident_b])
        kb.op("dve", lambda: dve.tensor_single_scalar(out=PiT[:], in_=iota_f[:], scalar=-64.0, op=ALU.is_equal), [iota_f], [PiT])
        kb.op("dve", lambda: dve.tensor_single_scalar(out=Am[:], in_=iota_f[:], scalar=64.0, op=ALU.is_equal), [iota_f], [Am])
        tt("dve", PiT[:], PiT[:], Am[:], ALU.subtract, [PiT, Am], [PiT])
        kb.op("dve", lambda: dve.tensor_single_scalar(out=Am[:], in_=iota_f[:], scalar=32.0, op=ALU.is_equal), [iota_f], [Am])
        kb.op("dve", lambda: dve.tensor_single_scalar(out=Bm[:], in_=iota_f[:], scalar=-32.0, op=ALU.is_equal), [iota_f], [Bm])
        kb.op("dve", lambda: dve.tensor_single_scalar(out=ptmp_i[:], in_=pidx_i[:], scalar=32, op=ALU.bitwise_and), [pidx_i], [ptmp_i])
        copy("dve", m_hi[:], ptmp_i[:], [ptmp_i], [m_hi])
        ts("dve", m_hi[:], m_hi[:], 1.0 / 32.0, None, ALU.mult, None, [m_hi], [m_hi])
        ts("dve", m_lo[:], m_hi[:], -1.0, 1.0, ALU.mult, ALU.add, [m_hi], [m_lo])
        ts("dve", sgn[:], m_hi[:], 2.0, -1.0, ALU.mult, ALU.add, [m_hi], [sgn])
        ts("dve", Pf[:], Am[:], m_lo[:, 0:1], None, ALU.mult, None, [Am, m_lo], [Pf])
        stt(Pm[:], Bm[:], m_hi[:, 0:1], Pf[:], ALU.mult, ALU.add, [Bm, m_hi, Pf], [Pm])
        memset("pool", ones_b[:], 1.0, [ones_b])
        memset("pool", ones_f[:], 1.0, [ones_f])
        memset("pool", avg1024[:], 1.0 / 1024.0, [avg1024])
        memset("pool", avg128[:], 1.0 / 128.0, [avg128])
        memset("pool", bd64[:], 0.0, [bd64])
        memset("pool", bd64[0:64, 0:64], 1.0 / 64.0, [bd64])
        memset("pool", bd64[64:128, 64:128], 1.0 / 64.0, [bd64])
        memset("pool", eps_t[:], EPS, [eps_t])
        memset("pool", halfpi[:], math.pi / 2, [halfpi])
        memset("pool", ones512[:], 1.0, [ones512])
        for o in range(4):
            kb.op("pool", lambda o=o: pool.affine_select(out=M4[:, o, :], in_=ones512[:], pattern=[[1, 512]],
                                                          compare_op=ALU.is_ge, fill=0.0, base=-128 * o,
                                                          channel_multiplier=-1), [ones512], [M4])
        def ld(b, dst, src):
            kb.dma("sp", dst, src, wr=[b], key="small", allow_slow_non_contiguous=True)

        ld(gn0, gn0[:], din["e_norm_g"].rearrange("(c p) -> p c", p=128))
        ld(gn1, gn1[:], din["o_norm_g"].rearrange("(c p) -> p c", p=128))
        for t_ in range(8):
            ld(kw_t, kw_t[:, t_, :], din["e_conv_w"][:, t_ * 128:(t_ + 1) * 128].rearrange("w p -> p w"))
        ld(convb, convb[:], din["e_conv_b"].rearrange("(t p) -> p t", p=128))
        ld(clng, clng[:], din["e_cln_g"].rearrange("(t p) -> p t", p=128))
        ld(clnb, clnb[:], din["e_cln_b"].rearrange("(t p) -> p t", p=128))
        ld(bglu, bglu[:], din["o_b_glu"].rearrange("(t p) -> p t", p=128))
        for hh in range(2):
            ld(qng, qng[hh * 64:(hh + 1) * 64, :], din["e_qn_g"].rearrange("(p o) -> p o", o=1))
            ld(kng, kng[hh * 64:(hh + 1) * 64, :], din["e_kn_g"].rearrange("(p o) -> p o", o=1))
        ld(sgl_, sgl_[:], din["e_subln_g"].rearrange("(p o) -> p o", o=1))
        for i, nme in enumerate(["e_lam_q1", "e_lam_k1", "e_lam_q2", "e_lam_k2"]):
            ld(lamv, lamv[:, i:i + 1], din[nme].rearrange("(p o) -> p o", o=1))
        copy("dve", kw_b[:], kw_t[:], [kw_t], [kw_b])
        lam_init = 0.8 - 0.6 * math.exp(-0.3 * 0)
        ts("dve", sgs[:], sgl_[:], 1.0 - lam_init, None, ALU.mult, None, [sgl_], [sgs])
        tt("dve", prod[:, 0:1], lamv[:, 0:1], lamv[:, 1:2], ALU.mult, [lamv], [prod])
        tt("dve", prod[:, 1:2], lamv[:, 2:3], lamv[:, 3:4], ALU.mult, [lamv], [prod])
        mm(lps[:, 0:2], ones_f[0:64, :], prod[:, :], True, True, [ones_f, prod], [lps])
        act(e2[:], lps[:, 0:2], AF.Exp, [lps], [e2])
        tt("dve", neglam[:], e2[:, 1:2], e2[:, 0:1], ALU.subtract, [e2], [neglam])
        ts("dve", neglam[:], neglam[:], -lam_init, None, ALU.add, None, [neglam], [neglam])

        kb.op("dve", lambda: dve.tensor_single_scalar(out=ptmp_i[:], in_=pidx_i[:], scalar=31, op=ALU.bitwise_and), [pidx_i, m_hi], [ptmp_i])
        copy("dve", freq[:], ptmp_i[:], [ptmp_i], [freq])
        act(freq[:], freq[:], AF.Exp, [freq], [freq], scale=-math.log(10000.0) / 32.0)
        act(cn[:], freq[:], AF.Sin, [freq, halfpi], [cn], bias=halfpi[:, 0:1])
        act(sn[:], freq[:], AF.Sin, [freq], [sn])
        ts("dve", nsn[:], sn[:], -1.0, None, ALU.mult, None, [sn], [nsn])
        memset("dve", COS[:, 0:1], 1.0, [COS])
        memset("dve", SIN[:, 0:1], 0.0, [SIN])
        n = 1
        while n < S:
            ts("dve", T1[:, 0:n], COS[:, 0:n], cn[:, 0:1], None, ALU.mult, None, [COS, cn], [T1])
            stt(COS[:, n:2 * n], SIN[:, 0:n], nsn[:, 0:1], T1[:, 0:n], ALU.mult, ALU.add, [SIN, nsn, T1, COS], [COS])
            ts("dve", T1[:, 0:n], SIN[:, 0:n], cn[:, 0:1], None, ALU.mult, None, [SIN, cn, COS], [T1])
            stt(SIN[:, n:2 * n], COS[:, 0:n], sn[:, 0:1], T1[:, 0:n], ALU.mult, ALU.add, [COS, sn, T1, SIN], [SIN])
            if 2 * n < S:
                ts("dve", tq[:], cn[:], cn[:, 0:1], None, ALU.mult, None, [cn, SIN], [tq])
                stt(tq[:], sn[:], nsn[:, 0:1], tq[:], ALU.mult, ALU.add, [sn, nsn, tq], [tq])
                ts("dve", sn[:], cn[:], sn[:, 0:1], 2.0, ALU.mult, ALU.mult, [cn, sn], [sn])
                copy("dve", cn[:], tq[:], [tq, sn], [cn])
                ts("dve", nsn[:], sn[:], -1.0, None, ALU.mult, None, [sn], [nsn])
            n *= 2
        ts("dve", SIN[:], SIN[:], sgn[:, 0:1], None, ALU.mult, None, [SIN, sgn], [SIN])
        kb.dma("sp", ROT.t[0], COS[:], rd=[COS], wr=[ROT], key="rot_st")
        kb.dma("sp", ROT.t[1], SIN[:], rd=[SIN], wr=[ROT], key="rot_st")
        conv_w(Wb_out0, din["e_w_out"], 4, "wc1")
        conv_w(Wb_in1, din["o_w_in"], 8, "wc2")
        conv_w(Wb_glu, din["o_w_glu"], 4, "wc3")
        conv_w(Wb_out1, din["o_w_out"], 4, "wc4")
        kb.flush()

    if stop_after == "setup":
        kb.final_wait()
        return nc

    x = din["x"]
    with ExitStack() as st:
        hT = kb.sb(st, "hT", [128, 16, 512], BF16)
        wgs = Rot([kb.sb(st, f"wg{i}", [128, 16, 512], BF16) for i in range(2)])
        rcs = Rot([kb.sb(st, f"rc{i}", [128, 2, 512], F32) for i in range(2)])
        xts = Rot([kb.sb(st, f"xt{i}", [128, D], F32) for i in range(2)])
        hn = kb.sb(st, "hn", [128, D], BF16)
        junk = kb.sb(st, "junk", [128, D], BF16)
        ss = kb.sb(st, "ss", [128, 1], F32)
        sd1 = kb.sb(st, "sd1", [128, 1], F32)
        rs1 = kb.sb(st, "rs1", [128, 1], F32)
        u = kb.sb(st, "u", [128, 8, 542], BF16)
        sgl = kb.sb(st, "sgl", [128, 4, 512], BF16)
        sga = kb.sb(st, "sga", [128, 8, 512], BF16)
        cc = kb.sb(st, "cc", [128, 8, 512], F32)
        csqs = Rot([kb.sb(st, f"csq{i}", [128, 512], F32) for i in range(2)])
        DG = kb.sb(st, "DG", [128, 31, 128], BF16)
        oa = kb.sb(st, "oa", [128, 8, 512], BF16)
        mean_sb = kb.sb(st, "mean_sb", [128, 512], F32)
        var = kb.sb(st, "var", [128, 512], F32)
        rsl = kb.sb(st, "rsl", [128, 512], F32)
        tln = Rot([kb.sb(st, f"tln{i}", [128, 512], F32) for i in range(2)])
        aln = Rot([kb.sb(st, f"aln{i}", [128, 512], BF16) for i in range(2)])
        qgs = Rot([kb.sb(st, f"qg{i}", [128, 512], BF16) for i in range(2)])
        sqs = Rot([kb.sb(st, f"sq{i}", [128, 512], BF16) for i in range(2)])
        rsq = kb.sb(st, "rsq", [128, 512], F32)
        t1 = kb.sb(st, "t1", [128, 512], F32)
        t2 = kb.sb(st, "t2", [128, 512], F32)
        qos = Rot([kb.sb(st, f"qo{i}", [128, 512], BF16) for i in range(2)])
        vts = Rot([kb.sb(st, f"vt{i}", [128, 512], BF16) for i in range(2)])
        gbts = Rot([kb.sb(st, f"gbt{i}", [128, 512], BF16) for i in range(2)])
        pss = Rot([kb.ps(st, f"ps{i}", [128, 512], F32) for i in range(4)])
        ptrs = Rot([kb.ps(st, f"ptr{i}", [128, 4, 128], BF16) for i in range(2)])
        mps = kb.ps(st, "mps", [128, 512], F32)
        qps = kb.ps(st, "qps", [128, 512], F32)

        memset("pool", u[:, :, 0:30], 0.0, [u])

        def load_wg(src, gi):
            wg = wgs.next()
            kb.dma("sp", wg[:], src.t[gi], rd=[src], wr=[wg])
            return wg

        def fm_tile(wg, j, hTb):
            ps = pss.next()
            for c in range(16):
                mm(ps[:], wg[:, c, j * 128:(j + 1) * 128], hTb[:, c, :], c == 0, c == 15, [wg, hTb], [ps])
            return ps

        def norm_transpose(xt, gn, hTb, col_ap_fn):
            act(junk[:], xt[:], AF.Square, [xt], [junk, ss], accum=ss[:, 0:1])
            act(sd1[:], ss[:], AF.Sqrt, [ss, eps_t], [sd1], bias=eps_t[:, 0:1], scale=1.0 / D)
            recip(rs1[:], sd1[:], [sd1], [rs1])
            act(hn[:], xt[:], AF.Copy, [xt, rs1], [hn], scale=rs1[:, 0:1])
            for c4 in range(4):
                ptr = ptrs.next()
                for q_ in range(4):
                    c = c4 * 4 + q_
                    kb.op("pe", lambda c=c, q_=q_, ptr=ptr: pe.transpose(out=ptr[:, q_, :], in_=hn[:, c * 128:(c + 1) * 128],
                                                                           identity=ident_b[:]), [hn, ident_b], [ptr])
                tt("dve", col_ap_fn(hTb, c4), ptr[:], bc(gn[:, c4 * 4:c4 * 4 + 4].unsqueeze(2), [128, 4, 128]), ALU.mult,
                   [ptr, gn], [hTb])

        for bi in range(NB):
            t0 = bi * 512
            rc = rcs.next()
            kb.dma("sp", rc[:], ROT.t[:, :, t0:t0 + 512].rearrange("a p t -> p a t"), rd=[ROT], wr=[rc])
            for tti in range(4):
                xt = xts.next()
                kb.dma("sp", xt[:], x[t0 + tti * 128:t0 + (tti + 1) * 128, :], wr=[xt])
                norm_transpose(xt, gn0, hT, lambda hTb, c4, tti=tti: hTb[:, c4 * 4:c4 * 4 + 4, tti * 128:(tti + 1) * 128])

            for half in range(2):
                wg = load_wg(Wb_in0, 2 + half)
                for j in range(4):
                    ps = fm_tile(wg, j, hT)
                    act(sgl[:, j, :], ps[:], AF.Sigmoid, [ps], [sgl])
                wg = load_wg(Wb_in0, 0 + half)
                for j in range(4):
                    jj = half * 4 + j
                    ps = fm_tile(wg, j, hT)
                    tt("dve", u[:, jj, 30:542], ps[:], sgl[:, j, :], ALU.mult, [ps, sgl], [u])
                    tt("pool", DG[:], bc(ident_b[:].unsqueeze(1), [128, 31, 128]),
                       bc(kw_b[:, jj, :].unsqueeze(2), [128, 31, 128]), ALU.mult, [ident_b, kw_b], [DG])
                    cps = pss.next()
                    for tap in range(31):
                        mm(cps[:], DG[:, tap, :], u[:, jj, tap:tap + 512], tap == 0, tap == 30, [DG, u], [cps])
                    act(cc[:, jj, :], cps[:], AF.Identity, [cps, convb], [cc], bias=convb[:, jj:jj + 1])
                    csq = csqs.next()
                    act(csq[:], cps[:], AF.Square, [cps, convb], [csq], bias=convb[:, jj:jj + 1])
                    mm(mps[:], avg1024[:], cc[:, jj, :], jj == 0, jj == 7, [avg1024, cc], [mps])
                    mm(qps[:], avg1024[:], csq[:], jj == 0, jj == 7, [avg1024, csq], [qps])
            copy("pool", u[:, :, 0:30], u[:, :, 512:542], [u], [u])
            for half in range(2):
                wg = load_wg(Wb_in0, 4 + half)
                for j in range(4):
                    ps = fm_tile(wg, j, hT)
                    act(sga[:, half * 4 + j, :], ps[:], AF.Silu, [ps], [sga])
            act(mean_sb[:], mps[:], AF.Copy, [mps], [mean_sb])
            tt("dve", var[:], mean_sb[:], mean_sb[:], ALU.mult, [mean_sb], [var])
            tt("dve", var[:], qps[:], var[:], ALU.subtract, [qps, var], [var])
            act(var[:], var[:], AF.Sqrt, [var, eps_t], [var], bias=eps_t[:, 0:1])
            recip(rsl[:], var[:], [var], [rsl])
            for jj in range(8):
                tl = tln.next()
                al = aln.next()
                tt("dve", tl[:], cc[:, jj, :], mean_sb[:], ALU.subtract, [cc, mean_sb], [tl])
                tt("dve", tl[:], tl[:], rsl[:], ALU.mult, [tl, rsl], [tl])
                act(al[:], tl[:], AF.Silu, [tl, clng, clnb], [al], scale=clng[:, jj:jj + 1], bias=clnb[:, jj:jj + 1])
                tt("pool", oa[:, jj, :], al[:], sga[:, jj, :], ALU.mult, [al, sga], [oa])
            kb.dma("pool", MIXT.t[0:8, :, t0:t0 + 512].rearrange("j p t -> p j t"), oa[:], rd=[oa], wr=[MIXT])

            for gi in (6, 7, 8, 9):
                wg = load_wg(Wb_in0, gi)
                isq = gi < 8
                gvec = qng if isq else kng
                dst = QT if isq else KT
                for j in range(4):
                    hh = (gi % 2) * 4 + j
                    ps = fm_tile(wg, j, hT)
                    qg = qgs.next()
                    sq = sqs.next()
                    qo = qos.next()
                    act(qg[:], ps[:], AF.Copy, [ps, gvec], [qg], scale=gvec[:, 0:1])
                    act(sq[:], ps[:], AF.Square, [ps], [sq])
                    stp = pss.next()
                    mm(stp[:], bd64[:], sq[:], True, True, [bd64, sq], [stp])
                    rtp = pss.next()
                    mm(rtp[:], Pm[:], qg[:], True, True, [Pm, qg], [rtp])
                    act(rsq[:], stp[:], AF.Sqrt, [stp, eps_t], [rsq], bias=eps_t[:, 0:1])
                    recip(rsq[:], rsq[:], [rsq], [rsq])
                    tt("pool", t1[:], qg[:], rc[:, 0, :], ALU.mult, [qg, rc], [t1])
                    tt("dve", t2[:], rtp[:], rc[:, 1, :], ALU.mult, [rtp, rc], [t2])
                    tt("pool", t1[:], t1[:], t2[:], ALU.add, [t1, t2], [t1])
                    stt(qo[:], t1[:], 0.125 if isq else 1.0, rsq[:], ALU.mult, ALU.mult, [t1, rsq], [qo])
                    kb.dma("pool", dst.t[hh, :, t0:t0 + 512], qo[:], rd=[qo], wr=[dst])
            for gi in (10, 11):
                wg = load_wg(Wb_in0, gi)
                for tti in range(4):
                    ps = pss.next()
                    for c in range(16):
                        mm(ps[:], hT[:, c, tti * 128:(tti + 1) * 128], wg[:, c, :], c == 0, c == 15, [wg, hT], [ps])
                    vt = vts.next()
                    act(vt[:], ps[:], AF.Copy, [ps], [vt])
                    h0 = (gi - 10) * 4
                    kb.dma("pool", VV.t[h0:h0 + 4, :, bi * 4 + tti, :].rearrange("h p d -> p h d"),
                           vt[:].rearrange("p (h d) -> p h d", h=4), rd=[vt], wr=[VV])
            for gi in (12, 13):
                wg = load_wg(Wb_in0, gi)
                for j in range(4):
                    hh = (gi - 12) * 4 + j
                    ps = fm_tile(wg, j, hT)
                    gbt = gbts.next()
                    act(gbt[:], ps[:], AF.Silu, [ps], [gbt])
                    kb.dma("pool", GB.t[hh, :, t0:t0 + 512], gbt[:], rd=[gbt], wr=[GB])
        kb.flush()

    if stop_after == "l0p":
        kb.final_wait()
        return nc

    with ExitStack() as st:
        kTas = Rot([kb.sb(st, f"kTa{i}", [128, S], BF16) for i in range(2)])
        kTbs = Rot([kb.sb(st, f"kTb{i}", [128, S], BF16) for i in range(2)])
        qTs = Rot([kb.sb(st, f"qT{i}", [128, S], BF16) for i in range(2)])
        vhs = Rot([kb.sb(st, f"vh{i}", [128, 32, 128], BF16) for i in range(2)])
        gbs = Rot([kb.sb(st, f"gbh{i}", [128, S], BF16) for i in range(2)])
        e1s = Rot([kb.sb(st, f"e1_{i}", [128, 512], BF16) for i in range(3)])
        e2s = Rot([kb.sb(st, f"e2_{i}", [128, 512], BF16) for i in range(3)])
        pscore = Rot([kb.ps(st, f"psc{i}", [128, 512], F32) for i in range(4)])
        o1 = kb.ps(st, "o1", [128, 512], F32)
        d1 = kb.ps(st, "d1", [128, 512], F32)
        o2 = kb.ps(st, "o2", [128, 512], F32)
        d2 = kb.ps(st, "d2", [128, 512], F32)

        def FA(name, n=2, dt=F32):
            return Rot([kb.sb(st, f"{name}{i}", [128, 512], dt) for i in range(n)])
        rd1s = FA("rd1"); rd2s = FA("rd2"); c1s = FA("c1"); c2s = FA("c2"); ods = FA("od"); osqs = FA("osq"); rsfs = FA("rsf")
        obs = FA("ob", 2, BF16)
        for b_ in kTas.bufs:
            memset("pool", b_[64:128, :], 0.0, [b_])
        for b_ in kTbs.bufs:
            memset("pool", b_[0:64, :], 0.0, [b_])

        def finalize1():
            rd1 = rd1s.next(); rd2 = rd2s.next(); c1 = c1s.next(); c2 = c2s.next(); od = ods.next(); osq = osqs.next()
            copy("dve", c1[:], o1[:], [o1], [c1])
            act(rd1[:], d1[:], AF.Ln, [d1], [rd1])
            copy("dve", c2[:], o2[:], [o2], [c2])
            act(rd2[:], d2[:], AF.Ln, [d2], [rd2])
            act(rd1[:], rd1[:], AF.Exp, [rd1], [rd1], scale=-1.0)
            act(rd2[:], rd2[:], AF.Exp, [rd2], [rd2], scale=-1.0)
            tt("dve", c1[:], c1[:], rd1[:], ALU.mult, [c1, rd1], [c1])
            tt("dve", c2[:], c2[:], rd2[:], ALU.mult, [c2, rd2], [c2])
            stt(od[:], c2[:], neglam[:, 0:1], c1[:], ALU.mult, ALU.add, [c2, neglam, c1], [od])
            act(osq[:], od[:], AF.Square, [od], [osq])
            return od, osq

        def finalize2(h, qs, gb, od, osq):
            rsf = rsfs.next()
            stp = pscore.next()
            mm(stp[:], avg128[:], osq[:], True, True, [avg128, osq], [stp])
            act(rsf[:], stp[:], AF.Ln, [stp, eps_t], [rsf], bias=eps_t[:, 0:1])
            act(rsf[:], rsf[:], AF.Exp, [rsf], [rsf], scale=-0.5)
            tt("dve", od[:], od[:], rsf[:], ALU.mult, [od, rsf], [od])
            ob = obs.next()
            stt(ob[:], od[:], sgs[:, 0:1], gb[:, qs], ALU.mult, ALU.mult, [od, sgs, gb], [ob])
            kb.dma("pool", MIXT.t[8 + h, :, qs], ob[:], rd=[ob], wr=[MIXT])

        pending_fin = None
        for h in range(8):
            kTa = kTas.next(); kTb = kTbs.next(); qT = qTs.next(); vh = vhs.next(); gb = gbs.next()
            kb.dma("sp", kTa[0:64, :], KT.t[h, 0:64, :], rd=[KT], wr=[kTa])
            kb.dma("sp", kTb[64:128, :], KT.t[h, 64:128, :], rd=[KT], wr=[kTb])
            kb.dma("sp", qT[:], QT.t[h], rd=[QT], wr=[qT])
            kb.dma("sp", vh[:], VV.t[h], rd=[VV], wr=[vh])
            kb.dma("sp", gb[:], GB.t[h], rd=[GB], wr=[gb])
            for qb in range(8):
                qs = slice(qb * 512, (qb + 1) * 512)
                nkt = 4 * (qb + 1)

                def scores(kt, qb=qb, qs=qs):
                    ks = slice(kt * 128, (kt + 1) * 128)
                    o = max(0, kt - 4 * qb)
                    c0 = 128 * o
                    qcs = slice(qb * 512 + c0, (qb + 1) * 512)
                    s1 = pscore.next(); s2 = pscore.next()
                    mm(s1[:, c0:512], kTa[:, ks], qT[:, qcs], True, True, [kTa, qT], [s1])
                    mm(s2[:, c0:512], kTb[:, ks], qT[:, qcs], True, True, [kTb, qT], [s2])
                    e1 = e1s.next(); e2 = e2s.next()
                    act(e1[:, c0:512], s1[:, c0:512], AF.Exp, [s1], [e1])
                    act(e2[:, c0:512], s2[:, c0:512], AF.Exp, [s2], [e2])
                    if kt >= 4 * qb:
                        tt("dve", e1[:, c0:c0 + 128], e1[:, c0:c0 + 128], M4[:, 0, 0:128], ALU.mult, [e1, M4], [e1])
                        tt("dve", e2[:, c0:c0 + 128], e2[:, c0:c0 + 128], M4[:, 0, 0:128], ALU.mult, [e2, M4], [e2])
                    return e1, e2, c0

                pend = scores(0)
                for kt in range(nkt):
                    nxt = scores(kt + 1) if kt + 1 < nkt else None
                    e1, e2, c0 = pend
                    first, lastk = kt == 0, kt == nkt - 1
                    mm(o1[:, c0:512], vh[:, kt, :], e1[:, c0:512], first, lastk, [vh, e1], [o1])
                    mm(d1[:, c0:512], ones_b[:], e1[:, c0:512], first, lastk, [ones_b, e1], [d1])
                    mm(o2[:, c0:512], vh[:, kt, :], e2[:, c0:512], first, lastk, [vh, e2], [o2])
                    mm(d2[:, c0:512], ones_b[:], e2[:, c0:512], first, lastk, [ones_b, e2], [d2])
                    pend = nxt
                    if kt == min(2, nkt - 1) and pending_fin is not None:
                        finalize2(*pending_fin)
                        pending_fin = None
                if pending_fin is not None:
                    finalize2(*pending_fin)
                od, osq = finalize1()
                pending_fin = (h, qs, gb, od, osq)
        finalize2(*pending_fin)
        kb.flush()

    if stop_after == "l0a":
        kb.final_wait()
        return nc

    UF = kb.dram("UF", [16, 128, S], BF16, kind=kind("UF"))
    GF = kb.dram("GF", [16, 128, S], BF16, kind=kind("GF"))
    with ExitStack() as st:
        mixTs = Rot([kb.sb(st, f"mixT{i}", [128, 16, 512], BF16) for i in range(2)])
        wgs = Rot([kb.sb(st, f"wg{i}", [128, 16, 512], BF16) for i in range(2)])
        xts4 = [kb.sb(st, f"xq{i}", [128, D], F32) for i in range(4)]
        hn = kb.sb(st, "hn", [128, D], BF16)
        junk = kb.sb(st, "junk", [128, D], BF16)
        ss = kb.sb(st, "ss", [128, 1], F32)
        sd1 = kb.sb(st, "sd1", [128, 1], F32)
        rs1 = kb.sb(st, "rs1", [128, 1], F32)
        h1T = kb.sb(st, "h1T", [128, 16, 512], BF16)
        ubs = Rot([kb.sb(st, f"ub{i}", [128, 512], BF16) for i in range(3)])
        pss = Rot([kb.ps(st, f"ps{i}", [128, 512], F32) for i in range(4)])
        ptrs = Rot([kb.ps(st, f"ptr{i}", [128, 4, 128], BF16) for i in range(2)])

        def load_wg(src, gi):
            wg = wgs.next()
            kb.dma("sp", wg[:], src.t[gi], rd=[src], wr=[wg])
            return wg

        def norm_transpose_perm(xt, gn, hTb, tti):
            act(junk[:], xt[:], AF.Square, [xt], [junk, ss], accum=ss[:, 0:1])
            act(sd1[:], ss[:], AF.Sqrt, [ss, eps_t], [sd1], bias=eps_t[:, 0:1], scale=1.0 / D)
            recip(rs1[:], sd1[:], [sd1], [rs1])
            act(hn[:], xt[:], AF.Copy, [xt, rs1], [hn], scale=rs1[:, 0:1])
            for c4 in range(4):
                ptr = ptrs.next()
                for q_ in range(4):
                    c = c4 * 4 + q_
                    kb.op("pe", lambda c=c, q_=q_, ptr=ptr: pe.transpose(out=ptr[:, q_, :], in_=hn[:, c * 128:(c + 1) * 128],
                                                                           identity=ident_b[:]), [hn, ident_b], [ptr])
                oap = hTb[:, c4 * 4:c4 * 4 + 4, :].rearrange("p k (t c) -> p k t c", t=8)[:, :, :, 16 * tti:16 * tti + 16]
                iap = ptr[:].rearrange("p k (c t) -> p k t c", t=8)
                tt("dve", oap, iap, bc(gn[:, c4 * 4:c4 * 4 + 4].unsqueeze(2).unsqueeze(3), [128, 4, 8, 16]), ALU.mult,
                   [ptr, gn], [hTb])

        for bi in range(NB):
            t0 = bi * 512
            mixT = mixTs.next()
            kb.dma("sp", mixT[:], MIXT.t[:, :, t0:t0 + 512].rearrange("c p t -> p c t"), rd=[MIXT], wr=[mixT])
            for tti in range(4):
                kb.dma("sp", xts4[tti][:], x[t0 + tti * 128:t0 + (tti + 1) * 128, :], wr=[xts4[tti]])
            for og in range(4):
                wg = load_wg(Wb_out0, og)
                for tti in range(4):
                    ps = pss.next()
                    for c in range(16):
                        mm(ps[:], mixT[:, c, tti * 128:(tti + 1) * 128], wg[:, c, :], c == 0, c == 15, [mixT, wg], [ps])
                    xs = xts4[tti][:, og * 512:(og + 1) * 512]
                    tt("dve", xs, ps[:], xs, ALU.add, [ps, xts4[tti]], [xts4[tti]])
            for tti in range(4):
                kb.dma("pool", X1.t[t0 + tti * 128:t0 + (tti + 1) * 128, :], xts4[tti][:], rd=[xts4[tti]], wr=[X1])
                norm_transpose_perm(xts4[tti], gn1, h1T, tti)
            for gi in range(8):
                wg = load_wg(Wb_in1, gi)
                for j in range(4):
                    ps = pss.next()
                    for c in range(16):
                        mm(ps[:], wg[:, c, j * 128:(j + 1) * 128], h1T[:, c, :], c == 0, c == 15, [wg, h1T], [ps])
                    ub = ubs.next()
                    ft = (gi % 4) * 4 + j
                    dst = UF if gi < 4 else GF
                    act(ub[:], ps[:], AF.Copy if gi < 4 else AF.Silu, [ps], [ub])
                    kb.dma("pool", dst.t[ft].rearrange("p (t c) -> p t c", t=8)[:, :, bi * 64:(bi + 1) * 64],
                           ub[:].rearrange("p (t c) -> p t c", t=8), rd=[ub], wr=[dst])
        kb.flush()

    if stop_after == "l1p":
        kb.final_wait()
        return nc

    S5W = kb.dram("S5W", [128, 128, 4, 128], BF16, kind=kind("S5W"))
    CLv = kb.sb(cs, "CLv", [128, 9, 128], F32)
    SLv = kb.sb(cs, "SLv", [128, 9, 128], F32)
    RLt = kb.sb(cs, "RLt", [128, 128], F32)
    Dc = kb.sb(cs, "Dc", [128, 128], F32)
    with ExitStack() as st:
        def T(name):
            return kb.sb(st, name, [128, 128], F32)
        AN = T("AN"); are = T("are"); aim = T("aim"); ldt = T("ldt"); dtt = T("dtt")
        th = T("th"); xr = T("xr"); kk = T("kk"); rr = T("rr"); r2 = T("r2"); acc = T("acc")
        sinT = T("sinT"); cosT = T("cosT"); EE = T("EE"); lR = T("lR"); lI = T("lI")
        den = T("den"); nr = T("nr"); fR = T("fR"); fI = T("fI"); nuR = T("nuR"); nuI = T("nuI")
        muI = T("muI"); l2R = T("l2R"); l2I = T("l2I"); l4R = T("l4R"); l4I = T("l4I")
        l8R = T("l8R"); l8I = T("l8I"); l7R = T("l7R"); l7I = T("l7I"); x1_ = T("x1_"); x2_ = T("x2_")
        x3_ = T("x3_"); x4_ = T("x4_"); mask8 = T("mask8"); onesq = T("onesq")
        tps = Rot([kb.ps(st, f"tps{i}", [128, 128], F32) for i in range(4)])

        def D_(fn, rd, wr):
            kb.op("dve", fn, rd, wr)

        def mul(o, a, b):
            tt("dve", o[:], a[:], b[:], ALU.mult, [a, b], [o])

        def add(o, a, b):
            tt("dve", o[:], a[:], b[:], ALU.add, [a, b], [o])

        def sub(o, a, b):
            tt("dve", o[:], a[:], b[:], ALU.subtract, [a, b], [o])

        def cmul(oR, oI, aR, aI, bR, bI):
            mul(x1_, aR, bR); mul(x2_, aI, bI); mul(x3_, aR, bI); mul(x4_, aI, bR)
            sub(oR, x1_, x2_); add(oI, x3_, x4_)

        def horner(o, xx, coef):
            n_ = len(coef) - 1
            ts("dve", o[:], xx[:], float(coef[n_]), None, ALU.mult, None, [xx], [o])
            for k_ in range(n_ - 1, 0, -1):
                stt(o[:], o[:], float(coef[k_]), xx[:], ALU.add, ALU.mult, [o, xx], [o])
            ts("dve", o[:], o[:], float(coef[0]), None, ALU.add, None, [o], [o])

        for src, dstt in ((din["o_A_re"], are), (din["o_A_im"], aim)):
            kb.dma("sp", AN[:, 0:64], src, wr=[AN], key="small")
            kb.dma("sp", AN[:, 64:128], src, wr=[AN], key="small")
            tp = tps.next()
            kb.op("pe", lambda tp=tp: pe.transpose(out=tp[:], in_=AN[:], identity=ident_f[:]), [AN, ident_f], [tp])
            copy("dve", dstt[:], tp[:], [tp], [dstt])
        kb.dma("sp", ldt[:], din["o_log_dt"].partition_broadcast(128), wr=[ldt], key="small", allow_slow_non_contiguous=True)
        for tau in range(8):
            kb.dma("sp", Dc[tau * 16:(tau + 1) * 16, :], din["o_D"].rearrange("(g m) -> m g", m=16), wr=[Dc], key="small",
                   allow_slow_non_contiguous=True)
        act(dtt[:], ldt[:], AF.Exp, [ldt], [dtt])
        mul(th, aim, dtt)
        mul(xr, are, dtt)
        MAGIC = 12582912.0
        ts("dve", kk[:], th[:], 1.0 / (2 * math.pi), MAGIC, ALU.mult, ALU.add, [th], [kk])
        ts("dve", kk[:], kk[:], -MAGIC, None, ALU.add, None, [kk], [kk])
        c1 = 6.28125
        c2 = float(np.float32(np.float32(2 * math.pi - c1).view(np.uint32) & np.uint32(0xFFFFF000)).view(np.float32)) if False else 0.0019350051879882812
        c3 = 2 * math.pi - c1 - c2
        stt(rr[:], kk[:], -c1, th[:], ALU.mult, ALU.add, [kk, th], [rr])
        stt(rr[:], kk[:], -c2, rr[:], ALU.mult, ALU.add, [kk, rr], [rr])
        stt(rr[:], kk[:], -c3, rr[:], ALU.mult, ALU.add, [kk, rr], [rr])
        mul(r2, rr, rr)
        horner(acc, r2, [(-1.0) ** k_ / math.factorial(2 * k_ + 1) for k_ in range(11)])
        mul(sinT, acc, rr)
        horner(cosT, r2, [(-1.0) ** k_ / math.factorial(2 * k_) for k_ in range(12)])
        horner(EE, xr, [1.0 / math.factorial(k_) for k_ in range(8)])
        mul(lR, EE, cosT)
        mul(lI, EE, sinT)
        mul(den, are, are); mul(x1_, aim, aim); add(den, den, x1_)
        recip(den[:], den[:], [den], [den])
        ts("dve", nr[:], lR[:], -1.0, None, ALU.add, None, [lR], [nr])
        mul(x1_, nr, are); mul(x2_, lI, aim); add(x1_, x1_, x2_); mul(fR, x1_, den)
        mul(x1_, lI, are); mul(x2_, nr, aim); sub(x1_, x1_, x2_); mul(fI, x1_, den)
        mul(x1_, EE, EE)
        recip(x1_[:], x1_[:], [x1_], [x1_])
        mul(nuR, lR, x1_)
        mul(nuI, lI, x1_)
        ts("dve", nuI[:], nuI[:], -1.0, None, ALU.mult, None, [nuI], [nuI])
        ts("dve", muI[:], lI[:], -1.0, None, ALU.mult, None, [lI], [muI])
        cmul(l2R, l2I, lR, lI, lR, lI)
        cmul(l4R, l4I, l2R, l2I, l2R, l2I)
        cmul(l8R, l8I, l4R, l4I, l4R, l4I)
        cmul(l7R, l7I, l4R, l4I, l2R, l2I)
        cmul(l7R, l7I, l7R, l7I, lR, lI)
        mul(RLt, EE, EE); mul(RLt, RLt, RLt); mul(RLt, RLt, RLt)
        recip(x1_[:], RLt[:], [RLt], [x1_])
        tt("dve", CLv[:, 0, :], l8R[:], x1_[:], ALU.mult, [l8R, x1_], [CLv])
        tt("dve", SLv[:, 0, :], l8I[:], x1_[:], ALU.mult, [l8I, x1_], [SLv])
        for j in range(8):
            tt("dve", x2_[:], CLv[:, j, :], CLv[:, j, :], ALU.mult, [CLv], [x2_])
            tt("dve", x3_[:], SLv[:, j, :], SLv[:, j, :], ALU.mult, [SLv], [x3_])
            tt("dve", CLv[:, j + 1, :], x2_[:], x3_[:], ALU.subtract, [x2_, x3_], [CLv])
            tt("dve", x2_[:], CLv[:, j, :], SLv[:, j, :], ALU.mult, [CLv, SLv], [x2_])
            ts("dve", SLv[:, j + 1, :], x2_[:], 2.0, None, ALU.mult, None, [x2_], [SLv])
        memset("pool", onesq[:], 1.0, [onesq])
        kb.op("pool", lambda: pool.affine_select(out=mask8[:], in_=onesq[:], pattern=[[16, 8], [0, 16]], compare_op=ALU.is_ge,
                                                  fill=0.0, base=15, channel_multiplier=-1), [onesq], [mask8])

        GB_ = 16
        BA = kb.sb(st, "BA", [128, GB_, 16], F32)
        BAp = kb.sb(st, "BAp", [128, GB_, 16], F32)
        CN = kb.sb(st, "CN", [128, 128], F32)
        CNp = kb.sb(st, "CNp", [128, 128], F32)
        VA = kb.sb(st, "VA", [128, GB_, 8, 16], F32)
        VAp = kb.sb(st, "VAp", [128, GB_, 8, 16], F32)
        WA = kb.sb(st, "WA", [128, GB_, 9, 16], F32)
        WAp = kb.sb(st, "WAp", [128, GB_, 9, 16], F32)
        WBA = kb.sb(st, "WBA", [128, GB_, 8, 16], F32)
        WBAp = kb.sb(st, "WBAp", [128, GB_, 8, 16], F32)
        y1 = kb.sb(st, "y1", [128, GB_, 16], F32)
        y2 = kb.sb(st, "y2", [128, GB_, 16], F32)
        y3 = kb.sb(st, "y3", [128, GB_, 16], F32)
        y4 = kb.sb(st, "y4", [128, GB_, 16], F32)
        S5ts = Rot([kb.sb(st, f"S5t{i}", [128, GB_, 4, 128], BF16) for i in range(2)])

        def cmat(oA, oAp, iA, iAp, zR, zI, g0, rdo, wro):
            zr = bc(zR[:, g0:g0 + GB_].unsqueeze(2), [128, GB_, 16])
            zi = bc(zI[:, g0:g0 + GB_].unsqueeze(2), [128, GB_, 16])
            tt("dve", y1[:], iA, zr, ALU.mult, rdo + [zR], [y1])
            tt("dve", y2[:], iAp, zi, ALU.mult, rdo + [zI], [y2])
            tt("pool", y3[:], iAp, zr, ALU.mult, rdo + [zR], [y3])
            tt("pool", y4[:], iA, zi, ALU.mult, rdo + [zI], [y4])
            tt("dve", oA, y1[:], y2[:], ALU.add, [y1, y2], wro)
            tt("pool", oAp, y3[:], y4[:], ALU.subtract, [y3, y4], wro)

        for gb_ in range(128 // GB_):
            g0 = gb_ * GB_
            gsl = slice(g0, g0 + GB_)
            kb.dma("sp", BA[0:64], din["o_B_re"][gsl].rearrange("g p m -> p g m"), wr=[BA], key="s5ld")
            kb.dma("sp", BA[64:128], din["o_B_im"][gsl].rearrange("g p m -> p g m"), wr=[BA], key="s5ld")
            kb.dma("sp", BAp[0:64], din["o_B_im"][gsl].rearrange("g p m -> p g m"), wr=[BAp], key="s5ld")
            kb.dma("sp", BAp[64:128], din["o_B_re"][gsl].rearrange("g p m -> p g m"), wr=[BAp], key="s5ld")
            ts("dve", BAp[0:64], BAp[0:64], -1.0, None, ALU.mult, None, [BAp], [BAp])
            for hb in range(GB_ // 8):
                gs8 = slice(g0 + hb * 8, g0 + hb * 8 + 8)
                kb.dma("sp", CN[:, 0:64], din["o_C_re"][gs8].rearrange("g m p -> (g m) p"), wr=[CN], key="s5ld")
                kb.dma("sp", CN[:, 64:128], din["o_C_im"][gs8].rearrange("g m p -> (g m) p"), wr=[CN], key="s5ld")
                kb.dma("sp", CNp[:, 0:64], din["o_C_im"][gs8].rearrange("g m p -> (g m) p"), wr=[CNp], key="s5ld")
                kb.dma("sp", CNp[:, 64:128], din["o_C_re"][gs8].rearrange("g m p -> (g m) p"), wr=[CNp], key="s5ld")
                tp = tps.next()
                kb.op("pe", lambda tp=tp: pe.transpose(out=tp[:], in_=CN[:], identity=ident_f[:]), [CN, ident_f], [tp])
                copy("dve", WA[:, hb * 8:hb * 8 + 8, 0, :], tp[:].rearrange("p (g m) -> p g m", m=16), [tp], [WA])
                tp = tps.next()
                kb.op("pe", lambda tp=tp: pe.transpose(out=tp[:], in_=CNp[:], identity=ident_f[:]), [CNp, ident_f], [tp])
                copy("dve", WAp[:, hb * 8:hb * 8 + 8, 0, :], tp[:].rearrange("p (g m) -> p g m", m=16), [tp], [WAp])
            ts("dve", WA[64:128, :, 0, :], WA[64:128, :, 0, :], -1.0, None, ALU.mult, None, [WA], [WA])
            cmat(VA[:, :, 0, :], VAp[:, :, 0, :], BA[:], BAp[:], fR, fI, g0, [BA, BAp], [VA, VAp])
            for sg in range(7):
                cmat(VA[:, :, sg + 1, :], VAp[:, :, sg + 1, :], VA[:, :, sg, :], VAp[:, :, sg, :], nuR, nuI, g0, [VA, VAp], [VA, VAp])
            for k_ in range(8):
                cmat(WA[:, :, k_ + 1, :], WAp[:, :, k_ + 1, :], WA[:, :, k_, :], WAp[:, :, k_, :], lR, muI, g0, [WA, WAp], [WA, WAp])
            for sg in range(8):
                cmat(WBA[:, :, sg, :], WBAp[:, :, sg, :], VA[:, :, sg, :], VAp[:, :, sg, :], l7R, l7I, g0, [VA, VAp], [WBA, WBAp])
            S5t = S5ts.next()
            for g in range(GB_):
                tp = tps.next()
                kb.op("pe", lambda tp=tp, g=g: pe.matmul(tp[:], lhsT=VA[:, g, :, :].rearrange("p s m -> p (s m)"),
                                                          rhs=WA[:, g, 0:8, :].rearrange("p s m -> p (s m)"), start=True, stop=True),
                      [VA, WA], [tp])
                tt("dve", S5t[:, g, 2, :], tp[:], mask8[:], ALU.mult, [tp, mask8], [S5t])
                tp = tps.next()
                kb.op("pe", lambda tp=tp, g=g: pe.transpose(out=tp[:], in_=WBA[:, g, :, :].rearrange("p s m -> p (s m)"),
                                                             identity=ident_f[:]), [WBA, ident_f], [tp])
                act(S5t[:, g, 0, :], tp[:], AF.Copy, [tp], [S5t])
                tp = tps.next()
                kb.op("pe", lambda tp=tp, g=g: pe.transpose(out=tp[:], in_=WBAp[:, g, :, :].rearrange("p s m -> p (s m)"),
                                                             identity=ident_f[:]), [WBAp, ident_f], [tp])
                act(S5t[:, g, 1, :], tp[:], AF.Copy, [tp], [S5t], scale=-1.0)
            copy("pool", S5t[:, :, 3, :].rearrange("p g (s m) -> p g s m", m=16), WA[:, :, 1:9, :], [WA], [S5t])
            kb.dma("pool", S5W.t[gsl].rearrange("g p k c -> p g k c"), S5t[:], rd=[S5t], wr=[S5W])
        kb.flush()

    if stop_after == "s5setup":
        kb.final_wait()
        return nc

    ZF = kb.dram("ZF", [16, 128, S], BF16, kind=kind("ZF"))
    with ExitStack() as st:
        TG = 4
        COSs = Rot([kb.sb(st, f"COSc{i}", [128, TG, 512], F32) for i in range(3)])
        SINs = Rot([kb.sb(st, f"SINc{i}", [128, TG, 512], F32) for i in range(3)])
        Y1 = kb.sb(st, "Y1", [128, TG, 256], F32)
        Y2 = kb.sb(st, "Y2", [128, TG, 256], F32)
        pAs = Rot([kb.ps(st, f"pA{i}", [128, 512], F32) for i in range(2)])
        pBs = Rot([kb.ps(st, f"pB{i}", [128, 512], F32) for i in range(2)])
        pYs = Rot([kb.ps(st, f"pY{i}", [128, 512], F32) for i in range(2)])
        pGs = Rot([kb.ps(st, f"pG{i}", [128, 512], F32) for i in range(2)])

        def FR(name, n, dt=F32):
            return Rot([kb.sb(st, f"{name}{i}", [128, 512], dt) for i in range(n)])
        t1s = FR("st1", 2); t2s = FR("st2", 2); cAs = FR("cA", 2); gAs = FR("gA", 4); t5s = FR("st5", 2); t6s = FR("st6", 2)
        ysbs = FR("ysb", 4); y2s = FR("y2", 2); sgms = FR("sgm", 3); Hps = FR("Hp", 3, BF16); zts = FR("zt", 3, BF16)
        Ucs = Rot([kb.sb(st, f"Ucm{i}", [128, 512], BF16) for i in range(8)])
        Wgs = Rot([kb.sb(st, f"Wgm{i}", [128, 4, 128], BF16) for i in range(8)])
        for hp in Hps.bufs:
            memset("pool", hp[:, 0:1], 0.0, [hp])
        tabs = {}

        def gen_tables(ts_):
            COSc = COSs.next(); SINc = SINs.next()
            memset("pool", COSc[:, :, 0:1], 1.0, [COSc])
            memset("pool", SINc[:, :, 0:1], 0.0, [SINc])
            gs_ = slice(TG * ts_, TG * ts_ + TG)
            for j in range(9):
                n = 1 << j
                cn_ = bc(CLv[:, j, gs_].unsqueeze(2), [128, TG, n])
                sn_ = bc(SLv[:, j, gs_].unsqueeze(2), [128, TG, n])
                tt("pool", Y1[:, :, 0:n], COSc[:, :, 0:n], cn_, ALU.mult, [COSc, CLv], [Y1])
                tt("pool", Y2[:, :, 0:n], SINc[:, :, 0:n], sn_, ALU.mult, [SINc, SLv], [Y2])
                tt("pool", COSc[:, :, n:2 * n], Y1[:, :, 0:n], Y2[:, :, 0:n], ALU.subtract, [Y1, Y2], [COSc])
                tt("pool", Y1[:, :, 0:n], SINc[:, :, 0:n], cn_, ALU.mult, [SINc, CLv], [Y1])
                tt("pool", Y2[:, :, 0:n], COSc[:, :, 0:n], sn_, ALU.mult, [COSc, SLv], [Y2])
                tt("pool", SINc[:, :, n:2 * n], Y1[:, :, 0:n], Y2[:, :, 0:n], ALU.add, [Y1, Y2], [SINc])
            tabs[ts_] = (COSc, SINc)

        ctx = {}

        def s0(g):
            ft, j8 = divmod(g, 8)
            c = ctx[g] = {}
            c["Uc"] = Uc = Ucs.next(); c["Wg"] = Wg = Wgs.next()
            kb.dma("sp", Uc[:], UF.t[ft, 16 * j8:16 * j8 + 16, :].rearrange("m (t c) -> t m c", t=8), rd=[UF], wr=[Uc])
            kb.dma("sp", Wg[:], S5W.t[g], rd=[S5W], wr=[Wg])
            c["pA"] = pA = pAs.next(); c["pB"] = pB = pBs.next()
            mm(pA[:], Wg[:, 0, :], Uc[:], True, True, [Wg, Uc], [pA])
            mm(pB[:], Wg[:, 1, :], Uc[:], True, True, [Wg, Uc], [pB])

        def s1(g):
            ft, j8 = divmod(g, 8)
            c = ctx[g]
            COSc, SINc = tabs[g // TG]
            co = COSc[:, g % TG, :]; si = SINc[:, g % TG, :]
            t1 = t1s.next(); t2 = t2s.next(); cA = cAs.next(); c["gA"] = gA = gAs.next()
            tt("dve", t1[:], c["pA"][:], co, ALU.mult, [c["pA"], COSc], [t1])
            tt("dve", t2[:], c["pB"][:], si, ALU.mult, [c["pB"], SINc], [t2])
            tt("dve", cA[:], t1[:], t2[:], ALU.add, [t1, t2], [cA])
            rl = bc(RLt[:, g:g + 1], [128, 512])
            kb.op("dve", lambda: dve.tensor_tensor_scan(out=gA[:], data0=rl, data1=cA[:], initial=0.0, op0=ALU.mult, op1=ALU.add),
                  [RLt, cA], [gA])

        def s2(g):
            c = ctx[g]
            c["pG"] = pG = pGs.next()
            mm(pG[:], PiT[:], c["gA"][:], True, True, [PiT, c["gA"]], [pG])

        def s3(g):
            ft, j8 = divmod(g, 8)
            c = ctx[g]
            COSc, SINc = tabs[g // TG]
            co = COSc[:, g % TG, :]; si = SINc[:, g % TG, :]
            t5 = t5s.next(); t6 = t6s.next(); c["Hp"] = Hp = Hps.next()
            tt("dve", t5[:], c["gA"][:], co, ALU.mult, [c["gA"], COSc], [t5])
            tt("dve", t6[:], c["pG"][:], si, ALU.mult, [c["pG"], SINc], [t6])
            tt("dve", Hp[:, 1:512], t5[:, 0:511], t6[:, 0:511], ALU.subtract, [t5, t6], [Hp])

        def s4(g):
            c = ctx[g]
            c["pY"] = pY = pYs.next()
            mm(pY[:], c["Wg"][:, 2, :], c["Uc"][:], True, False, [c["Wg"], c["Uc"]], [pY])
            mm(pY[:], c["Wg"][:, 3, :], c["Hp"][:], False, True, [c["Wg"], c["Hp"]], [pY])

        def s5(g):
            c = ctx[g]
            c["ysb"] = ysb = ysbs.next(); y2 = y2s.next(); c["y2"] = y2
            stt(ysb[:], c["Uc"][:], Dc[:, g:g + 1], c["pY"][:], ALU.mult, ALU.add, [c["Uc"], Dc, c["pY"]], [ysb])
            tt("dve", y2[:], ysb[:], ysb[:], ALU.mult, [ysb], [y2])
            ts("dve", y2[:], y2[:], 0.044715, 1.0, ALU.mult, ALU.add, [y2], [y2])
            tt("dve", y2[:], y2[:], ysb[:], ALU.mult, [y2, ysb], [y2])

        def s6(g):
            c = ctx[g]
            c["sg"] = sg = sgms.next()
            act(sg[:], c["y2"][:], AF.Sigmoid, [c["y2"]], [sg], scale=2.0 * math.sqrt(2.0 / math.pi))

        def s7(g):
            ft, j8 = divmod(g, 8)
            c = ctx.pop(g)
            zt = zts.next()
            tt("dve", zt[:], c["ysb"][:], c["sg"][:], ALU.mult, [c["ysb"], c["sg"]], [zt])
            kb.dma("sp", ZF.t[ft, 16 * j8:16 * j8 + 16, :].rearrange("m (t c) -> t m c", t=8), zt[:], rd=[zt], wr=[ZF])

        stages = [s0, s1, s2, s3, s4, s5, s6, s7]
        gen_tables(0)
        gen_tables(1)
        gen_tables(2)
        for it in range(128 + len(stages) - 1):
            if it >= TG + 3 and (it - (TG + 3)) % TG == 0:
                ts_ = (it - (TG + 3)) // TG + 3
                if ts_ < 128 // TG:
                    gen_tables(ts_)
            for si_ in range(len(stages) - 1, -1, -1):
                g = it - si_
                if 0 <= g < 128:
                    stages[si_](g)
        kb.flush()

    if stop_after == "l1s":
        kb.final_wait()
        return nc

    with ExitStack() as st:
        zTs = Rot([kb.sb(st, f"zT{i}", [128, 16, 512], BF16) for i in range(2)])
        gTs = Rot([kb.sb(st, f"gT{i}", [128, 16, 512], BF16) for i in range(2)])
        wgs = Rot([kb.sb(st, f"wg{i}", [128, 16, 512], BF16) for i in range(2)])
        oT = kb.sb(st, "oT", [128, 16, 512], BF16)
        xq = [kb.sb(st, f"xr{i}", [128, D], F32) for i in range(4)]
        sgs_ = Rot([kb.sb(st, f"sgg{i}", [128, 512], BF16) for i in range(2)])
        pss = Rot([kb.ps(st, f"ps{i}", [128, 512], F32) for i in range(6)])
        X1v = X1.t.rearrange("(c t) d -> t c d", t=8)
        OUTv = out_d.rearrange("(c t) d -> t c d", t=8)
        for bi in range(NB):
            zT = zTs.next(); gT = gTs.next()
            kb.dma("sp", zT[:], ZF.t[:, :, bi * 512:(bi + 1) * 512].rearrange("f p c -> p f c"), rd=[ZF], wr=[zT])
            kb.dma("sp", gT[:], GF.t[:, :, bi * 512:(bi + 1) * 512].rearrange("f p c -> p f c"), rd=[GF], wr=[gT])
            for tti in range(4):
                kb.dma("sp", xq[tti][:], X1v[bi, tti * 128:(tti + 1) * 128, :], rd=[X1], wr=[xq[tti]])
            for gg in range(4):
                wg = wgs.next()
                kb.dma("sp", wg[:], Wb_glu.t[gg], rd=[Wb_glu], wr=[wg])
                for j in range(4):
                    ft = gg * 4 + j
                    ps = pss.next()
                    for c in range(16):
                        mm(ps[:], wg[:, c, j * 128:(j + 1) * 128], zT[:, c, :], c == 0, c == 15, [wg, zT], [ps])
                    sgt = sgs_.next()
                    act(sgt[:], ps[:], AF.Sigmoid, [ps, bglu], [sgt], bias=bglu[:, ft:ft + 1])
                    tt("dve", sgt[:], sgt[:], zT[:, ft, :], ALU.mult, [sgt, zT], [sgt])
                    tt("pool", oT[:, ft, :], sgt[:], gT[:, ft, :], ALU.mult, [sgt, gT], [oT])
            for og in range(4):
                wg = wgs.next()
                kb.dma("sp", wg[:], Wb_out1.t[og], rd=[Wb_out1], wr=[wg])
                for tti in range(4):
                    ps = pss.next()
                    for c in range(16):
                        mm(ps[:], oT[:, c, tti * 128:(tti + 1) * 128], wg[:, c, :], c == 0, c == 15, [oT, wg], [ps])
                    xs = xq[tti][:, og * 512:(og + 1) * 512]
                    tt("dve", xs, ps[:], xs, ALU.add, [ps, xq[tti]], [xq[tti]])
            for tti in range(4):
                kb.dma("pool", OUTv[bi, tti * 128:(tti + 1) * 128, :], xq[tti][:], rd=[xq[tti]], key=f"outst{tti}")
        kb.flush()

    kb.final_wait()
    return nc


def make_in_maps(inputs):
    maps = []
    for c in range(NCORES):
        m = {"x": np.ascontiguousarray(inputs["x"][c % 4])}
        for n, shp in PARAMS:
            m[n] = np.ascontiguousarray(np.asarray(inputs[n]).reshape(shp))
        maps.append(m)
    return maps


def kernel(**inputs):
    nc = build()
    res = run_bass_kernel_spmd(nc, make_in_maps(inputs), core_ids=list(range(NCORES)))
    out = np.stack([res.results[c]["out"] for c in range(4)], axis=0)
    return out.astype(np.float32)
```

```python
import math
from contextlib import ExitStack

import numpy as np
import concourse.bass as bass
import concourse.mybir as mybir
from concourse.bass_utils import run_bass_kernel_spmd

F32 = mybir.dt.float32
BF16 = mybir.dt.bfloat16
I32 = mybir.dt.int32
AF = mybir.ActivationFunctionType
ALU = mybir.AluOpType

S = 4096
D = 2048
NB = 8
EPS = 1e-6
NCORES = 4


class Buf:
    def __init__(self, name, t, persist=False):
        self.name = name
        self.t = t
        self.w = {}
        self.r = {}
        self.ext = []
        self.persist = persist

    def __getitem__(self, idx):
        return self.t[idx]


class Op:
    __slots__ = ("eng", "fn", "rd", "wr", "is_dma", "key", "deps", "ext", "signal", "sem", "val")

    def __init__(self, eng, fn, rd, wr, is_dma=False, key=None):
        self.eng = eng
        self.fn = fn
        self.rd = rd
        self.wr = wr
        self.is_dma = is_dma
        self.key = key
        self.deps = ()
        self.ext = []
        self.signal = False
        self.sem = None
        self.val = 0


class Rot:
    def __init__(self, bufs):
        self.bufs = bufs
        self.i = 0

    def next(self):
        b = self.bufs[self.i % len(self.bufs)]
        self.i += 1
        return b


class KB:
    SB_LIMIT = 204 * 1024
    def __init__(self, nc):
        self.nc = nc
        self.es = ExitStack()
        self.eng = {"pe": nc.tensor, "act": nc.scalar, "dve": nc.vector, "pool": nc.gpsimd, "sp": nc.sync}
        self.esem = {k: self.es.enter_context(nc.semaphore("s_" + k)) for k in self.eng}
        self.ecnt = {k: 0 for k in self.eng}
        self.seen = {k: {} for k in self.eng}
        self.dsem = {}
        self.deferred = set()
        self.ops = []
        self.bufs = []
        self.nins = {k: 0 for k in self.eng}

    def sb(self, stack, name, shape, dtype, persist=False):
        self.uid = getattr(self, "uid", 0) + 1
        name = f"{name}_{self.uid}"
        nbytes = int(np.prod(shape[1:])) * (2 if dtype == BF16 else 4)
        nbytes = (nbytes + 31) // 32 * 32
        self.sbuf_used = getattr(self, "sbuf_used", 0) + nbytes
        self.sbuf_peak = max(getattr(self, "sbuf_peak", 0), self.sbuf_used)
        assert self.sbuf_used <= self.SB_LIMIT, f"SBUF budget exceeded: {self.sbuf_used} at {name}"

        def _free(nb=nbytes):
            self.sbuf_used -= nb
        stack.callback(_free)
        t = stack.enter_context(self.nc.sbuf_tensor(name, list(shape), dtype))
        b = Buf(name, t, persist)
        self.bufs.append(b)
        return b

    def ps(self, stack, name, shape, dtype):
        self.uid = getattr(self, "uid", 0) + 1
        name = f"{name}_{self.uid}"
        t = stack.enter_context(self.nc.psum_tensor(name, list(shape), dtype))
        b = Buf(name, t)
        self.bufs.append(b)
        return b

    def dram(self, name, shape, dtype, kind="Internal", persist=False):
        t = self.nc.dram_tensor(name, list(shape), dtype, kind=kind).ap()
        b = Buf(name, t, persist)
        self.bufs.append(b)
        return b

    def op(self, eng, fn, rd=(), wr=()):
        self.ops.append(Op(eng, fn, list(rd), list(wr)))

    def dma(self, q, out, in_, rd=(), wr=(), key=None, defer=False, **kw):
        h = self.eng[q]
        if key is None:
            key = (wr[0].name if wr else rd[0].name + "_st")
        if key not in self.dsem:
            self.dsem[key] = [self.es.enter_context(self.nc.semaphore("d_" + key)), 0]
        if defer:
            self.deferred.add(key)
        self.ops.append(Op(q, lambda: h.dma_start(out=out, in_=in_, **kw), list(rd), list(wr), True, key))

    def flush(self, barrier=True):
        ops = self.ops
        for i, op in enumerate(ops):
            deps = set()
            ext = []
            for b in op.rd:
                deps.update(b.w.values())
                ext.extend(b.ext)
            for b in op.wr:
                deps.update(b.w.values())
                deps.update(b.r.values())
                ext.extend(b.ext)
            if op.eng == "pe" and not op.is_dma:
                deps = {d for d in deps if ops[d].is_dma or ops[d].eng != "pe"}
            deps.discard(i)
            op.deps = sorted(deps)
            op.ext = ext
            k = ("dma", op.key) if op.is_dma else op.eng
            for b in op.wr:
                b.w[k] = i
            for b in op.rd:
                b.r[k] = i
            for d in deps:
                ops[d].signal = True
        last = {}
        for i, op in enumerate(ops):
            if not op.is_dma:
                last[op.eng] = i
        for i in last.values():
            ops[i].signal = True
        for op in ops:
            e = op.eng
            h = self.eng[e]
            need = [(ops[d].sem, ops[d].val) for d in op.deps] + list(op.ext)
            for sem, val in need:
                sk = id(sem)
                if self.seen[e].get(sk, 0) < val:
                    h.wait_ge(sem, val)
                    self.seen[e][sk] = val
            ins = op.fn()
            self.nins[e] += 1
            if op.is_dma:
                ent = self.dsem[op.key]
                ent[1] += 16
                ins.then_inc(ent[0], 16)
                op.sem, op.val = ent[0], ent[1]
            elif op.signal:
                self.ecnt[e] += 1
                ins.then_inc(self.esem[e], 1)
                op.sem, op.val = self.esem[e], self.ecnt[e]
        for b in self.bufs:
            if b.persist:
                for d in list(b.w.values()):
                    b.ext.append((ops[d].sem, ops[d].val))
            b.w = {}
            b.r = {}
        self.ops = []
        if barrier:
            for e, h in self.eng.items():
                for e2 in self.eng:
                    if e2 != e and self.ecnt[e2] > self.seen[e].get(id(self.esem[e2]), 0):
                        h.wait_ge(self.esem[e2], self.ecnt[e2])
                        self.seen[e][id(self.esem[e2])] = self.ecnt[e2]
                for key, (sem, cnt) in self.dsem.items():
                    if key in self.deferred:
                        continue
                    if cnt > self.seen[e].get(id(sem), 0):
                        h.wait_ge(sem, cnt)
                        self.seen[e][id(sem)] = cnt

    def final_wait(self):
        for e, h in self.eng.items():
            for key, (sem, cnt) in self.dsem.items():
                if cnt > self.seen[e].get(id(sem), 0):
                    h.wait_ge(sem, cnt)
                    self.seen[e][id(sem)] = cnt


def bc(ap, shape):
    return ap.to_broadcast(list(shape))


PARAMS = [
    ("e_norm_g", [D]), ("e_w_in", [D, 7168]), ("e_conv_w", [31, 1024]), ("e_conv_b", [1024]),
    ("e_cln_g", [1024]), ("e_cln_b", [1024]), ("e_qn_g", [64]), ("e_kn_g", [64]),
    ("e_lam_q1", [64]), ("e_lam_k1", [64]), ("e_lam_q2", [64]), ("e_lam_k2", [64]),
    ("e_subln_g", [128]), ("e_w_out", [D, D]),
    ("o_norm_g", [D]), ("o_w_in", [D, 2 * D]), ("o_A_re", [128, 64]), ("o_A_im", [128, 64]),
    ("o_log_dt", [128]), ("o_B_re", [128, 64, 16]), ("o_B_im", [128, 64, 16]),
    ("o_C_re", [128, 16, 64]), ("o_C_im", [128, 16, 64]), ("o_D", [D]),
    ("o_w_glu", [D, D]), ("o_b_glu", [D]), ("o_w_out", [D, D]),
]


def build(dbg=(), stop_after=None):
    nc = bass.Bass("TRN2", target_bir_lowering=False)
    kb = KB(nc)
    eng = kb.eng
    pe, act_, dve, pool = eng["pe"], eng["act"], eng["dve"], eng["pool"]

    def kind(name):
        return "ExternalOutput" if name in dbg else "Internal"

    din = {"x": nc.dram_tensor("x", [S, D], F32, kind="ExternalInput").ap()}
    for n, shp in PARAMS:
        din[n] = nc.dram_tensor(n, shp, F32, kind="ExternalInput").ap()
    out_d = nc.dram_tensor("out", [S, D], F32, kind="ExternalOutput").ap()

    Wb_in0 = kb.dram("Wb_in0", [14, 128, 16, 512], BF16, persist=True)
    Wb_out0 = kb.dram("Wb_out0", [4, 128, 16, 512], BF16, persist=True)
    Wb_in1 = kb.dram("Wb_in1", [8, 128, 16, 512], BF16, persist=True)
    Wb_glu = kb.dram("Wb_glu", [4, 128, 16, 512], BF16, persist=True)
    Wb_out1 = kb.dram("Wb_out1", [4, 128, 16, 512], BF16, persist=True)
    ROT = kb.dram("ROT", [2, 128, S], F32, kind=kind("ROT"))
    QT = kb.dram("QT", [8, 128, S], BF16, kind=kind("QT"))
    KT = kb.dram("KT", [8, 128, S], BF16, kind=kind("KT"))
    VV = kb.dram("VV", [8, 128, 32, 128], BF16, kind=kind("VV"))
    GB = kb.dram("GB", [8, 128, S], BF16, kind=kind("GB"))
    MIXT = kb.dram("MIXT", [16, 128, S], BF16, kind=kind("MIXT"))
    X1 = kb.dram("X1", [S, D], F32, kind=kind("X1"))
    TAB = kb.dram("TAB", [128, 128, 2, 512], F32, kind=kind("TAB"))

    def act(out, in_, func, rd, wr, bias=None, scale=None, accum=None):
        kw = {}
        if bias is not None:
            kw["bias"] = bias
        if scale is not None:
            kw["scale"] = scale
        if accum is not None:
            kw["accum_out"] = accum
        kb.op("act", lambda: act_.activation(out=out, in_=in_, func=func, **kw), rd, wr)

    def tt(e, out, in0, in1, op, rd, wr):
        h = eng[e]
        kb.op(e, lambda: h.tensor_tensor(out=out, in0=in0, in1=in1, op=op), rd, wr)

    def ts(e, out, in0, s1, s2, op0, op1, rd, wr):
        h = eng[e]
        if op1 is None:
            kb.op(e, lambda: h.tensor_scalar(out=out, in0=in0, scalar1=s1, scalar2=None, op0=op0), rd, wr)
        else:
            kb.op(e, lambda: h.tensor_scalar(out=out, in0=in0, scalar1=s1, scalar2=s2, op0=op0, op1=op1), rd, wr)

    def stt(out, in0, scalar, in1, op0, op1, rd, wr):
        kb.op("dve", lambda: dve.scalar_tensor_tensor(out=out, in0=in0, scalar=scalar, in1=in1, op0=op0, op1=op1), rd, wr)

    def mm(out, lhsT, rhs, start, stop, rd, wr):
        kb.op("pe", lambda: pe.matmul(out, lhsT=lhsT, rhs=rhs, start=start, stop=stop), rd, wr)

    def recip(out, in_, rd, wr):
        kb.op("dve", lambda: dve.reciprocal(out=out, in_=in_), rd, wr)

    def copy(e, out, in_, rd, wr):
        h = eng[e]
        kb.op(e, lambda: h.tensor_copy(out=out, in_=in_), rd, wr)

    def memset(e, ap, val, wr):
        h = eng[e]
        kb.op(e, lambda: h.memset(ap, val), (), wr)

    cs = kb.es

    def conv_w(dst, src, ng, key):
        v = src.rearrange("(c p) (g n) -> g p c n", p=128, n=512)
        for g in range(ng):
            kb.dma("pool", dst.t[g], v[g], wr=[dst], key=key, defer=True)

    conv_w(Wb_in0, din["e_w_in"], 14, "wc0")

    ident_f = kb.sb(cs, "ident_f", [128, 128], F32)
    ident_b = kb.sb(cs, "ident_b", [128, 128], BF16)
    ones_b = kb.sb(cs, "ones_b", [128, 128], BF16)
    ones_f = kb.sb(cs, "ones_f", [128, 128], F32)
    avg1024 = kb.sb(cs, "avg1024", [128, 128], F32)
    avg128 = kb.sb(cs, "avg128", [128, 128], F32)
    bd64 = kb.sb(cs, "bd64", [128, 128], BF16)
    Pm = kb.sb(cs, "Pm", [128, 128], BF16)
    M4 = kb.sb(cs, "M4", [128, 1, 128], BF16)
    eps_t = kb.sb(cs, "eps_t", [128, 1], F32)
    gn0 = kb.sb(cs, "gn0", [128, 16], F32)
    gn1 = kb.sb(cs, "gn1", [128, 16], F32)
    kw_t = kb.sb(cs, "kw_t", [128, 8, 31], F32)
    kw_b = kb.sb(cs, "kw_b", [128, 8, 31], BF16)
    convb = kb.sb(cs, "convb", [128, 8], F32)
    clng = kb.sb(cs, "clng", [128, 8], F32)
    clnb = kb.sb(cs, "clnb", [128, 8], F32)
    qng = kb.sb(cs, "qng", [128, 1], F32)
    kng = kb.sb(cs, "kng", [128, 1], F32)
    sgs = kb.sb(cs, "sgs", [128, 1], F32)
    neglam = kb.sb(cs, "neglam", [128, 1], F32)
    bglu = kb.sb(cs, "bglu", [128, 16], F32)
    sgn = kb.sb(cs, "sgn", [128, 1], F32)
    PiT = kb.sb(cs, "PiT", [128, 128], F32)

    with ExitStack() as st:
        iota_i = kb.sb(st, "iota_i", [128, 128], I32)
        iota_f = kb.sb(st, "iota_f", [128, 128], F32)
        pidx_i = kb.sb(st, "pidx_i", [128, 1], I32)
        ptmp_i = kb.sb(st, "ptmp_i", [128, 1], I32)
        m_hi = kb.sb(st, "m_hi", [128, 1], F32)
        m_lo = kb.sb(st, "m_lo", [128, 1], F32)
        Am = kb.sb(st, "Am", [128, 128], F32)
        Bm = kb.sb(st, "Bm", [128, 128], F32)
        Pf = kb.sb(st, "Pf", [128, 128], F32)
        ones512 = kb.sb(st, "ones512", [128, 512], BF16)
        freq = kb.sb(st, "freq", [128, 1], F32)
        halfpi = kb.sb(st, "halfpi", [128, 1], F32)
        cn = kb.sb(st, "cn", [128, 1], F32)
        sn = kb.sb(st, "sn", [128, 1], F32)
        nsn = kb.sb(st, "nsn", [128, 1], F32)
        tq = kb.sb(st, "tq", [128, 1], F32)
        COS = kb.sb(st, "COS", [128, S], F32)
        SIN = kb.sb(st, "SIN", [128, S], F32)
        T1 = kb.sb(st, "T1", [128, S // 2], F32)
        lamv = kb.sb(st, "lamv", [64, 4], F32)
        prod = kb.sb(st, "prod", [64, 2], F32)
        lps = kb.ps(st, "lps", [128, 512], F32)
        e2 = kb.sb(st, "e2", [128, 2], F32)
        sgl_ = kb.sb(st, "sgl_", [128, 1], F32)

        kb.op("pool", lambda: pool.iota(iota_i[:], pattern=[[1, 128]], base=0, channel_multiplier=-1), (), [iota_i])
        kb.op("pool", lambda: pool.iota(pidx_i[:], pattern=[[0, 1]], base=0, channel_multiplier=1), (), [pidx_i])
        copy("dve", iota_f[:], iota_i[:], [iota_i], [iota_f])
        kb.op("dve", lambda: dve.tensor_single_scalar(out=ident_f[:], in_=iota_f[:], scalar=0.0, op=ALU.is_equal), [iota_f], [ident_f])
        copy("dve", ident_b[:], ident_f[:], [ident_f], [ident_b])
        kb.op("dve", lambda: dve.tensor_single_scalar(out=PiT[:], in_=iota_f[:], scalar=-64.0, op=ALU.is_equal), [iota_f], [PiT])
        kb.op("dve", lambda: dve.tensor_single_scalar(out=Am[:], in_=iota_f[:], scalar=64.0, op=ALU.is_equal), [iota_f], [Am])
        tt("dve", PiT[:], PiT[:], Am[:], ALU.subtract, [PiT, Am], [PiT])
        kb.op("dve", lambda: dve.tensor_single_scalar(out=Am[:], in_=iota_f[:], scalar=32.0, op=ALU.is_equal), [iota_f], [Am])
        kb.op("dve", lambda: dve.tensor_single_scalar(out=Bm[:], in_=iota_f[:], scalar=-32.0, op=ALU.is_equal), [iota_f], [Bm])
        kb.op("dve", lambda: dve.tensor_single_scalar(out=ptmp_i[:], in_=pidx_i[:], scalar=32, op=ALU.bitwise_and), [pidx_i], [ptmp_i])
        copy("dve", m_hi[:], ptmp_i[:], [ptmp_i], [m_hi])
        ts("dve", m_hi[:], m_hi[:], 1.0 / 32.0, None, ALU.mult, None, [m_hi], [m_hi])
        ts("dve", m_lo[:], m_hi[:], -1.0, 1.0, ALU.mult, ALU.add, [m_hi], [m_lo])
        ts("dve", sgn[:], m_hi[:], 2.0, -1.0, ALU.mult, ALU.add, [m_hi], [sgn])
        ts("dve", Pf[:], Am[:], m_lo[:, 0:1], None, ALU.mult, None, [Am, m_lo], [Pf])
        stt(Pm[:], Bm[:], m_hi[:, 0:1], Pf[:], ALU.mult, ALU.add, [Bm, m_hi, Pf], [Pm])
        memset("pool", ones_b[:], 1.0, [ones_b])
        memset("pool", ones_f[:], 1.0, [ones_f])
        memset("pool", avg1024[:], 1.0 / 1024.0, [avg1024])
        memset("pool", avg128[:], 1.0 / 128.0, [avg128])
        memset("pool", bd64[:], 0.0, [bd64])
        memset("pool", bd64[0:64, 0:64], 1.0 / 64.0, [bd64])
        memset("pool", bd64[64:128, 64:128], 1.0 / 64.0, [bd64])
        memset("pool", eps_t[:], EPS, [eps_t])
        memset("pool", halfpi[:], math.pi / 2, [halfpi])
        memset("pool", ones512[:], 1.0, [ones512])
        kb.op("pool", lambda: pool.affine_select(out=M4[:, 0, :], in_=ones512[:, 0:128], pattern=[[1, 128]],
                                                  compare_op=ALU.is_ge, fill=0.0, base=0,
                                                  channel_multiplier=-1), [ones512], [M4])
        def ld(b, dst, src):
            kb.dma("sp", dst, src, wr=[b], key="small", allow_slow_non_contiguous=True)

        ld(gn0, gn0[:], din["e_norm_g"].rearrange("(c p) -> p c", p=128))
        ld(gn1, gn1[:], din["o_norm_g"].rearrange("(c p) -> p c", p=128))
        for t_ in range(8):
            ld(kw_t, kw_t[:, t_, :], din["e_conv_w"][:, t_ * 128:(t_ + 1) * 128].rearrange("w p -> p w"))
        ld(convb, convb[:], din["e_conv_b"].rearrange("(t p) -> p t", p=128))
        ld(clng, clng[:], din["e_cln_g"].rearrange("(t p) -> p t", p=128))
        ld(clnb, clnb[:], din["e_cln_b"].rearrange("(t p) -> p t", p=128))
        ld(bglu, bglu[:], din["o_b_glu"].rearrange("(t p) -> p t", p=128))
        for hh in range(2):
            ld(qng, qng[hh * 64:(hh + 1) * 64, :], din["e_qn_g"].rearrange("(p o) -> p o", o=1))
            ld(kng, kng[hh * 64:(hh + 1) * 64, :], din["e_kn_g"].rearrange("(p o) -> p o", o=1))
        ld(sgl_, sgl_[:], din["e_subln_g"].rearrange("(p o) -> p o", o=1))
        for i, nme in enumerate(["e_lam_q1", "e_lam_k1", "e_lam_q2", "e_lam_k2"]):
            ld(lamv, lamv[:, i:i + 1], din[nme].rearrange("(p o) -> p o", o=1))
        copy("dve", kw_b[:], kw_t[:], [kw_t], [kw_b])
        lam_init = 0.8 - 0.6 * math.exp(-0.3 * 0)
        ts("dve", sgs[:], sgl_[:], 1.0 - lam_init, None, ALU.mult, None, [sgl_], [sgs])
        tt("dve", prod[:, 0:1], lamv[:, 0:1], lamv[:, 1:2], ALU.mult, [lamv], [prod])
        tt("dve", prod[:, 1:2], lamv[:, 2:3], lamv[:, 3:4], ALU.mult, [lamv], [prod])
        mm(lps[:, 0:2], ones_f[0:64, :], prod[:, :], True, True, [ones_f, prod], [lps])
        act(e2[:], lps[:, 0:2], AF.Exp, [lps], [e2])
        tt("dve", neglam[:], e2[:, 1:2], e2[:, 0:1], ALU.subtract, [e2], [neglam])
        ts("dve", neglam[:], neglam[:], -lam_init, None, ALU.add, None, [neglam], [neglam])

        kb.op("dve", lambda: dve.tensor_single_scalar(out=ptmp_i[:], in_=pidx_i[:], scalar=31, op=ALU.bitwise_and), [pidx_i, m_hi], [ptmp_i])
        copy("dve", freq[:], ptmp_i[:], [ptmp_i], [freq])
        act(freq[:], freq[:], AF.Exp, [freq], [freq], scale=-math.log(10000.0) / 32.0)
        act(cn[:], freq[:], AF.Sin, [freq, halfpi], [cn], bias=halfpi[:, 0:1])
        act(sn[:], freq[:], AF.Sin, [freq], [sn])
        ts("dve", nsn[:], sn[:], -1.0, None, ALU.mult, None, [sn], [nsn])
        memset("dve", COS[:, 0:1], 1.0, [COS])
        memset("dve", SIN[:, 0:1], 0.0, [SIN])
        n = 1
        while n < S:
            ts("dve", T1[:, 0:n], COS[:, 0:n], cn[:, 0:1], None, ALU.mult, None, [COS, cn], [T1])
            stt(COS[:, n:2 * n], SIN[:, 0:n], nsn[:, 0:1], T1[:, 0:n], ALU.mult, ALU.add, [SIN, nsn, T1, COS], [COS])
            ts("dve", T1[:, 0:n], SIN[:, 0:n], cn[:, 0:1], None, ALU.mult, None, [SIN, cn, COS], [T1])
            stt(SIN[:, n:2 * n], COS[:, 0:n], sn[:, 0:1], T1[:, 0:n], ALU.mult, ALU.add, [COS, sn, T1, SIN], [SIN])
            if 2 * n < S:
                ts("dve", tq[:], cn[:], cn[:, 0:1], None, ALU.mult, None, [cn, SIN], [tq])
                stt(tq[:], sn[:], nsn[:, 0:1], tq[:], ALU.mult, ALU.add, [sn, nsn, tq], [tq])
                ts("dve", sn[:], cn[:], sn[:, 0:1], 2.0, ALU.mult, ALU.mult, [cn, sn], [sn])
                copy("dve", cn[:], tq[:], [tq, sn], [cn])
                ts("dve", nsn[:], sn[:], -1.0, None, ALU.mult, None, [sn], [nsn])
            n *= 2
        ts("dve", SIN[:], SIN[:], sgn[:, 0:1], None, ALU.mult, None, [SIN, sgn], [SIN])
        kb.dma("sp", ROT.t[0], COS[:], rd=[COS], wr=[ROT], key="rot_st")
        kb.dma("sp", ROT.t[1], SIN[:], rd=[SIN], wr=[ROT], key="rot_st")
        kb.flush()

    S5W = kb.dram("S5W", [128, 128, 4, 128], BF16, kind=kind("S5W"))
    CLv = kb.sb(cs, "CLv", [128, 9, 128], F32)
    SLv = kb.sb(cs, "SLv", [128, 9, 128], F32)
    RLt = kb.sb(cs, "RLt", [128, 128], F32)
    Dc = kb.sb(cs, "Dc", [128, 128], F32)
    keep_t = {n_: kb.sb(cs, n_, [128, 128], F32) for n_ in ("fR", "fI", "nuR", "nuI", "lR", "muI", "l7R", "l7I", "mask8")}
    with ExitStack() as st:
        def T(name):
            return keep_t[name] if name in keep_t else kb.sb(st, name, [128, 128], F32)
        AN = T("AN"); are = T("are"); aim = T("aim"); ldt = T("ldt"); dtt = T("dtt")
        th = T("th"); xr = T("xr"); kk = T("kk"); rr = T("rr"); r2 = T("r2"); acc = T("acc")
        sinT = T("sinT"); cosT = T("cosT"); EE = T("EE"); lR = T("lR"); lI = T("lI")
        den = T("den"); nr = T("nr"); fR = T("fR"); fI = T("fI"); nuR = T("nuR"); nuI = T("nuI")
        muI = T("muI"); l2R = T("l2R"); l2I = T("l2I"); l4R = T("l4R"); l4I = T("l4I")
        l8R = T("l8R"); l8I = T("l8I"); l7R = T("l7R"); l7I = T("l7I"); x1_ = T("x1_"); x2_ = T("x2_")
        x3_ = T("x3_"); x4_ = T("x4_"); mask8 = T("mask8"); onesq = T("onesq")
        tps = Rot([kb.ps(st, f"tps{i}", [128, 128], F32) for i in range(4)])

        def D_(fn, rd, wr):
            kb.op("dve", fn, rd, wr)

        def mul(o, a, b):
            tt("dve", o[:], a[:], b[:], ALU.mult, [a, b], [o])

        def add(o, a, b):
            tt("dve", o[:], a[:], b[:], ALU.add, [a, b], [o])

        def sub(o, a, b):
            tt("dve", o[:], a[:], b[:], ALU.subtract, [a, b], [o])

        def cmul(oR, oI, aR, aI, bR, bI):
            mul(x1_, aR, bR); mul(x2_, aI, bI); mul(x3_, aR, bI); mul(x4_, aI, bR)
            sub(oR, x1_, x2_); add(oI, x3_, x4_)

        def horner(o, xx, coef):
            n_ = len(coef) - 1
            ts("dve", o[:], xx[:], float(coef[n_]), None, ALU.mult, None, [xx], [o])
            for k_ in range(n_ - 1, 0, -1):
                stt(o[:], o[:], float(coef[k_]), xx[:], ALU.add, ALU.mult, [o, xx], [o])
            ts("dve", o[:], o[:], float(coef[0]), None, ALU.add, None, [o], [o])

        for src, dstt in ((din["o_A_re"], are), (din["o_A_im"], aim)):
            kb.dma("sp", AN[:, 0:64], src, wr=[AN], key="small")
            kb.dma("sp", AN[:, 64:128], src, wr=[AN], key="small")
            tp = tps.next()
            kb.op("pe", lambda tp=tp: pe.transpose(out=tp[:], in_=AN[:], identity=ident_f[:]), [AN, ident_f], [tp])
            copy("dve", dstt[:], tp[:], [tp], [dstt])
        kb.dma("sp", ldt[:], din["o_log_dt"].partition_broadcast(128), wr=[ldt], key="small", allow_slow_non_contiguous=True)
        for tau in range(8):
            kb.dma("sp", Dc[tau * 16:(tau + 1) * 16, :], din["o_D"].rearrange("(g m) -> m g", m=16), wr=[Dc], key="small",
                   allow_slow_non_contiguous=True)
        act(dtt[:], ldt[:], AF.Exp, [ldt], [dtt])
        mul(th, aim, dtt)
        mul(xr, are, dtt)
        MAGIC = 12582912.0
        ts("dve", kk[:], th[:], 1.0 / (2 * math.pi), MAGIC, ALU.mult, ALU.add, [th], [kk])
        ts("dve", kk[:], kk[:], -MAGIC, None, ALU.add, None, [kk], [kk])
        c1 = 6.28125
        c2 = float(np.float32(np.float32(2 * math.pi - c1).view(np.uint32) & np.uint32(0xFFFFF000)).view(np.float32)) if False else 0.0019350051879882812
        c3 = 2 * math.pi - c1 - c2
        stt(rr[:], kk[:], -c1, th[:], ALU.mult, ALU.add, [kk, th], [rr])
        stt(rr[:], kk[:], -c2, rr[:], ALU.mult, ALU.add, [kk, rr], [rr])
        stt(rr[:], kk[:], -c3, rr[:], ALU.mult, ALU.add, [kk, rr], [rr])
        mul(r2, rr, rr)
        horner(acc, r2, [(-1.0) ** k_ / math.factorial(2 * k_ + 1) for k_ in range(11)])
        mul(sinT, acc, rr)
        horner(cosT, r2, [(-1.0) ** k_ / math.factorial(2 * k_) for k_ in range(12)])
        horner(EE, xr, [1.0 / math.factorial(k_) for k_ in range(8)])
        mul(lR, EE, cosT)
        mul(lI, EE, sinT)
        mul(den, are, are); mul(x1_, aim, aim); add(den, den, x1_)
        recip(den[:], den[:], [den], [den])
        ts("dve", nr[:], lR[:], -1.0, None, ALU.add, None, [lR], [nr])
        mul(x1_, nr, are); mul(x2_, lI, aim); add(x1_, x1_, x2_); mul(fR, x1_, den)
        mul(x1_, lI, are); mul(x2_, nr, aim); sub(x1_, x1_, x2_); mul(fI, x1_, den)
        mul(x1_, EE, EE)
        recip(x1_[:], x1_[:], [x1_], [x1_])
        mul(nuR, lR, x1_)
        mul(nuI, lI, x1_)
        ts("dve", nuI[:], nuI[:], -1.0, None, ALU.mult, None, [nuI], [nuI])
        ts("dve", muI[:], lI[:], -1.0, None, ALU.mult, None, [lI], [muI])
        cmul(l2R, l2I, lR, lI, lR, lI)
        cmul(l4R, l4I, l2R, l2I, l2R, l2I)
        cmul(l8R, l8I, l4R, l4I, l4R, l4I)
        cmul(l7R, l7I, l4R, l4I, l2R, l2I)
        cmul(l7R, l7I, l7R, l7I, lR, lI)
        mul(RLt, EE, EE); mul(RLt, RLt, RLt); mul(RLt, RLt, RLt)
        recip(x1_[:], RLt[:], [RLt], [x1_])
        tt("dve", CLv[:, 0, :], l8R[:], x1_[:], ALU.mult, [l8R, x1_], [CLv])
        tt("dve", SLv[:, 0, :], l8I[:], x1_[:], ALU.mult, [l8I, x1_], [SLv])
        for j in range(8):
            tt("dve", x2_[:], CLv[:, j, :], CLv[:, j, :], ALU.mult, [CLv], [x2_])
            tt("dve", x3_[:], SLv[:, j, :], SLv[:, j, :], ALU.mult, [SLv], [x3_])
            tt("dve", CLv[:, j + 1, :], x2_[:], x3_[:], ALU.subtract, [x2_, x3_], [CLv])
            tt("dve", x2_[:], CLv[:, j, :], SLv[:, j, :], ALU.mult, [CLv, SLv], [x2_])
            ts("dve", SLv[:, j + 1, :], x2_[:], 2.0, None, ALU.mult, None, [x2_], [SLv])
        memset("pool", onesq[:], 1.0, [onesq])
        kb.op("pool", lambda: pool.affine_select(out=mask8[:], in_=onesq[:], pattern=[[16, 8], [0, 16]], compare_op=ALU.is_ge,
                                                  fill=0.0, base=15, channel_multiplier=-1), [onesq], [mask8])

        kb.flush()

    if stop_after == "setup":
        kb.final_wait()
        return nc

    x = din["x"]
    with ExitStack() as st:
        hT = kb.sb(st, "hT", [128, 16, 512], BF16)
        wgs = Rot([kb.sb(st, f"wg{i}", [128, 16, 512], BF16) for i in range(2)])
        rcs = Rot([kb.sb(st, f"rc{i}", [128, 2, 512], F32) for i in range(1)])
        xts = Rot([kb.sb(st, f"xt{i}", [128, D], F32) for i in range(1)])
        hn = kb.sb(st, "hn", [128, D], BF16)
        ss = kb.sb(st, "ss", [128, 1], F32)
        sd1 = kb.sb(st, "sd1", [128, 1], F32)
        rs1 = kb.sb(st, "rs1", [128, 1], F32)
        u = kb.sb(st, "u", [128, 8, 542], BF16)
        sgl = kb.sb(st, "sgl", [128, 4, 512], BF16)
        sga = kb.sb(st, "sga", [128, 8, 512], BF16)
        cc = kb.sb(st, "cc", [128, 8, 512], F32)
        csqs = Rot([kb.sb(st, f"csq{i}", [128, 512], F32) for i in range(2)])
        DG = kb.sb(st, "DG", [128, 31, 128], BF16)
        oas = Rot([kb.sb(st, f"oa{i}", [128, 512], BF16) for i in range(2)])
        mean_sb = kb.sb(st, "mean_sb", [128, 512], F32)
        var = kb.sb(st, "var", [128, 512], F32)
        rsl = kb.sb(st, "rsl", [128, 512], F32)
        tln = Rot([kb.sb(st, f"tln{i}", [128, 512], F32) for i in range(2)])
        aln = Rot([kb.sb(st, f"aln{i}", [128, 512], BF16) for i in range(2)])
        qgs = Rot([kb.sb(st, f"qg{i}", [128, 512], BF16) for i in range(2)])
        sqs = Rot([kb.sb(st, f"sq{i}", [128, 512], BF16) for i in range(2)])
        rsq = kb.sb(st, "rsq", [128, 512], F32)
        t1 = kb.sb(st, "t1", [128, 512], F32)
        t2 = kb.sb(st, "t2", [128, 512], F32)
        qos = Rot([kb.sb(st, f"qo{i}", [128, 512], BF16) for i in range(2)])
        vts = Rot([kb.sb(st, f"vt{i}", [128, 512], BF16) for i in range(2)])
        gbts = Rot([kb.sb(st, f"gbt{i}", [128, 512], BF16) for i in range(2)])
        pss = Rot([kb.ps(st, f"ps{i}", [128, 512], F32) for i in range(4)])
        ptrs = Rot([kb.ps(st, f"ptr{i}", [128, 4, 128], BF16) for i in range(2)])
        mps = kb.ps(st, "mps", [128, 512], F32)
        qps = kb.ps(st, "qps", [128, 512], F32)

        memset("pool", u[:, :, 0:30], 0.0, [u])

        def load_wg(src, gi):
            wg = wgs.next()
            kb.dma("sp", wg[:], src.t[gi], rd=[src], wr=[wg])
            return wg

        def fm_tile(wg, j, hTb):
            ps = pss.next()
            for c in range(16):
                mm(ps[:], wg[:, c, j * 128:(j + 1) * 128], hTb[:, c, :], c == 0, c == 15, [wg, hTb], [ps])
            return ps

        pend_pe = []

        def run_pending():
            todo = list(pend_pe)
            del pend_pe[:]
            for f_ in todo:
                f_()

        def fm_tile2(wg, j, hTb):
            ps = fm_tile(wg, j, hTb)
            run_pending()
            return ps

        hTs = Rot([hT, kb.sb(st, "hT2", [128, 16, 512], BF16)])
        hns = Rot([hn, kb.sb(st, "hn2", [128, D], BF16)])
        DGs = Rot([DG, kb.sb(st, "DG2", [128, 31, 128], BF16)])

        def norm_part(bi, tti):
            xt = xts.next()
            t0n = bi * 512
            kb.dma("sp", xt[:], x[t0n + tti * 128:t0n + (tti + 1) * 128, :], wr=[xt])
            hnb = hns.next()
            act(hnb[:], xt[:], AF.Square, [xt], [hnb, ss], accum=ss[:, 0:1])
            act(sd1[:], ss[:], AF.Sqrt, [ss, eps_t], [sd1], bias=eps_t[:, 0:1], scale=1.0 / D)
            recip(rs1[:], sd1[:], [sd1], [rs1])
            act(hnb[:], xt[:], AF.Copy, [xt, rs1], [hnb], scale=rs1[:, 0:1])
            return hnb

        def transpose_part(hnb, hTb, tti):
            for c4 in range(4):
                ptr = ptrs.next()
                for q_ in range(4):
                    c = c4 * 4 + q_
                    kb.op("pe", lambda c=c, q_=q_, ptr=ptr: pe.transpose(out=ptr[:, q_, :], in_=hnb[:, c * 128:(c + 1) * 128],
                                                                           identity=ident_b[:]), [hnb, ident_b], [ptr])
                tt("dve", hTb[:, c4 * 4:c4 * 4 + 4, tti * 128:(tti + 1) * 128], ptr[:],
                   bc(gn0[:, c4 * 4:c4 * 4 + 4].unsqueeze(2), [128, 4, 128]), ALU.mult, [ptr, gn0], [hTb])

        hT_cur = hTs.next()
        for tti in range(4):
            hnb = norm_part(0, tti)
            transpose_part(hnb, hT_cur, tti)

        for bi in range(NB):
            t0 = bi * 512
            hTc = hT_cur
            hT_next = hTs.next() if bi + 1 < NB else None
            rc = rcs.next()
            kb.dma("sp", rc[:], ROT.t[:, :, t0:t0 + 512].rearrange("a p t -> p a t"), rd=[ROT], wr=[rc])
            nxt_hn = {}

            def prefetch_step(k):
                if hT_next is None:
                    return
                if k >= 1:
                    transpose_part(nxt_hn[k - 1], hT_next, k - 1)
                if k < 4:
                    nxt_hn[k] = norm_part(bi + 1, k)

            def conv_stage(jj, DGb):
                def f_():
                    cps = pss.next()
                    for tap in range(31):
                        mm(cps[:], DGb[:, tap, :], u[:, jj, tap:tap + 512], tap == 0, tap == 30, [DGb, u], [cps])
                    act(cc[:, jj, :], cps[:], AF.Identity, [cps, convb], [cc], bias=convb[:, jj:jj + 1])
                    csq = csqs.next()
                    act(csq[:], cps[:], AF.Square, [cps, convb], [csq], bias=convb[:, jj:jj + 1])

                    def g_():
                        mm(mps[:], avg1024[:], cc[:, jj, :], jj == 0, jj == 7, [avg1024, cc], [mps])
                        mm(qps[:], avg1024[:], csq[:], jj == 0, jj == 7, [avg1024, csq], [qps])
                    pend_pe.append(g_)
                return f_

            for half in range(2):
                wg = load_wg(Wb_in0, 2 + half)
                for j in range(4):
                    ps = fm_tile2(wg, j, hTc)
                    act(sgl[:, j, :], ps[:], AF.Sigmoid, [ps], [sgl])
                wg = load_wg(Wb_in0, 0 + half)
                for j in range(4):
                    jj = half * 4 + j
                    ps = fm_tile2(wg, j, hTc)
                    tt("dve", u[:, jj, 30:542], ps[:], sgl[:, j, :], ALU.mult, [ps, sgl], [u])
                    DGb = DGs.next()
                    tt("pool", DGb[:], bc(ident_b[:].unsqueeze(1), [128, 31, 128]),
                       bc(kw_b[:, jj, :].unsqueeze(2), [128, 31, 128]), ALU.mult, [ident_b, kw_b], [DGb])
                    pend_pe.append(conv_stage(jj, DGb))
            for half in range(2):
                wg = load_wg(Wb_in0, 4 + half)
                for j in range(4):
                    ps = fm_tile2(wg, j, hTc)
                    act(sga[:, half * 4 + j, :], ps[:], AF.Silu, [ps], [sga])
            run_pending()
            run_pending()
            copy("pool", u[:, :, 0:30], u[:, :, 512:542], [u], [u])
            act(mean_sb[:], mps[:], AF.Copy, [mps], [mean_sb])
            tt("dve", var[:], mean_sb[:], mean_sb[:], ALU.mult, [mean_sb], [var])
            tt("dve", var[:], qps[:], var[:], ALU.subtract, [qps, var], [var])
            act(var[:], var[:], AF.Sqrt, [var, eps_t], [var], bias=eps_t[:, 0:1])
            recip(rsl[:], var[:], [var], [rsl])
            for jj in range(8):
                tl = tln.next()
                al = aln.next()
                tt("dve", tl[:], cc[:, jj, :], mean_sb[:], ALU.subtract, [cc, mean_sb], [tl])
                tt("dve", tl[:], tl[:], rsl[:], ALU.mult, [tl, rsl], [tl])
                act(al[:], tl[:], AF.Silu, [tl, clng, clnb], [al], scale=clng[:, jj:jj + 1], bias=clnb[:, jj:jj + 1])
                oa = oas.next()
                tt("pool", oa[:], al[:], sga[:, jj, :], ALU.mult, [al, sga], [oa])
                kb.dma("pool", MIXT.t[jj, :, t0:t0 + 512], oa[:], rd=[oa], wr=[MIXT])

            def qk_stage(qg, sq, isq, dst, hh):
                def f_():
                    qo = qos.next()
                    stp = pss.next()
                    mm(stp[:], bd64[:], sq[:], True, True, [bd64, sq], [stp])
                    rtp = pss.next()
                    mm(rtp[:], Pm[:], qg[:], True, True, [Pm, qg], [rtp])
                    act(rsq[:], stp[:], AF.Sqrt, [stp, eps_t], [rsq], bias=eps_t[:, 0:1])
                    recip(rsq[:], rsq[:], [rsq], [rsq])
                    tt("pool", t1[:], qg[:], rc[:, 0, :], ALU.mult, [qg, rc], [t1])
                    tt("dve", t2[:], rtp[:], rc[:, 1, :], ALU.mult, [rtp, rc], [t2])
                    tt("pool", t1[:], t1[:], t2[:], ALU.add, [t1, t2], [t1])
                    stt(qo[:], t1[:], 0.125 if isq else 1.0, rsq[:], ALU.mult, ALU.mult, [t1, rsq], [qo])
                    kb.dma("pool", dst.t[hh, :, t0:t0 + 512], qo[:], rd=[qo], wr=[dst])
                return f_

            for gidx, gi in enumerate((6, 7, 8, 9)):
                wg = load_wg(Wb_in0, gi)
                isq = gi < 8
                gvec = qng if isq else kng
                dst = QT if isq else KT
                for j in range(4):
                    hh = (gi % 2) * 4 + j
                    ps = fm_tile2(wg, j, hTc)
                    qg = qgs.next()
                    sq = sqs.next()
                    act(qg[:], ps[:], AF.Copy, [ps, gvec], [qg], scale=gvec[:, 0:1])
                    act(sq[:], ps[:], AF.Square, [ps], [sq])
                    pend_pe.append(qk_stage(qg, sq, isq, dst, hh))
                prefetch_step(gidx)
            for gi in (10, 11):
                wg = load_wg(Wb_in0, gi)
                for tti in range(4):
                    ps = pss.next()
                    for c in range(16):
                        mm(ps[:], hTc[:, c, tti * 128:(tti + 1) * 128], wg[:, c, :], c == 0, c == 15, [wg, hTc], [ps])
                    run_pending()
                    vt = vts.next()
                    act(vt[:], ps[:], AF.Copy, [ps], [vt])
                    h0 = (gi - 10) * 4
                    kb.dma("pool", VV.t[h0:h0 + 4, :, bi * 4 + tti, :].rearrange("h p d -> p h d"),
                           vt[:].rearrange("p (h d) -> p h d", h=4), rd=[vt], wr=[VV])
                if gi == 10:
                    prefetch_step(4)
            for gi in (12, 13):
                wg = load_wg(Wb_in0, gi)
                for j in range(4):
                    hh = (gi - 12) * 4 + j
                    ps = fm_tile2(wg, j, hTc)
                    gbt = gbts.next()
                    act(gbt[:], ps[:], AF.Silu, [ps], [gbt])
                    kb.dma("pool", GB.t[hh, :, t0:t0 + 512], gbt[:], rd=[gbt], wr=[GB])
            run_pending()
            hT_cur = hT_next
        kb.flush()

    if stop_after == "l0p":
        kb.final_wait()
        return nc

    with ExitStack() as st:
        kTas = Rot([kb.sb(st, f"kTa{i}", [128, S], BF16) for i in range(2)])
        kTbs = Rot([kb.sb(st, f"kTb{i}", [128, S], BF16) for i in range(2)])
        qTs = Rot([kb.sb(st, f"qT{i}", [128, S], BF16) for i in range(2)])
        vhs = Rot([kb.sb(st, f"vh{i}", [128, 32, 128], BF16) for i in range(2)])
        gbs = Rot([kb.sb(st, f"gbh{i}", [128, S], BF16) for i in range(2)])
        e1s = Rot([kb.sb(st, f"e1_{i}", [128, 512], BF16) for i in range(3)])
        e2s = Rot([kb.sb(st, f"e2_{i}", [128, 512], BF16) for i in range(3)])
        pscore = Rot([kb.ps(st, f"psc{i}", [128, 512], F32) for i in range(4)])
        o1 = kb.ps(st, "o1", [128, 512], F32)
        d1 = kb.ps(st, "d1", [128, 512], F32)
        o2 = kb.ps(st, "o2", [128, 512], F32)
        d2 = kb.ps(st, "d2", [128, 512], F32)

        def FA(name, n=2, dt=F32):
            return Rot([kb.sb(st, f"{name}{i}", [128, 512], dt) for i in range(n)])
        rd1s = FA("rd1"); rd2s = FA("rd2"); c1s = FA("c1"); c2s = FA("c2"); ods = FA("od"); osqs = FA("osq"); rsfs = FA("rsf")
        obs = FA("ob", 2, BF16)
        for b_ in kTas.bufs:
            memset("pool", b_[64:128, :], 0.0, [b_])
        for b_ in kTbs.bufs:
            memset("pool", b_[0:64, :], 0.0, [b_])

        def finalize1():
            rd1 = rd1s.next(); rd2 = rd2s.next(); c1 = c1s.next(); c2 = c2s.next(); od = ods.next(); osq = osqs.next()
            copy("dve", c1[:], o1[:], [o1], [c1])
            act(rd1[:], d1[:], AF.Ln, [d1], [rd1])
            copy("dve", c2[:], o2[:], [o2], [c2])
            act(rd2[:], d2[:], AF.Ln, [d2], [rd2])
            act(rd1[:], rd1[:], AF.Exp, [rd1], [rd1], scale=-1.0)
            act(rd2[:], rd2[:], AF.Exp, [rd2], [rd2], scale=-1.0)
            tt("dve", c1[:], c1[:], rd1[:], ALU.mult, [c1, rd1], [c1])
            tt("dve", c2[:], c2[:], rd2[:], ALU.mult, [c2, rd2], [c2])
            stt(od[:], c2[:], neglam[:, 0:1], c1[:], ALU.mult, ALU.add, [c2, neglam, c1], [od])
            act(osq[:], od[:], AF.Square, [od], [osq])
            return od, osq

        def finalize2(h, qs, gb, od, osq):
            rsf = rsfs.next()
            stp = pscore.next()
            mm(stp[:], avg128[:], osq[:], True, True, [avg128, osq], [stp])
            act(rsf[:], stp[:], AF.Ln, [stp, eps_t], [rsf], bias=eps_t[:, 0:1])
            act(rsf[:], rsf[:], AF.Exp, [rsf], [rsf], scale=-0.5)
            tt("dve", od[:], od[:], rsf[:], ALU.mult, [od, rsf], [od])
            ob = obs.next()
            stt(ob[:], od[:], sgs[:, 0:1], gb[:, qs], ALU.mult, ALU.mult, [od, sgs, gb], [ob])
            kb.dma("pool", MIXT.t[8 + h, :, qs], ob[:], rd=[ob], wr=[MIXT])

        conv_w(Wb_out0, din["e_w_out"], 4, "wc1")
        conv_w(Wb_in1, din["o_w_in"], 8, "wc2")
        conv_w(Wb_glu, din["o_w_glu"], 4, "wc3")
        conv_w(Wb_out1, din["o_w_out"], 4, "wc4")
        TGA = 2
        TBs = Rot([kb.sb(st, f"TB{i}", [128, TGA, 2, 512], F32) for i in range(2)])
        Yt1 = kb.sb(st, "Yt1", [128, TGA, 256], F32)
        Yt2 = kb.sb(st, "Yt2", [128, TGA, 256], F32)
        C16 = kb.sb(st, "C16", [128, 128, 16], F32)
        S16 = kb.sb(st, "S16", [128, 128, 16], F32)
        Zt1 = kb.sb(st, "Zt1", [128, 128, 8], F32)
        Zt2 = kb.sb(st, "Zt2", [128, 128, 8], F32)
        memset("pool", C16[:, :, 0:1], 1.0, [C16])
        memset("pool", S16[:, :, 0:1], 0.0, [S16])
        for j in range(4):
            n = 1 << j
            cn_ = bc(CLv[:, j, :].unsqueeze(2), [128, 128, n])
            sn_ = bc(SLv[:, j, :].unsqueeze(2), [128, 128, n])
            tt("pool", Zt1[:, :, 0:n], C16[:, :, 0:n], cn_, ALU.mult, [C16, CLv], [Zt1])
            tt("pool", Zt2[:, :, 0:n], S16[:, :, 0:n], sn_, ALU.mult, [S16, SLv], [Zt2])
            tt("pool", C16[:, :, n:2 * n], Zt1[:, :, 0:n], Zt2[:, :, 0:n], ALU.subtract, [Zt1, Zt2], [C16])
            tt("pool", Zt1[:, :, 0:n], S16[:, :, 0:n], cn_, ALU.mult, [S16, CLv], [Zt1])
            tt("pool", Zt2[:, :, 0:n], C16[:, :, 0:n], sn_, ALU.mult, [C16, SLv], [Zt2])
            tt("pool", S16[:, :, n:2 * n], Zt1[:, :, 0:n], Zt2[:, :, 0:n], ALU.add, [Zt1, Zt2], [S16])

        def gen_table_set(ts_):
            TB_ = TBs.next()
            gs_ = slice(TGA * ts_, TGA * ts_ + TGA)
            Cc = TB_[:, :, 0, :]
            Sc = TB_[:, :, 1, :]
            copy("pool", Cc[:, :, 0:16], C16[:, gs_, :], [C16], [TB_])
            copy("pool", Sc[:, :, 0:16], S16[:, gs_, :], [S16], [TB_])
            for j in range(4, 9):
                n = 1 << j
                cn_ = bc(CLv[:, j, gs_].unsqueeze(2), [128, TGA, n])
                sn_ = bc(SLv[:, j, gs_].unsqueeze(2), [128, TGA, n])
                tt("pool", Yt1[:, :, 0:n], Cc[:, :, 0:n], cn_, ALU.mult, [TB_, CLv], [Yt1])
                tt("pool", Yt2[:, :, 0:n], Sc[:, :, 0:n], sn_, ALU.mult, [TB_, SLv], [Yt2])
                tt("pool", Cc[:, :, n:2 * n], Yt1[:, :, 0:n], Yt2[:, :, 0:n], ALU.subtract, [Yt1, Yt2], [TB_])
                tt("pool", Yt1[:, :, 0:n], Sc[:, :, 0:n], cn_, ALU.mult, [TB_, CLv], [Yt1])
                tt("pool", Yt2[:, :, 0:n], Cc[:, :, 0:n], sn_, ALU.mult, [TB_, SLv], [Yt2])
                tt("pool", Sc[:, :, n:2 * n], Yt1[:, :, 0:n], Yt2[:, :, 0:n], ALU.add, [Yt1, Yt2], [TB_])
            kb.dma("pool", TAB.t[gs_].rearrange("g p a c -> p g a c"), TB_[:], rd=[TB_], wr=[TAB])

        next_set = [0]
        pending_fin = None
        for h in range(8):
            kTa = kTas.next(); kTb = kTbs.next(); qT = qTs.next(); vh = vhs.next(); gb = gbs.next()
            kb.dma("sp", kTa[0:64, :], KT.t[h, 0:64, :], rd=[KT], wr=[kTa])
            kb.dma("sp", kTb[64:128, :], KT.t[h, 64:128, :], rd=[KT], wr=[kTb])
            kb.dma("sp", qT[:], QT.t[h], rd=[QT], wr=[qT])
            kb.dma("sp", vh[:], VV.t[h], rd=[VV], wr=[vh])
            kb.dma("sp", gb[:], GB.t[h], rd=[GB], wr=[gb])
            for qb in range(8):
                qs = slice(qb * 512, (qb + 1) * 512)
                nkt = 4 * (qb + 1)

                def scores(kt, qb=qb, qs=qs):
                    ks = slice(kt * 128, (kt + 1) * 128)
                    o = max(0, kt - 4 * qb)
                    c0 = 128 * o
                    qcs = slice(qb * 512 + c0, (qb + 1) * 512)
                    s1 = pscore.next(); s2 = pscore.next()
                    mm(s1[:, c0:512], kTa[:, ks], qT[:, qcs], True, True, [kTa, qT], [s1])
                    mm(s2[:, c0:512], kTb[:, ks], qT[:, qcs], True, True, [kTb, qT], [s2])
                    e1 = e1s.next(); e2 = e2s.next()
                    act(e1[:, c0:512], s1[:, c0:512], AF.Exp, [s1], [e1])
                    act(e2[:, c0:512], s2[:, c0:512], AF.Exp, [s2], [e2])
                    if kt >= 4 * qb:
                        tt("dve", e1[:, c0:c0 + 128], e1[:, c0:c0 + 128], M4[:, 0, 0:128], ALU.mult, [e1, M4], [e1])
                        tt("dve", e2[:, c0:c0 + 128], e2[:, c0:c0 + 128], M4[:, 0, 0:128], ALU.mult, [e2, M4], [e2])
                    return e1, e2, c0

                pend = scores(0)
                for kt in range(nkt):
                    nxt = scores(kt + 1) if kt + 1 < nkt else None
                    e1, e2, c0 = pend
                    first, lastk = kt == 0, kt == nkt - 1
                    mm(o1[:, c0:512], vh[:, kt, :], e1[:, c0:512], first, lastk, [vh, e1], [o1])
                    mm(d1[:, c0:512], ones_b[:], e1[:, c0:512], first, lastk, [ones_b, e1], [d1])
                    mm(o2[:, c0:512], vh[:, kt, :], e2[:, c0:512], first, lastk, [vh, e2], [o2])
                    mm(d2[:, c0:512], ones_b[:], e2[:, c0:512], first, lastk, [ones_b, e2], [d2])
                    pend = nxt
                    if kt == min(2, nkt - 1) and pending_fin is not None:
                        finalize2(*pending_fin)
                        pending_fin = None
                if pending_fin is not None:
                    finalize2(*pending_fin)
                od, osq = finalize1()
                pending_fin = (h, qs, gb, od, osq)
                if next_set[0] < 128 // TGA:
                    gen_table_set(next_set[0])
                    next_set[0] += 1
        finalize2(*pending_fin)
        kb.flush()

    if stop_after == "l0a":
        kb.final_wait()
        return nc

    UF = kb.dram("UF", [16, 128, S], BF16, kind=kind("UF"))
    GF = kb.dram("GF", [16, 128, S], BF16, kind=kind("GF"))
    with ExitStack() as st:
        mixTs = Rot([kb.sb(st, f"mixT{i}", [128, 16, 512], BF16) for i in range(2)])
        wgs = Rot([kb.sb(st, f"wg{i}", [128, 16, 512], BF16) for i in range(2)])
        xts4 = [kb.sb(st, f"xq{i}", [128, D], F32) for i in range(4)]
        hn = kb.sb(st, "hn", [128, D], BF16)
        junk = kb.sb(st, "junk", [128, D], BF16)
        ss = kb.sb(st, "ss", [128, 1], F32)
        sd1 = kb.sb(st, "sd1", [128, 1], F32)
        rs1 = kb.sb(st, "rs1", [128, 1], F32)
        h1T = kb.sb(st, "h1T", [128, 16, 512], BF16)
        ubs = Rot([kb.sb(st, f"ub{i}", [128, 512], BF16) for i in range(3)])
        pss = Rot([kb.ps(st, f"ps{i}", [128, 512], F32) for i in range(4)])
        ptrs = Rot([kb.ps(st, f"ptr{i}", [128, 4, 128], BF16) for i in range(2)])

        def load_wg(src, gi):
            wg = wgs.next()
            kb.dma("sp", wg[:], src.t[gi], rd=[src], wr=[wg])
            return wg

        def norm_transpose_perm(xt, gn, hTb, tti):
            act(junk[:], xt[:], AF.Square, [xt], [junk, ss], accum=ss[:, 0:1])
            act(sd1[:], ss[:], AF.Sqrt, [ss, eps_t], [sd1], bias=eps_t[:, 0:1], scale=1.0 / D)
            recip(rs1[:], sd1[:], [sd1], [rs1])
            act(hn[:], xt[:], AF.Copy, [xt, rs1], [hn], scale=rs1[:, 0:1])
            for c4 in range(4):
                ptr = ptrs.next()
                for q_ in range(4):
                    c = c4 * 4 + q_
                    kb.op("pe", lambda c=c, q_=q_, ptr=ptr: pe.transpose(out=ptr[:, q_, :], in_=hn[:, c * 128:(c + 1) * 128],
                                                                           identity=ident_b[:]), [hn, ident_b], [ptr])
                oap = hTb[:, c4 * 4:c4 * 4 + 4, :].rearrange("p k (t c) -> p k t c", t=8)[:, :, :, 16 * tti:16 * tti + 16]
                iap = ptr[:].rearrange("p k (c t) -> p k t c", t=8)
                tt("dve", oap, iap, bc(gn[:, c4 * 4:c4 * 4 + 4].unsqueeze(2).unsqueeze(3), [128, 4, 8, 16]), ALU.mult,
                   [ptr, gn], [hTb])

        for bi in range(NB):
            t0 = bi * 512
            mixT = mixTs.next()
            kb.dma("sp", mixT[:], MIXT.t[:, :, t0:t0 + 512].rearrange("c p t -> p c t"), rd=[MIXT], wr=[mixT])
            for tti in range(4):
                kb.dma("sp", xts4[tti][:], x[t0 + tti * 128:t0 + (tti + 1) * 128, :], wr=[xts4[tti]])
            for og in range(4):
                wg = load_wg(Wb_out0, og)
                for tti in range(4):
                    ps = pss.next()
                    for c in range(16):
                        mm(ps[:], mixT[:, c, tti * 128:(tti + 1) * 128], wg[:, c, :], c == 0, c == 15, [mixT, wg], [ps])
                    xs = xts4[tti][:, og * 512:(og + 1) * 512]
                    tt("dve", xs, ps[:], xs, ALU.add, [ps, xts4[tti]], [xts4[tti]])
            for tti in range(4):
                kb.dma("pool", X1.t[t0 + tti * 128:t0 + (tti + 1) * 128, :], xts4[tti][:], rd=[xts4[tti]], wr=[X1])
                norm_transpose_perm(xts4[tti], gn1, h1T, tti)
            for gi in range(8):
                wg = load_wg(Wb_in1, gi)
                for j in range(4):
                    ps = pss.next()
                    for c in range(16):
                        mm(ps[:], wg[:, c, j * 128:(j + 1) * 128], h1T[:, c, :], c == 0, c == 15, [wg, h1T], [ps])
                    ub = ubs.next()
                    ft = (gi % 4) * 4 + j
                    dst = UF if gi < 4 else GF
                    act(ub[:], ps[:], AF.Copy if gi < 4 else AF.Silu, [ps], [ub])
                    kb.dma("pool", dst.t[ft].rearrange("p (t c) -> p t c", t=8)[:, :, bi * 64:(bi + 1) * 64],
                           ub[:].rearrange("p (t c) -> p t c", t=8), rd=[ub], wr=[dst])
        kb.flush()

    if stop_after == "l1p":
        kb.final_wait()
        return nc

    with ExitStack() as st:
        tps = Rot([kb.ps(st, f"tpsm{i}", [128, 128], F32) for i in range(4)])
        GB_ = 16
        BA = kb.sb(st, "BA", [128, GB_, 16], F32)
        BAp = kb.sb(st, "BAp", [128, GB_, 16], F32)
        CN = kb.sb(st, "CN", [128, 128], F32)
        CNp = kb.sb(st, "CNp", [128, 128], F32)
        VA = kb.sb(st, "VA", [128, GB_, 8, 16], F32)
        VAp = kb.sb(st, "VAp", [128, GB_, 8, 16], F32)
        WA = kb.sb(st, "WA", [128, GB_, 9, 16], F32)
        WAp = kb.sb(st, "WAp", [128, GB_, 9, 16], F32)
        WBA = kb.sb(st, "WBA", [128, GB_, 8, 16], F32)
        WBAp = kb.sb(st, "WBAp", [128, GB_, 8, 16], F32)
        y1 = kb.sb(st, "y1", [128, GB_, 16], F32)
        y2 = kb.sb(st, "y2", [128, GB_, 16], F32)
        y3 = kb.sb(st, "y3", [128, GB_, 16], F32)
        y4 = kb.sb(st, "y4", [128, GB_, 16], F32)
        S5ts = Rot([kb.sb(st, f"S5t{i}", [128, GB_, 4, 128], BF16) for i in range(2)])

        ytmp = {"dve": (y1, y2), "pool": (y3, y4)}

        def cmat(e_, oA, oAp, iA, iAp, zR, zI, g0, rdo, wro):
            zr = bc(zR[:, g0:g0 + GB_].unsqueeze(2), [128, GB_, 16])
            zi = bc(zI[:, g0:g0 + GB_].unsqueeze(2), [128, GB_, 16])
            ya, yb = ytmp[e_]
            tt(e_, ya[:], iA, zr, ALU.mult, rdo + [zR], [ya])
            tt(e_, yb[:], iAp, zi, ALU.mult, rdo + [zI], [yb])
            tt(e_, oA, ya[:], yb[:], ALU.add, [ya, yb], wro)
            tt(e_, ya[:], iAp, zr, ALU.mult, rdo + [zR], [ya])
            tt(e_, yb[:], iA, zi, ALU.mult, rdo + [zI], [yb])
            tt(e_, oAp, ya[:], yb[:], ALU.subtract, [ya, yb], wro)

        for gb_ in range(128 // GB_):
            g0 = gb_ * GB_
            gsl = slice(g0, g0 + GB_)
            kb.dma("sp", BA[0:64], din["o_B_re"][gsl].rearrange("g p m -> p g m"), wr=[BA], key="s5ld")
            kb.dma("sp", BA[64:128], din["o_B_im"][gsl].rearrange("g p m -> p g m"), wr=[BA], key="s5ld")
            kb.dma("sp", BAp[0:64], din["o_B_im"][gsl].rearrange("g p m -> p g m"), wr=[BAp], key="s5ld")
            kb.dma("sp", BAp[64:128], din["o_B_re"][gsl].rearrange("g p m -> p g m"), wr=[BAp], key="s5ld")
            ts("dve", BAp[0:64], BAp[0:64], -1.0, None, ALU.mult, None, [BAp], [BAp])
            for hb in range(GB_ // 8):
                gs8 = slice(g0 + hb * 8, g0 + hb * 8 + 8)
                kb.dma("sp", CN[:, 0:64], din["o_C_re"][gs8].rearrange("g m p -> (g m) p"), wr=[CN], key="s5ld")
                kb.dma("sp", CN[:, 64:128], din["o_C_im"][gs8].rearrange("g m p -> (g m) p"), wr=[CN], key="s5ld")
                kb.dma("sp", CNp[:, 0:64], din["o_C_im"][gs8].rearrange("g m p -> (g m) p"), wr=[CNp], key="s5ld")
                kb.dma("sp", CNp[:, 64:128], din["o_C_re"][gs8].rearrange("g m p -> (g m) p"), wr=[CNp], key="s5ld")
                tp = tps.next()
                kb.op("pe", lambda tp=tp: pe.transpose(out=tp[:], in_=CN[:], identity=ident_f[:]), [CN, ident_f], [tp])
                copy("dve", WA[:, hb * 8:hb * 8 + 8, 0, :], tp[:].rearrange("p (g m) -> p g m", m=16), [tp], [WA])
                tp = tps.next()
                kb.op("pe", lambda tp=tp: pe.transpose(out=tp[:], in_=CNp[:], identity=ident_f[:]), [CNp, ident_f], [tp])
                copy("dve", WAp[:, hb * 8:hb * 8 + 8, 0, :], tp[:].rearrange("p (g m) -> p g m", m=16), [tp], [WAp])
            ts("dve", WA[64:128, :, 0, :], WA[64:128, :, 0, :], -1.0, None, ALU.mult, None, [WA], [WA])
            cmat("dve", VA[:, :, 0, :], VAp[:, :, 0, :], BA[:], BAp[:], fR, fI, g0, [BA, BAp], [VA, VAp])
            for k_ in range(8):
                cmat("pool", WA[:, :, k_ + 1, :], WAp[:, :, k_ + 1, :], WA[:, :, k_, :], WAp[:, :, k_, :], lR, muI, g0, [WA, WAp], [WA, WAp])
            for sg in range(7):
                cmat("dve", VA[:, :, sg + 1, :], VAp[:, :, sg + 1, :], VA[:, :, sg, :], VAp[:, :, sg, :], nuR, nuI, g0, [VA, VAp], [VA, VAp])
            for sg in range(8):
                cmat("dve", WBA[:, :, sg, :], WBAp[:, :, sg, :], VA[:, :, sg, :], VAp[:, :, sg, :], l7R, l7I, g0, [VA, VAp], [WBA, WBAp])
            S5t = S5ts.next()
            for g in range(GB_):
                tp = tps.next()
                kb.op("pe", lambda tp=tp, g=g: pe.matmul(tp[:], lhsT=VA[:, g, :, :].rearrange("p s m -> p (s m)"),
                                                          rhs=WA[:, g, 0:8, :].rearrange("p s m -> p (s m)"), start=True, stop=True),
                      [VA, WA], [tp])
                tt("dve", S5t[:, g, 2, :], tp[:], mask8[:], ALU.mult, [tp, mask8], [S5t])
                tp = tps.next()
                kb.op("pe", lambda tp=tp, g=g: pe.transpose(out=tp[:], in_=WBA[:, g, :, :].rearrange("p s m -> p (s m)"),
                                                             identity=ident_f[:]), [WBA, ident_f], [tp])
                act(S5t[:, g, 0, :], tp[:], AF.Copy, [tp], [S5t])
                tp = tps.next()
                kb.op("pe", lambda tp=tp, g=g: pe.transpose(out=tp[:], in_=WBAp[:, g, :, :].rearrange("p s m -> p (s m)"),
                                                             identity=ident_f[:]), [WBAp, ident_f], [tp])
                act(S5t[:, g, 1, :], tp[:], AF.Copy, [tp], [S5t], scale=-1.0)
            copy("pool", S5t[:, :, 3, :].rearrange("p g (s m) -> p g s m", m=16), WA[:, :, 1:9, :], [WA], [S5t])
            kb.dma("pool", S5W.t[gsl].rearrange("g p k c -> p g k c"), S5t[:], rd=[S5t], wr=[S5W])
        kb.flush()

    if stop_after == "s5setup":
        kb.final_wait()
        return nc

    ZF = kb.dram("ZF", [16, 128, S], BF16, kind=kind("ZF"))
    with ExitStack() as st:
        Tbs = Rot([kb.sb(st, f"Tb{i}", [128, 2, 512], F32) for i in range(9)])
        pAs = Rot([kb.ps(st, f"pA{i}", [128, 512], F32) for i in range(2)])
        pBs = Rot([kb.ps(st, f"pB{i}", [128, 512], F32) for i in range(2)])
        pYs = Rot([kb.ps(st, f"pY{i}", [128, 512], F32) for i in range(2)])
        pGs = Rot([kb.ps(st, f"pG{i}", [128, 512], F32) for i in range(2)])

        def FR(name, n, dt=F32):
            return Rot([kb.sb(st, f"{name}{i}", [128, 512], dt) for i in range(n)])
        t1s = FR("st1", 2); t2s = FR("st2", 2); cAs = FR("cA", 2); gAs = FR("gA", 3); t5s = FR("st5", 2); t6s = FR("st6", 2)
        ysbs = FR("ysb", 5); y2s = FR("y2", 2); w2s = FR("w2", 2); sgms = FR("sgm", 2); Hps = FR("Hp", 2, BF16); zts = FR("zt", 3, BF16)
        Ucs = Rot([kb.sb(st, f"Ucm{i}", [128, 512], BF16) for i in range(9)])
        Wgs = Rot([kb.sb(st, f"Wgm{i}", [128, 4, 128], BF16) for i in range(9)])
        for hp in Hps.bufs:
            memset("pool", hp[:, 0:1], 0.0, [hp])
        ctx = {}

        def tab(g):
            Tb = ctx[g]["Tb"]
            return Tb, Tb, Tb[:, 0, :], Tb[:, 1, :]

        def s0(g):
            ft, j8 = divmod(g, 8)
            c = ctx[g] = {}
            c["Uc"] = Uc = Ucs.next(); c["Wg"] = Wg = Wgs.next()
            kb.dma("sp", Uc[:], UF.t[ft, 16 * j8:16 * j8 + 16, :].rearrange("m (t c) -> t m c", t=8), rd=[UF], wr=[Uc])
            kb.dma("sp", Wg[:], S5W.t[g], rd=[S5W], wr=[Wg])
            c["Tb"] = Tb = Tbs.next()
            kb.dma("sp", Tb[:], TAB.t[g], rd=[TAB], wr=[Tb])
            c["pA"] = pA = pAs.next(); c["pB"] = pB = pBs.next()
            mm(pA[:], Wg[:, 0, :], Uc[:], True, True, [Wg, Uc], [pA])
            mm(pB[:], Wg[:, 1, :], Uc[:], True, True, [Wg, Uc], [pB])

        def s1(g):
            c = ctx[g]
            COSc, SINc, co, si = tab(g)
            c["t1"] = t1 = t1s.next(); c["t2"] = t2 = t2s.next()
            tt("dve", t1[:], c["pA"][:], co, ALU.mult, [c["pA"], COSc], [t1])
            tt("dve", t2[:], c["pB"][:], si, ALU.mult, [c["pB"], SINc], [t2])

        def s2(g):
            c = ctx[g]
            c["cA"] = cA = cAs.next()
            tt("pool", cA[:], c["t1"][:], c["t2"][:], ALU.add, [c["t1"], c["t2"]], [cA])

        def s3(g):
            c = ctx[g]
            c["gA"] = gA = gAs.next()
            cA = c["cA"]
            rl = bc(RLt[:, g:g + 1], [128, 512])
            kb.op("dve", lambda: dve.tensor_tensor_scan(out=gA[:], data0=rl, data1=cA[:], initial=0.0, op0=ALU.mult, op1=ALU.add),
                  [RLt, cA], [gA])

        def s4(g):
            c = ctx[g]
            c["pG"] = pG = pGs.next()
            mm(pG[:], PiT[:], c["gA"][:], True, True, [PiT, c["gA"]], [pG])

        def s5(g):
            c = ctx[g]
            COSc, SINc, co, si = tab(g)
            c["t5"] = t5 = t5s.next(); c["t6"] = t6 = t6s.next()
            tt("dve", t5[:], c["gA"][:], co, ALU.mult, [c["gA"], COSc], [t5])
            tt("dve", t6[:], c["pG"][:], si, ALU.mult, [c["pG"], SINc], [t6])

        def s6(g):
            c = ctx[g]
            c["Hp"] = Hp = Hps.next()
            tt("pool", Hp[:, 1:512], c["t5"][:, 0:511], c["t6"][:, 0:511], ALU.subtract, [c["t5"], c["t6"]], [Hp])

        def s7(g):
            c = ctx[g]
            c["pY"] = pY = pYs.next()
            mm(pY[:], c["Wg"][:, 2, :], c["Uc"][:], True, False, [c["Wg"], c["Uc"]], [pY])
            mm(pY[:], c["Wg"][:, 3, :], c["Hp"][:], False, True, [c["Wg"], c["Hp"]], [pY])

        def s8(g):
            c = ctx[g]
            c["ysb"] = ysb = ysbs.next()
            stt(ysb[:], c["Uc"][:], Dc[:, g:g + 1], c["pY"][:], ALU.mult, ALU.add, [c["Uc"], Dc, c["pY"]], [ysb])

        def s9(g):
            c = ctx[g]
            c["y2"] = y2 = y2s.next()
            act(y2[:], c["ysb"][:], AF.Square, [c["ysb"]], [y2])
            act(y2[:], y2[:], AF.Identity, [y2], [y2], scale=0.044715, bias=1.0)

        def s10(g):
            c = ctx[g]
            c["w2"] = w2 = w2s.next()
            tt("dve", w2[:], c["y2"][:], c["ysb"][:], ALU.mult, [c["y2"], c["ysb"]], [w2])

        def s11(g):
            c = ctx[g]
            c["sg"] = sg = sgms.next()
            act(sg[:], c["w2"][:], AF.Sigmoid, [c["w2"]], [sg], scale=2.0 * math.sqrt(2.0 / math.pi))

        def s12(g):
            ft, j8 = divmod(g, 8)
            c = ctx.pop(g)
            zt = zts.next()
            tt("dve", zt[:], c["ysb"][:], c["sg"][:], ALU.mult, [c["ysb"], c["sg"]], [zt])
            kb.dma("sp", ZF.t[ft, 16 * j8:16 * j8 + 16, :].rearrange("m (t c) -> t m c", t=8), zt[:], rd=[zt], wr=[ZF])

        stages = [s0, s1, s2, s3, s4, s5, s6, s7, s8, s9, s10, s11, s12]
        for it in range(128 + len(stages) - 1):
            for si_ in range(len(stages) - 1, -1, -1):
                g = it - si_
                if 0 <= g < 128:
                    stages[si_](g)
        kb.flush()

    if stop_after == "l1s":
        kb.final_wait()
        return nc

    with ExitStack() as st:
        zTs = Rot([kb.sb(st, f"zT{i}", [128, 16, 512], BF16) for i in range(2)])
        gTs = Rot([kb.sb(st, f"gT{i}", [128, 16, 512], BF16) for i in range(2)])
        wgs = Rot([kb.sb(st, f"wg{i}", [128, 16, 512], BF16) for i in range(2)])
        oT = kb.sb(st, "oT", [128, 16, 512], BF16)
        xq = [kb.sb(st, f"xr{i}", [128, D], F32) for i in range(4)]
        sgs_ = Rot([kb.sb(st, f"sgg{i}", [128, 512], BF16) for i in range(2)])
        pss = Rot([kb.ps(st, f"ps{i}", [128, 512], F32) for i in range(6)])
        X1v = X1.t.rearrange("(c t) d -> t c d", t=8)
        OUTv = out_d.rearrange("(c t) d -> t c d", t=8)
        for bi in range(NB):
            zT = zTs.next(); gT = gTs.next()
            kb.dma("sp", zT[:], ZF.t[:, :, bi * 512:(bi + 1) * 512].rearrange("f p c -> p f c"), rd=[ZF], wr=[zT])
            kb.dma("sp", gT[:], GF.t[:, :, bi * 512:(bi + 1) * 512].rearrange("f p c -> p f c"), rd=[GF], wr=[gT])
            for tti in range(4):
                kb.dma("sp", xq[tti][:], X1v[bi, tti * 128:(tti + 1) * 128, :], rd=[X1], wr=[xq[tti]])
            for gg in range(4):
                wg = wgs.next()
                kb.dma("sp", wg[:], Wb_glu.t[gg], rd=[Wb_glu], wr=[wg])
                for j in range(4):
                    ft = gg * 4 + j
                    ps = pss.next()
                    for c in range(16):
                        mm(ps[:], wg[:, c, j * 128:(j + 1) * 128], zT[:, c, :], c == 0, c == 15, [wg, zT], [ps])
                    sgt = sgs_.next()
                    act(sgt[:], ps[:], AF.Sigmoid, [ps, bglu], [sgt], bias=bglu[:, ft:ft + 1])
                    tt("dve", sgt[:], sgt[:], zT[:, ft, :], ALU.mult, [sgt, zT], [sgt])
                    tt("pool", oT[:, ft, :], sgt[:], gT[:, ft, :], ALU.mult, [sgt, gT], [oT])
            for og in range(4):
                wg = wgs.next()
                kb.dma("sp", wg[:], Wb_out1.t[og], rd=[Wb_out1], wr=[wg])
                for tti in range(4):
                    ps = pss.next()
                    for c in range(16):
                        mm(ps[:], oT[:, c, tti * 128:(tti + 1) * 128], wg[:, c, :], c == 0, c == 15, [oT, wg], [ps])
                    xs = xq[tti][:, og * 512:(og + 1) * 512]
                    tt("dve", xs, ps[:], xs, ALU.add, [ps, xq[tti]], [xq[tti]])
            for tti in range(4):
                kb.dma("pool", OUTv[bi, tti * 128:(tti + 1) * 128, :], xq[tti][:], rd=[xq[tti]], key=f"outst{tti}")
        kb.flush()

    kb.final_wait()
    return nc


def make_in_maps(inputs):
    maps = []
    for c in range(NCORES):
        m = {"x": np.ascontiguousarray(inputs["x"][c % 4])}
        for n, shp in PARAMS:
            m[n] = np.ascontiguousarray(np.asarray(inputs[n]).reshape(shp))
        maps.append(m)
    return maps


def kernel(**inputs):
    nc = build()
    res = run_bass_kernel_spmd(nc, make_in_maps(inputs), core_ids=list(range(NCORES)))
    out = np.stack([res.results[c]["out"] for c in range(4)], axis=0)
    return out.astype(np.float32)
```

```python
import math
from contextlib import ExitStack

import numpy as np
import concourse.bass as bass
import concourse.mybir as mybir
from concourse.bass_utils import run_bass_kernel_spmd

F32 = mybir.dt.float32
BF16 = mybir.dt.bfloat16
I32 = mybir.dt.int32
AF = mybir.ActivationFunctionType
ALU = mybir.AluOpType

S = 4096
D = 2048
NB = 8
EPS = 1e-6
NCORES = 4


class Buf:
    def __init__(self, name, t, persist=False):
        self.name = name
        self.t = t
        self.w = {}
        self.r = {}
        self.ext = []
        self.persist = persist

    def __getitem__(self, idx):
        return self.t[idx]


class Op:
    __slots__ = ("eng", "fn", "rd", "wr", "is_dma", "key", "deps", "ext", "signal", "sem", "val")

    def __init__(self, eng, fn, rd, wr, is_dma=False, key=None):
        self.eng = eng
        self.fn = fn
        self.rd = rd
        self.wr = wr
        self.is_dma = is_dma
        self.key = key
        self.deps = ()
        self.ext = []
        self.signal = False
        self.sem = None
        self.val = 0


class Rot:
    def __init__(self, bufs):
        self.bufs = bufs
        self.i = 0

    def next(self):
        b = self.bufs[self.i % len(self.bufs)]
        self.i += 1
        return b


class KB:
    SB_LIMIT = 204 * 1024
    def __init__(self, nc):
        self.nc = nc
        self.es = ExitStack()
        self.eng = {"pe": nc.tensor, "act": nc.scalar, "dve": nc.vector, "pool": nc.gpsimd, "sp": nc.sync}
        self.esem = {k: self.es.enter_context(nc.semaphore("s_" + k)) for k in self.eng}
        self.ecnt = {k: 0 for k in self.eng}
        self.seen = {k: {} for k in self.eng}
        self.dsem = {}
        self.deferred = set()
        self.ops = []
        self.bufs = []
        self.nins = {k: 0 for k in self.eng}

    def sb(self, stack, name, shape, dtype, persist=False):
        self.uid = getattr(self, "uid", 0) + 1
        name = f"{name}_{self.uid}"
        nbytes = int(np.prod(shape[1:])) * (2 if dtype == BF16 else 4)
        nbytes = (nbytes + 31) // 32 * 32
        self.sbuf_used = getattr(self, "sbuf_used", 0) + nbytes
        self.sbuf_peak = max(getattr(self, "sbuf_peak", 0), self.sbuf_used)
        assert self.sbuf_used <= self.SB_LIMIT, f"SBUF budget exceeded: {self.sbuf_used} at {name}"

        def _free(nb=nbytes):
            self.sbuf_used -= nb
        stack.callback(_free)
        t = stack.enter_context(self.nc.sbuf_tensor(name, list(shape), dtype))
        b = Buf(name, t, persist)
        self.bufs.append(b)
        return b

    def ps(self, stack, name, shape, dtype):
        self.uid = getattr(self, "uid", 0) + 1
        name = f"{name}_{self.uid}"
        t = stack.enter_context(self.nc.psum_tensor(name, list(shape), dtype))
        b = Buf(name, t)
        self.bufs.append(b)
        return b

    def dram(self, name, shape, dtype, kind="Internal", persist=False):
        t = self.nc.dram_tensor(name, list(shape), dtype, kind=kind).ap()
        b = Buf(name, t, persist)
        self.bufs.append(b)
        return b

    def op(self, eng, fn, rd=(), wr=()):
        self.ops.append(Op(eng, fn, list(rd), list(wr)))

    def dma(self, q, out, in_, rd=(), wr=(), key=None, defer=False, **kw):
        h = self.eng[q]
        if key is None:
            key = (wr[0].name if wr else rd[0].name + "_st")
        if key not in self.dsem:
            self.dsem[key] = [self.es.enter_context(self.nc.semaphore("d_" + key)), 0]
        if defer:
            self.deferred.add(key)
        self.ops.append(Op(q, lambda: h.dma_start(out=out, in_=in_, **kw), list(rd), list(wr), True, key))

    def flush(self, barrier=True):
        ops = self.ops
        for i, op in enumerate(ops):
            deps = set()
            ext = []
            for b in op.rd:
                deps.update(b.w.values())
                ext.extend(b.ext)
            for b in op.wr:
                deps.update(b.w.values())
                deps.update(b.r.values())
                ext.extend(b.ext)
            if op.eng == "pe" and not op.is_dma:
                deps = {d for d in deps if ops[d].is_dma or ops[d].eng != "pe"}
            deps.discard(i)
            op.deps = sorted(deps)
            op.ext = ext
            k = ("dma", op.key) if op.is_dma else op.eng
            for b in op.wr:
                b.w[k] = i
            for b in op.rd:
                b.r[k] = i
            for d in deps:
                ops[d].signal = True
        last = {}
        for i, op in enumerate(ops):
            if not op.is_dma:
                last[op.eng] = i
        for i in last.values():
            ops[i].signal = True
        for op in ops:
            e = op.eng
            h = self.eng[e]
            need = [(ops[d].sem, ops[d].val) for d in op.deps] + list(op.ext)
            for sem, val in need:
                sk = id(sem)
                if self.seen[e].get(sk, 0) < val:
                    h.wait_ge(sem, val)
                    self.seen[e][sk] = val
            ins = op.fn()
            self.nins[e] += 1
            if op.is_dma:
                ent = self.dsem[op.key]
                ent[1] += 16
                ins.then_inc(ent[0], 16)
                op.sem, op.val = ent[0], ent[1]
            elif op.signal:
                self.ecnt[e] += 1
                ins.then_inc(self.esem[e], 1)
                op.sem, op.val = self.esem[e], self.ecnt[e]
        for b in self.bufs:
            if b.persist:
                for d in list(b.w.values()):
                    b.ext.append((ops[d].sem, ops[d].val))
            b.w = {}
            b.r = {}
        self.ops = []
        if barrier:
            for e, h in self.eng.items():
                for e2 in self.eng:
                    if e2 != e and self.ecnt[e2] > self.seen[e].get(id(self.esem[e2]), 0):
                        h.wait_ge(self.esem[e2], self.ecnt[e2])
                        self.seen[e][id(self.esem[e2])] = self.ecnt[e2]
                for key, (sem, cnt) in self.dsem.items():
                    if key in self.deferred:
                        continue
                    if cnt > self.seen[e].get(id(sem), 0):
                        h.wait_ge(sem, cnt)
                        self.seen[e][id(sem)] = cnt

    def final_wait(self):
        for e, h in self.eng.items():
            for key, (sem, cnt) in self.dsem.items():
                if cnt > self.seen[e].get(id(sem), 0):
                    h.wait_ge(sem, cnt)
                    self.seen[e][id(sem)] = cnt


def bc(ap, shape):
    return ap.to_broadcast(list(shape))


PARAMS = [
    ("e_norm_g", [D]), ("e_w_in", [D, 7168]), ("e_conv_w", [31, 1024]), ("e_conv_b", [1024]),
    ("e_cln_g", [1024]), ("e_cln_b", [1024]), ("e_qn_g", [64]), ("e_kn_g", [64]),
    ("e_lam_q1", [64]), ("e_lam_k1", [64]), ("e_lam_q2", [64]), ("e_lam_k2", [64]),
    ("e_subln_g", [128]), ("e_w_out", [D, D]),
    ("o_norm_g", [D]), ("o_w_in", [D, 2 * D]), ("o_A_re", [128, 64]), ("o_A_im", [128, 64]),
    ("o_log_dt", [128]), ("o_B_re", [128, 64, 16]), ("o_B_im", [128, 64, 16]),
    ("o_C_re", [128, 16, 64]), ("o_C_im", [128, 16, 64]), ("o_D", [D]),
    ("o_w_glu", [D, D]), ("o_b_glu", [D]), ("o_w_out", [D, D]),
]


def build(dbg=(), stop_after=None):
    nc = bass.Bass("TRN2", target_bir_lowering=False)
    kb = KB(nc)
    eng = kb.eng
    pe, act_, dve, pool = eng["pe"], eng["act"], eng["dve"], eng["pool"]

    def kind(name):
        return "ExternalOutput" if name in dbg else "Internal"

    din = {"x": nc.dram_tensor("x", [S, D], F32, kind="ExternalInput").ap()}
    for n, shp in PARAMS:
        din[n] = nc.dram_tensor(n, shp, F32, kind="ExternalInput").ap()
    out_d = nc.dram_tensor("out", [S, D], F32, kind="ExternalOutput").ap()

    Wb_in0 = kb.dram("Wb_in0", [14, 128, 16, 512], BF16, persist=True)
    Wb_out0 = kb.dram("Wb_out0", [4, 128, 16, 512], BF16, persist=True)
    Wb_in1 = kb.dram("Wb_in1", [8, 128, 16, 512], BF16, persist=True)
    Wb_glu = kb.dram("Wb_glu", [4, 128, 16, 512], BF16, persist=True)
    Wb_out1 = kb.dram("Wb_out1", [4, 128, 16, 512], BF16, persist=True)
    ROT = kb.dram("ROT", [2, 128, S], F32, kind=kind("ROT"))
    QT = kb.dram("QT", [8, 128, S], BF16, kind=kind("QT"))
    KT = kb.dram("KT", [8, 128, S], BF16, kind=kind("KT"))
    VV = kb.dram("VV", [8, 128, 32, 128], BF16, kind=kind("VV"))
    GB = kb.dram("GB", [8, 128, S], BF16, kind=kind("GB"))
    MIXT = kb.dram("MIXT", [16, 128, S], BF16, kind=kind("MIXT"))
    X1 = kb.dram("X1", [S, D], F32, kind=kind("X1"))
    TAB = kb.dram("TAB", [128, 128, 2, 512], F32, kind=kind("TAB"))

    def act(out, in_, func, rd, wr, bias=None, scale=None, accum=None):
        kw = {}
        if bias is not None:
            kw["bias"] = bias
        if scale is not None:
            kw["scale"] = scale
        if accum is not None:
            kw["accum_out"] = accum
        kb.op("act", lambda: act_.activation(out=out, in_=in_, func=func, **kw), rd, wr)

    def tt(e, out, in0, in1, op, rd, wr):
        h = eng[e]
        kb.op(e, lambda: h.tensor_tensor(out=out, in0=in0, in1=in1, op=op), rd, wr)

    def ts(e, out, in0, s1, s2, op0, op1, rd, wr):
        h = eng[e]
        if op1 is None:
            kb.op(e, lambda: h.tensor_scalar(out=out, in0=in0, scalar1=s1, scalar2=None, op0=op0), rd, wr)
        else:
            kb.op(e, lambda: h.tensor_scalar(out=out, in0=in0, scalar1=s1, scalar2=s2, op0=op0, op1=op1), rd, wr)

    def stt(out, in0, scalar, in1, op0, op1, rd, wr):
        kb.op("dve", lambda: dve.scalar_tensor_tensor(out=out, in0=in0, scalar=scalar, in1=in1, op0=op0, op1=op1), rd, wr)

    def mm(out, lhsT, rhs, start, stop, rd, wr):
        kb.op("pe", lambda: pe.matmul(out, lhsT=lhsT, rhs=rhs, start=start, stop=stop), rd, wr)

    def recip(out, in_, rd, wr):
        kb.op("dve", lambda: dve.reciprocal(out=out, in_=in_), rd, wr)

    def copy(e, out, in_, rd, wr):
        h = eng[e]
        kb.op(e, lambda: h.tensor_copy(out=out, in_=in_), rd, wr)

    def memset(e, ap, val, wr):
        h = eng[e]
        kb.op(e, lambda: h.memset(ap, val), (), wr)

    cs = kb.es

    def conv_w(dst, src, ng, key):
        v = src.rearrange("(c p) (g n) -> g p c n", p=128, n=512)
        for g in range(ng):
            kb.dma("pool", dst.t[g], v[g], wr=[dst], key=key, defer=True)

    conv_w(Wb_in0, din["e_w_in"], 14, "wc0")

    ident_f = kb.sb(cs, "ident_f", [128, 128], F32)
    ident_b = kb.sb(cs, "ident_b", [128, 128], BF16)
    ones_b = kb.sb(cs, "ones_b", [128, 128], BF16)
    ones_f = kb.sb(cs, "ones_f", [128, 128], F32)
    avg1024 = kb.sb(cs, "avg1024", [128, 128], F32)
    avg128 = kb.sb(cs, "avg128", [128, 128], F32)
    bd64 = kb.sb(cs, "bd64", [128, 128], BF16)
    Pm = kb.sb(cs, "Pm", [128, 128], BF16)
    M4 = kb.sb(cs, "M4", [128, 1, 128], BF16)
    eps_t = kb.sb(cs, "eps_t", [128, 1], F32)
    gn0 = kb.sb(cs, "gn0", [128, 16], F32)
    gn1 = kb.sb(cs, "gn1", [128, 16], F32)
    kw_t = kb.sb(cs, "kw_t", [128, 8, 31], F32)
    kw_b = kb.sb(cs, "kw_b", [128, 8, 31], BF16)
    convb = kb.sb(cs, "convb", [128, 8], F32)
    clng = kb.sb(cs, "clng", [128, 8], F32)
    clnb = kb.sb(cs, "clnb", [128, 8], F32)
    qng = kb.sb(cs, "qng", [128, 1], F32)
    kng = kb.sb(cs, "kng", [128, 1], F32)
    sgs = kb.sb(cs, "sgs", [128, 1], F32)
    neglam = kb.sb(cs, "neglam", [128, 1], F32)
    bglu = kb.sb(cs, "bglu", [128, 16], F32)
    sgn = kb.sb(cs, "sgn", [128, 1], F32)
    PiT = kb.sb(cs, "PiT", [128, 128], F32)

    with ExitStack() as st:
        iota_i = kb.sb(st, "iota_i", [128, 128], I32)
        iota_f = kb.sb(st, "iota_f", [128, 128], F32)
        pidx_i = kb.sb(st, "pidx_i", [128, 1], I32)
        ptmp_i = kb.sb(st, "ptmp_i", [128, 1], I32)
        m_hi = kb.sb(st, "m_hi", [128, 1], F32)
        m_lo = kb.sb(st, "m_lo", [128, 1], F32)
        Am = kb.sb(st, "Am", [128, 128], F32)
        Bm = kb.sb(st, "Bm", [128, 128], F32)
        Pf = kb.sb(st, "Pf", [128, 128], F32)
        ones512 = kb.sb(st, "ones512", [128, 512], BF16)
        freq = kb.sb(st, "freq", [128, 1], F32)
        halfpi = kb.sb(st, "halfpi", [128, 1], F32)
        cn = kb.sb(st, "cn", [128, 1], F32)
        sn = kb.sb(st, "sn", [128, 1], F32)
        nsn = kb.sb(st, "nsn", [128, 1], F32)
        tq = kb.sb(st, "tq", [128, 1], F32)
        COS = kb.sb(st, "COS", [128, S], F32)
        SIN = kb.sb(st, "SIN", [128, S], F32)
        T1 = kb.sb(st, "T1", [128, S // 2], F32)
        lamv = kb.sb(st, "lamv", [64, 4], F32)
        prod = kb.sb(st, "prod", [64, 2], F32)
        lps = kb.ps(st, "lps", [128, 512], F32)
        e2 = kb.sb(st, "e2", [128, 2], F32)
        sgl_ = kb.sb(st, "sgl_", [128, 1], F32)

        kb.op("pool", lambda: pool.iota(iota_i[:], pattern=[[1, 128]], base=0, channel_multiplier=-1), (), [iota_i])
        kb.op("pool", lambda: pool.iota(pidx_i[:], pattern=[[0, 1]], base=0, channel_multiplier=1), (), [pidx_i])
        copy("dve", iota_f[:], iota_i[:], [iota_i], [iota_f])
        kb.op("dve", lambda: dve.tensor_single_scalar(out=ident_f[:], in_=iota_f[:], scalar=0.0, op=ALU.is_equal), [iota_f], [ident_f])
        copy("dve", ident_b[:], ident_f[:], [ident_f], [ident_b])
        kb.op("dve", lambda: dve.tensor_single_scalar(out=PiT[:], in_=iota_f[:], scalar=-64.0, op=ALU.is_equal), [iota_f], [PiT])
        kb.op("dve", lambda: dve.tensor_single_scalar(out=Am[:], in_=iota_f[:], scalar=64.0, op=ALU.is_equal), [iota_f], [Am])
        tt("dve", PiT[:], PiT[:], Am[:], ALU.subtract, [PiT, Am], [PiT])
        kb.op("dve", lambda: dve.tensor_single_scalar(out=Am[:], in_=iota_f[:], scalar=32.0, op=ALU.is_equal), [iota_f], [Am])
        kb.op("dve", lambda: dve.tensor_single_scalar(out=Bm[:], in_=iota_f[:], scalar=-32.0, op=ALU.is_equal), [iota_f], [Bm])
        kb.op("dve", lambda: dve.tensor_single_scalar(out=ptmp_i[:], in_=pidx_i[:], scalar=32, op=ALU.bitwise_and), [pidx_i], [ptmp_i])
        copy("dve", m_hi[:], ptmp_i[:], [ptmp_i], [m_hi])
        ts("dve", m_hi[:], m_hi[:], 1.0 / 32.0, None, ALU.mult, None, [m_hi], [m_hi])
        ts("dve", m_lo[:], m_hi[:], -1.0, 1.0, ALU.mult, ALU.add, [m_hi], [m_lo])
        ts("dve", sgn[:], m_hi[:], 2.0, -1.0, ALU.mult, ALU.add, [m_hi], [sgn])
        ts("dve", Pf[:], Am[:], m_lo[:, 0:1], None, ALU.mult, None, [Am, m_lo], [Pf])
        stt(Pm[:], Bm[:], m_hi[:, 0:1], Pf[:], ALU.mult, ALU.add, [Bm, m_hi, Pf], [Pm])
        memset("pool", ones_b[:], 1.0, [ones_b])
        memset("pool", ones_f[:], 1.0, [ones_f])
        memset("pool", avg1024[:], 1.0 / 1024.0, [avg1024])
        memset("pool", avg128[:], 1.0 / 128.0, [avg128])
        memset("pool", bd64[:], 0.0, [bd64])
        memset("pool", bd64[0:64, 0:64], 1.0 / 64.0, [bd64])
        memset("pool", bd64[64:128, 64:128], 1.0 / 64.0, [bd64])
        memset("pool", eps_t[:], EPS, [eps_t])
        memset("pool", halfpi[:], math.pi / 2, [halfpi])
        memset("pool", ones512[:], 1.0, [ones512])
        kb.op("pool", lambda: pool.affine_select(out=M4[:, 0, :], in_=ones512[:, 0:128], pattern=[[1, 128]],
                                                  compare_op=ALU.is_ge, fill=0.0, base=0,
                                                  channel_multiplier=-1), [ones512], [M4])
        def ld(b, dst, src):
            kb.dma("sp", dst, src, wr=[b], key="small", allow_slow_non_contiguous=True)

        ld(gn0, gn0[:], din["e_norm_g"].rearrange("(c p) -> p c", p=128))
        ld(gn1, gn1[:], din["o_norm_g"].rearrange("(c p) -> p c", p=128))
        for t_ in range(8):
            ld(kw_t, kw_t[:, t_, :], din["e_conv_w"][:, t_ * 128:(t_ + 1) * 128].rearrange("w p -> p w"))
        ld(convb, convb[:], din["e_conv_b"].rearrange("(t p) -> p t", p=128))
        ld(clng, clng[:], din["e_cln_g"].rearrange("(t p) -> p t", p=128))
        ld(clnb, clnb[:], din["e_cln_b"].rearrange("(t p) -> p t", p=128))
        ld(bglu, bglu[:], din["o_b_glu"].rearrange("(t p) -> p t", p=128))
        for hh in range(2):
            ld(qng, qng[hh * 64:(hh + 1) * 64, :], din["e_qn_g"].rearrange("(p o) -> p o", o=1))
            ld(kng, kng[hh * 64:(hh + 1) * 64, :], din["e_kn_g"].rearrange("(p o) -> p o", o=1))
        ld(sgl_, sgl_[:], din["e_subln_g"].rearrange("(p o) -> p o", o=1))
        for i, nme in enumerate(["e_lam_q1", "e_lam_k1", "e_lam_q2", "e_lam_k2"]):
            ld(lamv, lamv[:, i:i + 1], din[nme].rearrange("(p o) -> p o", o=1))
        copy("dve", kw_b[:], kw_t[:], [kw_t], [kw_b])
        lam_init = 0.8 - 0.6 * math.exp(-0.3 * 0)
        ts("dve", sgs[:], sgl_[:], 1.0 - lam_init, None, ALU.mult, None, [sgl_], [sgs])
        tt("dve", prod[:, 0:1], lamv[:, 0:1], lamv[:, 1:2], ALU.mult, [lamv], [prod])
        tt("dve", prod[:, 1:2], lamv[:, 2:3], lamv[:, 3:4], ALU.mult, [lamv], [prod])
        mm(lps[:, 0:2], ones_f[0:64, :], prod[:, :], True, True, [ones_f, prod], [lps])
        act(e2[:], lps[:, 0:2], AF.Exp, [lps], [e2])
        tt("dve", neglam[:], e2[:, 1:2], e2[:, 0:1], ALU.subtract, [e2], [neglam])
        ts("dve", neglam[:], neglam[:], -lam_init, None, ALU.add, None, [neglam], [neglam])

        kb.op("dve", lambda: dve.tensor_single_scalar(out=ptmp_i[:], in_=pidx_i[:], scalar=31, op=ALU.bitwise_and), [pidx_i, m_hi], [ptmp_i])
        copy("dve", freq[:], ptmp_i[:], [ptmp_i], [freq])
        act(freq[:], freq[:], AF.Exp, [freq], [freq], scale=-math.log(10000.0) / 32.0)
        act(cn[:], freq[:], AF.Sin, [freq, halfpi], [cn], bias=halfpi[:, 0:1])
        act(sn[:], freq[:], AF.Sin, [freq], [sn])
        ts("dve", nsn[:], sn[:], -1.0, None, ALU.mult, None, [sn], [nsn])
        memset("dve", COS[:, 0:1], 1.0, [COS])
        memset("dve", SIN[:, 0:1], 0.0, [SIN])
        n = 1
        while n < S:
            ts("dve", T1[:, 0:n], COS[:, 0:n], cn[:, 0:1], None, ALU.mult, None, [COS, cn], [T1])
            stt(COS[:, n:2 * n], SIN[:, 0:n], nsn[:, 0:1], T1[:, 0:n], ALU.mult, ALU.add, [SIN, nsn, T1, COS], [COS])
            ts("dve", T1[:, 0:n], SIN[:, 0:n], cn[:, 0:1], None, ALU.mult, None, [SIN, cn, COS], [T1])
            stt(SIN[:, n:2 * n], COS[:, 0:n], sn[:, 0:1], T1[:, 0:n], ALU.mult, ALU.add, [COS, sn, T1, SIN], [SIN])
            if 2 * n < S:
                ts("dve", tq[:], cn[:], cn[:, 0:1], None, ALU.mult, None, [cn, SIN], [tq])
                stt(tq[:], sn[:], nsn[:, 0:1], tq[:], ALU.mult, ALU.add, [sn, nsn, tq], [tq])
                ts("dve", sn[:], cn[:], sn[:, 0:1], 2.0, ALU.mult, ALU.mult, [cn, sn], [sn])
                copy("dve", cn[:], tq[:], [tq, sn], [cn])
                ts("dve", nsn[:], sn[:], -1.0, None, ALU.mult, None, [sn], [nsn])
            n *= 2
        ts("dve", SIN[:], SIN[:], sgn[:, 0:1], None, ALU.mult, None, [SIN, sgn], [SIN])
        kb.dma("sp", ROT.t[0], COS[:], rd=[COS], wr=[ROT], key="rot_st")
        kb.dma("sp", ROT.t[1], SIN[:], rd=[SIN], wr=[ROT], key="rot_st")
        kb.flush()

    S5W = kb.dram("S5W", [128, 128, 4, 128], BF16, kind=kind("S5W"))
    CLv = kb.sb(cs, "CLv", [128, 9, 128], F32)
    SLv = kb.sb(cs, "SLv", [128, 9, 128], F32)
    RLt = kb.sb(cs, "RLt", [128, 128], F32)
    Dc = kb.sb(cs, "Dc", [128, 128], F32)
    keep_t = {n_: kb.sb(cs, n_, [128, 128], F32) for n_ in ("fR", "fI", "nuR", "nuI", "lR", "muI", "l7R", "l7I", "mask8")}
    with ExitStack() as st:
        def T(name):
            return keep_t[name] if name in keep_t else kb.sb(st, name, [128, 128], F32)
        AN = T("AN"); are = T("are"); aim = T("aim"); ldt = T("ldt"); dtt = T("dtt")
        th = T("th"); xr = T("xr"); kk = T("kk"); rr = T("rr"); r2 = T("r2"); acc = T("acc")
        sinT = T("sinT"); cosT = T("cosT"); EE = T("EE"); lR = T("lR"); lI = T("lI")
        den = T("den"); nr = T("nr"); fR = T("fR"); fI = T("fI"); nuR = T("nuR"); nuI = T("nuI")
        muI = T("muI"); l2R = T("l2R"); l2I = T("l2I"); l4R = T("l4R"); l4I = T("l4I")
        l8R = T("l8R"); l8I = T("l8I"); l7R = T("l7R"); l7I = T("l7I"); x1_ = T("x1_"); x2_ = T("x2_")
        x3_ = T("x3_"); x4_ = T("x4_"); mask8 = T("mask8"); onesq = T("onesq")
        tps = Rot([kb.ps(st, f"tps{i}", [128, 128], F32) for i in range(4)])

        def D_(fn, rd, wr):
            kb.op("dve", fn, rd, wr)

        def mul(o, a, b):
            tt("dve", o[:], a[:], b[:], ALU.mult, [a, b], [o])

        def add(o, a, b):
            tt("dve", o[:], a[:], b[:], ALU.add, [a, b], [o])

        def sub(o, a, b):
            tt("dve", o[:], a[:], b[:], ALU.subtract, [a, b], [o])

        def cmul(oR, oI, aR, aI, bR, bI):
            mul(x1_, aR, bR); mul(x2_, aI, bI); mul(x3_, aR, bI); mul(x4_, aI, bR)
            sub(oR, x1_, x2_); add(oI, x3_, x4_)

        def horner(o, xx, coef):
            n_ = len(coef) - 1
            ts("dve", o[:], xx[:], float(coef[n_]), None, ALU.mult, None, [xx], [o])
            for k_ in range(n_ - 1, 0, -1):
                stt(o[:], o[:], float(coef[k_]), xx[:], ALU.add, ALU.mult, [o, xx], [o])
            ts("dve", o[:], o[:], float(coef[0]), None, ALU.add, None, [o], [o])

        for src, dstt in ((din["o_A_re"], are), (din["o_A_im"], aim)):
            kb.dma("sp", AN[:, 0:64], src, wr=[AN], key="small")
            kb.dma("sp", AN[:, 64:128], src, wr=[AN], key="small")
            tp = tps.next()
            kb.op("pe", lambda tp=tp: pe.transpose(out=tp[:], in_=AN[:], identity=ident_f[:]), [AN, ident_f], [tp])
            copy("dve", dstt[:], tp[:], [tp], [dstt])
        kb.dma("sp", ldt[:], din["o_log_dt"].partition_broadcast(128), wr=[ldt], key="small", allow_slow_non_contiguous=True)
        for tau in range(8):
            kb.dma("sp", Dc[tau * 16:(tau + 1) * 16, :], din["o_D"].rearrange("(g m) -> m g", m=16), wr=[Dc], key="small",
                   allow_slow_non_contiguous=True)
        act(dtt[:], ldt[:], AF.Exp, [ldt], [dtt])
        mul(th, aim, dtt)
        mul(xr, are, dtt)
        MAGIC = 12582912.0
        ts("dve", kk[:], th[:], 1.0 / (2 * math.pi), MAGIC, ALU.mult, ALU.add, [th], [kk])
        ts("dve", kk[:], kk[:], -MAGIC, None, ALU.add, None, [kk], [kk])
        c1 = 6.28125
        c2 = float(np.float32(np.float32(2 * math.pi - c1).view(np.uint32) & np.uint32(0xFFFFF000)).view(np.float32)) if False else 0.0019350051879882812
        c3 = 2 * math.pi - c1 - c2
        stt(rr[:], kk[:], -c1, th[:], ALU.mult, ALU.add, [kk, th], [rr])
        stt(rr[:], kk[:], -c2, rr[:], ALU.mult, ALU.add, [kk, rr], [rr])
        stt(rr[:], kk[:], -c3, rr[:], ALU.mult, ALU.add, [kk, rr], [rr])
        mul(r2, rr, rr)
        horner(acc, r2, [(-1.0) ** k_ / math.factorial(2 * k_ + 1) for k_ in range(11)])
        mul(sinT, acc, rr)
        horner(cosT, r2, [(-1.0) ** k_ / math.factorial(2 * k_) for k_ in range(12)])
        horner(EE, xr, [1.0 / math.factorial(k_) for k_ in range(8)])
        mul(lR, EE, cosT)
        mul(lI, EE, sinT)
        mul(den, are, are); mul(x1_, aim, aim); add(den, den, x1_)
        recip(den[:], den[:], [den], [den])
        ts("dve", nr[:], lR[:], -1.0, None, ALU.add, None, [lR], [nr])
        mul(x1_, nr, are); mul(x2_, lI, aim); add(x1_, x1_, x2_); mul(fR, x1_, den)
        mul(x1_, lI, are); mul(x2_, nr, aim); sub(x1_, x1_, x2_); mul(fI, x1_, den)
        mul(x1_, EE, EE)
        recip(x1_[:], x1_[:], [x1_], [x1_])
        mul(nuR, lR, x1_)
        mul(nuI, lI, x1_)
        ts("dve", nuI[:], nuI[:], -1.0, None, ALU.mult, None, [nuI], [nuI])
        ts("dve", muI[:], lI[:], -1.0, None, ALU.mult, None, [lI], [muI])
        cmul(l2R, l2I, lR, lI, lR, lI)
        cmul(l4R, l4I, l2R, l2I, l2R, l2I)
        cmul(l8R, l8I, l4R, l4I, l4R, l4I)
        cmul(l7R, l7I, l4R, l4I, l2R, l2I)
        cmul(l7R, l7I, l7R, l7I, lR, lI)
        mul(RLt, EE, EE); mul(RLt, RLt, RLt); mul(RLt, RLt, RLt)
        recip(x1_[:], RLt[:], [RLt], [x1_])
        tt("dve", CLv[:, 0, :], l8R[:], x1_[:], ALU.mult, [l8R, x1_], [CLv])
        tt("dve", SLv[:, 0, :], l8I[:], x1_[:], ALU.mult, [l8I, x1_], [SLv])
        for j in range(8):
            tt("dve", x2_[:], CLv[:, j, :], CLv[:, j, :], ALU.mult, [CLv], [x2_])
            tt("dve", x3_[:], SLv[:, j, :], SLv[:, j, :], ALU.mult, [SLv], [x3_])
            tt("dve", CLv[:, j + 1, :], x2_[:], x3_[:], ALU.subtract, [x2_, x3_], [CLv])
            tt("dve", x2_[:], CLv[:, j, :], SLv[:, j, :], ALU.mult, [CLv, SLv], [x2_])
            ts("dve", SLv[:, j + 1, :], x2_[:], 2.0, None, ALU.mult, None, [x2_], [SLv])
        memset("pool", onesq[:], 1.0, [onesq])
        kb.op("pool", lambda: pool.affine_select(out=mask8[:], in_=onesq[:], pattern=[[16, 8], [0, 16]], compare_op=ALU.is_ge,
                                                  fill=0.0, base=15, channel_multiplier=-1), [onesq], [mask8])

        kb.flush()

    if stop_after == "setup":
        kb.final_wait()
        return nc

    x = din["x"]
    with ExitStack() as st:
        hT = kb.sb(st, "hT", [128, 16, 512], BF16)
        wgs = Rot([kb.sb(st, f"wg{i}", [128, 16, 512], BF16) for i in range(2)])
        rcs = Rot([kb.sb(st, f"rc{i}", [128, 2, 512], F32) for i in range(1)])
        xts = Rot([kb.sb(st, f"xt{i}", [128, D], F32) for i in range(1)])
        hn = kb.sb(st, "hn", [128, D], BF16)
        ss = kb.sb(st, "ss", [128, 1], F32)
        sd1 = kb.sb(st, "sd1", [128, 1], F32)
        rs1 = kb.sb(st, "rs1", [128, 1], F32)
        u = kb.sb(st, "u", [128, 8, 542], BF16)
        sgl = kb.sb(st, "sgl", [128, 4, 512], BF16)
        sga = kb.sb(st, "sga", [128, 8, 512], BF16)
        cc = kb.sb(st, "cc", [128, 8, 512], F32)
        csqs = Rot([kb.sb(st, f"csq{i}", [128, 512], F32) for i in range(2)])
        DG = kb.sb(st, "DG", [128, 31, 128], BF16)
        oas = Rot([kb.sb(st, f"oa{i}", [128, 512], BF16) for i in range(2)])
        mean_sb = kb.sb(st, "mean_sb", [128, 512], F32)
        var = kb.sb(st, "var", [128, 512], F32)
        rsl = kb.sb(st, "rsl", [128, 512], F32)
        tln = Rot([kb.sb(st, f"tln{i}", [128, 512], F32) for i in range(2)])
        aln = Rot([kb.sb(st, f"aln{i}", [128, 512], BF16) for i in range(2)])
        qgs = Rot([kb.sb(st, f"qg{i}", [128, 512], BF16) for i in range(2)])
        sqs = Rot([kb.sb(st, f"sq{i}", [128, 512], BF16) for i in range(2)])
        rsq = kb.sb(st, "rsq", [128, 512], F32)
        t1 = kb.sb(st, "t1", [128, 512], F32)
        t2 = kb.sb(st, "t2", [128, 512], F32)
        qos = Rot([kb.sb(st, f"qo{i}", [128, 512], BF16) for i in range(2)])
        vts = Rot([kb.sb(st, f"vt{i}", [128, 512], BF16) for i in range(2)])
        gbts = Rot([kb.sb(st, f"gbt{i}", [128, 512], BF16) for i in range(2)])
        pss = Rot([kb.ps(st, f"ps{i}", [128, 512], F32) for i in range(4)])
        ptrs = Rot([kb.ps(st, f"ptr{i}", [128, 4, 128], BF16) for i in range(2)])
        mps = kb.ps(st, "mps", [128, 512], F32)
        qps = kb.ps(st, "qps", [128, 512], F32)

        memset("pool", u[:, :, 0:30], 0.0, [u])

        def load_wg(src, gi):
            wg = wgs.next()
            kb.dma("sp", wg[:], src.t[gi], rd=[src], wr=[wg])
            return wg

        def fm_tile(wg, j, hTb):
            ps = pss.next()
            for c in range(16):
                mm(ps[:], wg[:, c, j * 128:(j + 1) * 128], hTb[:, c, :], c == 0, c == 15, [wg, hTb], [ps])
            return ps

        pend_pe = []

        def run_pending():
            todo = list(pend_pe)
            del pend_pe[:]
            for f_ in todo:
                f_()

        def fm_tile2(wg, j, hTb):
            ps = fm_tile(wg, j, hTb)
            run_pending()
            return ps

        hTs = Rot([hT, kb.sb(st, "hT2", [128, 16, 512], BF16)])
        hns = Rot([hn, kb.sb(st, "hn2", [128, D], BF16)])
        DGs = Rot([DG, kb.sb(st, "DG2", [128, 31, 128], BF16)])

        def norm_part(bi, tti):
            xt = xts.next()
            t0n = bi * 512
            kb.dma("sp", xt[:], x[t0n + tti * 128:t0n + (tti + 1) * 128, :], wr=[xt])
            hnb = hns.next()
            act(hnb[:], xt[:], AF.Square, [xt], [hnb, ss], accum=ss[:, 0:1])
            act(sd1[:], ss[:], AF.Sqrt, [ss, eps_t], [sd1], bias=eps_t[:, 0:1], scale=1.0 / D)
            recip(rs1[:], sd1[:], [sd1], [rs1])
            act(hnb[:], xt[:], AF.Copy, [xt, rs1], [hnb], scale=rs1[:, 0:1])
            return hnb

        def transpose_part(hnb, hTb, tti):
            for c4 in range(4):
                ptr = ptrs.next()
                for q_ in range(4):
                    c = c4 * 4 + q_
                    kb.op("pe", lambda c=c, q_=q_, ptr=ptr: pe.transpose(out=ptr[:, q_, :], in_=hnb[:, c * 128:(c + 1) * 128],
                                                                           identity=ident_b[:]), [hnb, ident_b], [ptr])
                tt("dve", hTb[:, c4 * 4:c4 * 4 + 4, tti * 128:(tti + 1) * 128], ptr[:],
                   bc(gn0[:, c4 * 4:c4 * 4 + 4].unsqueeze(2), [128, 4, 128]), ALU.mult, [ptr, gn0], [hTb])

        hT_cur = hTs.next()
        for tti in range(4):
            hnb = norm_part(0, tti)
            transpose_part(hnb, hT_cur, tti)

        for bi in range(NB):
            t0 = bi * 512
            hTc = hT_cur
            hT_next = hTs.next() if bi + 1 < NB else None
            rc = rcs.next()
            kb.dma("sp", rc[:], ROT.t[:, :, t0:t0 + 512].rearrange("a p t -> p a t"), rd=[ROT], wr=[rc])
            nxt_hn = {}

            def prefetch_step(k):
                if hT_next is None:
                    return
                if k >= 1:
                    transpose_part(nxt_hn[k - 1], hT_next, k - 1)
                if k < 4:
                    nxt_hn[k] = norm_part(bi + 1, k)

            def conv_stage(jj, DGb):
                def f_():
                    cps = pss.next()
                    for tap in range(31):
                        mm(cps[:], DGb[:, tap, :], u[:, jj, tap:tap + 512], tap == 0, tap == 30, [DGb, u], [cps])
                    act(cc[:, jj, :], cps[:], AF.Identity, [cps, convb], [cc], bias=convb[:, jj:jj + 1])
                    csq = csqs.next()
                    act(csq[:], cps[:], AF.Square, [cps, convb], [csq], bias=convb[:, jj:jj + 1])

                    def g_():
                        mm(mps[:], avg1024[:], cc[:, jj, :], jj == 0, jj == 7, [avg1024, cc], [mps])
                        mm(qps[:], avg1024[:], csq[:], jj == 0, jj == 7, [avg1024, csq], [qps])
                    pend_pe.append(g_)
                return f_

            for half in range(2):
                wg = load_wg(Wb_in0, 2 + half)
                for j in range(4):
                    ps = fm_tile2(wg, j, hTc)
                    act(sgl[:, j, :], ps[:], AF.Sigmoid, [ps], [sgl])
                wg = load_wg(Wb_in0, 0 + half)
                for j in range(4):
                    jj = half * 4 + j
                    ps = fm_tile2(wg, j, hTc)
                    tt("dve", u[:, jj, 30:542], ps[:], sgl[:, j, :], ALU.mult, [ps, sgl], [u])
                    DGb = DGs.next()
                    tt("pool", DGb[:], bc(ident_b[:].unsqueeze(1), [128, 31, 128]),
                       bc(kw_b[:, jj, :].unsqueeze(2), [128, 31, 128]), ALU.mult, [ident_b, kw_b], [DGb])
                    pend_pe.append(conv_stage(jj, DGb))
            for half in range(2):
                wg = load_wg(Wb_in0, 4 + half)
                for j in range(4):
                    ps = fm_tile2(wg, j, hTc)
                    act(sga[:, half * 4 + j, :], ps[:], AF.Silu, [ps], [sga])
            run_pending()
            run_pending()
            copy("pool", u[:, :, 0:30], u[:, :, 512:542], [u], [u])
            act(mean_sb[:], mps[:], AF.Copy, [mps], [mean_sb])
            tt("dve", var[:], mean_sb[:], mean_sb[:], ALU.mult, [mean_sb], [var])
            tt("dve", var[:], qps[:], var[:], ALU.subtract, [qps, var], [var])
            act(var[:], var[:], AF.Sqrt, [var, eps_t], [var], bias=eps_t[:, 0:1])
            recip(rsl[:], var[:], [var], [rsl])
            for jj in range(8):
                tl = tln.next()
                al = aln.next()
                tt("dve", tl[:], cc[:, jj, :], mean_sb[:], ALU.subtract, [cc, mean_sb], [tl])
                tt("dve", tl[:], tl[:], rsl[:], ALU.mult, [tl, rsl], [tl])
                act(al[:], tl[:], AF.Silu, [tl, clng, clnb], [al], scale=clng[:, jj:jj + 1], bias=clnb[:, jj:jj + 1])
                oa = oas.next()
                tt("pool", oa[:], al[:], sga[:, jj, :], ALU.mult, [al, sga], [oa])
                kb.dma("pool", MIXT.t[jj, :, t0:t0 + 512], oa[:], rd=[oa], wr=[MIXT])

            def qk_stage(qg, sq, isq, dst, hh):
                def f_():
                    qo = qos.next()
                    stp = pss.next()
                    mm(stp[:], bd64[:], sq[:], True, True, [bd64, sq], [stp])
                    rtp = pss.next()
                    mm(rtp[:], Pm[:], qg[:], True, True, [Pm, qg], [rtp])
                    act(rsq[:], stp[:], AF.Sqrt, [stp, eps_t], [rsq], bias=eps_t[:, 0:1])
                    recip(rsq[:], rsq[:], [rsq], [rsq])
                    tt("pool", t1[:], qg[:], rc[:, 0, :], ALU.mult, [qg, rc], [t1])
                    tt("dve", t2[:], rtp[:], rc[:, 1, :], ALU.mult, [rtp, rc], [t2])
                    tt("pool", t1[:], t1[:], t2[:], ALU.add, [t1, t2], [t1])
                    stt(qo[:], t1[:], 0.125 if isq else 1.0, rsq[:], ALU.mult, ALU.mult, [t1, rsq], [qo])
                    kb.dma("pool", dst.t[hh, :, t0:t0 + 512], qo[:], rd=[qo], wr=[dst])
                return f_

            for gidx, gi in enumerate((6, 7, 8, 9)):
                wg = load_wg(Wb_in0, gi)
                isq = gi < 8
                gvec = qng if isq else kng
                dst = QT if isq else KT
                for j in range(4):
                    hh = (gi % 2) * 4 + j
                    ps = fm_tile2(wg, j, hTc)
                    qg = qgs.next()
                    sq = sqs.next()
                    act(qg[:], ps[:], AF.Copy, [ps, gvec], [qg], scale=gvec[:, 0:1])
                    act(sq[:], ps[:], AF.Square, [ps], [sq])
                    pend_pe.append(qk_stage(qg, sq, isq, dst, hh))
                prefetch_step(gidx)
            for gi in (10, 11):
                wg = load_wg(Wb_in0, gi)
                for tti in range(4):
                    ps = pss.next()
                    for c in range(16):
                        mm(ps[:], hTc[:, c, tti * 128:(tti + 1) * 128], wg[:, c, :], c == 0, c == 15, [wg, hTc], [ps])
                    run_pending()
                    vt = vts.next()
                    act(vt[:], ps[:], AF.Copy, [ps], [vt])
                    h0 = (gi - 10) * 4
                    kb.dma("pool", VV.t[h0:h0 + 4, :, bi * 4 + tti, :].rearrange("h p d -> p h d"),
                           vt[:].rearrange("p (h d) -> p h d", h=4), rd=[vt], wr=[VV])
                if gi == 10:
                    prefetch_step(4)
            for gi in (12, 13):
                wg = load_wg(Wb_in0, gi)
                for j in range(4):
                    hh = (gi - 12) * 4 + j
                    ps = fm_tile2(wg, j, hTc)
                    gbt = gbts.next()
                    act(gbt[:], ps[:], AF.Silu, [ps], [gbt])
                    kb.dma("pool", GB.t[hh, :, t0:t0 + 512], gbt[:], rd=[gbt], wr=[GB])
            run_pending()
            hT_cur = hT_next
        kb.flush()

    if stop_after == "l0p":
        kb.final_wait()
        return nc

    with ExitStack() as st:
        kTas = Rot([kb.sb(st, f"kTa{i}", [128, S], BF16) for i in range(2)])
        kTbs = Rot([kb.sb(st, f"kTb{i}", [128, S], BF16) for i in range(2)])
        qTs = Rot([kb.sb(st, f"qT{i}", [128, S], BF16) for i in range(2)])
        vhs = Rot([kb.sb(st, f"vh{i}", [128, 32, 128], BF16) for i in range(2)])
        gbs = Rot([kb.sb(st, f"gbh{i}", [128, S], BF16) for i in range(2)])
        e1s = Rot([kb.sb(st, f"e1_{i}", [128, 512], BF16) for i in range(3)])
        e2s = Rot([kb.sb(st, f"e2_{i}", [128, 512], BF16) for i in range(3)])
        pscore = Rot([kb.ps(st, f"psc{i}", [128, 512], F32) for i in range(4)])
        o1 = kb.ps(st, "o1", [128, 512], F32)
        d1 = kb.ps(st, "d1", [128, 512], F32)
        o2 = kb.ps(st, "o2", [128, 512], F32)
        d2 = kb.ps(st, "d2", [128, 512], F32)

        def FA(name, n=2, dt=F32):
            return Rot([kb.sb(st, f"{name}{i}", [128, 512], dt) for i in range(n)])
        rd1s = FA("rd1"); rd2s = FA("rd2"); c1s = FA("c1"); c2s = FA("c2"); ods = FA("od"); osqs = FA("osq"); rsfs = FA("rsf")
        obs = FA("ob", 2, BF16)
        for b_ in kTas.bufs:
            memset("pool", b_[64:128, :], 0.0, [b_])
        for b_ in kTbs.bufs:
            memset("pool", b_[0:64, :], 0.0, [b_])

        def finalize1():
            rd1 = rd1s.next(); rd2 = rd2s.next(); c1 = c1s.next(); c2 = c2s.next(); od = ods.next(); osq = osqs.next()
            copy("dve", c1[:], o1[:], [o1], [c1])
            act(rd1[:], d1[:], AF.Ln, [d1], [rd1])
            copy("dve", c2[:], o2[:], [o2], [c2])
            act(rd2[:], d2[:], AF.Ln, [d2], [rd2])
            act(rd1[:], rd1[:], AF.Exp, [rd1], [rd1], scale=-1.0)
            act(rd2[:], rd2[:], AF.Exp, [rd2], [rd2], scale=-1.0)
            tt("dve", c1[:], c1[:], rd1[:], ALU.mult, [c1, rd1], [c1])
            tt("dve", c2[:], c2[:], rd2[:], ALU.mult, [c2, rd2], [c2])
            stt(od[:], c2[:], neglam[:, 0:1], c1[:], ALU.mult, ALU.add, [c2, neglam, c1], [od])
            act(osq[:], od[:], AF.Square, [od], [osq])
            return od, osq

        def finalize2(h, qs, gb, od, osq):
            rsf = rsfs.next()
            stp = pscore.next()
            mm(stp[:], avg128[:], osq[:], True, True, [avg128, osq], [stp])
            act(rsf[:], stp[:], AF.Ln, [stp, eps_t], [rsf], bias=eps_t[:, 0:1])
            act(rsf[:], rsf[:], AF.Exp, [rsf], [rsf], scale=-0.5)
            tt("dve", od[:], od[:], rsf[:], ALU.mult, [od, rsf], [od])
            ob = obs.next()
            stt(ob[:], od[:], sgs[:, 0:1], gb[:, qs], ALU.mult, ALU.mult, [od, sgs, gb], [ob])
            kb.dma("pool", MIXT.t[8 + h, :, qs], ob[:], rd=[ob], wr=[MIXT])

        wconv_jobs = ([(Wb_out0, din["e_w_out"], g_, "wc1") for g_ in range(4)] + [(Wb_in1, din["o_w_in"], g_, "wc2") for g_ in range(8)]
                      + [(Wb_glu, din["o_w_glu"], g_, "wc3") for g_ in range(4)] + [(Wb_out1, din["o_w_out"], g_, "wc4") for g_ in range(4)])

        def conv_some(n_):
            for _ in range(n_):
                if wconv_jobs:
                    dst_, src_, g_, key_ = wconv_jobs.pop(0)
                    v_ = src_.rearrange("(c p) (g n) -> g p c n", p=128, n=512)
                    kb.dma("pool", dst_.t[g_], v_[g_], wr=[dst_], key=key_, defer=True)
        TGA = 2
        TBs = Rot([kb.sb(st, f"TB{i}", [128, TGA, 2, 512], F32) for i in range(2)])
        Yt1 = kb.sb(st, "Yt1", [128, TGA, 256], F32)
        Yt2 = kb.sb(st, "Yt2", [128, TGA, 256], F32)
        C16 = kb.sb(st, "C16", [128, 128, 16], F32)
        S16 = kb.sb(st, "S16", [128, 128, 16], F32)
        Zt1 = kb.sb(st, "Zt1", [128, 128, 8], F32)
        Zt2 = kb.sb(st, "Zt2", [128, 128, 8], F32)
        memset("pool", C16[:, :, 0:1], 1.0, [C16])
        memset("pool", S16[:, :, 0:1], 0.0, [S16])
        for j in range(4):
            n = 1 << j
            cn_ = bc(CLv[:, j, :].unsqueeze(2), [128, 128, n])
            sn_ = bc(SLv[:, j, :].unsqueeze(2), [128, 128, n])
            tt("pool", Zt1[:, :, 0:n], C16[:, :, 0:n], cn_, ALU.mult, [C16, CLv], [Zt1])
            tt("pool", Zt2[:, :, 0:n], S16[:, :, 0:n], sn_, ALU.mult, [S16, SLv], [Zt2])
            tt("pool", C16[:, :, n:2 * n], Zt1[:, :, 0:n], Zt2[:, :, 0:n], ALU.subtract, [Zt1, Zt2], [C16])
            tt("pool", Zt1[:, :, 0:n], S16[:, :, 0:n], cn_, ALU.mult, [S16, CLv], [Zt1])
            tt("pool", Zt2[:, :, 0:n], C16[:, :, 0:n], sn_, ALU.mult, [C16, SLv], [Zt2])
            tt("pool", S16[:, :, n:2 * n], Zt1[:, :, 0:n], Zt2[:, :, 0:n], ALU.add, [Zt1, Zt2], [S16])

        def gen_table_set(ts_):
            TB_ = TBs.next()
            gs_ = slice(TGA * ts_, TGA * ts_ + TGA)
            Cc = TB_[:, :, 0, :]
            Sc = TB_[:, :, 1, :]
            copy("pool", Cc[:, :, 0:16], C16[:, gs_, :], [C16], [TB_])
            copy("pool", Sc[:, :, 0:16], S16[:, gs_, :], [S16], [TB_])
            for j in range(4, 9):
                n = 1 << j
                cn_ = bc(CLv[:, j, gs_].unsqueeze(2), [128, TGA, n])
                sn_ = bc(SLv[:, j, gs_].unsqueeze(2), [128, TGA, n])
                tt("pool", Yt1[:, :, 0:n], Cc[:, :, 0:n], cn_, ALU.mult, [TB_, CLv], [Yt1])
                tt("pool", Yt2[:, :, 0:n], Sc[:, :, 0:n], sn_, ALU.mult, [TB_, SLv], [Yt2])
                tt("pool", Cc[:, :, n:2 * n], Yt1[:, :, 0:n], Yt2[:, :, 0:n], ALU.subtract, [Yt1, Yt2], [TB_])
                tt("pool", Yt1[:, :, 0:n], Sc[:, :, 0:n], cn_, ALU.mult, [TB_, CLv], [Yt1])
                tt("pool", Yt2[:, :, 0:n], Cc[:, :, 0:n], sn_, ALU.mult, [TB_, SLv], [Yt2])
                tt("pool", Sc[:, :, n:2 * n], Yt1[:, :, 0:n], Yt2[:, :, 0:n], ALU.add, [Yt1, Yt2], [TB_])
            kb.dma("pool", TAB.t[gs_].rearrange("g p a c -> p g a c"), TB_[:], rd=[TB_], wr=[TAB])

        next_set = [0]
        pending_fin = None
        for h in range(8):
            kTa = kTas.next(); kTb = kTbs.next(); qT = qTs.next(); vh = vhs.next(); gb = gbs.next()
            kb.dma("sp", kTa[0:64, :], KT.t[h, 0:64, :], rd=[KT], wr=[kTa])
            kb.dma("sp", kTb[64:128, :], KT.t[h, 64:128, :], rd=[KT], wr=[kTb])
            kb.dma("sp", qT[:], QT.t[h], rd=[QT], wr=[qT])
            kb.dma("sp", vh[:], VV.t[h], rd=[VV], wr=[vh])
            kb.dma("sp", gb[:], GB.t[h], rd=[GB], wr=[gb])
            for qb in range(8):
                qs = slice(qb * 512, (qb + 1) * 512)
                nkt = 4 * (qb + 1)
                if qb in (2, 4, 6):
                    conv_some(1)

                def scores(kt, qb=qb, qs=qs):
                    ks = slice(kt * 128, (kt + 1) * 128)
                    o = max(0, kt - 4 * qb)
                    c0 = 128 * o
                    qcs = slice(qb * 512 + c0, (qb + 1) * 512)
                    s1 = pscore.next(); s2 = pscore.next()
                    mm(s1[:, c0:512], kTa[:, ks], qT[:, qcs], True, True, [kTa, qT], [s1])
                    mm(s2[:, c0:512], kTb[:, ks], qT[:, qcs], True, True, [kTb, qT], [s2])
                    e1 = e1s.next(); e2 = e2s.next()
                    act(e1[:, c0:512], s1[:, c0:512], AF.Exp, [s1], [e1])
                    act(e2[:, c0:512], s2[:, c0:512], AF.Exp, [s2], [e2])
                    if kt >= 4 * qb:
                        tt("dve", e1[:, c0:c0 + 128], e1[:, c0:c0 + 128], M4[:, 0, 0:128], ALU.mult, [e1, M4], [e1])
                        tt("dve", e2[:, c0:c0 + 128], e2[:, c0:c0 + 128], M4[:, 0, 0:128], ALU.mult, [e2, M4], [e2])
                    return e1, e2, c0

                pend = scores(0)
                for kt in range(nkt):
                    nxt = scores(kt + 1) if kt + 1 < nkt else None
                    e1, e2, c0 = pend
                    first, lastk = kt == 0, kt == nkt - 1
                    mm(o1[:, c0:512], vh[:, kt, :], e1[:, c0:512], first, lastk, [vh, e1], [o1])
                    mm(d1[:, c0:512], ones_b[:], e1[:, c0:512], first, lastk, [ones_b, e1], [d1])
                    mm(o2[:, c0:512], vh[:, kt, :], e2[:, c0:512], first, lastk, [vh, e2], [o2])
                    mm(d2[:, c0:512], ones_b[:], e2[:, c0:512], first, lastk, [ones_b, e2], [d2])
                    pend = nxt
                    if kt == min(2, nkt - 1) and pending_fin is not None:
                        finalize2(*pending_fin)
                        pending_fin = None
                if pending_fin is not None:
                    finalize2(*pending_fin)
                od, osq = finalize1()
                pending_fin = (h, qs, gb, od, osq)
                if next_set[0] < 128 // TGA:
                    gen_table_set(next_set[0])
                    next_set[0] += 1
        finalize2(*pending_fin)
        conv_some(len(wconv_jobs))
        kb.flush()

    if stop_after == "l0a":
        kb.final_wait()
        return nc

    UF = kb.dram("UF", [16, 128, S], BF16, kind=kind("UF"))
    GF = kb.dram("GF", [16, 128, S], BF16, kind=kind("GF"))
    with ExitStack() as st:
        mixTs = Rot([kb.sb(st, f"mixT{i}", [128, 16, 512], BF16) for i in range(2)])
        wgs = Rot([kb.sb(st, f"wg{i}", [128, 16, 512], BF16) for i in range(2)])
        xts8 = [kb.sb(st, f"xq{i}", [128, D], F32) for i in range(8)]
        hn = kb.sb(st, "hn", [128, D], BF16)
        junk = kb.sb(st, "junk", [128, D], BF16)
        ss = kb.sb(st, "ss", [128, 1], F32)
        sd1 = kb.sb(st, "sd1", [128, 1], F32)
        rs1 = kb.sb(st, "rs1", [128, 1], F32)
        h1T = kb.sb(st, "h1T", [128, 16, 512], BF16)
        ubs = Rot([kb.sb(st, f"ub{i}", [128, 512], BF16) for i in range(3)])
        pss = Rot([kb.ps(st, f"ps{i}", [128, 512], F32) for i in range(4)])
        ptrs = Rot([kb.ps(st, f"ptr{i}", [128, 4, 128], BF16) for i in range(2)])

        def load_wg(src, gi):
            wg = wgs.next()
            kb.dma("sp", wg[:], src.t[gi], rd=[src], wr=[wg])
            return wg

        def norm_transpose_perm(xt, gn, hTb, tti):
            act(junk[:], xt[:], AF.Square, [xt], [junk, ss], accum=ss[:, 0:1])
            act(sd1[:], ss[:], AF.Sqrt, [ss, eps_t], [sd1], bias=eps_t[:, 0:1], scale=1.0 / D)
            recip(rs1[:], sd1[:], [sd1], [rs1])
            act(hn[:], xt[:], AF.Copy, [xt, rs1], [hn], scale=rs1[:, 0:1])
            for c4 in range(4):
                ptr = ptrs.next()
                for q_ in range(4):
                    c = c4 * 4 + q_
                    kb.op("pe", lambda c=c, q_=q_, ptr=ptr: pe.transpose(out=ptr[:, q_, :], in_=hn[:, c * 128:(c + 1) * 128],
                                                                           identity=ident_b[:]), [hn, ident_b], [ptr])
                oap = hTb[:, c4 * 4:c4 * 4 + 4, :].rearrange("p k (t c) -> p k t c", t=8)[:, :, :, 16 * tti:16 * tti + 16]
                iap = ptr[:].rearrange("p k (c t) -> p k t c", t=8)
                tt("dve", oap, iap, bc(gn[:, c4 * 4:c4 * 4 + 4].unsqueeze(2).unsqueeze(3), [128, 4, 8, 16]), ALU.mult,
                   [ptr, gn], [hTb])

        def l0o_loads(bi_):
            t0_ = bi_ * 512
            mixT_ = mixTs.next()
            kb.dma("sp", mixT_[:], MIXT.t[:, :, t0_:t0_ + 512].rearrange("c p t -> p c t"), rd=[MIXT], wr=[mixT_])
            xs_ = xts8[(bi_ % 2) * 4:(bi_ % 2) * 4 + 4]
            for tti in range(4):
                kb.dma("sp", xs_[tti][:], x[t0_ + tti * 128:t0_ + (tti + 1) * 128, :], wr=[xs_[tti]])
            return mixT_, xs_

        nxt_loads = l0o_loads(0)
        for bi in range(NB):
            t0 = bi * 512
            mixT, xts4 = nxt_loads
            for og in range(4):
                wg = load_wg(Wb_out0, og)
                for tti in range(4):
                    ps = pss.next()
                    for c in range(16):
                        mm(ps[:], mixT[:, c, tti * 128:(tti + 1) * 128], wg[:, c, :], c == 0, c == 15, [mixT, wg], [ps])
                    xs = xts4[tti][:, og * 512:(og + 1) * 512]
                    tt("dve", xs, ps[:], xs, ALU.add, [ps, xts4[tti]], [xts4[tti]])
            if bi + 1 < NB:
                nxt_loads = l0o_loads(bi + 1)
            for tti in range(4):
                kb.dma("pool", X1.t[t0 + tti * 128:t0 + (tti + 1) * 128, :], xts4[tti][:], rd=[xts4[tti]], wr=[X1])
                norm_transpose_perm(xts4[tti], gn1, h1T, tti)
            for gi in range(8):
                wg = load_wg(Wb_in1, gi)
                for j in range(4):
                    ps = pss.next()
                    for c in range(16):
                        mm(ps[:], wg[:, c, j * 128:(j + 1) * 128], h1T[:, c, :], c == 0, c == 15, [wg, h1T], [ps])
                    ub = ubs.next()
                    ft = (gi % 4) * 4 + j
                    dst = UF if gi < 4 else GF
                    act(ub[:], ps[:], AF.Copy if gi < 4 else AF.Silu, [ps], [ub])
                    kb.dma("pool", dst.t[ft].rearrange("p (t c) -> p t c", t=8)[:, :, bi * 64:(bi + 1) * 64],
                           ub[:].rearrange("p (t c) -> p t c", t=8), rd=[ub], wr=[dst])
        kb.flush()

    if stop_after == "l1p":
        kb.final_wait()
        return nc

    with ExitStack() as st:
        tps = Rot([kb.ps(st, f"tpsm{i}", [128, 128], F32) for i in range(4)])
        fR = keep_t["fR"]; fI = keep_t["fI"]; nuR = keep_t["nuR"]; nuI = keep_t["nuI"]; lR = keep_t["lR"]; muI = keep_t["muI"]
        l7R = keep_t["l7R"]; l7I = keep_t["l7I"]; mask8 = keep_t["mask8"]
        GB_ = 16

        def R2(name, shape):
            return Rot([kb.sb(st, f"{name}{i}", shape, F32) for i in range(2)])
        BAs = R2("BA", [128, GB_, 16]); BAps = R2("BAp", [128, GB_, 16])
        CN = kb.sb(st, "CN", [128, 128], F32)
        CNp = kb.sb(st, "CNp", [128, 128], F32)
        VAs = R2("VA", [128, GB_, 8, 16]); VAps = R2("VAp", [128, GB_, 8, 16])
        WAs = R2("WA", [128, GB_, 9, 16]); WAps = R2("WAp", [128, GB_, 9, 16])
        WBAs = R2("WBA", [128, GB_, 8, 16]); WBAps = R2("WBAp", [128, GB_, 8, 16])
        y1 = kb.sb(st, "y1", [128, GB_, 16], F32)
        y2 = kb.sb(st, "y2", [128, GB_, 16], F32)
        y3 = kb.sb(st, "y3", [128, GB_, 16], F32)
        y4 = kb.sb(st, "y4", [128, GB_, 16], F32)
        S5ts = Rot([kb.sb(st, f"S5t{i}", [128, GB_, 4, 128], BF16) for i in range(2)])

        ytmp = {"dve": (y1, y2), "pool": (y3, y4)}

        def cmat(e_, oA, oAp, iA, iAp, zR, zI, g0, rdo, wro):
            zr = bc(zR[:, g0:g0 + GB_].unsqueeze(2), [128, GB_, 16])
            zi = bc(zI[:, g0:g0 + GB_].unsqueeze(2), [128, GB_, 16])
            ya, yb = ytmp[e_]
            tt(e_, ya[:], iA, zr, ALU.mult, rdo + [zR], [ya])
            tt(e_, yb[:], iAp, zi, ALU.mult, rdo + [zI], [yb])
            tt(e_, oA, ya[:], yb[:], ALU.add, [ya, yb], wro)
            tt(e_, ya[:], iAp, zr, ALU.mult, rdo + [zR], [ya])
            tt(e_, yb[:], iA, zi, ALU.mult, rdo + [zI], [yb])
            tt(e_, oAp, ya[:], yb[:], ALU.subtract, [ya, yb], wro)

        def stage_A(gb_):
            g0 = gb_ * GB_
            gsl = slice(g0, g0 + GB_)
            BA = BAs.next(); BAp = BAps.next(); VA = VAs.next(); VAp = VAps.next(); WA = WAs.next(); WAp = WAps.next()
            WBA = WBAs.next(); WBAp = WBAps.next()
            kb.dma("sp", BA[0:64], din["o_B_re"][gsl].rearrange("g p m -> p g m"), wr=[BA], key="s5ld")
            kb.dma("sp", BA[64:128], din["o_B_im"][gsl].rearrange("g p m -> p g m"), wr=[BA], key="s5ld")
            kb.dma("sp", BAp[0:64], din["o_B_im"][gsl].rearrange("g p m -> p g m"), wr=[BAp], key="s5ld")
            kb.dma("sp", BAp[64:128], din["o_B_re"][gsl].rearrange("g p m -> p g m"), wr=[BAp], key="s5ld")
            ts("dve", BAp[0:64], BAp[0:64], -1.0, None, ALU.mult, None, [BAp], [BAp])
            for hb in range(GB_ // 8):
                gs8 = slice(g0 + hb * 8, g0 + hb * 8 + 8)
                kb.dma("sp", CN[:, 0:64], din["o_C_re"][gs8].rearrange("g m p -> (g m) p"), wr=[CN], key="s5ld")
                kb.dma("sp", CN[:, 64:128], din["o_C_im"][gs8].rearrange("g m p -> (g m) p"), wr=[CN], key="s5ld")
                kb.dma("sp", CNp[:, 0:64], din["o_C_im"][gs8].rearrange("g m p -> (g m) p"), wr=[CNp], key="s5ld")
                kb.dma("sp", CNp[:, 64:128], din["o_C_re"][gs8].rearrange("g m p -> (g m) p"), wr=[CNp], key="s5ld")
                tp = tps.next()
                kb.op("pe", lambda tp=tp: pe.transpose(out=tp[:], in_=CN[:], identity=ident_f[:]), [CN, ident_f], [tp])
                copy("dve", WA[:, hb * 8:hb * 8 + 8, 0, :], tp[:].rearrange("p (g m) -> p g m", m=16), [tp], [WA])
                tp = tps.next()
                kb.op("pe", lambda tp=tp: pe.transpose(out=tp[:], in_=CNp[:], identity=ident_f[:]), [CNp, ident_f], [tp])
                copy("dve", WAp[:, hb * 8:hb * 8 + 8, 0, :], tp[:].rearrange("p (g m) -> p g m", m=16), [tp], [WAp])
            ts("dve", WA[64:128, :, 0, :], WA[64:128, :, 0, :], -1.0, None, ALU.mult, None, [WA], [WA])
            cmat("dve", VA[:, :, 0, :], VAp[:, :, 0, :], BA[:], BAp[:], fR, fI, g0, [BA, BAp], [VA, VAp])
            for k_ in range(8):
                cmat("pool", WA[:, :, k_ + 1, :], WAp[:, :, k_ + 1, :], WA[:, :, k_, :], WAp[:, :, k_, :], lR, muI, g0, [WA, WAp], [WA, WAp])
            for sg in range(7):
                cmat("dve", VA[:, :, sg + 1, :], VAp[:, :, sg + 1, :], VA[:, :, sg, :], VAp[:, :, sg, :], nuR, nuI, g0, [VA, VAp], [VA, VAp])
            for sg in range(8):
                cmat("dve", WBA[:, :, sg, :], WBAp[:, :, sg, :], VA[:, :, sg, :], VAp[:, :, sg, :], l7R, l7I, g0, [VA, VAp], [WBA, WBAp])
            return gsl, VA, WA, WBA, WBAp

        def stage_B(stA):
            gsl, VA, WA, WBA, WBAp = stA
            S5t = S5ts.next()
            for g in range(GB_):
                tp = tps.next()
                kb.op("pe", lambda tp=tp, g=g: pe.matmul(tp[:], lhsT=VA[:, g, :, :].rearrange("p s m -> p (s m)"),
                                                          rhs=WA[:, g, 0:8, :].rearrange("p s m -> p (s m)"), start=True, stop=True),
                      [VA, WA], [tp])
                tt("dve", S5t[:, g, 2, :], tp[:], mask8[:], ALU.mult, [tp, mask8], [S5t])
                tp = tps.next()
                kb.op("pe", lambda tp=tp, g=g: pe.transpose(out=tp[:], in_=WBA[:, g, :, :].rearrange("p s m -> p (s m)"),
                                                             identity=ident_f[:]), [WBA, ident_f], [tp])
                act(S5t[:, g, 0, :], tp[:], AF.Copy, [tp], [S5t])
                tp = tps.next()
                kb.op("pe", lambda tp=tp, g=g: pe.transpose(out=tp[:], in_=WBAp[:, g, :, :].rearrange("p s m -> p (s m)"),
                                                             identity=ident_f[:]), [WBAp, ident_f], [tp])
                act(S5t[:, g, 1, :], tp[:], AF.Copy, [tp], [S5t], scale=-1.0)
            copy("pool", S5t[:, :, 3, :].rearrange("p g (s m) -> p g s m", m=16), WA[:, :, 1:9, :], [WA], [S5t])
            kb.dma("pool", S5W.t[gsl].rearrange("g p k c -> p g k c"), S5t[:], rd=[S5t], wr=[S5W])

        nb_ = 128 // GB_
        pendA = stage_A(0)
        for gb_ in range(nb_):
            nxtA = stage_A(gb_ + 1) if gb_ + 1 < nb_ else None
            stage_B(pendA)
            pendA = nxtA
        kb.flush()

    if stop_after == "s5setup":
        kb.final_wait()
        return nc

    ZF = kb.dram("ZF", [16, 128, S], BF16, kind=kind("ZF"))
    with ExitStack() as st:
        Tbs = Rot([kb.sb(st, f"Tb{i}", [128, 2, 512], F32) for i in range(9)])
        pAs = Rot([kb.ps(st, f"pA{i}", [128, 512], F32) for i in range(2)])
        pBs = Rot([kb.ps(st, f"pB{i}", [128, 512], F32) for i in range(2)])
        pYs = Rot([kb.ps(st, f"pY{i}", [128, 512], F32) for i in range(2)])
        pGs = Rot([kb.ps(st, f"pG{i}", [128, 512], F32) for i in range(2)])

        def FR(name, n, dt=F32):
            return Rot([kb.sb(st, f"{name}{i}", [128, 512], dt) for i in range(n)])
        t1s = FR("st1", 2); t2s = FR("st2", 2); cAs = FR("cA", 2); gAs = FR("gA", 3); t5s = FR("st5", 2); t6s = FR("st6", 2)
        ysbs = FR("ysb", 5); y2s = FR("y2", 2); w2s = FR("w2", 2); sgms = FR("sgm", 2); Hps = FR("Hp", 2, BF16); zts = FR("zt", 3, BF16)
        Ucs = Rot([kb.sb(st, f"Ucm{i}", [128, 512], BF16) for i in range(9)])
        Wgs = Rot([kb.sb(st, f"Wgm{i}", [128, 4, 128], BF16) for i in range(9)])
        for hp in Hps.bufs:
            memset("pool", hp[:, 0:1], 0.0, [hp])
        ctx = {}

        def tab(g):
            Tb = ctx[g]["Tb"]
            return Tb, Tb, Tb[:, 0, :], Tb[:, 1, :]

        def s0(g):
            ft, j8 = divmod(g, 8)
            c = ctx[g] = {}
            c["Uc"] = Uc = Ucs.next(); c["Wg"] = Wg = Wgs.next()
            kb.dma("sp", Uc[:], UF.t[ft, 16 * j8:16 * j8 + 16, :].rearrange("m (t c) -> t m c", t=8), rd=[UF], wr=[Uc])
            kb.dma("sp", Wg[:], S5W.t[g], rd=[S5W], wr=[Wg])
            c["Tb"] = Tb = Tbs.next()
            kb.dma("sp", Tb[:], TAB.t[g], rd=[TAB], wr=[Tb])
            c["pA"] = pA = pAs.next(); c["pB"] = pB = pBs.next()
            mm(pA[:], Wg[:, 0, :], Uc[:], True, True, [Wg, Uc], [pA])
            mm(pB[:], Wg[:, 1, :], Uc[:], True, True, [Wg, Uc], [pB])

        def s1(g):
            c = ctx[g]
            COSc, SINc, co, si = tab(g)
            c["t1"] = t1 = t1s.next(); c["t2"] = t2 = t2s.next()
            tt("dve", t1[:], c["pA"][:], co, ALU.mult, [c["pA"], COSc], [t1])
            tt("dve", t2[:], c["pB"][:], si, ALU.mult, [c["pB"], SINc], [t2])

        def s2(g):
            c = ctx[g]
            c["cA"] = cA = cAs.next()
            tt("pool", cA[:], c["t1"][:], c["t2"][:], ALU.add, [c["t1"], c["t2"]], [cA])

        def s3(g):
            c = ctx[g]
            c["gA"] = gA = gAs.next()
            cA = c["cA"]
            rl = bc(RLt[:, g:g + 1], [128, 512])
            kb.op("dve", lambda: dve.tensor_tensor_scan(out=gA[:], data0=rl, data1=cA[:], initial=0.0, op0=ALU.mult, op1=ALU.add),
                  [RLt, cA], [gA])

        def s4(g):
            c = ctx[g]
            c["pG"] = pG = pGs.next()
            mm(pG[:], PiT[:], c["gA"][:], True, True, [PiT, c["gA"]], [pG])

        def s5(g):
            c = ctx[g]
            COSc, SINc, co, si = tab(g)
            c["t5"] = t5 = t5s.next(); c["t6"] = t6 = t6s.next()
            tt("dve", t5[:], c["gA"][:], co, ALU.mult, [c["gA"], COSc], [t5])
            tt("dve", t6[:], c["pG"][:], si, ALU.mult, [c["pG"], SINc], [t6])

        def s6(g):
            c = ctx[g]
            c["Hp"] = Hp = Hps.next()
            tt("pool", Hp[:, 1:512], c["t5"][:, 0:511], c["t6"][:, 0:511], ALU.subtract, [c["t5"], c["t6"]], [Hp])

        def s7(g):
            c = ctx[g]
            c["pY"] = pY = pYs.next()
            mm(pY[:], c["Wg"][:, 2, :], c["Uc"][:], True, False, [c["Wg"], c["Uc"]], [pY])
            mm(pY[:], c["Wg"][:, 3, :], c["Hp"][:], False, True, [c["Wg"], c["Hp"]], [pY])

        def s8(g):
            c = ctx[g]
            c["ysb"] = ysb = ysbs.next()
            stt(ysb[:], c["Uc"][:], Dc[:, g:g + 1], c["pY"][:], ALU.mult, ALU.add, [c["Uc"], Dc, c["pY"]], [ysb])

        def s9(g):
            c = ctx[g]
            c["y2"] = y2 = y2s.next()
            act(y2[:], c["ysb"][:], AF.Square, [c["ysb"]], [y2])
            act(y2[:], y2[:], AF.Identity, [y2], [y2], scale=0.044715, bias=1.0)

        def s10(g):
            c = ctx[g]
            c["w2"] = w2 = w2s.next()
            tt("dve", w2[:], c["y2"][:], c["ysb"][:], ALU.mult, [c["y2"], c["ysb"]], [w2])

        def s11(g):
            c = ctx[g]
            c["sg"] = sg = sgms.next()
            act(sg[:], c["w2"][:], AF.Sigmoid, [c["w2"]], [sg], scale=2.0 * math.sqrt(2.0 / math.pi))

        def s12(g):
            ft, j8 = divmod(g, 8)
            c = ctx.pop(g)
            zt = zts.next()
            tt("pool", zt[:], c["ysb"][:], c["sg"][:], ALU.mult, [c["ysb"], c["sg"]], [zt])
            kb.dma("sp", ZF.t[ft, 16 * j8:16 * j8 + 16, :].rearrange("m (t c) -> t m c", t=8), zt[:], rd=[zt], wr=[ZF])

        stages = [s0, s1, s2, s3, s4, s5, s6, s7, s8, s9, s10, s11, s12]
        for it in range(128 + len(stages) - 1):
            for si_ in range(len(stages) - 1, -1, -1):
                g = it - si_
                if 0 <= g < 128:
                    stages[si_](g)
        kb.flush()

    if stop_after == "l1s":
        kb.final_wait()
        return nc

    with ExitStack() as st:
        zTs = Rot([kb.sb(st, f"zT{i}", [128, 16, 512], BF16) for i in range(2)])
        gTs = Rot([kb.sb(st, f"gT{i}", [128, 16, 512], BF16) for i in range(2)])
        wgs = Rot([kb.sb(st, f"wg{i}", [128, 16, 512], BF16) for i in range(2)])
        oT = kb.sb(st, "oT", [128, 16, 512], BF16)
        xq8 = [kb.sb(st, f"xr{i}", [128, D], F32) for i in range(8)]
        sgs_ = Rot([kb.sb(st, f"sgg{i}", [128, 512], BF16) for i in range(2)])
        pss = Rot([kb.ps(st, f"ps{i}", [128, 512], F32) for i in range(6)])
        X1v = X1.t.rearrange("(c t) d -> t c d", t=8)
        OUTv = out_d.rearrange("(c t) d -> t c d", t=8)
        def l1g_loads(bi_):
            zT_ = zTs.next(); gT_ = gTs.next()
            kb.dma("sp", zT_[:], ZF.t[:, :, bi_ * 512:(bi_ + 1) * 512].rearrange("f p c -> p f c"), rd=[ZF], wr=[zT_])
            kb.dma("sp", gT_[:], GF.t[:, :, bi_ * 512:(bi_ + 1) * 512].rearrange("f p c -> p f c"), rd=[GF], wr=[gT_])
            xq_ = xq8[(bi_ % 2) * 4:(bi_ % 2) * 4 + 4]
            for tti in range(4):
                kb.dma("sp", xq_[tti][:], X1v[bi_, tti * 128:(tti + 1) * 128, :], rd=[X1], wr=[xq_[tti]])
            return zT_, gT_, xq_

        nxt_loads = l1g_loads(0)
        for bi in range(NB):
            zT, gT, xq = nxt_loads
            for gg in range(4):
                wg = wgs.next()
                kb.dma("sp", wg[:], Wb_glu.t[gg], rd=[Wb_glu], wr=[wg])
                for j in range(4):
                    ft = gg * 4 + j
                    ps = pss.next()
                    for c in range(16):
                        mm(ps[:], wg[:, c, j * 128:(j + 1) * 128], zT[:, c, :], c == 0, c == 15, [wg, zT], [ps])
                    sgt = sgs_.next()
                    act(sgt[:], ps[:], AF.Sigmoid, [ps, bglu], [sgt], bias=bglu[:, ft:ft + 1])
                    tt("dve", sgt[:], sgt[:], zT[:, ft, :], ALU.mult, [sgt, zT], [sgt])
                    tt("pool", oT[:, ft, :], sgt[:], gT[:, ft, :], ALU.mult, [sgt, gT], [oT])
            if bi + 1 < NB:
                nxt_loads = l1g_loads(bi + 1)
            for og in range(4):
                wg = wgs.next()
                kb.dma("sp", wg[:], Wb_out1.t[og], rd=[Wb_out1], wr=[wg])
                for tti in range(4):
                    ps = pss.next()
                    for c in range(16):
                        mm(ps[:], oT[:, c, tti * 128:(tti + 1) * 128], wg[:, c, :], c == 0, c == 15, [oT, wg], [ps])
                    xs = xq[tti][:, og * 512:(og + 1) * 512]
                    tt("dve", xs, ps[:], xs, ALU.add, [ps, xq[tti]], [xq[tti]])
            for tti in range(4):
                kb.dma("pool", OUTv[bi, tti * 128:(tti + 1) * 128, :], xq[tti][:], rd=[xq[tti]], key=f"outst{tti}")
        kb.flush()

    kb.final_wait()
    return nc


def make_in_maps(inputs):
    maps = []
    for c in range(NCORES):
        m = {"x": np.ascontiguousarray(inputs["x"][c % 4])}
        for n, shp in PARAMS:
            m[n] = np.ascontiguousarray(np.asarray(inputs[n]).reshape(shp))
        maps.append(m)
    return maps


def kernel(**inputs):
    nc = build()
    res = run_bass_kernel_spmd(nc, make_in_maps(inputs), core_ids=list(range(NCORES)))
    out = np.stack([res.results[c]["out"] for c in range(4)], axis=0)
    return out.astype(np.float32)
```

```python
import math
from contextlib import ExitStack

import numpy as np
import concourse.bass as bass
import concourse.mybir as mybir
from concourse.bass_utils import run_bass_kernel_spmd

F32 = mybir.dt.float32
BF16 = mybir.dt.bfloat16
I32 = mybir.dt.int32
AF = mybir.ActivationFunctionType
ALU = mybir.AluOpType

S = 4096
D = 2048
NB = 8
EPS = 1e-6
NCORES = 4


class Buf:
    def __init__(self, name, t, persist=False):
        self.name = name
        self.t = t
        self.w = {}
        self.r = {}
        self.ext = []
        self.persist = persist

    def __getitem__(self, idx):
        return self.t[idx]


class Op:
    __slots__ = ("eng", "fn", "rd", "wr", "is_dma", "key", "deps", "ext", "signal", "sem", "val")

    def __init__(self, eng, fn, rd, wr, is_dma=False, key=None):
        self.eng = eng
        self.fn = fn
        self.rd = rd
        self.wr = wr
        self.is_dma = is_dma
        self.key = key
        self.deps = ()
        self.ext = []
        self.signal = False
        self.sem = None
        self.val = 0


class Rot:
    def __init__(self, bufs):
        self.bufs = bufs
        self.i = 0

    def next(self):
        b = self.bufs[self.i % len(self.bufs)]
        self.i += 1
        return b


class KB:
    RELAX_SAME_ENGINE = True
    SB_LIMIT = 204 * 1024
    def __init__(self, nc):
        self.nc = nc
        self.es = ExitStack()
        self.eng = {"pe": nc.tensor, "act": nc.scalar, "dve": nc.vector, "pool": nc.gpsimd, "sp": nc.sync}
        self.esem = {k: self.es.enter_context(nc.semaphore("s_" + k)) for k in self.eng}
        self.ecnt = {k: 0 for k in self.eng}
        self.seen = {k: {} for k in self.eng}
        self.dsem = {}
        self.deferred = set()
        self.ops = []
        self.bufs = []
        self.nins = {k: 0 for k in self.eng}

    def sb(self, stack, name, shape, dtype, persist=False):
        self.uid = getattr(self, "uid", 0) + 1
        name = f"{name}_{self.uid}"
        nbytes = int(np.prod(shape[1:])) * (2 if dtype == BF16 else 4)
        nbytes = (nbytes + 31) // 32 * 32
        self.sbuf_used = getattr(self, "sbuf_used", 0) + nbytes
        self.sbuf_peak = max(getattr(self, "sbuf_peak", 0), self.sbuf_used)
        assert self.sbuf_used <= self.SB_LIMIT, f"SBUF budget exceeded: {self.sbuf_used} at {name}"

        def _free(nb=nbytes):
            self.sbuf_used -= nb
        stack.callback(_free)
        t = stack.enter_context(self.nc.sbuf_tensor(name, list(shape), dtype))
        b = Buf(name, t, persist)
        self.bufs.append(b)
        return b

    def ps(self, stack, name, shape, dtype):
        self.uid = getattr(self, "uid", 0) + 1
        name = f"{name}_{self.uid}"
        t = stack.enter_context(self.nc.psum_tensor(name, list(shape), dtype))
        b = Buf(name, t)
        self.bufs.append(b)
        return b

    def dram(self, name, shape, dtype, kind="Internal", persist=False):
        t = self.nc.dram_tensor(name, list(shape), dtype, kind=kind).ap()
        b = Buf(name, t, persist)
        self.bufs.append(b)
        return b

    def op(self, eng, fn, rd=(), wr=()):
        self.ops.append(Op(eng, fn, list(rd), list(wr)))

    def dma(self, q, out, in_, rd=(), wr=(), key=None, defer=False, **kw):
        h = self.eng[q]
        if key is None:
            key = (wr[0].name if wr else rd[0].name + "_st")
        if key not in self.dsem:
            self.dsem[key] = [self.es.enter_context(self.nc.semaphore("d_" + key)), 0]
        if defer:
            self.deferred.add(key)
        self.ops.append(Op(q, lambda: h.dma_start(out=out, in_=in_, **kw), list(rd), list(wr), True, key))

    def flush(self, barrier=True):
        ops = self.ops
        for i, op in enumerate(ops):
            deps = set()
            other = set()
            ext = []
            for b in op.rd:
                deps.update(b.w.values())
                ext.extend(b.ext)
            for b in op.wr:
                other.update(b.w.values())
                other.update(b.r.values())
                ext.extend(b.ext)
            if self.RELAX_SAME_ENGINE and op.eng in ("act", "dve") and not op.is_dma:
                other = {d for d in other if ops[d].is_dma or ops[d].eng != op.eng}
            deps |= other
            if op.eng == "pe" and not op.is_dma:
                deps = {d for d in deps if ops[d].is_dma or ops[d].eng != "pe"}
            deps.discard(i)
            op.deps = sorted(deps)
            op.ext = ext
            k = ("dma", op.key) if op.is_dma else op.eng
            for b in op.wr:
                b.w[k] = i
            for b in op.rd:
                b.r[k] = i
            for d in deps:
                ops[d].signal = True
        last = {}
        for i, op in enumerate(ops):
            if not op.is_dma:
                last[op.eng] = i
        for i in last.values():
            ops[i].signal = True
        for op in ops:
            e = op.eng
            h = self.eng[e]
            need = [(ops[d].sem, ops[d].val) for d in op.deps] + list(op.ext)
            for sem, val in need:
                sk = id(sem)
                if self.seen[e].get(sk, 0) < val:
                    h.wait_ge(sem, val)
                    self.seen[e][sk] = val
            ins = op.fn()
            self.nins[e] += 1
            if op.is_dma:
                ent = self.dsem[op.key]
                ent[1] += 16
                ins.then_inc(ent[0], 16)
                op.sem, op.val = ent[0], ent[1]
            elif op.signal:
                self.ecnt[e] += 1
                ins.then_inc(self.esem[e], 1)
                op.sem, op.val = self.esem[e], self.ecnt[e]
        for b in self.bufs:
            if b.persist:
                for d in list(b.w.values()):
                    b.ext.append((ops[d].sem, ops[d].val))
            b.w = {}
            b.r = {}
        self.ops = []
        if barrier:
            for e, h in self.eng.items():
                for e2 in self.eng:
                    if e2 != e and self.ecnt[e2] > self.seen[e].get(id(self.esem[e2]), 0):
                        h.wait_ge(self.esem[e2], self.ecnt[e2])
                        self.seen[e][id(self.esem[e2])] = self.ecnt[e2]
                for key, (sem, cnt) in self.dsem.items():
                    if key in self.deferred:
                        continue
                    if cnt > self.seen[e].get(id(sem), 0):
                        h.wait_ge(sem, cnt)
                        self.seen[e][id(sem)] = cnt

    def final_wait(self):
        for e, h in self.eng.items():
            for key, (sem, cnt) in self.dsem.items():
                if cnt > self.seen[e].get(id(sem), 0):
                    h.wait_ge(sem, cnt)
                    self.seen[e][id(sem)] = cnt


def bc(ap, shape):
    return ap.to_broadcast(list(shape))


PARAMS = [
    ("e_norm_g", [D]), ("e_w_in", [D, 7168]), ("e_conv_w", [31, 1024]), ("e_conv_b", [1024]),
    ("e_cln_g", [1024]), ("e_cln_b", [1024]), ("e_qn_g", [64]), ("e_kn_g", [64]),
    ("e_lam_q1", [64]), ("e_lam_k1", [64]), ("e_lam_q2", [64]), ("e_lam_k2", [64]),
    ("e_subln_g", [128]), ("e_w_out", [D, D]),
    ("o_norm_g", [D]), ("o_w_in", [D, 2 * D]), ("o_A_re", [128, 64]), ("o_A_im", [128, 64]),
    ("o_log_dt", [128]), ("o_B_re", [128, 64, 16]), ("o_B_im", [128, 64, 16]),
    ("o_C_re", [128, 16, 64]), ("o_C_im", [128, 16, 64]), ("o_D", [D]),
    ("o_w_glu", [D, D]), ("o_b_glu", [D]), ("o_w_out", [D, D]),
]


def build(dbg=(), stop_after=None):
    nc = bass.Bass("TRN2", target_bir_lowering=False)
    kb = KB(nc)
    eng = kb.eng
    pe, act_, dve, pool = eng["pe"], eng["act"], eng["dve"], eng["pool"]

    def kind(name):
        return "ExternalOutput" if name in dbg else "Internal"

    din = {"x": nc.dram_tensor("x", [S, D], F32, kind="ExternalInput").ap()}
    for n, shp in PARAMS:
        din[n] = nc.dram_tensor(n, shp, F32, kind="ExternalInput").ap()
    out_d = nc.dram_tensor("out", [S, D], F32, kind="ExternalOutput").ap()

    Wb_in0 = kb.dram("Wb_in0", [14, 128, 16, 512], BF16, persist=True)
    Wb_out0 = kb.dram("Wb_out0", [4, 128, 16, 512], BF16, persist=True)
    Wb_in1 = kb.dram("Wb_in1", [8, 128, 16, 512], BF16, persist=True)
    Wb_glu = kb.dram("Wb_glu", [4, 128, 16, 512], BF16, persist=True)
    Wb_out1 = kb.dram("Wb_out1", [4, 128, 16, 512], BF16, persist=True)
    ROT = kb.dram("ROT", [2, 128, S], F32, kind=kind("ROT"))
    QT = kb.dram("QT", [8, 128, S], BF16, kind=kind("QT"))
    KT = kb.dram("KT", [8, 128, S], BF16, kind=kind("KT"))
    VV = kb.dram("VV", [8, 128, 32, 128], BF16, kind=kind("VV"))
    GB = kb.dram("GB", [8, 128, S], BF16, kind=kind("GB"))
    MIXT = kb.dram("MIXT", [16, 128, S], BF16, kind=kind("MIXT"))
    X1 = kb.dram("X1", [S, D], F32, kind=kind("X1"))
    TAB = kb.dram("TAB", [128, 128, 2, 512], F32, kind=kind("TAB"))

    def act(out, in_, func, rd, wr, bias=None, scale=None, accum=None):
        kw = {}
        if bias is not None:
            kw["bias"] = bias
        if scale is not None:
            kw["scale"] = scale
        if accum is not None:
            kw["accum_out"] = accum
        kb.op("act", lambda: act_.activation(out=out, in_=in_, func=func, **kw), rd, wr)

    def tt(e, out, in0, in1, op, rd, wr):
        h = eng[e]
        kb.op(e, lambda: h.tensor_tensor(out=out, in0=in0, in1=in1, op=op), rd, wr)

    def ts(e, out, in0, s1, s2, op0, op1, rd, wr):
        h = eng[e]
        if op1 is None:
            kb.op(e, lambda: h.tensor_scalar(out=out, in0=in0, scalar1=s1, scalar2=None, op0=op0), rd, wr)
        else:
            kb.op(e, lambda: h.tensor_scalar(out=out, in0=in0, scalar1=s1, scalar2=s2, op0=op0, op1=op1), rd, wr)

    def stt(out, in0, scalar, in1, op0, op1, rd, wr):
        kb.op("dve", lambda: dve.scalar_tensor_tensor(out=out, in0=in0, scalar=scalar, in1=in1, op0=op0, op1=op1), rd, wr)

    def mm(out, lhsT, rhs, start, stop, rd, wr):
        kb.op("pe", lambda: pe.matmul(out, lhsT=lhsT, rhs=rhs, start=start, stop=stop), rd, wr)

    def recip(out, in_, rd, wr):
        kb.op("dve", lambda: dve.reciprocal(out=out, in_=in_), rd, wr)

    def copy(e, out, in_, rd, wr):
        h = eng[e]
        kb.op(e, lambda: h.tensor_copy(out=out, in_=in_), rd, wr)

    def memset(e, ap, val, wr):
        h = eng[e]
        kb.op(e, lambda: h.memset(ap, val), (), wr)

    cs = kb.es

    def conv_w(dst, src, ng, key):
        v = src.rearrange("(c p) (g n) -> g p c n", p=128, n=512)
        for g in range(ng):
            kb.dma("pool", dst.t[g], v[g], wr=[dst], key=key, defer=True)

    conv_w(Wb_in0, din["e_w_in"], 14, "wc0")

    ident_f = kb.sb(cs, "ident_f", [128, 128], F32)
    ident_b = kb.sb(cs, "ident_b", [128, 128], BF16)
    ones_b = kb.sb(cs, "ones_b", [128, 128], BF16)
    ones_f = kb.sb(cs, "ones_f", [128, 128], F32)
    avg1024 = kb.sb(cs, "avg1024", [128, 128], F32)
    avg128 = kb.sb(cs, "avg128", [128, 128], F32)
    bd64 = kb.sb(cs, "bd64", [128, 128], BF16)
    Pm = kb.sb(cs, "Pm", [128, 128], BF16)
    M4 = kb.sb(cs, "M4", [128, 1, 128], BF16)
    eps_t = kb.sb(cs, "eps_t", [128, 1], F32)
    gn0 = kb.sb(cs, "gn0", [128, 16], F32)
    gn1 = kb.sb(cs, "gn1", [128, 16], F32)
    kw_t = kb.sb(cs, "kw_t", [128, 8, 31], F32)
    kw_b = kb.sb(cs, "kw_b", [128, 8, 31], BF16)
    convb = kb.sb(cs, "convb", [128, 8], F32)
    clng = kb.sb(cs, "clng", [128, 8], F32)
    clnb = kb.sb(cs, "clnb", [128, 8], F32)
    qng = kb.sb(cs, "qng", [128, 1], F32)
    kng = kb.sb(cs, "kng", [128, 1], F32)
    sgs = kb.sb(cs, "sgs", [128, 1], F32)
    neglam = kb.sb(cs, "neglam", [128, 1], F32)
    bglu = kb.sb(cs, "bglu", [128, 16], F32)
    sgn = kb.sb(cs, "sgn", [128, 1], F32)
    PiT = kb.sb(cs, "PiT", [128, 128], F32)

    with ExitStack() as st:
        iota_i = kb.sb(st, "iota_i", [128, 128], I32)
        iota_f = kb.sb(st, "iota_f", [128, 128], F32)
        pidx_i = kb.sb(st, "pidx_i", [128, 1], I32)
        ptmp_i = kb.sb(st, "ptmp_i", [128, 1], I32)
        m_hi = kb.sb(st, "m_hi", [128, 1], F32)
        m_lo = kb.sb(st, "m_lo", [128, 1], F32)
        Am = kb.sb(st, "Am", [128, 128], F32)
        Bm = kb.sb(st, "Bm", [128, 128], F32)
        Pf = kb.sb(st, "Pf", [128, 128], F32)
        ones512 = kb.sb(st, "ones512", [128, 512], BF16)
        freq = kb.sb(st, "freq", [128, 1], F32)
        halfpi = kb.sb(st, "halfpi", [128, 1], F32)
        cn = kb.sb(st, "cn", [128, 1], F32)
        sn = kb.sb(st, "sn", [128, 1], F32)
        nsn = kb.sb(st, "nsn", [128, 1], F32)
        tq = kb.sb(st, "tq", [128, 1], F32)
        COS = kb.sb(st, "COS", [128, S], F32)
        SIN = kb.sb(st, "SIN", [128, S], F32)
        T1 = kb.sb(st, "T1", [128, S // 2], F32)
        lamv = kb.sb(st, "lamv", [64, 4], F32)
        prod = kb.sb(st, "prod", [64, 2], F32)
        lps = kb.ps(st, "lps", [128, 512], F32)
        e2 = kb.sb(st, "e2", [128, 2], F32)
        sgl_ = kb.sb(st, "sgl_", [128, 1], F32)

        kb.op("pool", lambda: pool.iota(iota_i[:], pattern=[[1, 128]], base=0, channel_multiplier=-1), (), [iota_i])
        kb.op("pool", lambda: pool.iota(pidx_i[:], pattern=[[0, 1]], base=0, channel_multiplier=1), (), [pidx_i])
        copy("dve", iota_f[:], iota_i[:], [iota_i], [iota_f])
        kb.op("dve", lambda: dve.tensor_single_scalar(out=ident_f[:], in_=iota_f[:], scalar=0.0, op=ALU.is_equal), [iota_f], [ident_f])
        copy("dve", ident_b[:], ident_f[:], [ident_f], [ident_b])
        kb.op("dve", lambda: dve.tensor_single_scalar(out=PiT[:], in_=iota_f[:], scalar=-64.0, op=ALU.is_equal), [iota_f], [PiT])
        kb.op("dve", lambda: dve.tensor_single_scalar(out=Am[:], in_=iota_f[:], scalar=64.0, op=ALU.is_equal), [iota_f], [Am])
        tt("dve", PiT[:], PiT[:], Am[:], ALU.subtract, [PiT, Am], [PiT])
        kb.op("dve", lambda: dve.tensor_single_scalar(out=Am[:], in_=iota_f[:], scalar=32.0, op=ALU.is_equal), [iota_f], [Am])
        kb.op("dve", lambda: dve.tensor_single_scalar(out=Bm[:], in_=iota_f[:], scalar=-32.0, op=ALU.is_equal), [iota_f], [Bm])
        kb.op("dve", lambda: dve.tensor_single_scalar(out=ptmp_i[:], in_=pidx_i[:], scalar=32, op=ALU.bitwise_and), [pidx_i], [ptmp_i])
        copy("dve", m_hi[:], ptmp_i[:], [ptmp_i], [m_hi])
        ts("dve", m_hi[:], m_hi[:], 1.0 / 32.0, None, ALU.mult, None, [m_hi], [m_hi])
        ts("dve", m_lo[:], m_hi[:], -1.0, 1.0, ALU.mult, ALU.add, [m_hi], [m_lo])
        ts("dve", sgn[:], m_hi[:], 2.0, -1.0, ALU.mult, ALU.add, [m_hi], [sgn])
        ts("dve", Pf[:], Am[:], m_lo[:, 0:1], None, ALU.mult, None, [Am, m_lo], [Pf])
        stt(Pm[:], Bm[:], m_hi[:, 0:1], Pf[:], ALU.mult, ALU.add, [Bm, m_hi, Pf], [Pm])
        memset("pool", ones_b[:], 1.0, [ones_b])
        memset("pool", ones_f[:], 1.0, [ones_f])
        memset("pool", avg1024[:], 1.0 / 1024.0, [avg1024])
        memset("pool", avg128[:], 1.0 / 128.0, [avg128])
        memset("pool", bd64[:], 0.0, [bd64])
        memset("pool", bd64[0:64, 0:64], 1.0 / 64.0, [bd64])
        memset("pool", bd64[64:128, 64:128], 1.0 / 64.0, [bd64])
        memset("pool", eps_t[:], EPS, [eps_t])
        memset("pool", halfpi[:], math.pi / 2, [halfpi])
        memset("pool", ones512[:], 1.0, [ones512])
        kb.op("pool", lambda: pool.affine_select(out=M4[:, 0, :], in_=ones512[:, 0:128], pattern=[[1, 128]],
                                                  compare_op=ALU.is_ge, fill=0.0, base=0,
                                                  channel_multiplier=-1), [ones512], [M4])
        def ld(b, dst, src):
            kb.dma("sp", dst, src, wr=[b], key="small", allow_slow_non_contiguous=True)

        ld(gn0, gn0[:], din["e_norm_g"].rearrange("(c p) -> p c", p=128))
        ld(gn1, gn1[:], din["o_norm_g"].rearrange("(c p) -> p c", p=128))
        for t_ in range(8):
            ld(kw_t, kw_t[:, t_, :], din["e_conv_w"][:, t_ * 128:(t_ + 1) * 128].rearrange("w p -> p w"))
        ld(convb, convb[:], din["e_conv_b"].rearrange("(t p) -> p t", p=128))
        ld(clng, clng[:], din["e_cln_g"].rearrange("(t p) -> p t", p=128))
        ld(clnb, clnb[:], din["e_cln_b"].rearrange("(t p) -> p t", p=128))
        ld(bglu, bglu[:], din["o_b_glu"].rearrange("(t p) -> p t", p=128))
        for hh in range(2):
            ld(qng, qng[hh * 64:(hh + 1) * 64, :], din["e_qn_g"].rearrange("(p o) -> p o", o=1))
            ld(kng, kng[hh * 64:(hh + 1) * 64, :], din["e_kn_g"].rearrange("(p o) -> p o", o=1))
        ld(sgl_, sgl_[:], din["e_subln_g"].rearrange("(p o) -> p o", o=1))
        for i, nme in enumerate(["e_lam_q1", "e_lam_k1", "e_lam_q2", "e_lam_k2"]):
            ld(lamv, lamv[:, i:i + 1], din[nme].rearrange("(p o) -> p o", o=1))
        copy("dve", kw_b[:], kw_t[:], [kw_t], [kw_b])
        lam_init = 0.8 - 0.6 * math.exp(-0.3 * 0)
        ts("dve", sgs[:], sgl_[:], 1.0 - lam_init, None, ALU.mult, None, [sgl_], [sgs])
        tt("dve", prod[:, 0:1], lamv[:, 0:1], lamv[:, 1:2], ALU.mult, [lamv], [prod])
        tt("dve", prod[:, 1:2], lamv[:, 2:3], lamv[:, 3:4], ALU.mult, [lamv], [prod])
        mm(lps[:, 0:2], ones_f[0:64, :], prod[:, :], True, True, [ones_f, prod], [lps])
        act(e2[:], lps[:, 0:2], AF.Exp, [lps], [e2])
        tt("dve", neglam[:], e2[:, 1:2], e2[:, 0:1], ALU.subtract, [e2], [neglam])
        ts("dve", neglam[:], neglam[:], -lam_init, None, ALU.add, None, [neglam], [neglam])

        kb.op("dve", lambda: dve.tensor_single_scalar(out=ptmp_i[:], in_=pidx_i[:], scalar=31, op=ALU.bitwise_and), [pidx_i, m_hi], [ptmp_i])
        copy("dve", freq[:], ptmp_i[:], [ptmp_i], [freq])
        act(freq[:], freq[:], AF.Exp, [freq], [freq], scale=-math.log(10000.0) / 32.0)
        act(cn[:], freq[:], AF.Sin, [freq, halfpi], [cn], bias=halfpi[:, 0:1])
        act(sn[:], freq[:], AF.Sin, [freq], [sn])
        ts("dve", nsn[:], sn[:], -1.0, None, ALU.mult, None, [sn], [nsn])
        memset("dve", COS[:, 0:1], 1.0, [COS])
        memset("dve", SIN[:, 0:1], 0.0, [SIN])
        n = 1
        while n < S:
            ts("dve", T1[:, 0:n], COS[:, 0:n], cn[:, 0:1], None, ALU.mult, None, [COS, cn], [T1])
            stt(COS[:, n:2 * n], SIN[:, 0:n], nsn[:, 0:1], T1[:, 0:n], ALU.mult, ALU.add, [SIN, nsn, T1, COS], [COS])
            ts("dve", T1[:, 0:n], SIN[:, 0:n], cn[:, 0:1], None, ALU.mult, None, [SIN, cn, COS], [T1])
            stt(SIN[:, n:2 * n], COS[:, 0:n], sn[:, 0:1], T1[:, 0:n], ALU.mult, ALU.add, [COS, sn, T1, SIN], [SIN])
            if 2 * n < S:
                ts("dve", tq[:], cn[:], cn[:, 0:1], None, ALU.mult, None, [cn, SIN], [tq])
                stt(tq[:], sn[:], nsn[:, 0:1], tq[:], ALU.mult, ALU.add, [sn, nsn, tq], [tq])
                ts("dve", sn[:], cn[:], sn[:, 0:1], 2.0, ALU.mult, ALU.mult, [cn, sn], [sn])
                copy("dve", cn[:], tq[:], [tq, sn], [cn])
                ts("dve", nsn[:], sn[:], -1.0, None, ALU.mult, None, [sn], [nsn])
            n *= 2
        ts("dve", SIN[:], SIN[:], sgn[:, 0:1], None, ALU.mult, None, [SIN, sgn], [SIN])
        kb.dma("sp", ROT.t[0], COS[:], rd=[COS], wr=[ROT], key="rot_st")
        kb.dma("sp", ROT.t[1], SIN[:], rd=[SIN], wr=[ROT], key="rot_st")
        kb.flush()

    S5W = kb.dram("S5W", [128, 128, 4, 128], BF16, kind=kind("S5W"))
    CLv = kb.sb(cs, "CLv", [128, 9, 128], F32)
    SLv = kb.sb(cs, "SLv", [128, 9, 128], F32)
    RLt = kb.sb(cs, "RLt", [128, 128], F32)
    Dc = kb.sb(cs, "Dc", [128, 128], F32)
    keep_t = {n_: kb.sb(cs, n_, [128, 128], F32) for n_ in ("fR", "fI", "nuR", "nuI", "lR", "muI", "l7R", "l7I", "mask8")}
    with ExitStack() as st:
        def T(name):
            return keep_t[name] if name in keep_t else kb.sb(st, name, [128, 128], F32)
        AN = T("AN"); are = T("are"); aim = T("aim"); ldt = T("ldt"); dtt = T("dtt")
        th = T("th"); xr = T("xr"); kk = T("kk"); rr = T("rr"); r2 = T("r2"); acc = T("acc")
        sinT = T("sinT"); cosT = T("cosT"); EE = T("EE"); lR = T("lR"); lI = T("lI")
        den = T("den"); nr = T("nr"); fR = T("fR"); fI = T("fI"); nuR = T("nuR"); nuI = T("nuI")
        muI = T("muI"); l2R = T("l2R"); l2I = T("l2I"); l4R = T("l4R"); l4I = T("l4I")
        l8R = T("l8R"); l8I = T("l8I"); l7R = T("l7R"); l7I = T("l7I"); x1_ = T("x1_"); x2_ = T("x2_")
        x3_ = T("x3_"); x4_ = T("x4_"); mask8 = T("mask8"); onesq = T("onesq")
        tps = Rot([kb.ps(st, f"tps{i}", [128, 128], F32) for i in range(4)])

        def D_(fn, rd, wr):
            kb.op("dve", fn, rd, wr)

        def mul(o, a, b):
            tt("dve", o[:], a[:], b[:], ALU.mult, [a, b], [o])

        def add(o, a, b):
            tt("dve", o[:], a[:], b[:], ALU.add, [a, b], [o])

        def sub(o, a, b):
            tt("dve", o[:], a[:], b[:], ALU.subtract, [a, b], [o])

        def cmul(oR, oI, aR, aI, bR, bI):
            mul(x1_, aR, bR); mul(x2_, aI, bI); mul(x3_, aR, bI); mul(x4_, aI, bR)
            sub(oR, x1_, x2_); add(oI, x3_, x4_)

        def horner(o, xx, coef):
            n_ = len(coef) - 1
            ts("dve", o[:], xx[:], float(coef[n_]), None, ALU.mult, None, [xx], [o])
            for k_ in range(n_ - 1, 0, -1):
                stt(o[:], o[:], float(coef[k_]), xx[:], ALU.add, ALU.mult, [o, xx], [o])
            ts("dve", o[:], o[:], float(coef[0]), None, ALU.add, None, [o], [o])

        for src, dstt in ((din["o_A_re"], are), (din["o_A_im"], aim)):
            kb.dma("sp", AN[:, 0:64], src, wr=[AN], key="small")
            kb.dma("sp", AN[:, 64:128], src, wr=[AN], key="small")
            tp = tps.next()
            kb.op("pe", lambda tp=tp: pe.transpose(out=tp[:], in_=AN[:], identity=ident_f[:]), [AN, ident_f], [tp])
            copy("dve", dstt[:], tp[:], [tp], [dstt])
        kb.dma("sp", ldt[:], din["o_log_dt"].partition_broadcast(128), wr=[ldt], key="small", allow_slow_non_contiguous=True)
        for tau in range(8):
            kb.dma("sp", Dc[tau * 16:(tau + 1) * 16, :], din["o_D"].rearrange("(g m) -> m g", m=16), wr=[Dc], key="small",
                   allow_slow_non_contiguous=True)
        act(dtt[:], ldt[:], AF.Exp, [ldt], [dtt])
        mul(th, aim, dtt)
        mul(xr, are, dtt)
        MAGIC = 12582912.0
        ts("dve", kk[:], th[:], 1.0 / (2 * math.pi), MAGIC, ALU.mult, ALU.add, [th], [kk])
        ts("dve", kk[:], kk[:], -MAGIC, None, ALU.add, None, [kk], [kk])
        c1 = 6.28125
        c2 = float(np.float32(np.float32(2 * math.pi - c1).view(np.uint32) & np.uint32(0xFFFFF000)).view(np.float32)) if False else 0.0019350051879882812
        c3 = 2 * math.pi - c1 - c2
        stt(rr[:], kk[:], -c1, th[:], ALU.mult, ALU.add, [kk, th], [rr])
        stt(rr[:], kk[:], -c2, rr[:], ALU.mult, ALU.add, [kk, rr], [rr])
        stt(rr[:], kk[:], -c3, rr[:], ALU.mult, ALU.add, [kk, rr], [rr])
        mul(r2, rr, rr)
        horner(acc, r2, [(-1.0) ** k_ / math.factorial(2 * k_ + 1) for k_ in range(11)])
        mul(sinT, acc, rr)
        horner(cosT, r2, [(-1.0) ** k_ / math.factorial(2 * k_) for k_ in range(12)])
        horner(EE, xr, [1.0 / math.factorial(k_) for k_ in range(8)])
        mul(lR, EE, cosT)
        mul(lI, EE, sinT)
        mul(den, are, are); mul(x1_, aim, aim); add(den, den, x1_)
        recip(den[:], den[:], [den], [den])
        ts("dve", nr[:], lR[:], -1.0, None, ALU.add, None, [lR], [nr])
        mul(x1_, nr, are); mul(x2_, lI, aim); add(x1_, x1_, x2_); mul(fR, x1_, den)
        mul(x1_, lI, are); mul(x2_, nr, aim); sub(x1_, x1_, x2_); mul(fI, x1_, den)
        mul(x1_, EE, EE)
        recip(x1_[:], x1_[:], [x1_], [x1_])
        mul(nuR, lR, x1_)
        mul(nuI, lI, x1_)
        ts("dve", nuI[:], nuI[:], -1.0, None, ALU.mult, None, [nuI], [nuI])
        ts("dve", muI[:], lI[:], -1.0, None, ALU.mult, None, [lI], [muI])
        cmul(l2R, l2I, lR, lI, lR, lI)
        cmul(l4R, l4I, l2R, l2I, l2R, l2I)
        cmul(l8R, l8I, l4R, l4I, l4R, l4I)
        cmul(l7R, l7I, l4R, l4I, l2R, l2I)
        cmul(l7R, l7I, l7R, l7I, lR, lI)
        mul(RLt, EE, EE); mul(RLt, RLt, RLt); mul(RLt, RLt, RLt)
        recip(x1_[:], RLt[:], [RLt], [x1_])
        tt("dve", CLv[:, 0, :], l8R[:], x1_[:], ALU.mult, [l8R, x1_], [CLv])
        tt("dve", SLv[:, 0, :], l8I[:], x1_[:], ALU.mult, [l8I, x1_], [SLv])
        for j in range(8):
            tt("dve", x2_[:], CLv[:, j, :], CLv[:, j, :], ALU.mult, [CLv], [x2_])
            tt("dve", x3_[:], SLv[:, j, :], SLv[:, j, :], ALU.mult, [SLv], [x3_])
            tt("dve", CLv[:, j + 1, :], x2_[:], x3_[:], ALU.subtract, [x2_, x3_], [CLv])
            tt("dve", x2_[:], CLv[:, j, :], SLv[:, j, :], ALU.mult, [CLv, SLv], [x2_])
            ts("dve", SLv[:, j + 1, :], x2_[:], 2.0, None, ALU.mult, None, [x2_], [SLv])
        memset("pool", onesq[:], 1.0, [onesq])
        kb.op("pool", lambda: pool.affine_select(out=mask8[:], in_=onesq[:], pattern=[[16, 8], [0, 16]], compare_op=ALU.is_ge,
                                                  fill=0.0, base=15, channel_multiplier=-1), [onesq], [mask8])

        kb.flush()

    if stop_after == "setup":
        kb.final_wait()
        return nc

    x = din["x"]
    with ExitStack() as st:
        hT = kb.sb(st, "hT", [128, 16, 512], BF16)
        wgs = Rot([kb.sb(st, f"wg{i}", [128, 16, 512], BF16) for i in range(2)])
        rcs = Rot([kb.sb(st, f"rc{i}", [128, 2, 512], F32) for i in range(1)])
        xts = Rot([kb.sb(st, f"xt{i}", [128, D], F32) for i in range(1)])
        hn = kb.sb(st, "hn", [128, D], BF16)
        ss = kb.sb(st, "ss", [128, 1], F32)
        sd1 = kb.sb(st, "sd1", [128, 1], F32)
        rs1 = kb.sb(st, "rs1", [128, 1], F32)
        u = kb.sb(st, "u", [128, 8, 542], BF16)
        sgl = kb.sb(st, "sgl", [128, 4, 512], BF16)
        sga = kb.sb(st, "sga", [128, 8, 512], BF16)
        cc = kb.sb(st, "cc", [128, 8, 512], F32)
        csqs = Rot([kb.sb(st, f"csq{i}", [128, 512], F32) for i in range(2)])
        DG = kb.sb(st, "DG", [128, 31, 128], BF16)
        oas = Rot([kb.sb(st, f"oa{i}", [128, 512], BF16) for i in range(2)])
        mean_sb = kb.sb(st, "mean_sb", [128, 512], F32)
        var = kb.sb(st, "var", [128, 512], F32)
        rsl = kb.sb(st, "rsl", [128, 512], F32)
        tln = Rot([kb.sb(st, f"tln{i}", [128, 512], F32) for i in range(2)])
        aln = Rot([kb.sb(st, f"aln{i}", [128, 512], BF16) for i in range(2)])
        qgs = Rot([kb.sb(st, f"qg{i}", [128, 512], BF16) for i in range(2)])
        sqs = Rot([kb.sb(st, f"sq{i}", [128, 512], BF16) for i in range(2)])
        rsq = kb.sb(st, "rsq", [128, 512], F32)
        t1 = kb.sb(st, "t1", [128, 512], F32)
        t2 = kb.sb(st, "t2", [128, 512], F32)
        qos = Rot([kb.sb(st, f"qo{i}", [128, 512], BF16) for i in range(2)])
        vts = Rot([kb.sb(st, f"vt{i}", [128, 512], BF16) for i in range(2)])
        gbts = Rot([kb.sb(st, f"gbt{i}", [128, 512], BF16) for i in range(2)])
        pss = Rot([kb.ps(st, f"ps{i}", [128, 512], F32) for i in range(4)])
        ptrs = Rot([kb.ps(st, f"ptr{i}", [128, 4, 128], BF16) for i in range(2)])
        mps = kb.ps(st, "mps", [128, 512], F32)
        qps = kb.ps(st, "qps", [128, 512], F32)

        memset("pool", u[:, :, 0:30], 0.0, [u])

        def load_wg(src, gi):
            wg = wgs.next()
            kb.dma("sp", wg[:], src.t[gi], rd=[src], wr=[wg])
            return wg

        def fm_tile(wg, j, hTb):
            ps = pss.next()
            for c in range(16):
                mm(ps[:], wg[:, c, j * 128:(j + 1) * 128], hTb[:, c, :], c == 0, c == 15, [wg, hTb], [ps])
            return ps

        pend_pe = []

        def run_pending():
            todo = list(pend_pe)
            del pend_pe[:]
            for f_ in todo:
                f_()

        def fm_tile2(wg, j, hTb):
            ps = fm_tile(wg, j, hTb)
            run_pending()
            return ps

        hTs = Rot([hT, kb.sb(st, "hT2", [128, 16, 512], BF16)])
        hns = Rot([hn, kb.sb(st, "hn2", [128, D], BF16)])
        DGs = Rot([DG, kb.sb(st, "DG2", [128, 31, 128], BF16)])

        def norm_part(bi, tti):
            xt = xts.next()
            t0n = bi * 512
            kb.dma("sp", xt[:], x[t0n + tti * 128:t0n + (tti + 1) * 128, :], wr=[xt])
            hnb = hns.next()
            act(hnb[:], xt[:], AF.Square, [xt], [hnb, ss], accum=ss[:, 0:1])
            act(sd1[:], ss[:], AF.Ln, [ss, eps_t], [sd1], bias=eps_t[:, 0:1], scale=1.0 / D)
            act(rs1[:], sd1[:], AF.Exp, [sd1], [rs1], scale=-0.5)
            act(hnb[:], xt[:], AF.Copy, [xt, rs1], [hnb], scale=rs1[:, 0:1])
            return hnb

        def transpose_part(hnb, hTb, tti):
            for c4 in range(4):
                ptr = ptrs.next()
                for q_ in range(4):
                    c = c4 * 4 + q_
                    kb.op("pe", lambda c=c, q_=q_, ptr=ptr: pe.transpose(out=ptr[:, q_, :], in_=hnb[:, c * 128:(c + 1) * 128],
                                                                           identity=ident_b[:]), [hnb, ident_b], [ptr])
                tt("dve", hTb[:, c4 * 4:c4 * 4 + 4, tti * 128:(tti + 1) * 128], ptr[:],
                   bc(gn0[:, c4 * 4:c4 * 4 + 4].unsqueeze(2), [128, 4, 128]), ALU.mult, [ptr, gn0], [hTb])

        hT_cur = hTs.next()
        for tti in range(4):
            hnb = norm_part(0, tti)
            transpose_part(hnb, hT_cur, tti)

        for bi in range(NB):
            t0 = bi * 512
            hTc = hT_cur
            hT_next = hTs.next() if bi + 1 < NB else None
            rc = rcs.next()
            kb.dma("sp", rc[:], ROT.t[:, :, t0:t0 + 512].rearrange("a p t -> p a t"), rd=[ROT], wr=[rc])
            nxt_hn = {}

            def prefetch_step(k):
                if hT_next is None:
                    return
                if k >= 1:
                    transpose_part(nxt_hn[k - 1], hT_next, k - 1)
                if k < 4:
                    nxt_hn[k] = norm_part(bi + 1, k)

            def conv_stage(jj, DGb):
                def f_():
                    cps = pss.next()
                    for tap in range(31):
                        mm(cps[:], DGb[:, tap, :], u[:, jj, tap:tap + 512], tap == 0, tap == 30, [DGb, u], [cps])
                    act(cc[:, jj, :], cps[:], AF.Identity, [cps, convb], [cc], bias=convb[:, jj:jj + 1])
                    csq = csqs.next()
                    act(csq[:], cps[:], AF.Square, [cps, convb], [csq], bias=convb[:, jj:jj + 1])

                    def g_():
                        mm(mps[:], avg1024[:], cc[:, jj, :], jj == 0, jj == 7, [avg1024, cc], [mps])
                        mm(qps[:], avg1024[:], csq[:], jj == 0, jj == 7, [avg1024, csq], [qps])
                    pend_pe.append(g_)
                return f_

            for half in range(2):
                wg = load_wg(Wb_in0, 2 + half)
                for j in range(4):
                    ps = fm_tile2(wg, j, hTc)
                    act(sgl[:, j, :], ps[:], AF.Sigmoid, [ps], [sgl])
                wg = load_wg(Wb_in0, 0 + half)
                for j in range(4):
                    jj = half * 4 + j
                    ps = fm_tile2(wg, j, hTc)
                    tt("dve", u[:, jj, 30:542], ps[:], sgl[:, j, :], ALU.mult, [ps, sgl], [u])
                    DGb = DGs.next()
                    tt("pool", DGb[:], bc(ident_b[:].unsqueeze(1), [128, 31, 128]),
                       bc(kw_b[:, jj, :].unsqueeze(2), [128, 31, 128]), ALU.mult, [ident_b, kw_b], [DGb])
                    pend_pe.append(conv_stage(jj, DGb))
            for half in range(2):
                wg = load_wg(Wb_in0, 4 + half)
                for j in range(4):
                    ps = fm_tile2(wg, j, hTc)
                    act(sga[:, half * 4 + j, :], ps[:], AF.Silu, [ps], [sga])
            run_pending()
            run_pending()
            copy("pool", u[:, :, 0:30], u[:, :, 512:542], [u], [u])
            ln_jobs = []

            def ln_head():
                act(mean_sb[:], mps[:], AF.Copy, [mps], [mean_sb])
                tt("dve", var[:], mean_sb[:], mean_sb[:], ALU.mult, [mean_sb], [var])
                tt("dve", var[:], qps[:], var[:], ALU.subtract, [qps, var], [var])
                act(var[:], var[:], AF.Ln, [var, eps_t], [var], bias=eps_t[:, 0:1])
                act(rsl[:], var[:], AF.Exp, [var], [rsl], scale=-0.5)
            ln_jobs.append(ln_head)

            def ln_tile(jj):
                def f_():
                    tl = tln.next()
                    al = aln.next()
                    tt("dve", tl[:], cc[:, jj, :], mean_sb[:], ALU.subtract, [cc, mean_sb], [tl])
                    tt("dve", tl[:], tl[:], rsl[:], ALU.mult, [tl, rsl], [tl])
                    act(al[:], tl[:], AF.Silu, [tl, clng, clnb], [al], scale=clng[:, jj:jj + 1], bias=clnb[:, jj:jj + 1])
                    oa = oas.next()
                    tt("pool", oa[:], al[:], sga[:, jj, :], ALU.mult, [al, sga], [oa])
                    kb.dma("pool", MIXT.t[jj, :, t0:t0 + 512], oa[:], rd=[oa], wr=[MIXT])
                return f_
            for jj in range(8):
                ln_jobs.append(ln_tile(jj))

            def qk_stage(qg, sq, isq, dst, hh):
                def f_():
                    qo = qos.next()
                    stp = pss.next()
                    mm(stp[:], bd64[:], sq[:], True, True, [bd64, sq], [stp])
                    rtp = pss.next()
                    mm(rtp[:], Pm[:], qg[:], True, True, [Pm, qg], [rtp])
                    act(rsq[:], stp[:], AF.Ln, [stp, eps_t], [rsq], bias=eps_t[:, 0:1])
                    act(rsq[:], rsq[:], AF.Exp, [rsq], [rsq], scale=-0.5)
                    tt("pool", t1[:], qg[:], rc[:, 0, :], ALU.mult, [qg, rc], [t1])
                    tt("dve", t2[:], rtp[:], rc[:, 1, :], ALU.mult, [rtp, rc], [t2])
                    tt("pool", t1[:], t1[:], t2[:], ALU.add, [t1, t2], [t1])
                    stt(qo[:], t1[:], 0.125 if isq else 1.0, rsq[:], ALU.mult, ALU.mult, [t1, rsq], [qo])
                    kb.dma("pool", dst.t[hh, :, t0:t0 + 512], qo[:], rd=[qo], wr=[dst])
                return f_

            for gidx, gi in enumerate((6, 7, 8, 9)):
                wg = load_wg(Wb_in0, gi)
                isq = gi < 8
                gvec = qng if isq else kng
                dst = QT if isq else KT
                for j in range(4):
                    hh = (gi % 2) * 4 + j
                    ps = fm_tile2(wg, j, hTc)
                    qg = qgs.next()
                    sq = sqs.next()
                    act(qg[:], ps[:], AF.Copy, [ps, gvec], [qg], scale=gvec[:, 0:1])
                    act(sq[:], ps[:], AF.Square, [ps], [sq])
                    pend_pe.append(qk_stage(qg, sq, isq, dst, hh))
                    if ln_jobs:
                        ln_jobs.pop(0)()
                prefetch_step(gidx)
            for gi in (10, 11):
                wg = load_wg(Wb_in0, gi)
                for tti in range(4):
                    ps = pss.next()
                    for c in range(16):
                        mm(ps[:], hTc[:, c, tti * 128:(tti + 1) * 128], wg[:, c, :], c == 0, c == 15, [wg, hTc], [ps])
                    run_pending()
                    vt = vts.next()
                    act(vt[:], ps[:], AF.Copy, [ps], [vt])
                    h0 = (gi - 10) * 4
                    kb.dma("pool", VV.t[h0:h0 + 4, :, bi * 4 + tti, :].rearrange("h p d -> p h d"),
                           vt[:].rearrange("p (h d) -> p h d", h=4), rd=[vt], wr=[VV])
                if gi == 10:
                    prefetch_step(4)
            for gi in (12, 13):
                wg = load_wg(Wb_in0, gi)
                for j in range(4):
                    hh = (gi - 12) * 4 + j
                    ps = fm_tile2(wg, j, hTc)
                    gbt = gbts.next()
                    act(gbt[:], ps[:], AF.Silu, [ps], [gbt])
                    kb.dma("pool", GB.t[hh, :, t0:t0 + 512], gbt[:], rd=[gbt], wr=[GB])
            run_pending()
            hT_cur = hT_next
        kb.flush()

    if stop_after == "l0p":
        kb.final_wait()
        return nc

    with ExitStack() as st:
        kTas = Rot([kb.sb(st, f"kTa{i}", [128, S], BF16) for i in range(2)])
        kTbs = Rot([kb.sb(st, f"kTb{i}", [128, S], BF16) for i in range(2)])
        qTs = Rot([kb.sb(st, f"qT{i}", [128, S], BF16) for i in range(2)])
        vhs = Rot([kb.sb(st, f"vh{i}", [128, 32, 128], BF16) for i in range(2)])
        gbs = Rot([kb.sb(st, f"gbh{i}", [128, S], BF16) for i in range(2)])
        e1s = Rot([kb.sb(st, f"e1_{i}", [128, 512], BF16) for i in range(3)])
        e2s = Rot([kb.sb(st, f"e2_{i}", [128, 512], BF16) for i in range(3)])
        pscore = Rot([kb.ps(st, f"psc{i}", [128, 512], F32) for i in range(4)])
        o1 = kb.ps(st, "o1", [128, 512], F32)
        d1 = kb.ps(st, "d1", [128, 512], F32)
        o2 = kb.ps(st, "o2", [128, 512], F32)
        d2 = kb.ps(st, "d2", [128, 512], F32)

        def FA(name, n=2, dt=F32):
            return Rot([kb.sb(st, f"{name}{i}", [128, 512], dt) for i in range(n)])
        rd1s = FA("rd1"); rd2s = FA("rd2"); c1s = FA("c1"); c2s = FA("c2"); ods = FA("od"); osqs = FA("osq"); rsfs = FA("rsf")
        obs = FA("ob", 2, BF16)
        for b_ in kTas.bufs:
            memset("pool", b_[64:128, :], 0.0, [b_])
        for b_ in kTbs.bufs:
            memset("pool", b_[0:64, :], 0.0, [b_])

        def finalize1():
            rd1 = rd1s.next(); rd2 = rd2s.next(); c1 = c1s.next(); c2 = c2s.next(); od = ods.next(); osq = osqs.next()
            copy("dve", c1[:], o1[:], [o1], [c1])
            act(rd1[:], d1[:], AF.Ln, [d1], [rd1])
            copy("dve", c2[:], o2[:], [o2], [c2])
            act(rd2[:], d2[:], AF.Ln, [d2], [rd2])
            act(rd1[:], rd1[:], AF.Exp, [rd1], [rd1], scale=-1.0)
            act(rd2[:], rd2[:], AF.Exp, [rd2], [rd2], scale=-1.0)
            tt("dve", c1[:], c1[:], rd1[:], ALU.mult, [c1, rd1], [c1])
            tt("dve", c2[:], c2[:], rd2[:], ALU.mult, [c2, rd2], [c2])
            stt(od[:], c2[:], neglam[:, 0:1], c1[:], ALU.mult, ALU.add, [c2, neglam, c1], [od])
            return od, osq

        def finalize2(h, qs, gb, od, osq):
            rsf = rsfs.next()
            act(osq[:], od[:], AF.Square, [od], [osq])
            stp = pscore.next()
            mm(stp[:], avg128[:], osq[:], True, True, [avg128, osq], [stp])
            act(rsf[:], stp[:], AF.Ln, [stp, eps_t], [rsf], bias=eps_t[:, 0:1])
            act(rsf[:], rsf[:], AF.Exp, [rsf], [rsf], scale=-0.5)
            tt("dve", od[:], od[:], rsf[:], ALU.mult, [od, rsf], [od])
            ob = obs.next()
            stt(ob[:], od[:], sgs[:, 0:1], gb[:, qs], ALU.mult, ALU.mult, [od, sgs, gb], [ob])
            kb.dma("pool", MIXT.t[8 + h, :, qs], ob[:], rd=[ob], wr=[MIXT])

        wconv_jobs = ([(Wb_out0, din["e_w_out"], g_, "wc1") for g_ in range(4)] + [(Wb_in1, din["o_w_in"], g_, "wc2") for g_ in range(8)]
                      + [(Wb_glu, din["o_w_glu"], g_, "wc3") for g_ in range(4)] + [(Wb_out1, din["o_w_out"], g_, "wc4") for g_ in range(4)])

        def conv_some(n_):
            for _ in range(n_):
                if wconv_jobs:
                    dst_, src_, g_, key_ = wconv_jobs.pop(0)
                    v_ = src_.rearrange("(c p) (g n) -> g p c n", p=128, n=512)
                    kb.dma("pool", dst_.t[g_], v_[g_], wr=[dst_], key=key_, defer=True)
        TGA = 2
        TBs = Rot([kb.sb(st, f"TB{i}", [128, TGA, 2, 512], F32) for i in range(2)])
        Yt1 = kb.sb(st, "Yt1", [128, TGA, 256], F32)
        Yt2 = kb.sb(st, "Yt2", [128, TGA, 256], F32)
        C16 = kb.sb(st, "C16", [128, 128, 16], F32)
        S16 = kb.sb(st, "S16", [128, 128, 16], F32)
        Zt1 = kb.sb(st, "Zt1", [128, 128, 8], F32)
        Zt2 = kb.sb(st, "Zt2", [128, 128, 8], F32)
        memset("pool", C16[:, :, 0:1], 1.0, [C16])
        memset("pool", S16[:, :, 0:1], 0.0, [S16])
        for j in range(4):
            n = 1 << j
            cn_ = bc(CLv[:, j, :].unsqueeze(2), [128, 128, n])
            sn_ = bc(SLv[:, j, :].unsqueeze(2), [128, 128, n])
            tt("pool", Zt1[:, :, 0:n], C16[:, :, 0:n], cn_, ALU.mult, [C16, CLv], [Zt1])
            tt("pool", Zt2[:, :, 0:n], S16[:, :, 0:n], sn_, ALU.mult, [S16, SLv], [Zt2])
            tt("pool", C16[:, :, n:2 * n], Zt1[:, :, 0:n], Zt2[:, :, 0:n], ALU.subtract, [Zt1, Zt2], [C16])
            tt("pool", Zt1[:, :, 0:n], S16[:, :, 0:n], cn_, ALU.mult, [S16, CLv], [Zt1])
            tt("pool", Zt2[:, :, 0:n], C16[:, :, 0:n], sn_, ALU.mult, [C16, SLv], [Zt2])
            tt("pool", S16[:, :, n:2 * n], Zt1[:, :, 0:n], Zt2[:, :, 0:n], ALU.add, [Zt1, Zt2], [S16])

        def gen_table_set(ts_):
            TB_ = TBs.next()
            gs_ = slice(TGA * ts_, TGA * ts_ + TGA)
            Cc = TB_[:, :, 0, :]
            Sc = TB_[:, :, 1, :]
            copy("pool", Cc[:, :, 0:16], C16[:, gs_, :], [C16], [TB_])
            copy("pool", Sc[:, :, 0:16], S16[:, gs_, :], [S16], [TB_])
            for j in range(4, 9):
                n = 1 << j
                cn_ = bc(CLv[:, j, gs_].unsqueeze(2), [128, TGA, n])
                sn_ = bc(SLv[:, j, gs_].unsqueeze(2), [128, TGA, n])
                tt("pool", Yt1[:, :, 0:n], Cc[:, :, 0:n], cn_, ALU.mult, [TB_, CLv], [Yt1])
                tt("pool", Yt2[:, :, 0:n], Sc[:, :, 0:n], sn_, ALU.mult, [TB_, SLv], [Yt2])
                tt("pool", Cc[:, :, n:2 * n], Yt1[:, :, 0:n], Yt2[:, :, 0:n], ALU.subtract, [Yt1, Yt2], [TB_])
                tt("pool", Yt1[:, :, 0:n], Sc[:, :, 0:n], cn_, ALU.mult, [TB_, CLv], [Yt1])
                tt("pool", Yt2[:, :, 0:n], Cc[:, :, 0:n], sn_, ALU.mult, [TB_, SLv], [Yt2])
                tt("pool", Sc[:, :, n:2 * n], Yt1[:, :, 0:n], Yt2[:, :, 0:n], ALU.add, [Yt1, Yt2], [TB_])
            kb.dma("pool", TAB.t[gs_].rearrange("g p a c -> p g a c"), TB_[:], rd=[TB_], wr=[TAB])

        next_set = [0]
        pending_fin = None
        for h in range(8):
            kTa = kTas.next(); kTb = kTbs.next(); qT = qTs.next(); vh = vhs.next(); gb = gbs.next()
            kb.dma("sp", kTa[0:64, :], KT.t[h, 0:64, :], rd=[KT], wr=[kTa])
            kb.dma("sp", kTb[64:128, :], KT.t[h, 64:128, :], rd=[KT], wr=[kTb])
            kb.dma("sp", qT[:], QT.t[h], rd=[QT], wr=[qT])
            kb.dma("sp", vh[:], VV.t[h], rd=[VV], wr=[vh])
            kb.dma("sp", gb[:], GB.t[h], rd=[GB], wr=[gb])
            for qb in range(8):
                qs = slice(qb * 512, (qb + 1) * 512)
                nkt = 4 * (qb + 1)
                if qb in (2, 4, 6):
                    conv_some(1)

                def scores(kt, qb=qb, qs=qs):
                    ks = slice(kt * 128, (kt + 1) * 128)
                    o = max(0, kt - 4 * qb)
                    c0 = 128 * o
                    qcs = slice(qb * 512 + c0, (qb + 1) * 512)
                    s1 = pscore.next(); s2 = pscore.next()
                    mm(s1[:, c0:512], kTa[:, ks], qT[:, qcs], True, True, [kTa, qT], [s1])
                    mm(s2[:, c0:512], kTb[:, ks], qT[:, qcs], True, True, [kTb, qT], [s2])
                    e1 = e1s.next(); e2 = e2s.next()
                    act(e1[:, c0:512], s1[:, c0:512], AF.Exp, [s1], [e1])
                    act(e2[:, c0:512], s2[:, c0:512], AF.Exp, [s2], [e2])
                    if kt >= 4 * qb:
                        tt("dve", e1[:, c0:c0 + 128], e1[:, c0:c0 + 128], M4[:, 0, 0:128], ALU.mult, [e1, M4], [e1])
                        tt("dve", e2[:, c0:c0 + 128], e2[:, c0:c0 + 128], M4[:, 0, 0:128], ALU.mult, [e2, M4], [e2])
                    return e1, e2, c0

                pend = scores(0)
                for kt in range(nkt):
                    nxt = scores(kt + 1) if kt + 1 < nkt else None
                    e1, e2, c0 = pend
                    first, lastk = kt == 0, kt == nkt - 1
                    mm(o1[:, c0:512], vh[:, kt, :], e1[:, c0:512], first, lastk, [vh, e1], [o1])
                    mm(d1[:, c0:512], ones_b[:], e1[:, c0:512], first, lastk, [ones_b, e1], [d1])
                    mm(o2[:, c0:512], vh[:, kt, :], e2[:, c0:512], first, lastk, [vh, e2], [o2])
                    mm(d2[:, c0:512], ones_b[:], e2[:, c0:512], first, lastk, [ones_b, e2], [d2])
                    pend = nxt
                    if kt == min(2, nkt - 1) and pending_fin is not None:
                        finalize2(*pending_fin)
                        pending_fin = None
                if pending_fin is not None:
                    finalize2(*pending_fin)
                od, osq = finalize1()
                pending_fin = (h, qs, gb, od, osq)
                if next_set[0] < 128 // TGA:
                    gen_table_set(next_set[0])
                    next_set[0] += 1
        finalize2(*pending_fin)
        conv_some(len(wconv_jobs))
        kb.flush()

    if stop_after == "l0a":
        kb.final_wait()
        return nc

    UF = kb.dram("UF", [16, 128, S], BF16, kind=kind("UF"))
    GF = kb.dram("GF", [16, 128, S], BF16, kind=kind("GF"))
    with ExitStack() as st:
        mixTs = Rot([kb.sb(st, f"mixT{i}", [128, 16, 512], BF16) for i in range(2)])
        wgs = Rot([kb.sb(st, f"wg{i}", [128, 16, 512], BF16) for i in range(2)])
        xts8 = [kb.sb(st, f"xq{i}", [128, D], F32) for i in range(8)]
        hn = kb.sb(st, "hn", [128, D], BF16)
        junk = kb.sb(st, "junk", [128, D], BF16)
        ss = kb.sb(st, "ss", [128, 1], F32)
        sd1 = kb.sb(st, "sd1", [128, 1], F32)
        rs1 = kb.sb(st, "rs1", [128, 1], F32)
        h1T = kb.sb(st, "h1T", [128, 16, 512], BF16)
        ubs = Rot([kb.sb(st, f"ub{i}", [128, 512], BF16) for i in range(3)])
        pss = Rot([kb.ps(st, f"ps{i}", [128, 512], F32) for i in range(4)])
        ptrs = Rot([kb.ps(st, f"ptr{i}", [128, 4, 128], BF16) for i in range(2)])

        def load_wg(src, gi):
            wg = wgs.next()
            kb.dma("sp", wg[:], src.t[gi], rd=[src], wr=[wg])
            return wg

        def norm_transpose_perm(xt, gn, hTb, tti):
            act(junk[:], xt[:], AF.Square, [xt], [junk, ss], accum=ss[:, 0:1])
            act(sd1[:], ss[:], AF.Ln, [ss, eps_t], [sd1], bias=eps_t[:, 0:1], scale=1.0 / D)
            act(rs1[:], sd1[:], AF.Exp, [sd1], [rs1], scale=-0.5)
            act(hn[:], xt[:], AF.Copy, [xt, rs1], [hn], scale=rs1[:, 0:1])
            for c4 in range(4):
                ptr = ptrs.next()
                for q_ in range(4):
                    c = c4 * 4 + q_
                    kb.op("pe", lambda c=c, q_=q_, ptr=ptr: pe.transpose(out=ptr[:, q_, :], in_=hn[:, c * 128:(c + 1) * 128],
                                                                           identity=ident_b[:]), [hn, ident_b], [ptr])
                oap = hTb[:, c4 * 4:c4 * 4 + 4, :].rearrange("p k (t c) -> p k t c", t=8)[:, :, :, 16 * tti:16 * tti + 16]
                iap = ptr[:].rearrange("p k (c t) -> p k t c", t=8)
                tt("dve", oap, iap, bc(gn[:, c4 * 4:c4 * 4 + 4].unsqueeze(2).unsqueeze(3), [128, 4, 8, 16]), ALU.mult,
                   [ptr, gn], [hTb])

        def l0o_loads(bi_):
            t0_ = bi_ * 512
            mixT_ = mixTs.next()
            kb.dma("sp", mixT_[:], MIXT.t[:, :, t0_:t0_ + 512].rearrange("c p t -> p c t"), rd=[MIXT], wr=[mixT_])
            xs_ = xts8[(bi_ % 2) * 4:(bi_ % 2) * 4 + 4]
            for tti in range(4):
                kb.dma("sp", xs_[tti][:], x[t0_ + tti * 128:t0_ + (tti + 1) * 128, :], wr=[xs_[tti]])
            return mixT_, xs_

        nxt_loads = l0o_loads(0)
        for bi in range(NB):
            t0 = bi * 512
            mixT, xts4 = nxt_loads
            for og in range(4):
                wg = load_wg(Wb_out0, og)
                for tti in range(4):
                    ps = pss.next()
                    for c in range(16):
                        mm(ps[:], mixT[:, c, tti * 128:(tti + 1) * 128], wg[:, c, :], c == 0, c == 15, [mixT, wg], [ps])
                    xs = xts4[tti][:, og * 512:(og + 1) * 512]
                    tt("dve", xs, ps[:], xs, ALU.add, [ps, xts4[tti]], [xts4[tti]])
            if bi + 1 < NB:
                nxt_loads = l0o_loads(bi + 1)
            for tti in range(4):
                kb.dma("pool", X1.t[t0 + tti * 128:t0 + (tti + 1) * 128, :], xts4[tti][:], rd=[xts4[tti]], wr=[X1])
                norm_transpose_perm(xts4[tti], gn1, h1T, tti)
            for gi in range(8):
                wg = load_wg(Wb_in1, gi)
                for j in range(4):
                    ps = pss.next()
                    for c in range(16):
                        mm(ps[:], wg[:, c, j * 128:(j + 1) * 128], h1T[:, c, :], c == 0, c == 15, [wg, h1T], [ps])
                    ub = ubs.next()
                    ft = (gi % 4) * 4 + j
                    dst = UF if gi < 4 else GF
                    act(ub[:], ps[:], AF.Copy if gi < 4 else AF.Silu, [ps], [ub])
                    kb.dma("pool", dst.t[ft].rearrange("p (t c) -> p t c", t=8)[:, :, bi * 64:(bi + 1) * 64],
                           ub[:].rearrange("p (t c) -> p t c", t=8), rd=[ub], wr=[dst])
        kb.flush()

    if stop_after == "l1p":
        kb.final_wait()
        return nc

    with ExitStack() as st:
        tps = Rot([kb.ps(st, f"tpsm{i}", [128, 128], F32) for i in range(4)])
        fR = keep_t["fR"]; fI = keep_t["fI"]; nuR = keep_t["nuR"]; nuI = keep_t["nuI"]; lR = keep_t["lR"]; muI = keep_t["muI"]
        l7R = keep_t["l7R"]; l7I = keep_t["l7I"]; mask8 = keep_t["mask8"]
        GB_ = 16

        def R2(name, shape):
            return Rot([kb.sb(st, f"{name}{i}", shape, F32) for i in range(2)])
        BAs = R2("BA", [128, GB_, 16]); BAps = R2("BAp", [128, GB_, 16])
        CN = kb.sb(st, "CN", [128, 128], F32)
        CNp = kb.sb(st, "CNp", [128, 128], F32)
        VAs = R2("VA", [128, GB_, 8, 16]); VAps = R2("VAp", [128, GB_, 8, 16])
        WAs = R2("WA", [128, GB_, 9, 16]); WAps = R2("WAp", [128, GB_, 9, 16])
        WBAs = R2("WBA", [128, GB_, 8, 16]); WBAps = R2("WBAp", [128, GB_, 8, 16])
        y1 = kb.sb(st, "y1", [128, GB_, 16], F32)
        y2 = kb.sb(st, "y2", [128, GB_, 16], F32)
        y3 = kb.sb(st, "y3", [128, GB_, 16], F32)
        y4 = kb.sb(st, "y4", [128, GB_, 16], F32)
        S5ts = Rot([kb.sb(st, f"S5t{i}", [128, GB_, 4, 128], BF16) for i in range(2)])

        ytmp = {"dve": (y1, y2), "pool": (y3, y4)}

        def cmat(e_, oA, oAp, iA, iAp, zR, zI, g0, rdo, wro):
            zr = bc(zR[:, g0:g0 + GB_].unsqueeze(2), [128, GB_, 16])
            zi = bc(zI[:, g0:g0 + GB_].unsqueeze(2), [128, GB_, 16])
            ya, yb = ytmp[e_]
            tt(e_, ya[:], iA, zr, ALU.mult, rdo + [zR], [ya])
            tt(e_, yb[:], iAp, zi, ALU.mult, rdo + [zI], [yb])
            tt(e_, oA, ya[:], yb[:], ALU.add, [ya, yb], wro)
            tt(e_, ya[:], iAp, zr, ALU.mult, rdo + [zR], [ya])
            tt(e_, yb[:], iA, zi, ALU.mult, rdo + [zI], [yb])
            tt(e_, oAp, ya[:], yb[:], ALU.subtract, [ya, yb], wro)

        def stage_A(gb_):
            g0 = gb_ * GB_
            gsl = slice(g0, g0 + GB_)
            BA = BAs.next(); BAp = BAps.next(); VA = VAs.next(); VAp = VAps.next(); WA = WAs.next(); WAp = WAps.next()
            WBA = WBAs.next(); WBAp = WBAps.next()
            kb.dma("sp", BA[0:64], din["o_B_re"][gsl].rearrange("g p m -> p g m"), wr=[BA], key="s5ld")
            kb.dma("sp", BA[64:128], din["o_B_im"][gsl].rearrange("g p m -> p g m"), wr=[BA], key="s5ld")
            kb.dma("sp", BAp[0:64], din["o_B_im"][gsl].rearrange("g p m -> p g m"), wr=[BAp], key="s5ld")
            kb.dma("sp", BAp[64:128], din["o_B_re"][gsl].rearrange("g p m -> p g m"), wr=[BAp], key="s5ld")
            ts("dve", BAp[0:64], BAp[0:64], -1.0, None, ALU.mult, None, [BAp], [BAp])
            for hb in range(GB_ // 8):
                gs8 = slice(g0 + hb * 8, g0 + hb * 8 + 8)
                kb.dma("sp", CN[:, 0:64], din["o_C_re"][gs8].rearrange("g m p -> (g m) p"), wr=[CN], key="s5ld")
                kb.dma("sp", CN[:, 64:128], din["o_C_im"][gs8].rearrange("g m p -> (g m) p"), wr=[CN], key="s5ld")
                kb.dma("sp", CNp[:, 0:64], din["o_C_im"][gs8].rearrange("g m p -> (g m) p"), wr=[CNp], key="s5ld")
                kb.dma("sp", CNp[:, 64:128], din["o_C_re"][gs8].rearrange("g m p -> (g m) p"), wr=[CNp], key="s5ld")
                tp = tps.next()
                kb.op("pe", lambda tp=tp: pe.transpose(out=tp[:], in_=CN[:], identity=ident_f[:]), [CN, ident_f], [tp])
                copy("dve", WA[:, hb * 8:hb * 8 + 8, 0, :], tp[:].rearrange("p (g m) -> p g m", m=16), [tp], [WA])
                tp = tps.next()
                kb.op("pe", lambda tp=tp: pe.transpose(out=tp[:], in_=CNp[:], identity=ident_f[:]), [CNp, ident_f], [tp])
                copy("dve", WAp[:, hb * 8:hb * 8 + 8, 0, :], tp[:].rearrange("p (g m) -> p g m", m=16), [tp], [WAp])
            ts("dve", WA[64:128, :, 0, :], WA[64:128, :, 0, :], -1.0, None, ALU.mult, None, [WA], [WA])
            cmat("dve", VA[:, :, 0, :], VAp[:, :, 0, :], BA[:], BAp[:], fR, fI, g0, [BA, BAp], [VA, VAp])
            for k_ in range(8):
                cmat("pool", WA[:, :, k_ + 1, :], WAp[:, :, k_ + 1, :], WA[:, :, k_, :], WAp[:, :, k_, :], lR, muI, g0, [WA, WAp], [WA, WAp])
            for sg in range(7):
                cmat("dve", VA[:, :, sg + 1, :], VAp[:, :, sg + 1, :], VA[:, :, sg, :], VAp[:, :, sg, :], nuR, nuI, g0, [VA, VAp], [VA, VAp])
            for sg in range(8):
                cmat("dve", WBA[:, :, sg, :], WBAp[:, :, sg, :], VA[:, :, sg, :], VAp[:, :, sg, :], l7R, l7I, g0, [VA, VAp], [WBA, WBAp])
            return gsl, VA, WA, WBA, WBAp

        def stage_B(stA):
            gsl, VA, WA, WBA, WBAp = stA
            S5t = S5ts.next()
            for g in range(GB_):
                tp = tps.next()
                kb.op("pe", lambda tp=tp, g=g: pe.matmul(tp[:], lhsT=VA[:, g, :, :].rearrange("p s m -> p (s m)"),
                                                          rhs=WA[:, g, 0:8, :].rearrange("p s m -> p (s m)"), start=True, stop=True),
                      [VA, WA], [tp])
                tt("dve", S5t[:, g, 2, :], tp[:], mask8[:], ALU.mult, [tp, mask8], [S5t])
                stt(S5t[:, g, 2, :], ident_f[:], Dc[:, gsl.start + g:gsl.start + g + 1], S5t[:, g, 2, :], ALU.mult, ALU.add,
                    [ident_f, Dc, S5t], [S5t])
                tp = tps.next()
                kb.op("pe", lambda tp=tp, g=g: pe.transpose(out=tp[:], in_=WBA[:, g, :, :].rearrange("p s m -> p (s m)"),
                                                             identity=ident_f[:]), [WBA, ident_f], [tp])
                act(S5t[:, g, 0, :], tp[:], AF.Copy, [tp], [S5t])
                tp = tps.next()
                kb.op("pe", lambda tp=tp, g=g: pe.transpose(out=tp[:], in_=WBAp[:, g, :, :].rearrange("p s m -> p (s m)"),
                                                             identity=ident_f[:]), [WBAp, ident_f], [tp])
                act(S5t[:, g, 1, :], tp[:], AF.Copy, [tp], [S5t], scale=-1.0)
            copy("pool", S5t[:, :, 3, :].rearrange("p g (s m) -> p g s m", m=16), WA[:, :, 1:9, :], [WA], [S5t])
            kb.dma("pool", S5W.t[gsl].rearrange("g p k c -> p g k c"), S5t[:], rd=[S5t], wr=[S5W])

        nb_ = 128 // GB_
        pendA = stage_A(0)
        for gb_ in range(nb_):
            nxtA = stage_A(gb_ + 1) if gb_ + 1 < nb_ else None
            stage_B(pendA)
            pendA = nxtA
        kb.flush()

    if stop_after == "s5setup":
        kb.final_wait()
        return nc

    ZF = kb.dram("ZF", [16, 128, S], BF16, kind=kind("ZF"))
    with ExitStack() as st:
        Tbs = Rot([kb.sb(st, f"Tb{i}", [128, 2, 512], F32) for i in range(9)])
        pAs = Rot([kb.ps(st, f"pA{i}", [128, 512], F32) for i in range(2)])
        pBs = Rot([kb.ps(st, f"pB{i}", [128, 512], F32) for i in range(2)])
        pYs = Rot([kb.ps(st, f"pY{i}", [128, 512], F32) for i in range(2)])
        pGs = Rot([kb.ps(st, f"pG{i}", [128, 512], F32) for i in range(2)])

        def FR(name, n, dt=F32):
            return Rot([kb.sb(st, f"{name}{i}", [128, 512], dt) for i in range(n)])
        t1s = FR("st1", 2); t2s = FR("st2", 2); cAs = FR("cA", 2); gAs = FR("gA", 3); t5s = FR("st5", 2); t6s = FR("st6", 2)
        ysbs = FR("ysb", 5); y2s = FR("y2", 2); w2s = FR("w2", 2); sgms = FR("sgm", 2); Hps = FR("Hp", 2, BF16); zts = FR("zt", 3, BF16)
        Ucs = Rot([kb.sb(st, f"Ucm{i}", [128, 512], BF16) for i in range(9)])
        Wgs = Rot([kb.sb(st, f"Wgm{i}", [128, 4, 128], BF16) for i in range(9)])
        for hp in Hps.bufs:
            memset("pool", hp[:, 0:1], 0.0, [hp])
        ctx = {}

        def tab(g):
            Tb = ctx[g]["Tb"]
            return Tb, Tb, Tb[:, 0, :], Tb[:, 1, :]

        def s0(g):
            ft, j8 = divmod(g, 8)
            c = ctx[g] = {}
            c["Uc"] = Uc = Ucs.next(); c["Wg"] = Wg = Wgs.next()
            kb.dma("sp", Uc[:], UF.t[ft, 16 * j8:16 * j8 + 16, :].rearrange("m (t c) -> t m c", t=8), rd=[UF], wr=[Uc])
            kb.dma("sp", Wg[:], S5W.t[g], rd=[S5W], wr=[Wg])
            c["Tb"] = Tb = Tbs.next()
            kb.dma("sp", Tb[:], TAB.t[g], rd=[TAB], wr=[Tb])
            c["pA"] = pA = pAs.next(); c["pB"] = pB = pBs.next()
            mm(pA[:], Wg[:, 0, :], Uc[:], True, True, [Wg, Uc], [pA])
            mm(pB[:], Wg[:, 1, :], Uc[:], True, True, [Wg, Uc], [pB])

        def s1(g):
            c = ctx[g]
            COSc, SINc, co, si = tab(g)
            c["t1"] = t1 = t1s.next(); c["t2"] = t2 = t2s.next()
            tt("dve", t1[:], c["pA"][:], co, ALU.mult, [c["pA"], COSc], [t1])
            tt("dve", t2[:], c["pB"][:], si, ALU.mult, [c["pB"], SINc], [t2])

        def s2(g):
            c = ctx[g]
            c["cA"] = cA = cAs.next()
            tt("pool", cA[:], c["t1"][:], c["t2"][:], ALU.add, [c["t1"], c["t2"]], [cA])

        def s3(g):
            c = ctx[g]
            c["gA"] = gA = gAs.next()
            cA = c["cA"]
            rl = bc(RLt[:, g:g + 1], [128, 512])
            kb.op("dve", lambda: dve.tensor_tensor_scan(out=gA[:], data0=rl, data1=cA[:], initial=0.0, op0=ALU.mult, op1=ALU.add),
                  [RLt, cA], [gA])

        def s4(g):
            c = ctx[g]
            c["pG"] = pG = pGs.next()
            mm(pG[:], PiT[:], c["gA"][:], True, True, [PiT, c["gA"]], [pG])

        def s5(g):
            c = ctx[g]
            COSc, SINc, co, si = tab(g)
            c["t5"] = t5 = t5s.next(); c["t6"] = t6 = t6s.next()
            tt("dve", t5[:], c["gA"][:], co, ALU.mult, [c["gA"], COSc], [t5])
            tt("dve", t6[:], c["pG"][:], si, ALU.mult, [c["pG"], SINc], [t6])

        def s6(g):
            c = ctx[g]
            c["Hp"] = Hp = Hps.next()
            tt("pool", Hp[:, 1:512], c["t5"][:, 0:511], c["t6"][:, 0:511], ALU.subtract, [c["t5"], c["t6"]], [Hp])

        def s7(g):
            c = ctx[g]
            c["pY"] = pY = pYs.next()
            mm(pY[:], c["Wg"][:, 2, :], c["Uc"][:], True, False, [c["Wg"], c["Uc"]], [pY])
            mm(pY[:], c["Wg"][:, 3, :], c["Hp"][:], False, True, [c["Wg"], c["Hp"]], [pY])

        def s8(g):
            c = ctx[g]
            c["ysb"] = ysb = ysbs.next()
            act(ysb[:], c["pY"][:], AF.Copy, [c["pY"]], [ysb])

        def s9(g):
            c = ctx[g]
            c["y2"] = y2 = y2s.next()
            act(y2[:], c["ysb"][:], AF.Square, [c["ysb"]], [y2])
            act(y2[:], y2[:], AF.Identity, [y2], [y2], scale=0.044715, bias=1.0)

        def s10(g):
            c = ctx[g]
            c["w2"] = w2 = w2s.next()
            tt("dve", w2[:], c["y2"][:], c["ysb"][:], ALU.mult, [c["y2"], c["ysb"]], [w2])

        def s11(g):
            c = ctx[g]
            c["sg"] = sg = sgms.next()
            act(sg[:], c["w2"][:], AF.Sigmoid, [c["w2"]], [sg], scale=2.0 * math.sqrt(2.0 / math.pi))

        def s12(g):
            ft, j8 = divmod(g, 8)
            c = ctx.pop(g)
            zt = zts.next()
            tt("pool", zt[:], c["ysb"][:], c["sg"][:], ALU.mult, [c["ysb"], c["sg"]], [zt])
            kb.dma("sp", ZF.t[ft, 16 * j8:16 * j8 + 16, :].rearrange("m (t c) -> t m c", t=8), zt[:], rd=[zt], wr=[ZF])

        stages = [s0, s1, s2, s3, s4, s5, s6, s7, s8, s9, s10, s11, s12]
        for it in range(128 + len(stages) - 1):
            for si_ in range(len(stages) - 1, -1, -1):
                g = it - si_
                if 0 <= g < 128:
                    stages[si_](g)
        kb.flush()

    if stop_after == "l1s":
        kb.final_wait()
        return nc

    with ExitStack() as st:
        zTs = Rot([kb.sb(st, f"zT{i}", [128, 16, 512], BF16) for i in range(2)])
        gTs = Rot([kb.sb(st, f"gT{i}", [128, 16, 512], BF16) for i in range(2)])
        wgs = Rot([kb.sb(st, f"wg{i}", [128, 16, 512], BF16) for i in range(2)])
        oT = kb.sb(st, "oT", [128, 16, 512], BF16)
        xq8 = [kb.sb(st, f"xr{i}", [128, D], F32) for i in range(8)]
        sgs_ = Rot([kb.sb(st, f"sgg{i}", [128, 512], BF16) for i in range(2)])
        pss = Rot([kb.ps(st, f"ps{i}", [128, 512], F32) for i in range(6)])
        X1v = X1.t.rearrange("(c t) d -> t c d", t=8)
        OUTv = out_d.rearrange("(c t) d -> t c d", t=8)
        def l1g_loads(bi_):
            zT_ = zTs.next(); gT_ = gTs.next()
            kb.dma("sp", zT_[:], ZF.t[:, :, bi_ * 512:(bi_ + 1) * 512].rearrange("f p c -> p f c"), rd=[ZF], wr=[zT_])
            kb.dma("sp", gT_[:], GF.t[:, :, bi_ * 512:(bi_ + 1) * 512].rearrange("f p c -> p f c"), rd=[GF], wr=[gT_])
            xq_ = xq8[(bi_ % 2) * 4:(bi_ % 2) * 4 + 4]
            for tti in range(4):
                kb.dma("sp", xq_[tti][:], X1v[bi_, tti * 128:(tti + 1) * 128, :], rd=[X1], wr=[xq_[tti]])
            return zT_, gT_, xq_

        nxt_loads = l1g_loads(0)
        for bi in range(NB):
            zT, gT, xq = nxt_loads
            for gg in range(4):
                wg = wgs.next()
                kb.dma("sp", wg[:], Wb_glu.t[gg], rd=[Wb_glu], wr=[wg])
                for j in range(4):
                    ft = gg * 4 + j
                    ps = pss.next()
                    for c in range(16):
                        mm(ps[:], wg[:, c, j * 128:(j + 1) * 128], zT[:, c, :], c == 0, c == 15, [wg, zT], [ps])
                    sgt = sgs_.next()
                    act(sgt[:], ps[:], AF.Sigmoid, [ps, bglu], [sgt], bias=bglu[:, ft:ft + 1])
                    tt("dve", sgt[:], sgt[:], zT[:, ft, :], ALU.mult, [sgt, zT], [sgt])
                    tt("pool", oT[:, ft, :], sgt[:], gT[:, ft, :], ALU.mult, [sgt, gT], [oT])
            if bi + 1 < NB:
                nxt_loads = l1g_loads(bi + 1)
            for og in range(4):
                wg = wgs.next()
                kb.dma("sp", wg[:], Wb_out1.t[og], rd=[Wb_out1], wr=[wg])
                for tti in range(4):
                    ps = pss.next()
                    for c in range(16):
                        mm(ps[:], oT[:, c, tti * 128:(tti + 1) * 128], wg[:, c, :], c == 0, c == 15, [oT, wg], [ps])
                    xs = xq[tti][:, og * 512:(og + 1) * 512]
                    tt("dve", xs, ps[:], xs, ALU.add, [ps, xq[tti]], [xq[tti]])
            for tti in range(4):
                kb.dma("pool", OUTv[bi, tti * 128:(tti + 1) * 128, :], xq[tti][:], rd=[xq[tti]], key=f"outst{tti}")
        kb.flush()

    kb.final_wait()
    return nc


def make_in_maps(inputs):
    maps = []
    for c in range(NCORES):
        m = {"x": np.ascontiguousarray(inputs["x"][c % 4])}
        for n, shp in PARAMS:
            m[n] = np.ascontiguousarray(np.asarray(inputs[n]).reshape(shp))
        maps.append(m)
    return maps


def kernel(**inputs):
    nc = build()
    res = run_bass_kernel_spmd(nc, make_in_maps(inputs), core_ids=list(range(NCORES)))
    out = np.stack([res.results[c]["out"] for c in range(4)], axis=0)
    return out.astype(np.float32)
```
